# Optimizing a Trainium2 kernel written in Bass

```python
import math
import jax, jax.numpy as jnp
from jax import lax
import numpy as np

D_MODEL = 2048
BATCH = 1
SEQ = 8192
DEPTH = 4

DIL_PAIRS = ((128, 1), (512, 4), (2048, 16))
N_GROUPS = len(DIL_PAIRS)
A_HEADS = 8
A_HEAD_DIM = 128
A_WIDTH = A_HEADS * A_HEAD_DIM
A_SPAN = 128
A_BLOCK = 128
G_HEADS = 4
G_DK = D_MODEL // 2
G_DV = D_MODEL
G_HK = G_DK // G_HEADS
G_HV = G_DV // G_HEADS
G_RANK = 16
G_TAU = 16.0
G_CHUNK = 64
N_BUCKETS = 32
MAX_DIST = 2048
D_FF = 5632
CONV_W = 3
LN_EPS = 1e-5
ALPHA = (2 * DEPTH) ** 0.25
BETA = (8 * DEPTH) ** -0.25
IN_SIZES = (A_WIDTH,) * (3 * N_GROUPS) + (G_DK, G_DK, G_DV, G_DV, G_RANK, D_MODEL, D_MODEL)
IN_COLS = sum(IN_SIZES)

kernel_name = 'hybrid_dilated_gla_convffn'


def _layernorm(x, g, b):
    xf = x.astype(jnp.float32)
    mu = jnp.mean(xf, axis=-1, keepdims=True)
    var = jnp.mean(jnp.square(xf - mu), axis=-1, keepdims=True)
    return ((xf - mu) * lax.rsqrt(var + LN_EPS) * g + b).astype(x.dtype)


def _t5_bucket(dist):
    max_exact = N_BUCKETS // 2
    df = jnp.maximum(dist, 1).astype(jnp.float32)
    large = max_exact + (jnp.log(df / max_exact) / math.log(MAX_DIST / max_exact)
                         * (N_BUCKETS - max_exact)).astype(jnp.int32)
    return jnp.where(dist < max_exact, dist, jnp.minimum(large, N_BUCKETS - 1))


def _dilated_group(q, k, v, bias_tab, dilation):
    b, s, h, e = q.shape
    n_sub = s // dilation
    nb = -(-n_sub // A_BLOCK)
    pad = nb * A_BLOCK - n_sub

    def to_blocks(t):
        t = t.reshape(b, n_sub, dilation, h, e).transpose(0, 2, 1, 3, 4)
        t = jnp.pad(t, ((0, 0), (0, 0), (0, pad), (0, 0), (0, 0)))
        return t.reshape(b, dilation, nb, A_BLOCK, h, e)

    def with_prev(t):
        prev = jnp.concatenate([jnp.zeros_like(t[:, :, :1]), t[:, :, :-1]], axis=2)
        return jnp.concatenate([prev, t], axis=3)

    qb = to_blocks(q)
    kb = with_prev(to_blocks(k))
    vb = with_prev(to_blocks(v))
    qi = jnp.arange(A_BLOCK)[:, None]
    ci = jnp.arange(2 * A_BLOCK)[None, :]
    off = A_BLOCK + qi - ci
    valid = (off >= 0) & (off <= A_SPAN)
    first = (jnp.arange(nb) == 0)[:, None, None] & (ci < A_BLOCK)[None]
    mask = valid[None] & ~first
    bucket = _t5_bucket(dilation * jnp.clip(off, 0, A_SPAN))
    bias = jnp.transpose(bias_tab[bucket], (2, 0, 1)).astype(jnp.float32)
    sc = jnp.einsum('brnqhe,brnkhe->brnhqk', qb, kb).astype(jnp.float32) * (e ** -0.5) + bias
    sc = jnp.where(mask[None, None, :, None], sc, -1e30)
    m = jnp.max(sc, axis=-1, keepdims=True)
    p = jnp.exp(sc - m)
    l = jnp.sum(p, axis=-1, keepdims=True)
    o = jnp.einsum('brnhqk,brnkhe->brnqhe', (p / l).astype(v.dtype), vb)
    lse = (m + jnp.log(l))[..., 0]
    o = o.reshape(b, dilation, nb * A_BLOCK, h, e)[:, :, :n_sub]
    o = o.transpose(0, 2, 1, 3, 4).reshape(b, s, h, e)
    lse = lse.transpose(0, 1, 2, 4, 3).reshape(b, dilation, nb * A_BLOCK, h)[:, :, :n_sub]
    lse = lse.transpose(0, 2, 1, 3).reshape(b, s, h)
    return o, lse


def _gla(q, k, v, log_a):
    b, s, h, dk = q.shape
    dv = v.shape[-1]
    n = s // G_CHUNK
    q, k, log_a = (t.astype(jnp.float32).reshape(b, n, G_CHUNK, h, dk) for t in (q, k, log_a))
    v = v.astype(jnp.float32).reshape(b, n, G_CHUNK, h, dv)
    cum = jnp.cumsum(log_a, axis=2)
    last = cum[:, :, -1]
    q_dec = q * jnp.exp(cum)
    k_inv = k * jnp.exp(-cum)
    k_end = k * jnp.exp(last[:, :, None] - cum)
    causal = jnp.tril(jnp.ones((G_CHUNK, G_CHUNK), dtype=bool))
    att = jnp.where(causal, jnp.einsum('bnihd,bnjhd->bnhij', q_dec, k_inv), 0.0)
    o_intra = jnp.einsum('bnhij,bnjhe->bnihe', att, v)

    def step(state, inp):
        qc, kc, vc, lc = inp
        o_c = jnp.einsum('bihd,bhde->bihe', qc, state)
        state = state * jnp.exp(lc)[..., None] + jnp.einsum('bjhd,bjhe->bhde', kc, vc)
        return state, o_c

    xs = tuple(jnp.moveaxis(t, 1, 0) for t in (q_dec, k_end, v, last))
    _, o_inter = lax.scan(step, jnp.zeros((b, h, dk, dv), jnp.float32), xs)
    o = o_intra + jnp.moveaxis(o_inter, 0, 1)
    return o.reshape(b, s, h, dv)


def _mixer(x, w_in, gla_gate_w, gla_gate_b, gla_norm_g, w_proj_a, w_proj_b, w_out, rel_bias):
    b, s, _ = x.shape
    cuts = np.cumsum(IN_SIZES)[:-1].tolist()
    parts = jnp.split(x @ w_in, cuts, axis=-1)
    a_parts = parts[:3 * N_GROUPS]
    gq, gk, gv, gr, g_low, m_a, m_b = parts[3 * N_GROUPS:]
    outs, lses = [], []
    for g, (_, dil) in enumerate(DIL_PAIRS):
        qg, kg, vg = (t.reshape(b, s, A_HEADS, A_HEAD_DIM) for t in a_parts[3 * g:3 * g + 3])
        o, lse = _dilated_group(qg, kg, vg, rel_bias[:, g * A_HEADS:(g + 1) * A_HEADS], dil)
        outs.append(o)
        lses.append(lse)
    wts = jax.nn.softmax(jnp.stack(lses), axis=0)
    ya = jnp.einsum('gbsh,gbshe->bshe', wts, jnp.stack(outs).astype(jnp.float32))
    ya = ya.reshape(b, s, A_WIDTH).astype(x.dtype) @ w_proj_a
    log_a = jax.nn.log_sigmoid((g_low @ gla_gate_w + gla_gate_b).astype(jnp.float32)) / G_TAU
    o = _gla(gq.reshape(b, s, G_HEADS, G_HK) * (G_HK ** -0.5),
             gk.reshape(b, s, G_HEADS, G_HK),
             gv.reshape(b, s, G_HEADS, G_HV),
             log_a.reshape(b, s, G_HEADS, G_HK))
    o = o * lax.rsqrt(jnp.mean(o * o, axis=-1, keepdims=True) + LN_EPS) * gla_norm_g
    yb = (o.reshape(b, s, G_DV).astype(x.dtype) * jax.nn.silu(gr)) @ w_proj_b
    y = jax.nn.sigmoid(m_a) * ya + jax.nn.sigmoid(m_b) * yb
    return y @ w_out


def _conv_ffn(x, w_gate, w_up, conv_w, conv_b, w_down):
    s = x.shape[1]
    g = x @ w_gate
    gp = jnp.pad(g, ((0, 0), (CONV_W - 1, 0), (0, 0)))
    gc = conv_b
    for i in range(CONV_W):
        gc = gc + conv_w[i] * gp[:, i:i + s]
    return (jax.nn.silu(gc) * (x @ w_up)) @ w_down


def setup_inputs(seed: int = 0) -> dict:
    key = jax.random.key(seed)
    ks = jax.random.split(key, 20)

    def nrm(k, shape, scale):
        return jax.random.normal(k, shape, jnp.float32) * scale

    L = DEPTH
    return {
        'x': nrm(ks[0], (BATCH, SEQ, D_MODEL), 1.0),
        'w_in': nrm(ks[1], (L, D_MODEL, IN_COLS), D_MODEL ** -0.5),
        'gla_gate_w': nrm(ks[2], (L, G_RANK, G_DK), G_RANK ** -0.5),
        'gla_gate_b': nrm(ks[3], (L, G_DK), 0.1),
        'gla_norm_g': 1.0 + nrm(ks[4], (L, G_HV), 0.02),
        'w_proj_a': nrm(ks[5], (L, A_WIDTH, D_MODEL), A_WIDTH ** -0.5),
        'w_proj_b': nrm(ks[6], (L, G_DV, D_MODEL), G_DV ** -0.5),
        'w_out': nrm(ks[7], (L, D_MODEL, D_MODEL), BETA * D_MODEL ** -0.5),
        'rel_bias': nrm(ks[8], (N_BUCKETS, N_GROUPS * A_HEADS), 0.5),
        'ln1_g': 1.0 + nrm(ks[9], (L, D_MODEL), 0.02),
        'ln1_b': nrm(ks[10], (L, D_MODEL), 0.02),
        'ffn_w_gate': nrm(ks[11], (L, D_MODEL, D_FF), D_MODEL ** -0.5),
        'ffn_w_up': nrm(ks[12], (L, D_MODEL, D_FF), D_MODEL ** -0.5),
        'ffn_conv_w': nrm(ks[13], (L, CONV_W, D_FF), CONV_W ** -0.5),
        'ffn_conv_b': nrm(ks[14], (L, D_FF), 0.02),
        'ffn_w_down': nrm(ks[15], (L, D_FF, D_MODEL), BETA * D_FF ** -0.5),
        'ln2_g': 1.0 + nrm(ks[16], (L, D_MODEL), 0.02),
        'ln2_b': nrm(ks[17], (L, D_MODEL), 0.02),
    }


def reference(x, w_in, gla_gate_w, gla_gate_b, gla_norm_g, w_proj_a, w_proj_b, w_out, rel_bias,
              ln1_g, ln1_b, ffn_w_gate, ffn_w_up, ffn_conv_w, ffn_conv_b, ffn_w_down, ln2_g, ln2_b):
    for i in range(DEPTH):
        mix = _mixer(x, w_in[i], gla_gate_w[i], gla_gate_b[i], gla_norm_g[i],
                     w_proj_a[i], w_proj_b[i], w_out[i], rel_bias)
        x = _layernorm(ALPHA * x + mix, ln1_g[i], ln1_b[i])
        ffn = _conv_ffn(x, ffn_w_gate[i], ffn_w_up[i], ffn_conv_w[i], ffn_conv_b[i], ffn_w_down[i])
        x = _layernorm(ALPHA * x + ffn, ln2_g[i], ln2_b[i])
    return x
```

```python
import contextlib
import numpy as np
import concourse.bass as bass
import concourse.mybir as mybir
from concourse.bass_utils import run_bass_kernel_spmd

F32 = mybir.dt.float32
BF16 = mybir.dt.bfloat16
AF = mybir.ActivationFunctionType
ALU = mybir.AluOpType

ENGS = ("pe", "act", "dve", "pool", "sp")
SAME_ENGINE_SYNC = True


class Prog:
    def __init__(self, nc):
        self.nc = nc
        self.stack = contextlib.ExitStack()
        self.q = {e: [] for e in ENGS}
        self.cnt = {}
        self.seen = {e: {} for e in ENGS}
        self.res = {}
        self.semh = {}
        self.nsb = 0

    def sem(self, key):
        if key not in self.semh:
            self.semh[key] = self.stack.enter_context(self.nc.semaphore("s_" + str(key)))
            self.cnt[key] = 0
        return self.semh[key]

    def sb(self, name, shape, dt):
        return self.stack.enter_context(self.nc.sbuf_tensor(name, list(shape), dt))

    def ps(self, name, shape, dt=F32):
        return self.stack.enter_context(self.nc.psum_tensor(name, list(shape), dt))

    def dram(self, name, shape, dt, kind="Internal"):
        return self.nc.dram_tensor(name, list(shape), dt, kind=kind).ap()

    def add(self, eng, fn, reads=(), writes=(), dsem=None):
        need = {}

        def want(tok):
            if tok is None:
                return
            k, v = tok
            if need.get(k, 0) < v:
                need[k] = v

        for k in reads:
            st = self.res.get(k)
            if st:
                want(st["w"])
        for k in writes:
            st = self.res.get(k)
            if st:
                want(st["w"])
                for t in st["r"].items():
                    want(t)
        waits = []
        for k, v in need.items():
            if k == eng and (eng == "pe" or not SAME_ENGINE_SYNC):
                continue
            if isinstance(k, tuple) and k[0] == "dma":
                v = self.cnt[k]
            if self.seen[eng].get(k, 0) >= v:
                continue
            self.seen[eng][k] = v
            waits.append((k, v))
        if dsem is None:
            sk = eng
            self.sem(sk)
            self.cnt[sk] += 1
            inc = 1
        else:
            sk = ("dma", dsem)
            self.sem(sk)
            self.cnt[sk] += 16
            inc = 16
        tok = (sk, self.cnt[sk])
        self.q[eng].append((waits, fn, sk, inc))
        for k in reads:
            st = self.res.setdefault(k, {"w": None, "r": {}})
            if st["r"].get(sk, 0) < tok[1]:
                st["r"][sk] = tok[1]
        for k in writes:
            self.res[k] = {"w": tok, "r": {}}
        return tok

    def dma(self, out, in_, reads=(), writes=(), dsem=None, eng="sp"):
        assert dsem is not None
        return self.add(eng, lambda e: e.dma_start(out=out, in_=in_), reads, writes, dsem=dsem)

    def final_wait(self, eng="sp"):
        self.finals = eng

    def emit(self):
        nc = self.nc
        semh = self.semh
        q = self.q
        cnt = self.cnt

        def replay(name, e, final=False):
            for waits, fn, sk, inc in q[name]:
                for k, v in waits:
                    e.wait_ge(semh[k], v)
                ins = fn(e)
                ins.then_inc(semh[sk], inc)
            if final:
                for k, h in semh.items():
                    if cnt[k] > 0:
                        e.wait_ge(h, cnt[k])

        with nc.Block() as block:
            @block.sync
            def _(e):
                replay("sp", e, final=True)

            @block.tensor
            def _(e):
                replay("pe", e)

            @block.scalar
            def _(e):
                replay("act", e)

            @block.vector
            def _(e):
                replay("dve", e)

            @block.gpsimd
            def _(e):
                replay("pool", e)
        self.stack.close()


T = 1024
D = 2048
KC = D // 128
INC = 19472
WT = 256


def build_stage1(nc=None):
    nc = nc or bass.Bass("TRN2", target_bir_lowering=False)
    p = Prog(nc)
    xT = p.dram("xT", [D, T], F32, "ExternalInput")
    w_in = p.dram("w_in", [D, INC], F32, "ExternalInput")
    gate = p.dram("gate", [17, 1024], F32, "ExternalInput")
    QT = p.dram("QT", [3 * 1024, T], BF16, "ExternalOutput")
    KT = p.dram("KT", [3 * 1024, T], BF16, "ExternalOutput")
    V = p.dram("V", [3, T, 1024], BF16, "ExternalOutput")
    GQT = p.dram("GQT", [1024, T], BF16, "ExternalOutput")
    GKT = p.dram("GKT", [1024, T], BF16, "ExternalOutput")
    GKM = p.dram("GKM", [T, 1024], BF16, "ExternalOutput")
    GVM = p.dram("GVM", [T, 2048], BF16, "ExternalOutput")
    SGR = p.dram("SGR", [2048, T], BF16, "ExternalOutput")
    SMA = p.dram("SMA", [2048, T], BF16, "ExternalOutput")
    SMB = p.dram("SMB", [2048, T], BF16, "ExternalOutput")
    LAM = p.dram("LAM", [T, 1024], F32, "ExternalOutput")
    emit_stage1(p, xT, w_in, gate, QT, KT, V, GQT, GKT, GKM, GVM, SGR, SMA, SMB, LAM)
    p.emit()
    return nc


class Rot:
    def __init__(self, tiles, key):
        self.tiles = tiles
        self.key = key
        self.i = 0

    def next(self):
        i = self.i % len(self.tiles)
        self.i += 1
        return self.tiles[i], (self.key, i)


def load_xT_bf16(p, xT, xb, nt=T):
    xs = [p.sb(f"xs{i}", [128, nt], F32) for i in range(2)]
    for kc in range(KC):
        s = xs[kc % 2]
        p.dma(s[:, :], xT[kc * 128:(kc + 1) * 128, :], writes=[("xs", kc % 2)], dsem=f"xs{kc % 2}")
        eng = "dve" if kc % 2 == 0 else "pool"
        p.add(eng, lambda e, s=s, kc=kc: e.tensor_copy(out=xb[:, kc, :], in_=s[:, :]),
              reads=[("xs", kc % 2)], writes=[("xb", kc)])


class WStream:
    def __init__(self, p, kc, wt=WT, nbuf=2, name="w"):
        self.p = p
        self.kc = kc
        self.wt = wt
        self.name = name
        self.ws = [p.sb(f"{name}s{i}", [128, kc, wt], F32) for i in range(nbuf)]
        self.wb = [p.sb(f"{name}b{i}", [128, kc, wt], BF16) for i in range(nbuf)]
        self.i = 0
        self.nbuf = nbuf

    def load(self, w, c0, nc_):
        p = self.p
        i = self.i % self.nbuf
        self.i += 1
        s, b = self.ws[i], self.wb[i]
        src = w[:, c0:c0 + nc_].rearrange("(k p) c -> p k c", p=128)
        half = self.kc // 2
        nm = self.name
        p.dma(s[:, 0:half, 0:nc_], src[:, 0:half, :], writes=[(nm + "s", i, 0)], dsem=f"{nm}s{i}")
        p.dma(s[:, half:, 0:nc_], src[:, half:, :], writes=[(nm + "s", i, 1)], dsem=f"{nm}s{i}")
        p.add("pool", lambda e: e.tensor_copy(out=b[:, :, 0:nc_], in_=s[:, :, 0:nc_]),
              reads=[(nm + "s", i, 0), (nm + "s", i, 1)], writes=[(nm + "b", i)])
        return b, (nm + "b", i)


def emit_stage1(p, xT, w_in, gate, QT, KT, V, GQT, GKT, GKM, GVM, SGR, SMA, SMB, LAM):
    xb = p.sb("xb", [128, KC, T], BF16)
    load_xT_bf16(p, xT, xb)
    xkeys = [("xb", kc) for kc in range(KC)]
    ws = WStream(p, KC)
    psum = Rot([p.ps(f"ps{i}", [128, 512], F32) for i in range(8)], "ps")
    ob = Rot([p.sb(f"ob{i}", [128, 512], BF16) for i in range(4)], "ob")
    evac_i = [0]

    def evac(dst_sb, src_ps, func, rk, wk):
        if func is None:
            evac_i[0] += 1
            if evac_i[0] % 2 == 0:
                p.add("dve", lambda e: e.tensor_copy(out=dst_sb, in_=src_ps), reads=rk, writes=wk)
                return
            func = AF.Copy
        p.add("act", lambda e: e.activation(out=dst_sb, in_=src_ps, func=func), reads=rk, writes=wk)

    def fm_job(c0, ncols, dst, func=None):
        for t0 in range(0, ncols, WT):
            n = min(WT, ncols - t0)
            wb, wk = ws.load(w_in, c0 + t0, n)
            for m0 in range(0, n, 128):
                mc = min(128, n - m0)
                for th in range(T // 512):
                    ps, pk = psum.next()
                    for kc in range(KC):
                        p.add("pe", lambda e, ps=ps, wb=wb, kc=kc, m0=m0, mc=mc, th=th: e.matmul(
                            ps[0:mc, :], lhsT=wb[:, kc, m0:m0 + mc], rhs=xb[:, kc, th * 512:(th + 1) * 512],
                            start=(kc == 0), stop=(kc == KC - 1)),
                            reads=[wk, xkeys[kc]], writes=[pk])
                    o, ok = ob.next()
                    evac(o[0:mc, :], ps[0:mc, :], func, [pk], [ok])
                    p.dma(dst[t0 + m0:t0 + m0 + mc, th * 512:(th + 1) * 512], o[0:mc, :],
                          reads=[ok], dsem=f"ob{ok[1]}")

    def tm_job(c0, ncols, dst):
        for t0 in range(0, ncols, WT):
            n = min(WT, ncols - t0)
            wb, wk = ws.load(w_in, c0 + t0, n)
            for tp in range(T // 256):
                ps, pk = psum.next()
                for j in range(2):
                    tt = tp * 2 + j
                    for kc in range(KC):
                        p.add("pe", lambda e, ps=ps, wb=wb, kc=kc, tt=tt, j=j, n=n: e.matmul(
                            ps[:, j * 256:j * 256 + n], lhsT=xb[:, kc, tt * 128:(tt + 1) * 128], rhs=wb[:, kc, 0:n],
                            start=(kc == 0), stop=(kc == KC - 1)),
                            reads=[wk, xkeys[kc]], writes=[pk])
                o, ok = ob.next()
                evac(o[:, :], ps[:, :], None, [pk], [ok])
                for j in range(2):
                    tt = tp * 2 + j
                    p.dma(dst[tt * 128:(tt + 1) * 128, t0:t0 + n], o[:, j * 256:j * 256 + n],
                          reads=[ok], dsem=f"ob{ok[1]}")

    for g in range(3):
        fm_job((3 * g) * 1024, 1024, QT[g * 1024:(g + 1) * 1024, :])
        fm_job((3 * g + 1) * 1024, 1024, KT[g * 1024:(g + 1) * 1024, :])
        tm_job((3 * g + 2) * 1024, 1024, V[g])
    fm_job(9216, 1024, GQT)
    fm_job(10240, 1024, GKT)
    tm_job(10240, 1024, GKM)
    tm_job(11264, 2048, GVM)
    fm_job(13312, 2048, SGR, AF.Silu)
    fm_job(15376, 2048, SMA, AF.Sigmoid)
    fm_job(17424, 2048, SMB, AF.Sigmoid)

    gl = p.sb("gl", [17, T], F32)
    gw = p.sb("gw", [17, 1024], F32)
    p.dma(gw[:, :], gate[:, :], writes=["gw"], dsem="gw")
    p.add("pool", lambda e: e.memset(gl[:, :], 1.0), writes=["gl"])
    wb, wk = ws.load(w_in, 15360, 16)
    for th in range(T // 512):
        ps, pk = psum.next()
        for kc in range(KC):
            p.add("pe", lambda e, ps=ps, wb=wb, kc=kc, th=th: e.matmul(
                ps[0:16, :], lhsT=wb[:, kc, 0:16], rhs=xb[:, kc, th * 512:(th + 1) * 512],
                start=(kc == 0), stop=(kc == KC - 1)), reads=[wk, xkeys[kc]], writes=[pk])
        p.add("dve", lambda e, ps=ps, th=th: e.tensor_copy(out=gl[0:16, th * 512:(th + 1) * 512], in_=ps[0:16, :]),
              reads=[pk], writes=["gl"])
    la = Rot([p.sb(f"la{i}", [128, 512], F32) for i in range(2)], "la")
    for tt in range(T // 128):
        for ch in range(2):
            ps, pk = psum.next()
            p.add("pe", lambda e, ps=ps, tt=tt, ch=ch: e.matmul(
                ps[:, :], lhsT=gl[:, tt * 128:(tt + 1) * 128], rhs=gw[:, ch * 512:(ch + 1) * 512],
                start=True, stop=True), reads=["gl", "gw"], writes=[pk])
            o, ok = la.next()
            p.add("act", lambda e, o=o, ps=ps: e.activation(out=o[:, :], in_=ps[:, :], func=AF.Exp, scale=-1.0),
                  reads=[pk], writes=[ok])
            p.add("act", lambda e, o=o: e.activation(out=o[:, :], in_=o[:, :], func=AF.Ln, bias=1.0),
                  reads=[ok], writes=[ok])
            p.dma(LAM[tt * 128:(tt + 1) * 128, ch * 512:(ch + 1) * 512], o[:, :], reads=[ok], dsem=f"la{ok[1]}")


NT = T // 128


def gla_consts():
    j = np.arange(128)[:, None]
    t = np.arange(128)[None, :]
    same = (j // 64) == (t // 64)
    tri = np.where(same & (j <= t), -1.0 / 16.0, 0.0).astype(np.float32)
    trirev = np.where(same & (j > t), -1.0 / 16.0, 0.0).astype(np.float32)
    mask = np.where(same & (j <= t), 1.0, 0.0).astype(np.float32)
    return np.concatenate([tri, trirev, mask], axis=1)


def build_stage2():
    nc = bass.Bass("TRN2", target_bir_lowering=False)
    p = Prog(nc)
    GQT = p.dram("GQT", [1024, T], BF16, "ExternalInput")
    GKT = p.dram("GKT", [1024, T], BF16, "ExternalInput")
    GKM = p.dram("GKM", [T, 1024], BF16, "ExternalInput")
    GVM = p.dram("GVM", [T, 2048], BF16, "ExternalInput")
    LAM = p.dram("LAM", [T, 1024], F32, "ExternalInput")
    GC = p.dram("GC", [128, 384], F32, "ExternalInput")
    OL = p.dram("OL", [2048, T], F32, "ExternalOutput")
    QTL = p.dram("QTL", [1024, T], BF16, "ExternalOutput")
    U = p.dram("U", [1024, 512], F32, "ExternalOutput")
    DT = p.dram("DT", [1024, 1], F32, "ExternalOutput")
    emit_stage2(p, GQT, GKT, GKM, GVM, LAM, GC, OL, QTL, U, DT)
    p.emit()
    return nc


def emit_stage2(p, GQT, GKT, GKM, GVM, LAM, GC, OL, QTL, U, DT):
    gc = p.sb("gc", [128, 384], F32)
    p.dma(gc[:, :], GC[:, :], writes=["gc"], dsem="gc")
    tri, trirev, mask = gc[:, 0:128], gc[:, 128:256], gc[:, 256:384]
    qT = [p.sb(f"g_qT{i}", [128, T], BF16) for i in range(2)]
    kT = [p.sb(f"g_kT{i}", [128, T], BF16) for i in range(2)]
    ktm = p.sb("g_ktm", [128, NT, 256], BF16)
    vtm = p.sb("g_vtm", [128, NT, 512], BF16)
    lam = p.sb("g_lam", [128, NT, 256], F32)
    E = [p.sb(f"g_E{i}", [128, T], F32) for i in range(2)]
    qd = [p.sb(f"g_qd{i}", [128, T], BF16) for i in range(2)]
    ki = [p.sb(f"g_ki{i}", [128, T], BF16) for i in range(2)]
    ke = p.sb("g_ke", [128, NT, 256], BF16)
    qtl = [p.sb(f"g_qtl{i}", [128, T], BF16) for i in range(2)]
    S = [p.sb(f"g_S{i}", [128, 512], F32) for i in range(2)]
    Sb = [p.sb(f"g_Sb{i}", [128, 512], BF16) for i in range(2)]
    G = [p.sb(f"g_G{i}", [128, 1], F32) for i in range(2)]
    tmpf = Rot([p.sb(f"g_tmp{i}", [128, 256], F32) for i in range(3)], "g_tmp")
    attb = Rot([p.sb(f"g_att{i}", [128, 128], BF16) for i in range(2)], "g_att")
    osb = Rot([p.sb(f"g_o{i}", [128, 512], F32) for i in range(2)], "g_o")
    psA = Rot([p.ps(f"g_psA{i}", [128, 512], F32) for i in range(3)], "g_psA")
    psO = Rot([p.ps(f"g_psO{i}", [128, 512], F32) for i in range(2)], "g_psO")
    psS = Rot([p.ps(f"g_psS{i}", [128, 512], F32) for i in range(3)], "g_psS")

    for h in range(4):
        for dc in range(2):
            r0 = h * 256 + dc * 128
            p.dma(qT[dc][:, :], GQT[r0:r0 + 128, :], writes=[("qT", dc)], dsem=f"gq{dc}")
            p.dma(kT[dc][:, :], GKT[r0:r0 + 128, :], writes=[("kT", dc)], dsem=f"gk{dc}")
        p.dma(ktm[:, :, :], GKM[:, h * 256:(h + 1) * 256].rearrange("(n p) c -> p n c", p=128),
              writes=["ktm"], dsem="gktm")
        p.dma(vtm[:, :, :], GVM[:, h * 512:(h + 1) * 512].rearrange("(n p) c -> p n c", p=128),
              writes=["vtm"], dsem="gvtm")
        p.dma(lam[:, :, :], LAM[:, h * 256:(h + 1) * 256].rearrange("(n p) c -> p n c", p=128),
              writes=["lam"], dsem="glam")
        for dc in range(2):
            p.add("pool", lambda e, dc=dc: e.memset(S[dc][:, :], 0.0), writes=[("S", dc)])
            p.add("pool", lambda e, dc=dc: e.memset(Sb[dc][:, :], 0.0), writes=[("Sb", dc)])
            p.add("pool", lambda e, dc=dc: e.memset(G[dc][:, :], 1.0), writes=[("G", dc)])
        for tt in range(NT):
            cs = slice(tt * 128, (tt + 1) * 128)
            for dc in range(2):
                ps, pk = psA.next()
                p.add("pe", lambda e, ps=ps, tt=tt, dc=dc: e.matmul(
                    ps[:, 0:128], lhsT=lam[:, tt, dc * 128:(dc + 1) * 128], rhs=tri, start=True, stop=True),
                    reads=["lam", "gc"], writes=[pk])
                p.add("act", lambda e, ps=ps, dc=dc, cs=cs: e.activation(out=E[dc][:, cs], in_=ps[:, 0:128], func=AF.Exp),
                      reads=[pk], writes=[("E", dc, tt)])
                tm, tk = tmpf.next()
                p.add("act", lambda e, ps=ps, tm=tm: e.activation(out=tm[:, 0:128], in_=ps[:, 0:128], func=AF.Exp, scale=-1.0),
                      reads=[pk], writes=[tk])
                p.add("dve", lambda e, dc=dc, cs=cs: e.scalar_tensor_tensor(
                    out=qd[dc][:, cs], in0=qT[dc][:, cs], scalar=0.0625, in1=E[dc][:, cs], op0=ALU.mult, op1=ALU.mult),
                    reads=[("qT", dc), ("E", dc, tt)], writes=[("qd", dc, tt)])
                p.add("dve", lambda e, dc=dc, cs=cs, tm=tm: e.tensor_tensor(
                    out=ki[dc][:, cs], in0=kT[dc][:, cs], in1=tm[:, 0:128], op=ALU.mult),
                    reads=[("kT", dc), tk], writes=[("ki", dc, tt)])
            ps, pk = psA.next()
            p.add("pe", lambda e, ps=ps, tt=tt: e.matmul(
                ps[:, 0:256], lhsT=trirev, rhs=lam[:, tt, :], start=True, stop=True),
                reads=["lam", "gc"], writes=[pk])
            tm, tk = tmpf.next()
            p.add("act", lambda e, ps=ps, tm=tm: e.activation(out=tm[:, :], in_=ps[:, 0:256], func=AF.Exp),
                  reads=[pk], writes=[tk])
            p.add("dve", lambda e, tt=tt, tm=tm: e.tensor_tensor(
                out=ke[:, tt, :], in0=ktm[:, tt, :], in1=tm[:, :], op=ALU.mult),
                reads=["ktm", tk], writes=[("ke", tt)])
        for tt in range(NT):
            cs = slice(tt * 128, (tt + 1) * 128)
            ps, pk = psA.next()
            for dc in range(2):
                p.add("pe", lambda e, ps=ps, dc=dc, cs=cs: e.matmul(
                    ps[:, 0:128], lhsT=ki[dc][:, cs], rhs=qd[dc][:, cs], start=(dc == 0), stop=(dc == 1)),
                    reads=[("ki", dc, tt), ("qd", dc, tt)], writes=[pk])
            ab, ak = attb.next()
            p.add("dve", lambda e, ps=ps, ab=ab: e.tensor_tensor(out=ab[:, :], in0=ps[:, 0:128], in1=mask, op=ALU.mult),
                  reads=[pk, "gc"], writes=[ak])
            po, pok = psO.next()
            for par in range(2):
                c0 = tt * 128 + par * 64
                pr = slice(par * 64, par * 64 + 64)
                for dc in range(2):
                    p.add("dve", lambda e, dc=dc, c0=c0: e.tensor_scalar(
                        out=qtl[dc][:, c0:c0 + 64], in0=qd[dc][:, c0:c0 + 64], scalar1=G[dc][:, 0:1], scalar2=None,
                        op0=ALU.mult), reads=[("qd", dc, tt), ("G", dc)], writes=[("qtl", dc)])
                    p.add("dve", lambda e, dc=dc, c0=c0: e.tensor_tensor(
                        out=G[dc][:, :], in0=G[dc][:, :], in1=E[dc][:, c0 + 63:c0 + 64], op=ALU.mult),
                        reads=[("G", dc), ("E", dc, tt)], writes=[("G", dc)])
                for ec in range(4):
                    oc = slice(ec * 128 + par * 64, ec * 128 + par * 64 + 64)
                    es = slice(ec * 128, (ec + 1) * 128)
                    for dc in range(2):
                        p.add("pe", lambda e, po=po, oc=oc, es=es, dc=dc, c0=c0: e.matmul(
                            po[:, oc], lhsT=Sb[dc][:, es], rhs=qd[dc][:, c0:c0 + 64], start=(dc == 0), stop=False),
                            reads=[("Sb", dc), ("qd", dc, tt)], writes=[pok])
                    p.add("pe", lambda e, po=po, oc=oc, es=es, pr=pr, tt=tt, ab=ab: e.matmul(
                        po[:, oc], lhsT=vtm[pr, tt, es], rhs=ab[pr, pr], start=False, stop=True),
                        reads=["vtm", ak], writes=[pok])
                for dc in range(2):
                    pss, psk = psS.next()
                    p.add("pe", lambda e, pss=pss, pr=pr, tt=tt, dc=dc: e.matmul(
                        pss[:, :], lhsT=ke[pr, tt, dc * 128:(dc + 1) * 128], rhs=vtm[pr, tt, :], start=True, stop=True),
                        reads=[("ke", tt), "vtm"], writes=[psk])
                    p.add("dve", lambda e, pss=pss, dc=dc, c0=c0: e.scalar_tensor_tensor(
                        out=S[dc][:, :], in0=S[dc][:, :], scalar=E[dc][:, c0 + 63:c0 + 64], in1=pss[:, :],
                        op0=ALU.mult, op1=ALU.add), reads=[("S", dc), ("E", dc, tt), psk], writes=[("S", dc)])
                    p.add("act", lambda e, dc=dc: e.activation(out=Sb[dc][:, :], in_=S[dc][:, :], func=AF.Copy),
                          reads=[("S", dc)], writes=[("Sb", dc)])
            o, ok = osb.next()
            p.add("act", lambda e, o=o, po=po: e.activation(out=o[:, :], in_=po[:, :], func=AF.Copy),
                  reads=[pok], writes=[ok])
            p.dma(OL[h * 512:(h + 1) * 512, cs].rearrange("(c p) t -> p c t", p=128),
                  o[:, :].rearrange("p (c t) -> p c t", c=4), reads=[ok], dsem=f"go{ok[1]}")
        for dc in range(2):
            r0 = h * 256 + dc * 128
            p.dma(QTL[r0:r0 + 128, :], qtl[dc][:, :], reads=[("qtl", dc)], dsem=f"gqtl{dc}")
            p.dma(U[r0:r0 + 128, :], S[dc][:, :], reads=[("S", dc)], dsem=f"gU{dc}")
            p.dma(DT[r0:r0 + 128, :], G[dc][:, :], reads=[("G", dc)], dsem=f"gD{dc}")


HALO = (128, 512, 2048)
DIL = (1, 4, 16)
SCALE = 128 ** -0.5
GROUPS = [0, 1, 2]


def build_stage3a(do_attn=True, do_gla=True):
    nc = bass.Bass("TRN2", target_bir_lowering=False)
    p = Prog(nc)
    d = {}
    d["QT"] = p.dram("QT", [3 * 1024, T], BF16, "ExternalInput")
    for g in range(3):
        d[f"KH{g}"] = p.dram(f"KH{g}", [1024, HALO[g] + T], BF16, "ExternalInput")
        d[f"VH{g}"] = p.dram(f"VH{g}", [HALO[g] + T, 1024], BF16, "ExternalInput")
    d["BT"] = p.dram("BT", [24, 128, 256], F32, "ExternalInput")
    d["VALID"] = p.dram("VALID", [128, 256], F32, "ExternalInput")
    d["HMASK"] = p.dram("HMASK", [128, 3], F32, "ExternalInput")
    d["OL"] = p.dram("OL", [2048, T], F32, "ExternalInput")
    d["QTL"] = p.dram("QTL", [1024, T], BF16, "ExternalInput")
    d["UALL"] = p.dram("UALL", [8, 1024, 512], F32, "ExternalInput")
    d["DALL"] = p.dram("DALL", [1024, 8], F32, "ExternalInput")
    d["CMASK"] = p.dram("CMASK", [128, 8], F32, "ExternalInput")
    d["SGR"] = p.dram("SGR", [2048, T], BF16, "ExternalInput")
    d["NG"] = p.dram("NG", [128, 4], F32, "ExternalInput")
    d["YAT"] = p.dram("YAT", [1024, T], BF16, "ExternalOutput")
    d["YBT"] = p.dram("YBT", [2048, T], BF16, "ExternalOutput")
    if do_attn: emit_attn(p, d)
    if do_gla: emit_gla_fin(p, d)
    p.emit()
    return nc


def emit_attn(p, d):
    QT, BT, VALID, HMASK, YAT = d["QT"], d["BT"], d["VALID"], d["HMASK"], d["YAT"]
    valid = p.sb("a_valid", [128, 256], F32)
    hmask = p.sb("a_hmask", [128, 3], F32)
    ones = p.sb("a_ones", [128, 128], BF16)
    p.dma(valid[:, :], VALID[:, :], writes=["valid"], dsem="a_c0")
    p.dma(hmask[:, :], HMASK[:, :], writes=["hmask"], dsem="a_c1")
    p.add("pool", lambda e: e.memset(ones[:, :], 1.0), writes=["ones"])
    qT = Rot([p.sb(f"a_q{i}", [128, T], BF16) for i in range(2)], "a_q")
    kT = Rot([p.sb(f"a_k{i}", [128, 2048 + T], BF16) for i in range(2)], "a_k")
    vt = Rot([p.sb(f"a_v{i}", [128, 32, 128], BF16) for i in range(2)], "a_v")
    bt = Rot([p.sb(f"a_bt{i}", [128, 256], F32) for i in range(2)], "a_bt")
    tfull = Rot([p.sb(f"a_tf{i}", [128, 256], F32) for i in range(2)], "a_tf")
    tfirst = Rot([p.sb(f"a_t1{i}", [128, 256], F32) for i in range(2)], "a_t1")
    pf = Rot([p.sb(f"a_pf{i}", [128, 256], F32) for i in range(3)], "a_pf")
    pb = Rot([p.sb(f"a_pb{i}", [128, 256], BF16) for i in range(3)], "a_pb")
    num = p.sb("a_num", [128, T], F32)
    den = p.sb("a_den", [128, T], F32)
    ya = Rot([p.sb(f"a_ya{i}", [128, T], BF16) for i in range(2)], "a_ya")
    psS = Rot([p.ps(f"a_psS{i}", [128, 512], F32) for i in range(3)], "a_psS")
    psO = Rot([p.ps(f"a_psO{i}", [128, 512], F32) for i in range(3)], "a_psO")

    for h in range(8):
        for g in GROUPS:
            H, dl = HALO[g], DIL[g]
            q, qk = qT.next()
            k, kk = kT.next()
            v, vk = vt.next()
            r0 = g * 1024 + h * 128
            p.dma(q[:, :], QT[r0:r0 + 128, :], writes=[qk], dsem=f"a_q{qk[1]}")
            p.dma(k[:, 0:H + T], d[f"KH{g}"][h * 128:(h + 1) * 128, :], writes=[kk], dsem=f"a_k{kk[1]}")
            VH = d[f"VH{g}"]
            if g < 2:
                ntile = (H + T) // (128 * dl)
                for r in range(dl):
                    src = VH[r:H + T:dl, h * 128:(h + 1) * 128] if dl > 1 else VH[:, h * 128:(h + 1) * 128]
                    p.dma(v[:, r * ntile:(r + 1) * ntile, :], src.rearrange("(j p) c -> p j c", p=128),
                          writes=[(vk, r)], dsem=f"a_v{vk[1]}")
            else:
                for r in range(16):
                    srcA = VH[r:H:16, h * 128:(h + 1) * 128]
                    p.dma(v[:, r, :], srcA, writes=[(vk, r)], dsem=f"a_v{vk[1]}")
                srcB = VH[H:H + T, h * 128:(h + 1) * 128].rearrange("(i r) c -> i r c", r=16)
                p.dma(v[0:64, 16:32, :], srcB, writes=[(vk, 16)], dsem=f"a_v{vk[1]}")
            b, bk = bt.next()
            tf, tfk = tfull.next()
            t1, t1k = tfirst.next()
            p.dma(b[:, :], BT[g * 8 + h], writes=[bk], dsem=f"a_bt{bk[1]}")
            p.add("act", lambda e, b=b: e.activation(out=b[:, :], in_=b[:, :], func=AF.Exp), reads=[bk], writes=[bk])
            p.add("dve", lambda e, b=b, tf=tf: e.tensor_tensor(out=tf[:, :], in0=b[:, :], in1=valid[:, :], op=ALU.mult),
                  reads=[bk, "valid"], writes=[tfk])
            p.add("dve", lambda e, tf=tf, t1=t1: e.tensor_copy(out=t1[:, 128:256], in_=tf[:, 128:256]),
                  reads=[tfk], writes=[t1k])
            p.add("dve", lambda e, tf=tf, t1=t1, g=g: e.tensor_scalar(
                out=t1[:, 0:128], in0=tf[:, 0:128], scalar1=hmask[:, g:g + 1], scalar2=None, op0=ALU.mult),
                reads=[tfk, "hmask", t1k], writes=[t1k])
            vkeys = [(vk, r) for r in range(18)]
            if g < 2:
                nq = T // (128 * dl)
                ntile = (H + T) // (128 * dl)
                for r in range(dl):
                    for m in range(nq):
                        qs = slice(r + dl * 128 * m, r + dl * 128 * m + dl * 127 + 1, dl)
                        kprev = slice(r + dl * 128 * m, r + dl * 128 * m + dl * 127 + 1, dl)
                        kcur = slice(r + dl * 128 * (m + 1), r + dl * 128 * (m + 1) + dl * 127 + 1, dl)
                        tab, tabk = (t1, t1k) if m == 0 else (tf, tfk)
                        ps, pk = psS.next()
                        p.add("pe", lambda e, ps=ps, k=k, q=q, kprev=kprev, qs=qs: e.matmul(
                            ps[:, 0:128], lhsT=k[:, kprev], rhs=q[:, qs], start=True, stop=True),
                            reads=[kk, qk], writes=[pk])
                        p.add("pe", lambda e, ps=ps, k=k, q=q, kcur=kcur, qs=qs: e.matmul(
                            ps[:, 128:256], lhsT=k[:, kcur], rhs=q[:, qs], start=True, stop=True),
                            reads=[kk, qk], writes=[pk])
                        f, fk = pf.next()
                        pbb, pbk = pb.next()
                        p.add("act", lambda e, f=f, ps=ps: e.activation(out=f[:, :], in_=ps[:, 0:256], func=AF.Exp, scale=SCALE),
                              reads=[pk], writes=[fk])
                        p.add("dve", lambda e, f=f, pbb=pbb, tab=tab: e.tensor_tensor(out=pbb[:, :], in0=f[:, :], in1=tab[:, :], op=ALU.mult),
                              reads=[fk, tabk], writes=[pbk])
                        po, pok = psO.next()
                        j0 = r * ntile + m
                        p.add("pe", lambda e, po=po, v=v, pbb=pbb, j0=j0: e.matmul(
                            po[:, 0:128], lhsT=v[:, j0, :], rhs=pbb[:, 0:128], start=True, stop=False),
                            reads=vkeys + [pbk], writes=[pok])
                        p.add("pe", lambda e, po=po, v=v, pbb=pbb, j0=j0: e.matmul(
                            po[:, 0:128], lhsT=v[:, j0 + 1, :], rhs=pbb[:, 128:256], start=False, stop=True),
                            reads=vkeys + [pbk], writes=[pok])
                        p.add("pe", lambda e, po=po, pbb=pbb: e.matmul(
                            po[:, 128:256], lhsT=ones[:, :], rhs=pbb[:, 0:128], start=True, stop=False),
                            reads=["ones", pbk], writes=[pok])
                        p.add("pe", lambda e, po=po, pbb=pbb: e.matmul(
                            po[:, 128:256], lhsT=ones[:, :], rhs=pbb[:, 128:256], start=False, stop=True),
                            reads=["ones", pbk], writes=[pok])
                        if g == GROUPS[0]:
                            p.add("dve", lambda e, po=po, qs=qs: e.tensor_copy(out=num[:, qs], in_=po[:, 0:128]),
                                  reads=[pok], writes=["num"])
                            p.add("dve", lambda e, po=po, qs=qs: e.tensor_copy(out=den[:, qs], in_=po[:, 128:256]),
                                  reads=[pok], writes=["den"])
                        else:
                            p.add("dve", lambda e, po=po, qs=qs: e.tensor_tensor(out=num[:, qs], in0=po[:, 0:128], in1=num[:, qs], op=ALU.add),
                                  reads=[pok, "num"], writes=["num"])
                            p.add("dve", lambda e, po=po, qs=qs: e.tensor_tensor(out=den[:, qs], in0=po[:, 128:256], in1=den[:, qs], op=ALU.add),
                                  reads=[pok, "den"], writes=["den"])
            else:
                for r in range(16):
                    qs = slice(r, T, 16)
                    kprev = slice(r, H, 16)
                    kcur = slice(H + r, H + T, 16)
                    ps, pk = psS.next()
                    p.add("pe", lambda e, ps=ps, k=k, q=q, kprev=kprev, qs=qs: e.matmul(
                        ps[:, 0:64], lhsT=k[:, kprev], rhs=q[:, qs], start=True, stop=True),
                        reads=[kk, qk], writes=[pk])
                    p.add("pe", lambda e, ps=ps, k=k, q=q, kcur=kcur, qs=qs: e.matmul(
                        ps[0:64, 64:128], lhsT=k[:, kcur], rhs=q[:, qs], start=True, stop=True),
                        reads=[kk, qk], writes=[pk])
                    f, fk = pf.next()
                    pbb, pbk = pb.next()
                    p.add("act", lambda e, f=f, ps=ps: e.activation(out=f[:, 0:64], in_=ps[:, 0:64], func=AF.Exp, scale=SCALE),
                          reads=[pk], writes=[fk])
                    p.add("act", lambda e, f=f, ps=ps: e.activation(out=f[0:64, 64:128], in_=ps[0:64, 64:128], func=AF.Exp, scale=SCALE),
                          reads=[pk, fk], writes=[fk])
                    p.add("dve", lambda e, f=f, pbb=pbb, t1=t1: e.tensor_tensor(out=pbb[:, 0:64], in0=f[:, 0:64], in1=t1[:, 0:64], op=ALU.mult),
                          reads=[fk, t1k], writes=[pbk])
                    p.add("dve", lambda e, f=f, pbb=pbb, t1=t1: e.tensor_tensor(out=pbb[0:64, 64:128], in0=f[0:64, 64:128], in1=t1[0:64, 128:192], op=ALU.mult),
                          reads=[fk, t1k, pbk], writes=[pbk])
                    po, pok = psO.next()
                    p.add("pe", lambda e, po=po, v=v, pbb=pbb, r=r: e.matmul(
                        po[:, 0:64], lhsT=v[:, r, :], rhs=pbb[:, 0:64], start=True, stop=False),
                        reads=vkeys + [pbk], writes=[pok])
                    p.add("pe", lambda e, po=po, v=v, pbb=pbb, r=r: e.matmul(
                        po[:, 0:64], lhsT=v[0:64, 16 + r, :], rhs=pbb[0:64, 64:128], start=False, stop=True),
                        reads=vkeys + [pbk], writes=[pok])
                    p.add("pe", lambda e, po=po, pbb=pbb: e.matmul(
                        po[:, 128:192], lhsT=ones[:, :], rhs=pbb[:, 0:64], start=True, stop=False),
                        reads=["ones", pbk], writes=[pok])
                    p.add("pe", lambda e, po=po, pbb=pbb: e.matmul(
                        po[:, 128:192], lhsT=ones[0:64, :], rhs=pbb[0:64, 64:128], start=False, stop=True),
                        reads=["ones", pbk], writes=[pok])
                    p.add("dve", lambda e, po=po, qs=qs: e.tensor_tensor(out=num[:, qs], in0=po[:, 0:64], in1=num[:, qs], op=ALU.add),
                          reads=[pok, "num"], writes=["num"])
                    p.add("dve", lambda e, po=po, qs=qs: e.tensor_tensor(out=den[:, qs], in0=po[:, 128:192], in1=den[:, qs], op=ALU.add),
                          reads=[pok, "den"], writes=["den"])
        y, yk = ya.next()
        p.add("dve", lambda e: e.reciprocal(out=den[:, :], in_=den[:, :]), reads=["den"], writes=["den"])
        p.add("dve", lambda e, y=y: e.tensor_tensor(out=y[:, :], in0=num[:, :], in1=den[:, :], op=ALU.mult),
              reads=["num", "den"], writes=[yk])
        p.dma(YAT[h * 128:(h + 1) * 128, :], y[:, :], reads=[yk], dsem=f"a_ya{yk[1]}")


def emit_gla_fin(p, d):
    OL, QTL, UALL, DALL, CMASK, SGR, NG, YBT = (d[k] for k in ("OL", "QTL", "UALL", "DALL", "CMASK", "SGR", "NG", "YBT"))
    cm = p.sb("f_cm", [128, 8], F32)
    ng = p.sb("f_ng", [128, 4], F32)
    onesb = p.sb("f_ones", [128, 128], BF16)
    p.dma(cm[:, :], CMASK[:, :], writes=["cm"], dsem="f_c0")
    p.dma(ng[:, :], NG[:, :], writes=["ng"], dsem="f_c1")
    p.add("pool", lambda e: e.memset(onesb[:, :], 1.0), writes=["f_ones"])
    dall = p.sb("f_dall", [128, 8], F32)
    acoef = p.sb("f_a", [128, 8], F32)
    Sin = [p.sb(f"f_S{i}", [128, 512], F32) for i in range(2)]
    Sb = [p.sb(f"f_Sb{i}", [128, 512], BF16) for i in range(2)]
    ut = Rot([p.sb(f"f_u{i}", [128, 512], F32) for i in range(3)], "f_u")
    qtl = [p.sb(f"f_q{i}", [128, T], BF16) for i in range(2)]
    o = [p.sb(f"f_o{i}", [128, T], F32) for i in range(4)]
    osq = [p.sb(f"f_osq{i}", [128, T], BF16) for i in range(4)]
    rstd = p.sb("f_rstd", [128, T], F32)
    sgr = Rot([p.sb(f"f_sgr{i}", [128, T], BF16) for i in range(2)], "f_sgr")
    tmp = Rot([p.sb(f"f_tmp{i}", [128, T], F32) for i in range(2)], "f_tmp")
    yb = Rot([p.sb(f"f_yb{i}", [128, T], BF16) for i in range(2)], "f_yb")
    psC = Rot([p.ps(f"f_psC{i}", [128, 512], F32) for i in range(2)], "f_psC")

    for h in range(4):
        for dc in range(2):
            r0 = h * 256 + dc * 128
            p.dma(dall[:, :], DALL[r0:r0 + 128, :], writes=["dall"], dsem="f_dall")
            p.add("dve", lambda e: e.scalar_tensor_tensor(out=acoef[:, :], in0=dall[:, :], scalar=-1.0, in1=cm[:, :],
                                                          op0=ALU.add, op1=ALU.mult), reads=["dall", "cm"], writes=["acoef"])
            p.add("dve", lambda e: e.tensor_scalar(out=acoef[:, :], in0=acoef[:, :], scalar1=1.0, scalar2=None, op0=ALU.add),
                  reads=["acoef"], writes=["acoef"])
            p.add("pool", lambda e, dc=dc: e.memset(Sin[dc][:, :], 0.0), writes=[("Sin", dc)])
            for c in range(8):
                u, uk = ut.next()
                p.dma(u[:, :], UALL[c, r0:r0 + 128, :], writes=[uk], dsem=f"f_u{uk[1]}")
                p.add("dve", lambda e, u=u, c=c: e.tensor_scalar(out=u[:, :], in0=u[:, :], scalar1=cm[:, c:c + 1], scalar2=None, op0=ALU.mult),
                      reads=[uk, "cm"], writes=[uk])
                p.add("dve", lambda e, u=u, c=c, dc=dc: e.scalar_tensor_tensor(
                    out=Sin[dc][:, :], in0=Sin[dc][:, :], scalar=acoef[:, c:c + 1], in1=u[:, :], op0=ALU.mult, op1=ALU.add),
                    reads=[("Sin", dc), "acoef", uk], writes=[("Sin", dc)])
            p.add("act", lambda e, dc=dc: e.activation(out=Sb[dc][:, :], in_=Sin[dc][:, :], func=AF.Copy),
                  reads=[("Sin", dc)], writes=[("Sb", dc)])
            p.dma(qtl[dc][:, :], QTL[r0:r0 + 128, :], writes=[("qtl", dc)], dsem=f"f_q{dc}")
        for ec in range(4):
            r0 = h * 512 + ec * 128
            p.dma(o[ec][:, :], OL[r0:r0 + 128, :], writes=[("o", ec)], dsem=f"f_o{ec}")
            for th in range(2):
                ts = slice(th * 512, (th + 1) * 512)
                ps, pk = psC.next()
                for dc in range(2):
                    p.add("pe", lambda e, ps=ps, dc=dc, ec=ec, ts=ts: e.matmul(
                        ps[:, :], lhsT=Sb[dc][:, ec * 128:(ec + 1) * 128], rhs=qtl[dc][:, ts], start=(dc == 0), stop=(dc == 1)),
                        reads=[("Sb", dc), ("qtl", dc)], writes=[pk])
                p.add("dve", lambda e, ps=ps, ec=ec, ts=ts: e.tensor_tensor(out=o[ec][:, ts], in0=ps[:, :], in1=o[ec][:, ts], op=ALU.add),
                      reads=[pk, ("o", ec)], writes=[("o", ec)])
            p.add("act", lambda e, ec=ec: e.activation(out=osq[ec][:, :], in_=o[ec][:, :], func=AF.Square),
                  reads=[("o", ec)], writes=[("osq", ec)])
        for th in range(2):
            ts = slice(th * 512, (th + 1) * 512)
            ps, pk = psC.next()
            for ec in range(4):
                p.add("pe", lambda e, ps=ps, ec=ec, ts=ts: e.matmul(
                    ps[:, :], lhsT=onesb[:, :], rhs=osq[ec][:, ts], start=(ec == 0), stop=(ec == 3)),
                    reads=["f_ones", ("osq", ec)], writes=[pk])
            p.add("dve", lambda e, ps=ps, ts=ts: e.tensor_scalar(out=rstd[:, ts], in0=ps[:, :], scalar1=1.0 / 512.0, scalar2=1e-5,
                                                                 op0=ALU.mult, op1=ALU.add), reads=[pk], writes=["rstd"])
        p.add("act", lambda e: e.activation(out=rstd[:, :], in_=rstd[:, :], func=AF.Ln), reads=["rstd"], writes=["rstd"])
        p.add("act", lambda e: e.activation(out=rstd[:, :], in_=rstd[:, :], func=AF.Exp, scale=-0.5), reads=["rstd"], writes=["rstd"])
        for ec in range(4):
            r0 = h * 512 + ec * 128
            s, sk = sgr.next()
            p.dma(s[:, :], SGR[r0:r0 + 128, :], writes=[sk], dsem=f"f_sgr{sk[1]}")
            t, tk = tmp.next()
            y, yk = yb.next()
            p.add("dve", lambda e, t=t, ec=ec: e.scalar_tensor_tensor(out=t[:, :], in0=o[ec][:, :], scalar=ng[:, ec:ec + 1], in1=rstd[:, :],
                                                                     op0=ALU.mult, op1=ALU.mult), reads=[("o", ec), "ng", "rstd"], writes=[tk])
            p.add("dve", lambda e, t=t, y=y, s=s: e.tensor_tensor(out=y[:, :], in0=t[:, :], in1=s[:, :], op=ALU.mult),
                  reads=[tk, sk], writes=[yk])
            p.dma(YBT[r0:r0 + 128, :], y[:, :], reads=[yk], dsem=f"f_yb{yk[1]}")


ALPHA = float((2 * 4) ** 0.25)
DFF = 5632
FC = DFF // 128


def emit_out_ln(p, pre, act, akeys, kcn, W, XT, LNG, LNB, OUT, SCR, psY):
    ws = WStream(p, kcn, wt=128, name=pre + "w")
    ones = p.sb(pre + "ones", [128, 128], F32)
    p.add("pool", lambda e: e.memset(ones[:, :], 1.0), writes=[pre + "ones"])
    lng = p.sb(pre + "lng", [128, 16], F32)
    lnb = p.sb(pre + "lnb", [128, 16], F32)
    p.dma(lng[:, :], LNG[:, :], writes=[pre + "lng"], dsem=pre + "lng")
    p.dma(lnb[:, :], LNB[:, :], writes=[pre + "lnb"], dsem=pre + "lnb")
    xt = Rot([p.sb(f"{pre}xt{i}", [128, 512], F32) for i in range(3)], pre + "xt")
    ut = Rot([p.sb(f"{pre}ut{i}", [128, 512], F32) for i in range(3)], pre + "ut")
    usq = Rot([p.sb(f"{pre}usq{i}", [128, 512], F32) for i in range(2)], pre + "usq")
    pst = [p.ps(f"{pre}pst{i}", [128, 512], F32) for i in range(4)]
    for cc in range(16):
        wb, wk = ws.load(W, cc * 128, 128)
        for th in range(2):
            ts = slice(th * 512, (th + 1) * 512)
            ps, pk = psY.next()
            for kc in range(kcn):
                p.add("pe", lambda e, ps=ps, wb=wb, kc=kc, ts=ts: e.matmul(
                    ps[:, :], lhsT=wb[:, kc, :], rhs=act[:, kc, ts], start=(kc == 0), stop=(kc == kcn - 1)),
                    reads=[wk, akeys[kc]], writes=[pk])
            x, xk = xt.next()
            p.dma(x[:, :], XT[cc * 128:(cc + 1) * 128, ts], writes=[xk], dsem=f"{pre}xt{xk[1]}")
            u, uk = ut.next()
            p.add("dve", lambda e, u=u, x=x, ps=ps: e.scalar_tensor_tensor(
                out=u[:, :], in0=x[:, :], scalar=ALPHA, in1=ps[:, :], op0=ALU.mult, op1=ALU.add),
                reads=[xk, pk], writes=[uk])
            sq, sqk = usq.next()
            p.add("act", lambda e, sq=sq, u=u: e.activation(out=sq[:, :], in_=u[:, :], func=AF.Square),
                  reads=[uk], writes=[sqk])
            p.add("pe", lambda e, u=u, th=th, cc=cc: e.matmul(pst[th][:, :], lhsT=ones[:, :], rhs=u[:, :],
                                                             start=(cc == 0), stop=(cc == 15)),
                  reads=[pre + "ones", uk], writes=[(pre + "pst", th)])
            p.add("pe", lambda e, sq=sq, th=th, cc=cc: e.matmul(pst[2 + th][:, :], lhsT=ones[:, :], rhs=sq[:, :],
                                                               start=(cc == 0), stop=(cc == 15)),
                  reads=[pre + "ones", sqk], writes=[(pre + "pst", 2 + th)])
            p.dma(SCR[cc * 128:(cc + 1) * 128, ts], u[:, :], reads=[uk], writes=[(pre + "scr", cc, th)],
                  dsem=f"{pre}ut{uk[1]}")
    mean = p.sb(pre + "mean", [128, T], F32)
    rstd = p.sb(pre + "rstd", [128, T], F32)
    for th in range(2):
        ts = slice(th * 512, (th + 1) * 512)
        p.add("dve", lambda e, th=th, ts=ts: e.tensor_scalar(out=mean[:, ts], in0=pst[th][:, :], scalar1=1.0 / 2048.0,
                                                             scalar2=None, op0=ALU.mult),
              reads=[(pre + "pst", th)], writes=[pre + "mean"])
        p.add("dve", lambda e, ts=ts: e.tensor_tensor(out=rstd[:, ts], in0=mean[:, ts], in1=mean[:, ts], op=ALU.mult),
              reads=[pre + "mean"], writes=[pre + "rstd"])
        p.add("dve", lambda e, th=th, ts=ts: e.scalar_tensor_tensor(
            out=rstd[:, ts], in0=pst[2 + th][:, :], scalar=1.0 / 2048.0, in1=rstd[:, ts], op0=ALU.mult, op1=ALU.subtract),
            reads=[(pre + "pst", 2 + th), pre + "rstd"], writes=[pre + "rstd"])
    p.add("dve", lambda e: e.tensor_scalar(out=rstd[:, :], in0=rstd[:, :], scalar1=1e-5, scalar2=None, op0=ALU.add),
          reads=[pre + "rstd"], writes=[pre + "rstd"])
    p.add("act", lambda e: e.activation(out=rstd[:, :], in_=rstd[:, :], func=AF.Ln), reads=[pre + "rstd"], writes=[pre + "rstd"])
    p.add("act", lambda e: e.activation(out=rstd[:, :], in_=rstd[:, :], func=AF.Exp, scale=-0.5),
          reads=[pre + "rstd"], writes=[pre + "rstd"])
    for cc in range(16):
        for th in range(2):
            ts = slice(th * 512, (th + 1) * 512)
            u, uk = ut.next()
            p.dma(u[:, :], SCR[cc * 128:(cc + 1) * 128, ts], reads=[(pre + "scr", cc, th)], writes=[uk],
                  dsem=f"{pre}ut{uk[1]}")
            p.add("dve", lambda e, u=u, ts=ts: e.tensor_tensor(out=u[:, :], in0=u[:, :], in1=mean[:, ts], op=ALU.subtract),
                  reads=[uk, pre + "mean"], writes=[uk])
            p.add("dve", lambda e, u=u, ts=ts: e.tensor_tensor(out=u[:, :], in0=u[:, :], in1=rstd[:, ts], op=ALU.mult),
                  reads=[uk, pre + "rstd"], writes=[uk])
            p.add("dve", lambda e, u=u, cc=cc: e.tensor_scalar(out=u[:, :], in0=u[:, :], scalar1=lng[:, cc:cc + 1],
                                                               scalar2=lnb[:, cc:cc + 1], op0=ALU.mult, op1=ALU.add),
                  reads=[uk, pre + "lng", pre + "lnb"], writes=[uk])
            p.dma(OUT[cc * 128:(cc + 1) * 128, ts], u[:, :], reads=[uk], dsem=f"{pre}ut{uk[1]}")


def build_stage3b():
    nc = bass.Bass("TRN2", target_bir_lowering=False)
    p = Prog(nc)
    YAT = p.dram("YAT", [1024, T], BF16, "ExternalInput")
    YBT = p.dram("YBT", [2048, T], BF16, "ExternalInput")
    SMA = p.dram("SMA", [2048, T], BF16, "ExternalInput")
    SMB = p.dram("SMB", [2048, T], BF16, "ExternalInput")
    XT = p.dram("xT", [2048, T], F32, "ExternalInput")
    WA = p.dram("w_proj_a", [1024, 2048], F32, "ExternalInput")
    WB = p.dram("w_proj_b", [2048, 2048], F32, "ExternalInput")
    WO = p.dram("w_out", [2048, 2048], F32, "ExternalInput")
    LNG = p.dram("LNG", [128, 16], F32, "ExternalInput")
    LNB = p.dram("LNB", [128, 16], F32, "ExternalInput")
    OUT = p.dram("X1T", [2048, T], F32, "ExternalOutput")
    SCR = p.dram("SCR", [2048, T], F32, "Internal")
    emit_stage3b(p, YAT, YBT, SMA, SMB, XT, WA, WB, WO, LNG, LNB, OUT, SCR)
    p.emit()
    return nc


def emit_stage3b(p, YAT, YBT, SMA, SMB, XT, WA, WB, WO, LNG, LNB, OUT, SCR):
    ya = p.sb("b_ya", [128, 8, T], BF16)
    yb = p.sb("b_yb", [128, 16, T], BF16)
    yT = p.sb("b_yT", [128, 16, T], BF16)
    for kc in range(8):
        p.dma(ya[:, kc, :], YAT[kc * 128:(kc + 1) * 128, :], writes=[("b_ya", kc)], dsem="b_ya")
    for kc in range(16):
        p.dma(yb[:, kc, :], YBT[kc * 128:(kc + 1) * 128, :], writes=[("b_yb", kc)], dsem="b_yb")
    wsa = WStream(p, 8, wt=128, name="b_wa")
    wsb = WStream(p, 16, wt=128, name="b_wb")
    psY = Rot([p.ps(f"b_ps{i}", [128, 512], F32) for i in range(4)], "b_ps")
    sm = Rot([p.sb(f"b_sm{i}", [128, T], BF16) for i in range(4)], "b_sm")
    t1 = Rot([p.sb(f"b_t1{i}", [128, 512], F32) for i in range(2)], "b_t1")
    t2 = Rot([p.sb(f"b_t2{i}", [128, 512], F32) for i in range(2)], "b_t2")
    for cc in range(16):
        wa, wak = wsa.load(WA, cc * 128, 128)
        wb, wbk = wsb.load(WB, cc * 128, 128)
        sa, sak = sm.next()
        sb_, sbk = sm.next()
        p.dma(sa[:, :], SMA[cc * 128:(cc + 1) * 128, :], writes=[sak], dsem=f"b_sm{sak[1]}")
        p.dma(sb_[:, :], SMB[cc * 128:(cc + 1) * 128, :], writes=[sbk], dsem=f"b_sm{sbk[1]}")
        for th in range(2):
            ts = slice(th * 512, (th + 1) * 512)
            pa, pak = psY.next()
            for kc in range(8):
                p.add("pe", lambda e, pa=pa, wa=wa, kc=kc, ts=ts: e.matmul(
                    pa[:, :], lhsT=wa[:, kc, :], rhs=ya[:, kc, ts], start=(kc == 0), stop=(kc == 7)),
                    reads=[wak, ("b_ya", kc)], writes=[pak])
            pb, pbk = psY.next()
            for kc in range(16):
                p.add("pe", lambda e, pb=pb, wb=wb, kc=kc, ts=ts: e.matmul(
                    pb[:, :], lhsT=wb[:, kc, :], rhs=yb[:, kc, ts], start=(kc == 0), stop=(kc == 15)),
                    reads=[wbk, ("b_yb", kc)], writes=[pbk])
            a, ak = t1.next()
            b, bk = t2.next()
            p.add("dve", lambda e, a=a, pa=pa, sa=sa, ts=ts: e.tensor_tensor(out=a[:, :], in0=pa[:, :], in1=sa[:, ts], op=ALU.mult),
                  reads=[pak, sak], writes=[ak])
            p.add("dve", lambda e, b=b, pb=pb, sb_=sb_, ts=ts: e.tensor_tensor(out=b[:, :], in0=pb[:, :], in1=sb_[:, ts], op=ALU.mult),
                  reads=[pbk, sbk], writes=[bk])
            p.add("dve", lambda e, a=a, b=b, cc=cc, ts=ts: e.tensor_tensor(out=yT[:, cc, ts], in0=a[:, :], in1=b[:, :], op=ALU.add),
                  reads=[ak, bk], writes=[("b_yT", cc)])
    emit_out_ln(p, "b_", yT, [("b_yT", kc) for kc in range(16)], 16, WO, XT, LNG, LNB, OUT, SCR, psY)


def build_stage4a():
    nc = bass.Bass("TRN2", target_bir_lowering=False)
    p = Prog(nc)
    X1T = p.dram("X1T", [2048, T], F32, "ExternalInput")
    X1H = p.dram("X1H", [2048, 2], F32, "ExternalInput")
    WG = p.dram("ffn_w_gate", [2048, DFF], F32, "ExternalInput")
    WU = p.dram("ffn_w_up", [2048, DFF], F32, "ExternalInput")
    CW = p.dram("CW", [128, FC, 3], F32, "ExternalInput")
    CB = p.dram("CB", [128, FC], F32, "ExternalInput")
    HT = p.dram("HT", [DFF, T], BF16, "ExternalOutput")
    emit_stage4a(p, X1T, X1H, WG, WU, CW, CB, HT)
    p.emit()
    return nc


def emit_stage4a(p, X1T, X1H, WG, WU, CW, CB, HT):
    xb = p.sb("c_xb", [128, 16, T + 2], BF16)
    xs = [p.sb(f"c_xs{i}", [128, T + 2], F32) for i in range(2)]
    for kc in range(16):
        s = xs[kc % 2]
        p.dma(s[:, 2:], X1T[kc * 128:(kc + 1) * 128, :], writes=[("c_xs", kc % 2, 0)], dsem=f"c_xs{kc % 2}")
        p.dma(s[:, 0:2], X1H[kc * 128:(kc + 1) * 128, :], writes=[("c_xs", kc % 2, 1)], dsem=f"c_xs{kc % 2}")
        p.add("dve" if kc % 2 == 0 else "pool", lambda e, s=s, kc=kc: e.tensor_copy(out=xb[:, kc, :], in_=s[:, :]),
              reads=[("c_xs", kc % 2, 0), ("c_xs", kc % 2, 1)], writes=[("c_xb", kc)])
    xkeys = [("c_xb", kc) for kc in range(16)]
    cw = p.sb("c_cw", [128, FC, 3], F32)
    cb = p.sb("c_cb", [128, FC], F32)
    p.dma(cw[:, :, :], CW[:, :, :], writes=["c_cw"], dsem="c_cw")
    p.dma(cb[:, :], CB[:, :], writes=["c_cb"], dsem="c_cb")
    ws = WStream(p, 16, wt=128, nbuf=3, name="c_w")
    psG = Rot([p.ps(f"c_psg{i}", [128, 512], F32) for i in range(3)], "c_psg")
    psH = Rot([p.ps(f"c_psh{i}", [128, 512], F32) for i in range(1)], "c_psh")
    psU = Rot([p.ps(f"c_psu{i}", [128, 512], F32) for i in range(4)], "c_psu")
    gx = Rot([p.sb(f"c_gx{i}", [128, T + 2], F32) for i in range(2)], "c_gx")
    acc = Rot([p.sb(f"c_acc{i}", [128, T], F32) for i in range(2)], "c_acc")
    hh = Rot([p.sb(f"c_h{i}", [128, T], BF16) for i in range(2)], "c_h")
    for fc in range(FC):
        wg, wgk = ws.load(WG, fc * 128, 128)
        wu, wuk = ws.load(WU, fc * 128, 128)
        g, gk = gx.next()
        ph, phk = psH.next()
        for kc in range(16):
            p.add("pe", lambda e, ph=ph, wg=wg, kc=kc: e.matmul(
                ph[:, 0:2], lhsT=wg[:, kc, :], rhs=xb[:, kc, 0:2], start=(kc == 0), stop=(kc == 15)),
                reads=[wgk, xkeys[kc]], writes=[phk])
        p.add("act", lambda e, g=g, ph=ph: e.activation(out=g[:, 0:2], in_=ph[:, 0:2], func=AF.Copy),
              reads=[phk], writes=[(gk, 2)])
        for th in range(2):
            pg, pgk = psG.next()
            for kc in range(16):
                p.add("pe", lambda e, pg=pg, wg=wg, kc=kc, th=th: e.matmul(
                    pg[:, :], lhsT=wg[:, kc, :], rhs=xb[:, kc, 2 + th * 512:2 + (th + 1) * 512], start=(kc == 0), stop=(kc == 15)),
                    reads=[wgk, xkeys[kc]], writes=[pgk])
            p.add("act", lambda e, g=g, pg=pg, th=th: e.activation(out=g[:, 2 + th * 512:2 + (th + 1) * 512], in_=pg[:, :], func=AF.Copy),
                  reads=[pgk], writes=[(gk, th)])
        gkeys = [(gk, 0), (gk, 1), (gk, 2)]
        a, ak = acc.next()
        p.add("dve", lambda e, a=a, g=g, fc=fc: e.tensor_scalar(out=a[:, :], in0=g[:, 0:T], scalar1=cw[:, fc, 0:1], scalar2=cb[:, fc:fc + 1],
                                                               op0=ALU.mult, op1=ALU.add), reads=gkeys + ["c_cw", "c_cb"], writes=[ak])
        p.add("dve", lambda e, a=a, g=g, fc=fc: e.scalar_tensor_tensor(out=a[:, :], in0=g[:, 1:T + 1], scalar=cw[:, fc, 1:2], in1=a[:, :],
                                                                      op0=ALU.mult, op1=ALU.add), reads=gkeys + ["c_cw", ak], writes=[ak])
        p.add("dve", lambda e, a=a, g=g, fc=fc: e.scalar_tensor_tensor(out=a[:, :], in0=g[:, 2:T + 2], scalar=cw[:, fc, 2:3], in1=a[:, :],
                                                                      op0=ALU.mult, op1=ALU.add), reads=gkeys + ["c_cw", ak], writes=[ak])
        p.add("act", lambda e, a=a: e.activation(out=a[:, :], in_=a[:, :], func=AF.Silu), reads=[ak], writes=[ak])
        h, hk = hh.next()
        for th in range(2):
            ts = slice(th * 512, (th + 1) * 512)
            pu, puk = psU.next()
            for kc in range(16):
                p.add("pe", lambda e, pu=pu, wu=wu, kc=kc, th=th: e.matmul(
                    pu[:, :], lhsT=wu[:, kc, :], rhs=xb[:, kc, 2 + th * 512:2 + (th + 1) * 512], start=(kc == 0), stop=(kc == 15)),
                    reads=[wuk, xkeys[kc]], writes=[puk])
            p.add("dve", lambda e, h=h, a=a, pu=pu, ts=ts: e.tensor_tensor(out=h[:, ts], in0=pu[:, :], in1=a[:, ts], op=ALU.mult),
                  reads=[puk, ak], writes=[(hk, th)])
        p.dma(HT[fc * 128:(fc + 1) * 128, :], h[:, :], reads=[(hk, 0), (hk, 1)], dsem=f"c_h{hk[1]}")


def build_stage4b():
    nc = bass.Bass("TRN2", target_bir_lowering=False)
    p = Prog(nc)
    HT = p.dram("HT", [DFF, T], BF16, "ExternalInput")
    X1T = p.dram("X1T", [2048, T], F32, "ExternalInput")
    WD = p.dram("ffn_w_down", [DFF, 2048], F32, "ExternalInput")
    LNG = p.dram("LNG", [128, 16], F32, "ExternalInput")
    LNB = p.dram("LNB", [128, 16], F32, "ExternalInput")
    OUT = p.dram("X2T", [2048, T], F32, "ExternalOutput")
    SCR = p.dram("SCR", [2048, T], F32, "Internal")
    hT = p.sb("d_hT", [128, FC, T], BF16)
    for kc in range(FC):
        p.dma(hT[:, kc, :], HT[kc * 128:(kc + 1) * 128, :], writes=[("d_hT", kc)], dsem=f"d_hT{kc % 4}")
    psY = Rot([p.ps(f"d_ps{i}", [128, 512], F32) for i in range(4)], "d_ps")
    emit_out_ln(p, "d_", hT, [("d_hT", kc) for kc in range(FC)], FC, WD, X1T, LNG, LNB, OUT, SCR, psY)
    p.emit()
    return nc


HALO = (128, 512, 2048); DIL = (1, 4, 16)

def t5_bucket(dist):
    dist = np.asarray(dist)
    df = np.maximum(dist, 1).astype(np.float32)
    large = 16 + (np.log(df / np.float32(16)) / np.float32(np.log(2048 / 16)) * np.float32(16)).astype(np.int32)
    return np.where(dist < 16, dist, np.minimum(large, 31))

def bias_tables(rel_bias):
    ki = np.arange(128)[:, None]; qi = np.arange(128)[None, :]
    off_prev = 128 + qi - ki
    off_cur = qi - ki
    valid = np.concatenate([(off_prev <= 128), (off_cur >= 0)], axis=1).astype(np.float32)
    BT = np.zeros((24, 128, 256), np.float32)
    for g in range(3):
        bp = t5_bucket(DIL[g] * np.clip(off_prev, 0, 128))
        bc = t5_bucket(DIL[g] * np.clip(off_cur, 0, 128))
        for h in range(8):
            BT[g * 8 + h, :, 0:128] = rel_bias[bp, g * 8 + h]
            BT[g * 8 + h, :, 128:256] = rel_bias[bc, g * 8 + h]
    return BT, valid

def hmask(c):
    m = np.zeros((128, 3), np.float32)
    if c > 0:
        m[:, 0] = 1; m[:, 1] = 1
    if c == 1:
        m[64:, 2] = 1
    elif c >= 2:
        m[:, 2] = 1
    return m

def cmask(c):
    m = np.zeros((128, 8), np.float32)
    m[:, :c] = 1
    return m


_PROGS = {}


def _prog(name, builder):
    if name not in _PROGS:
        _PROGS[name] = builder()
    return _PROGS[name]


def _run(name, builder, in_maps):
    nc = _prog(name, builder)
    res = run_bass_kernel_spmd(nc, in_maps, core_ids=list(range(NCORES)))
    return res.results


NCORES = 8


def _lnp(v):
    return np.ascontiguousarray(v.reshape(16, 128).T)


def kernel(x, w_in, gla_gate_w, gla_gate_b, gla_norm_g, w_proj_a, w_proj_b, w_out, rel_bias,
           ln1_g, ln1_b, ffn_w_gate, ffn_w_up, ffn_conv_w, ffn_conv_b, ffn_w_down, ln2_g, ln2_b):
    f32 = np.float32
    x = np.asarray(x, f32)[0]
    C = NCORES
    xT = [np.ascontiguousarray(x[c * T:(c + 1) * T].T) for c in range(C)]
    BT, valid = bias_tables(np.asarray(rel_bias, f32))
    gcon = gla_consts()
    hm = [hmask(c) for c in range(C)]
    cm = [cmask(c) for c in range(C)]
    for l in range(4):
        gate = np.concatenate([np.asarray(gla_gate_w[l], f32), np.asarray(gla_gate_b[l], f32)[None]], 0)
        wl = np.asarray(w_in[l], f32)
        r1 = _run("s1", build_stage1, [{"xT": xT[c], "w_in": wl, "gate": gate} for c in range(C)])
        r2 = _run("s2", build_stage2, [{"GQT": r1[c]["GQT"], "GKT": r1[c]["GKT"], "GKM": r1[c]["GKM"], "GVM": r1[c]["GVM"],
                                        "LAM": r1[c]["LAM"], "GC": gcon} for c in range(C)])
        UALL = np.stack([r2[c]["U"] for c in range(C)], 0)
        DALL = np.ascontiguousarray(np.concatenate([r2[c]["DT"] for c in range(C)], 1))
        KH, VH = [], []
        for g in range(3):
            H = HALO[g]
            kt_all = np.concatenate([r1[c]["KT"][g * 1024:(g + 1) * 1024] for c in range(C)], 1)
            v_all = np.concatenate([r1[c]["V"][g] for c in range(C)], 0)
            kt_pad = np.concatenate([np.zeros((1024, H), kt_all.dtype), kt_all], 1)
            v_pad = np.concatenate([np.zeros((H, 1024), v_all.dtype), v_all], 0)
            KH.append([np.ascontiguousarray(kt_pad[:, c * T:c * T + H + T]) for c in range(C)])
            VH.append([np.ascontiguousarray(v_pad[c * T:c * T + H + T]) for c in range(C)])
        ng = np.ascontiguousarray(np.asarray(gla_norm_g[l], f32).reshape(4, 128).T)
        im = []
        for c in range(C):
            m = {"QT": r1[c]["QT"], "BT": BT, "VALID": valid, "HMASK": hm[c], "OL": r2[c]["OL"], "QTL": r2[c]["QTL"],
                 "UALL": UALL, "DALL": DALL, "CMASK": cm[c], "SGR": r1[c]["SGR"], "NG": ng}
            for g in range(3):
                m[f"KH{g}"] = KH[g][c]
                m[f"VH{g}"] = VH[g][c]
            im.append(m)
        r3 = _run("s3a", build_stage3a, im)
        r3b = _run("s3b", build_stage3b, [{"YAT": r3[c]["YAT"], "YBT": r3[c]["YBT"], "SMA": r1[c]["SMA"], "SMB": r1[c]["SMB"],
                                           "xT": xT[c], "w_proj_a": np.asarray(w_proj_a[l], f32),
                                           "w_proj_b": np.asarray(w_proj_b[l], f32), "w_out": np.asarray(w_out[l], f32),
                                           "LNG": _lnp(np.asarray(ln1_g[l], f32)), "LNB": _lnp(np.asarray(ln1_b[l], f32))}
                                          for c in range(C)])
        x1T = [r3b[c]["X1T"] for c in range(C)]
        x1h = [np.zeros((2048, 2), f32)] + [np.ascontiguousarray(x1T[c - 1][:, T - 2:T]) for c in range(1, C)]
        cw = np.ascontiguousarray(np.asarray(ffn_conv_w[l], f32).reshape(3, FC, 128).transpose(2, 1, 0))
        cb = np.ascontiguousarray(np.asarray(ffn_conv_b[l], f32).reshape(FC, 128).T)
        r4 = _run("s4a", build_stage4a, [{"X1T": x1T[c], "X1H": x1h[c], "ffn_w_gate": np.asarray(ffn_w_gate[l], f32),
                                          "ffn_w_up": np.asarray(ffn_w_up[l], f32), "CW": cw, "CB": cb} for c in range(C)])
        r5 = _run("s4b", build_stage4b, [{"HT": r4[c]["HT"], "X1T": x1T[c], "ffn_w_down": np.asarray(ffn_w_down[l], f32),
                                          "LNG": _lnp(np.asarray(ln2_g[l], f32)), "LNB": _lnp(np.asarray(ln2_b[l], f32))}
                                         for c in range(C)])
        xT = [r5[c]["X2T"] for c in range(C)]
    out = np.concatenate([np.ascontiguousarray(np.asarray(xT[c], f32).T) for c in range(C)], 0)
    return out[None].astype(f32)
```

```python
import contextlib
import numpy as np
import concourse.bass as bass
import concourse.mybir as mybir
from concourse.bass_utils import run_bass_kernel_spmd

F32 = mybir.dt.float32
BF16 = mybir.dt.bfloat16
AF = mybir.ActivationFunctionType
ALU = mybir.AluOpType

ENGS = ("pe", "act", "dve", "pool", "sp")
SAME_ENGINE_SYNC = True


class Prog:
    _uid = [0]

    def __init__(self, nc):
        self.nc = nc
        Prog._uid[0] += 1
        self.pfx = f"P{Prog._uid[0]}_"
        self.stack = contextlib.ExitStack()
        self.q = {e: [] for e in ENGS}
        self.cnt = {}
        self.seen = {e: {} for e in ENGS}
        self.res = {}
        self.semh = {}
        self.nsb = 0
        self.same_sync = SAME_ENGINE_SYNC

    GLOBAL_SEMS = {}

    def sem(self, key):
        if isinstance(key, tuple) and key[0] == "dma" and str(key[1]).startswith("GL_"):
            g = Prog.GLOBAL_SEMS
            if key[1] not in g:
                g[key[1]] = [self.nc.alloc_semaphore(name=key[1]), 0]
            if key not in self.semh:
                self.semh[key] = g[key[1]][0]
                self.cnt[key] = g[key[1]][1]
            return self.semh[key]
        if key not in self.semh:
            self.semh[key] = self.stack.enter_context(self.nc.semaphore(self.pfx + "s_" + str(key).replace(" ", "").replace("'", "").replace("(", "").replace(")", "").replace(",", "_")))
            self.cnt[key] = 0
        return self.semh[key]

    def sb(self, name, shape, dt):
        return self.stack.enter_context(self.nc.sbuf_tensor(self.pfx + name, list(shape), dt))

    def ps(self, name, shape, dt=F32):
        return self.stack.enter_context(self.nc.psum_tensor(self.pfx + name, list(shape), dt))

    def dram(self, name, shape, dt, kind="Internal"):
        return self.nc.dram_tensor(name, list(shape), dt, kind=kind).ap()

    def add(self, eng, fn, reads=(), writes=(), dsem=None, inc=16):
        need = {}

        def want(tok):
            if tok is None:
                return
            k, v = tok
            if need.get(k, 0) < v:
                need[k] = v

        for k in reads:
            st = self.res.get(k)
            if st:
                want(st["w"])
        for k in writes:
            st = self.res.get(k)
            if st:
                want(st["w"])
                for t in st["r"].items():
                    want(t)
        waits = []
        for k, v in need.items():
            if k == eng and (eng == "pe" or not self.same_sync):
                continue
            if isinstance(k, tuple) and k[0] == "dma":
                v = self.cnt[k]
            if self.seen[eng].get(k, 0) >= v:
                continue
            self.seen[eng][k] = v
            waits.append((k, v))
        if dsem is None:
            sk = eng
            self.sem(sk)
            self.cnt[sk] += 1
            inc = 1
        else:
            sk = ("dma", dsem)
            self.sem(sk)
            self.cnt[sk] += inc
            if str(dsem).startswith("GL_"):
                Prog.GLOBAL_SEMS[dsem][1] = self.cnt[sk]
        tok = (sk, self.cnt[sk])
        self.q[eng].append((waits, fn, sk, inc))
        for k in reads:
            st = self.res.setdefault(k, {"w": None, "r": {}})
            if st["r"].get(sk, 0) < tok[1]:
                st["r"][sk] = tok[1]
        for k in writes:
            self.res[k] = {"w": tok, "r": {}}
        return tok

    def dma(self, out, in_, reads=(), writes=(), dsem=None, eng="sp", **kw):
        assert dsem is not None
        return self.add(eng, lambda e: e.dma_start(out=out, in_=in_, **kw), reads, writes, dsem=dsem)

    def final_wait(self, eng="sp"):
        self.finals = eng

    def emit(self):
        nc = self.nc
        semh = self.semh
        q = self.q
        cnt = self.cnt

        def replay(name, e, final=False):
            for waits, fn, sk, inc in q[name]:
                for k, v in waits:
                    e.wait_ge(semh[k], v)
                ins = fn(e)
                ins.then_inc(semh[sk], inc)
            if final:
                for k, h in semh.items():
                    if cnt[k] > 0:
                        e.wait_ge(h, cnt[k])

        with nc.Block() as block:
            @block.sync
            def _(e):
                replay("sp", e, final=True)

            @block.tensor
            def _(e):
                replay("pe", e)

            @block.scalar
            def _(e):
                replay("act", e)

            @block.vector
            def _(e):
                replay("dve", e)

            @block.gpsimd
            def _(e):
                replay("pool", e)
        self.stack.close()


T = 1024
D = 2048
KC = D // 128
INC = 19472
WT = 256


def build_stage1(nc=None):
    nc = nc or bass.Bass("TRN2", target_bir_lowering=False)
    p = Prog(nc)
    xT = p.dram("xT", [D, T], F32, "ExternalInput")
    w_in = p.dram("w_in", [D, INC], F32, "ExternalInput")
    gate = p.dram("gate", [17, 1024], F32, "ExternalInput")
    QT = p.dram("QT", [3 * 1024, T], BF16, "ExternalOutput")
    KT = p.dram("KT", [3 * 1024, T], BF16, "ExternalOutput")
    V = p.dram("V", [3, T, 1024], BF16, "ExternalOutput")
    GQT = p.dram("GQT", [1024, T], BF16, "ExternalOutput")
    GKT = p.dram("GKT", [1024, T], BF16, "ExternalOutput")
    GKM = p.dram("GKM", [T, 1024], BF16, "ExternalOutput")
    GVM = p.dram("GVM", [T, 2048], BF16, "ExternalOutput")
    SGR = p.dram("SGR", [2048, T], BF16, "ExternalOutput")
    SMA = p.dram("SMA", [2048, T], BF16, "ExternalOutput")
    SMB = p.dram("SMB", [2048, T], BF16, "ExternalOutput")
    LAM = p.dram("LAM", [T, 1024], F32, "ExternalOutput")
    emit_stage1(p, xT, w_in, gate, QT, KT, V, GQT, GKT, GKM, GVM, SGR, SMA, SMB, LAM)
    p.emit()
    return nc


class Rot:
    def __init__(self, tiles, key):
        self.tiles = tiles
        self.key = key
        self.i = 0

    def next(self):
        i = self.i % len(self.tiles)
        self.i += 1
        return self.tiles[i], (self.key, i)


def load_xT_bf16(p, xT, xb, nt=T):
    xs = [p.sb(f"xs{i}", [128, nt], F32) for i in range(2)]
    for kc in range(KC):
        s = xs[kc % 2]
        p.dma(s[:, :], xT[kc * 128:(kc + 1) * 128, :], writes=[("xs", kc % 2)], dsem=f"xs{kc % 2}")
        eng = "dve" if kc % 2 == 0 else "pool"
        p.add(eng, lambda e, s=s, kc=kc: e.tensor_copy(out=xb[:, kc, :], in_=s[:, :]),
              reads=[("xs", kc % 2)], writes=[("xb", kc)])


class WStream:
    def __init__(self, p, kc, wt=WT, nbuf=2, name="w"):
        self.p = p
        self.kc = kc
        self.wt = wt
        self.name = name
        self.ws = [p.sb(f"{name}s{i}", [128, kc, wt], F32) for i in range(nbuf)]
        self.wb = [p.sb(f"{name}b{i}", [128, kc, wt], BF16) for i in range(nbuf)]
        self.i = 0
        self.nbuf = nbuf

    def load(self, w, c0, nc_):
        p = self.p
        i = self.i % self.nbuf
        self.i += 1
        s, b = self.ws[i], self.wb[i]
        src = w[:, c0:c0 + nc_].rearrange("(k p) c -> p k c", p=128)
        half = self.kc // 2
        nm = self.name
        p.dma(s[:, 0:half, 0:nc_], src[:, 0:half, :], writes=[(nm + "s", i, 0)], dsem=f"{nm}s{i}")
        p.dma(s[:, half:, 0:nc_], src[:, half:, :], writes=[(nm + "s", i, 1)], dsem=f"{nm}s{i}")
        p.add("pool", lambda e: e.tensor_copy(out=b[:, :, 0:nc_], in_=s[:, :, 0:nc_]),
              reads=[(nm + "s", i, 0), (nm + "s", i, 1)], writes=[(nm + "b", i)])
        return b, (nm + "b", i)


def emit_stage1(p, xT, w_in, gate, QT, KT, V, GQT, GKT, GKM, GVM, SGR, SMA, SMB, LAM):
    xb = p.sb("xb", [128, KC, T], BF16)
    load_xT_bf16(p, xT, xb)
    xkeys = [("xb", kc) for kc in range(KC)]
    ws = WStream(p, KC)
    psum = Rot([p.ps(f"ps{i}", [128, 512], F32) for i in range(8)], "ps")
    ob = Rot([p.sb(f"ob{i}", [128, 512], BF16) for i in range(4)], "ob")
    evac_i = [0]

    def evac(dst_sb, src_ps, func, rk, wk):
        if func is None:
            evac_i[0] += 1
            if evac_i[0] % 2 == 0:
                p.add("dve", lambda e: e.tensor_copy(out=dst_sb, in_=src_ps), reads=rk, writes=wk)
                return
            func = AF.Copy
        p.add("act", lambda e: e.activation(out=dst_sb, in_=src_ps, func=func), reads=rk, writes=wk)

    def fm_job(c0, ncols, dst, func=None):
        for t0 in range(0, ncols, WT):
            n = min(WT, ncols - t0)
            wb, wk = ws.load(w_in, c0 + t0, n)
            for m0 in range(0, n, 128):
                mc = min(128, n - m0)
                for th in range(T // 512):
                    ps, pk = psum.next()
                    for kc in range(KC):
                        p.add("pe", lambda e, ps=ps, wb=wb, kc=kc, m0=m0, mc=mc, th=th: e.matmul(
                            ps[0:mc, :], lhsT=wb[:, kc, m0:m0 + mc], rhs=xb[:, kc, th * 512:(th + 1) * 512],
                            start=(kc == 0), stop=(kc == KC - 1)),
                            reads=[wk, xkeys[kc]], writes=[pk])
                    o, ok = ob.next()
                    evac(o[0:mc, :], ps[0:mc, :], func, [pk], [ok])
                    p.dma(dst[t0 + m0:t0 + m0 + mc, th * 512:(th + 1) * 512], o[0:mc, :],
                          reads=[ok], dsem=f"ob{ok[1]}", eng="act")

    def tm_job(c0, ncols, dst):
        for t0 in range(0, ncols, WT):
            n = min(WT, ncols - t0)
            wb, wk = ws.load(w_in, c0 + t0, n)
            for tp in range(T // 256):
                ps, pk = psum.next()
                for j in range(2):
                    tt = tp * 2 + j
                    for kc in range(KC):
                        p.add("pe", lambda e, ps=ps, wb=wb, kc=kc, tt=tt, j=j, n=n: e.matmul(
                            ps[:, j * 256:j * 256 + n], lhsT=xb[:, kc, tt * 128:(tt + 1) * 128], rhs=wb[:, kc, 0:n],
                            start=(kc == 0), stop=(kc == KC - 1)),
                            reads=[wk, xkeys[kc]], writes=[pk])
                o, ok = ob.next()
                evac(o[:, :], ps[:, :], None, [pk], [ok])
                for j in range(2):
                    tt = tp * 2 + j
                    p.dma(dst[tt * 128:(tt + 1) * 128, t0:t0 + n], o[:, j * 256:j * 256 + n],
                          reads=[ok], dsem=f"ob{ok[1]}", eng="act")

    for g in range(3):
        fm_job((3 * g) * 1024, 1024, QT[g * 1024:(g + 1) * 1024, :])
        fm_job((3 * g + 1) * 1024, 1024, KT[g * 1024:(g + 1) * 1024, :])
        tm_job((3 * g + 2) * 1024, 1024, V[g])
    fm_job(9216, 1024, GQT)
    fm_job(10240, 1024, GKT)
    tm_job(10240, 1024, GKM)
    tm_job(11264, 2048, GVM)
    fm_job(13312, 2048, SGR, AF.Silu)
    fm_job(15376, 2048, SMA, AF.Sigmoid)
    fm_job(17424, 2048, SMB, AF.Sigmoid)

    gl = p.sb("gl", [17, T], F32)
    gw = p.sb("gw", [17, 1024], F32)
    p.dma(gw[:, :], gate[:, :], writes=["gw"], dsem="gw")
    p.add("pool", lambda e: e.memset(gl[:, :], 1.0), writes=["gl"])
    wb, wk = ws.load(w_in, 15360, 16)
    for th in range(T // 512):
        ps, pk = psum.next()
        for kc in range(KC):
            p.add("pe", lambda e, ps=ps, wb=wb, kc=kc, th=th: e.matmul(
                ps[0:16, :], lhsT=wb[:, kc, 0:16], rhs=xb[:, kc, th * 512:(th + 1) * 512],
                start=(kc == 0), stop=(kc == KC - 1)), reads=[wk, xkeys[kc]], writes=[pk])
        p.add("dve", lambda e, ps=ps, th=th: e.tensor_copy(out=gl[0:16, th * 512:(th + 1) * 512], in_=ps[0:16, :]),
              reads=[pk], writes=["gl"])
    la = Rot([p.sb(f"la{i}", [128, 512], F32) for i in range(2)], "la")
    for tt in range(T // 128):
        for ch in range(2):
            ps, pk = psum.next()
            p.add("pe", lambda e, ps=ps, tt=tt, ch=ch: e.matmul(
                ps[:, :], lhsT=gl[:, tt * 128:(tt + 1) * 128], rhs=gw[:, ch * 512:(ch + 1) * 512],
                start=True, stop=True), reads=["gl", "gw"], writes=[pk])
            o, ok = la.next()
            p.add("act", lambda e, o=o, ps=ps: e.activation(out=o[:, :], in_=ps[:, :], func=AF.Exp, scale=-1.0),
                  reads=[pk], writes=[ok])
            p.add("act", lambda e, o=o: e.activation(out=o[:, :], in_=o[:, :], func=AF.Ln, bias=1.0),
                  reads=[ok], writes=[ok])
            p.dma(LAM[tt * 128:(tt + 1) * 128, ch * 512:(ch + 1) * 512], o[:, :], reads=[ok], dsem=f"la{ok[1]}", eng="act")


NT = T // 128


def gla_consts():
    j = np.arange(128)[:, None]
    t = np.arange(128)[None, :]
    same = (j // 64) == (t // 64)
    tri = np.where(same & (j <= t), -1.0 / 16.0, 0.0).astype(np.float32)
    trirev = np.where(same & (j > t), -1.0 / 16.0, 0.0).astype(np.float32)
    mask = np.where(same & (j <= t), 1.0, 0.0).astype(np.float32)
    return np.concatenate([tri, trirev, mask], axis=1)


def build_stage2():
    nc = bass.Bass("TRN2", target_bir_lowering=False)
    p = Prog(nc)
    GQT = p.dram("GQT", [1024, T], BF16, "ExternalInput")
    GKT = p.dram("GKT", [1024, T], BF16, "ExternalInput")
    GKM = p.dram("GKM", [T, 1024], BF16, "ExternalInput")
    GVM = p.dram("GVM", [T, 2048], BF16, "ExternalInput")
    LAM = p.dram("LAM", [T, 1024], F32, "ExternalInput")
    GC = p.dram("GC", [128, 384], F32, "ExternalInput")
    OL = p.dram("OL", [2048, T], F32, "ExternalOutput")
    QTL = p.dram("QTL", [1024, T], BF16, "ExternalOutput")
    U = p.dram("U", [1024, 512], F32, "ExternalOutput")
    DT = p.dram("DT", [1024, 1], F32, "ExternalOutput")
    emit_stage2(p, GQT, GKT, GKM, GVM, LAM, GC, OL, QTL, U, DT)
    p.emit()
    return nc


def emit_stage2(p, GQT, GKT, GKM, GVM, LAM, GC, OL, QTL, U, DT):
    gc = p.sb("gc", [128, 384], F32)
    p.dma(gc[:, :], GC[:, :], writes=["gc"], dsem="gc")
    tri, trirev, mask = gc[:, 0:128], gc[:, 128:256], gc[:, 256:384]
    qT = [p.sb(f"g_qT{i}", [128, T], BF16) for i in range(2)]
    kT = [p.sb(f"g_kT{i}", [128, T], BF16) for i in range(2)]
    ktm = p.sb("g_ktm", [128, NT, 256], BF16)
    vtm = p.sb("g_vtm", [128, NT, 512], BF16)
    lam = p.sb("g_lam", [128, NT, 256], F32)
    E = [p.sb(f"g_E{i}", [128, T], F32) for i in range(2)]
    qd = [p.sb(f"g_qd{i}", [128, T], BF16) for i in range(2)]
    ki = [p.sb(f"g_ki{i}", [128, T], BF16) for i in range(2)]
    ke = p.sb("g_ke", [128, NT, 256], BF16)
    qtl = [p.sb(f"g_qtl{i}", [128, T], BF16) for i in range(2)]
    S = [p.sb(f"g_S{i}", [128, 512], F32) for i in range(2)]
    Sb = [p.sb(f"g_Sb{i}", [128, 512], BF16) for i in range(2)]
    G = [p.sb(f"g_G{i}", [128, 1], F32) for i in range(2)]
    tmpf = Rot([p.sb(f"g_tmp{i}", [128, 256], F32) for i in range(3)], "g_tmp")
    attb = Rot([p.sb(f"g_att{i}", [128, 128], BF16) for i in range(2)], "g_att")
    osb = Rot([p.sb(f"g_o{i}", [128, 512], F32) for i in range(2)], "g_o")
    psA = Rot([p.ps(f"g_psA{i}", [128, 512], F32) for i in range(3)], "g_psA")
    psO = Rot([p.ps(f"g_psO{i}", [128, 512], F32) for i in range(2)], "g_psO")
    psS = Rot([p.ps(f"g_psS{i}", [128, 512], F32) for i in range(3)], "g_psS")

    for h in range(4):
        for dc in range(2):
            r0 = h * 256 + dc * 128
            p.dma(qT[dc][:, :], GQT[r0:r0 + 128, :], writes=[("qT", dc)], dsem=f"gq{dc}")
            p.dma(kT[dc][:, :], GKT[r0:r0 + 128, :], writes=[("kT", dc)], dsem=f"gk{dc}")
        p.dma(ktm[:, :, :], GKM[:, h * 256:(h + 1) * 256].rearrange("(n p) c -> p n c", p=128),
              writes=["ktm"], dsem="gktm")
        p.dma(vtm[:, :, :], GVM[:, h * 512:(h + 1) * 512].rearrange("(n p) c -> p n c", p=128),
              writes=["vtm"], dsem="gvtm")
        p.dma(lam[:, :, :], LAM[:, h * 256:(h + 1) * 256].rearrange("(n p) c -> p n c", p=128),
              writes=["lam"], dsem="glam")
        for dc in range(2):
            p.add("pool", lambda e, dc=dc: e.memset(S[dc][:, :], 0.0), writes=[("S", dc)])
            p.add("pool", lambda e, dc=dc: e.memset(Sb[dc][:, :], 0.0), writes=[("Sb", dc)])
            p.add("pool", lambda e, dc=dc: e.memset(G[dc][:, :], 1.0), writes=[("G", dc)])
        for tt in range(NT):
            cs = slice(tt * 128, (tt + 1) * 128)
            for dc in range(2):
                ps, pk = psA.next()
                p.add("pe", lambda e, ps=ps, tt=tt, dc=dc: e.matmul(
                    ps[:, 0:128], lhsT=lam[:, tt, dc * 128:(dc + 1) * 128], rhs=tri, start=True, stop=True),
                    reads=["lam", "gc"], writes=[pk])
                p.add("act", lambda e, ps=ps, dc=dc, cs=cs: e.activation(out=E[dc][:, cs], in_=ps[:, 0:128], func=AF.Exp),
                      reads=[pk], writes=[("E", dc, tt)])
                tm, tk = tmpf.next()
                p.add("act", lambda e, ps=ps, tm=tm: e.activation(out=tm[:, 0:128], in_=ps[:, 0:128], func=AF.Exp, scale=-1.0),
                      reads=[pk], writes=[tk])
                p.add("dve", lambda e, dc=dc, cs=cs: e.scalar_tensor_tensor(
                    out=qd[dc][:, cs], in0=qT[dc][:, cs], scalar=0.0625, in1=E[dc][:, cs], op0=ALU.mult, op1=ALU.mult),
                    reads=[("qT", dc), ("E", dc, tt)], writes=[("qd", dc, tt)])
                p.add("dve", lambda e, dc=dc, cs=cs, tm=tm: e.tensor_tensor(
                    out=ki[dc][:, cs], in0=kT[dc][:, cs], in1=tm[:, 0:128], op=ALU.mult),
                    reads=[("kT", dc), tk], writes=[("ki", dc, tt)])
            ps, pk = psA.next()
            p.add("pe", lambda e, ps=ps, tt=tt: e.matmul(
                ps[:, 0:256], lhsT=trirev, rhs=lam[:, tt, :], start=True, stop=True),
                reads=["lam", "gc"], writes=[pk])
            tm, tk = tmpf.next()
            p.add("act", lambda e, ps=ps, tm=tm: e.activation(out=tm[:, :], in_=ps[:, 0:256], func=AF.Exp),
                  reads=[pk], writes=[tk])
            p.add("dve", lambda e, tt=tt, tm=tm: e.tensor_tensor(
                out=ke[:, tt, :], in0=ktm[:, tt, :], in1=tm[:, :], op=ALU.mult),
                reads=["ktm", tk], writes=[("ke", tt)])
        for tt in range(NT):
            cs = slice(tt * 128, (tt + 1) * 128)
            ps, pk = psA.next()
            for dc in range(2):
                p.add("pe", lambda e, ps=ps, dc=dc, cs=cs: e.matmul(
                    ps[:, 0:128], lhsT=ki[dc][:, cs], rhs=qd[dc][:, cs], start=(dc == 0), stop=(dc == 1)),
                    reads=[("ki", dc, tt), ("qd", dc, tt)], writes=[pk])
            ab, ak = attb.next()
            p.add("dve", lambda e, ps=ps, ab=ab: e.tensor_tensor(out=ab[:, :], in0=ps[:, 0:128], in1=mask, op=ALU.mult),
                  reads=[pk, "gc"], writes=[ak])
            po, pok = psO.next()
            for par in range(2):
                c0 = tt * 128 + par * 64
                pr = slice(par * 64, par * 64 + 64)
                for dc in range(2):
                    p.add("dve", lambda e, dc=dc, c0=c0: e.tensor_scalar(
                        out=qtl[dc][:, c0:c0 + 64], in0=qd[dc][:, c0:c0 + 64], scalar1=G[dc][:, 0:1], scalar2=None,
                        op0=ALU.mult), reads=[("qd", dc, tt), ("G", dc)], writes=[("qtl", dc)])
                    p.add("dve", lambda e, dc=dc, c0=c0: e.tensor_tensor(
                        out=G[dc][:, :], in0=G[dc][:, :], in1=E[dc][:, c0 + 63:c0 + 64], op=ALU.mult),
                        reads=[("G", dc), ("E", dc, tt)], writes=[("G", dc)])
                for ec in range(4):
                    oc = slice(ec * 128 + par * 64, ec * 128 + par * 64 + 64)
                    es = slice(ec * 128, (ec + 1) * 128)
                    for dc in range(2):
                        p.add("pe", lambda e, po=po, oc=oc, es=es, dc=dc, c0=c0: e.matmul(
                            po[:, oc], lhsT=Sb[dc][:, es], rhs=qd[dc][:, c0:c0 + 64], start=(dc == 0), stop=False),
                            reads=[("Sb", dc), ("qd", dc, tt)], writes=[pok])
                    p.add("pe", lambda e, po=po, oc=oc, es=es, pr=pr, tt=tt, ab=ab: e.matmul(
                        po[:, oc], lhsT=vtm[pr, tt, es], rhs=ab[pr, pr], start=False, stop=True),
                        reads=["vtm", ak], writes=[pok])
                for dc in range(2):
                    pss, psk = psS.next()
                    p.add("pe", lambda e, pss=pss, pr=pr, tt=tt, dc=dc: e.matmul(
                        pss[:, :], lhsT=ke[pr, tt, dc * 128:(dc + 1) * 128], rhs=vtm[pr, tt, :], start=True, stop=True),
                        reads=[("ke", tt), "vtm"], writes=[psk])
                    p.add("dve", lambda e, pss=pss, dc=dc, c0=c0: e.scalar_tensor_tensor(
                        out=S[dc][:, :], in0=S[dc][:, :], scalar=E[dc][:, c0 + 63:c0 + 64], in1=pss[:, :],
                        op0=ALU.mult, op1=ALU.add), reads=[("S", dc), ("E", dc, tt), psk], writes=[("S", dc)])
                    p.add("act", lambda e, dc=dc: e.activation(out=Sb[dc][:, :], in_=S[dc][:, :], func=AF.Copy),
                          reads=[("S", dc)], writes=[("Sb", dc)])
            o, ok = osb.next()
            p.add("act", lambda e, o=o, po=po: e.activation(out=o[:, :], in_=po[:, :], func=AF.Copy),
                  reads=[pok], writes=[ok])
            p.dma(OL[h * 512:(h + 1) * 512, cs].rearrange("(c p) t -> p c t", p=128),
                  o[:, :].rearrange("p (c t) -> p c t", c=4), reads=[ok], dsem=f"go{ok[1]}", eng="act")
        for dc in range(2):
            r0 = h * 256 + dc * 128
            p.dma(QTL[r0:r0 + 128, :], qtl[dc][:, :], reads=[("qtl", dc)], dsem=f"gqtl{dc}")
            p.dma(U[r0:r0 + 128, :], S[dc][:, :], reads=[("S", dc)], dsem=f"gU{dc}")
            p.dma(DT[r0:r0 + 128, :], G[dc][:, :], reads=[("G", dc)], dsem=f"gD{dc}")


HALO = (128, 512, 2048)
DIL = (1, 4, 16)
SCALE = 128 ** -0.5
GROUPS = [0, 1, 2]


def build_stage3a(do_attn=True, do_gla=True):
    nc = bass.Bass("TRN2", target_bir_lowering=False)
    p = Prog(nc)
    d = {}
    d["QT"] = p.dram("QT", [3 * 1024, T], BF16, "ExternalInput")
    for g in range(3):
        d[f"KH{g}"] = p.dram(f"KH{g}", [1024, HALO[g] + T], BF16, "ExternalInput")
        d[f"VH{g}"] = p.dram(f"VH{g}", [HALO[g] + T, 1024], BF16, "ExternalInput")
    d["BT"] = p.dram("BT", [24, 128, 256], F32, "ExternalInput")
    d["VALID"] = p.dram("VALID", [128, 256], F32, "ExternalInput")
    d["HMASK"] = p.dram("HMASK", [128, 3], F32, "ExternalInput")
    d["OL"] = p.dram("OL", [2048, T], F32, "ExternalInput")
    d["QTL"] = p.dram("QTL", [1024, T], BF16, "ExternalInput")
    d["UALL"] = p.dram("UALL", [8, 1024, 512], F32, "ExternalInput")
    d["DALL"] = p.dram("DALL", [1024, 8], F32, "ExternalInput").rearrange("r (c o) -> r c o", o=1)
    d["CMASK"] = p.dram("CMASK", [128, 8], F32, "ExternalInput")
    d["SGR"] = p.dram("SGR", [2048, T], BF16, "ExternalInput")
    d["NG"] = p.dram("NG", [128, 4], F32, "ExternalInput")
    d["YAT"] = p.dram("YAT", [1024, T], BF16, "ExternalOutput")
    d["YBT"] = p.dram("YBT", [2048, T], BF16, "ExternalOutput")
    if do_attn: emit_attn(p, d)
    if do_gla: emit_gla_fin(p, d)
    p.emit()
    return nc


def emit_attn(p, d):
    QT, BT, VALID, HMASK, YAT = d["QT"], d["BT"], d["VALID"], d["HMASK"], d["YAT"]
    valid = p.sb("a_valid", [128, 256], F32)
    hmask = p.sb("a_hmask", [128, 3], F32)
    ones = p.sb("a_ones", [128, 128], BF16)
    p.dma(valid[:, :], VALID[:, :], writes=["valid"], dsem="a_c0")
    p.dma(hmask[:, :], HMASK[:, :], writes=["hmask"], dsem="a_c1")
    p.add("pool", lambda e: e.memset(ones[:, :], 1.0), writes=["ones"])
    qT = Rot([p.sb(f"a_q{i}", [128, T], BF16) for i in range(2)], "a_q")
    kT = Rot([p.sb(f"a_k{i}", [128, 2048 + T], BF16) for i in range(2)], "a_k")
    vt = Rot([p.sb(f"a_v{i}", [128, 32, 128], BF16) for i in range(2)], "a_v")
    bt = Rot([p.sb(f"a_bt{i}", [128, 256], F32) for i in range(2)], "a_bt")
    tfull = Rot([p.sb(f"a_tf{i}", [128, 256], F32) for i in range(2)], "a_tf")
    tfirst = Rot([p.sb(f"a_t1{i}", [128, 256], F32) for i in range(2)], "a_t1")
    pf = Rot([p.sb(f"a_pf{i}", [128, 256], F32) for i in range(3)], "a_pf")
    pb = Rot([p.sb(f"a_pb{i}", [128, 256], BF16) for i in range(3)], "a_pb")
    num = p.sb("a_num", [128, T], F32)
    den = p.sb("a_den", [128, T], F32)
    ya = Rot([p.sb(f"a_ya{i}", [128, T], BF16) for i in range(2)], "a_ya")
    psS = Rot([p.ps(f"a_psS{i}", [128, 512], F32) for i in range(3)], "a_psS")
    psO = Rot([p.ps(f"a_psO{i}", [128, 512], F32) for i in range(3)], "a_psO")

    for h in range(8):
        for g in GROUPS:
            H, dl = HALO[g], DIL[g]
            q, qk = qT.next()
            k, kk = kT.next()
            v, vk = vt.next()
            r0 = g * 1024 + h * 128
            p.dma(q[:, :], QT[r0:r0 + 128, :], writes=[qk], dsem=f"a_q{qk[1]}")
            p.dma(k[:, 0:H + T], d[f"KH{g}"][h * 128:(h + 1) * 128, :], writes=[kk], dsem=f"a_k{kk[1]}")
            VH = d[f"VH{g}"]
            if g < 2:
                ntile = (H + T) // (128 * dl)
                for r in range(dl):
                    src = VH[r:H + T:dl, h * 128:(h + 1) * 128] if dl > 1 else VH[:, h * 128:(h + 1) * 128]
                    p.dma(v[:, r * ntile:(r + 1) * ntile, :], src.rearrange("(j p) c -> p j c", p=128),
                          writes=[(vk, r)], dsem=f"a_v{vk[1]}")
            else:
                for r in range(16):
                    srcA = VH[r:H:16, h * 128:(h + 1) * 128]
                    p.dma(v[:, r, :], srcA, writes=[(vk, r)], dsem=f"a_v{vk[1]}")
                srcB = VH[H:H + T, h * 128:(h + 1) * 128].rearrange("(i r) c -> i r c", r=16)
                p.dma(v[0:64, 16:32, :], srcB, writes=[(vk, 16)], dsem=f"a_v{vk[1]}")
            b, bk = bt.next()
            tf, tfk = tfull.next()
            t1, t1k = tfirst.next()
            p.dma(b[:, :], BT[g * 8 + h], writes=[bk], dsem=f"a_bt{bk[1]}")
            p.add("act", lambda e, b=b: e.activation(out=b[:, :], in_=b[:, :], func=AF.Exp), reads=[bk], writes=[bk])
            p.add("dve", lambda e, b=b, tf=tf: e.tensor_tensor(out=tf[:, :], in0=b[:, :], in1=valid[:, :], op=ALU.mult),
                  reads=[bk, "valid"], writes=[tfk])
            p.add("dve", lambda e, tf=tf, t1=t1: e.tensor_copy(out=t1[:, 128:256], in_=tf[:, 128:256]),
                  reads=[tfk], writes=[t1k])
            p.add("dve", lambda e, tf=tf, t1=t1, g=g: e.tensor_scalar(
                out=t1[:, 0:128], in0=tf[:, 0:128], scalar1=hmask[:, g:g + 1], scalar2=None, op0=ALU.mult),
                reads=[tfk, "hmask", t1k], writes=[t1k])
            vkeys = [(vk, r) for r in range(18)]
            if g < 2:
                nq = T // (128 * dl)
                ntile = (H + T) // (128 * dl)
                for r in range(dl):
                    for m in range(nq):
                        qs = slice(r + dl * 128 * m, r + dl * 128 * m + dl * 127 + 1, dl)
                        kprev = slice(r + dl * 128 * m, r + dl * 128 * m + dl * 127 + 1, dl)
                        kcur = slice(r + dl * 128 * (m + 1), r + dl * 128 * (m + 1) + dl * 127 + 1, dl)
                        tab, tabk = (t1, t1k) if m == 0 else (tf, tfk)
                        ps, pk = psS.next()
                        p.add("pe", lambda e, ps=ps, k=k, q=q, kprev=kprev, qs=qs: e.matmul(
                            ps[:, 0:128], lhsT=k[:, kprev], rhs=q[:, qs], start=True, stop=True),
                            reads=[kk, qk], writes=[pk])
                        p.add("pe", lambda e, ps=ps, k=k, q=q, kcur=kcur, qs=qs: e.matmul(
                            ps[:, 128:256], lhsT=k[:, kcur], rhs=q[:, qs], start=True, stop=True),
                            reads=[kk, qk], writes=[pk])
                        f, fk = pf.next()
                        pbb, pbk = pb.next()
                        p.add("act", lambda e, f=f, ps=ps: e.activation(out=f[:, :], in_=ps[:, 0:256], func=AF.Exp, scale=SCALE),
                              reads=[pk], writes=[fk])
                        p.add("dve", lambda e, f=f, pbb=pbb, tab=tab: e.tensor_tensor(out=pbb[:, :], in0=f[:, :], in1=tab[:, :], op=ALU.mult),
                              reads=[fk, tabk], writes=[pbk])
                        po, pok = psO.next()
                        j0 = r * ntile + m
                        p.add("pe", lambda e, po=po, v=v, pbb=pbb, j0=j0: e.matmul(
                            po[:, 0:128], lhsT=v[:, j0, :], rhs=pbb[:, 0:128], start=True, stop=False),
                            reads=vkeys + [pbk], writes=[pok])
                        p.add("pe", lambda e, po=po, v=v, pbb=pbb, j0=j0: e.matmul(
                            po[:, 0:128], lhsT=v[:, j0 + 1, :], rhs=pbb[:, 128:256], start=False, stop=True),
                            reads=vkeys + [pbk], writes=[pok])
                        p.add("pe", lambda e, po=po, pbb=pbb: e.matmul(
                            po[:, 128:256], lhsT=ones[:, :], rhs=pbb[:, 0:128], start=True, stop=False),
                            reads=["ones", pbk], writes=[pok])
                        p.add("pe", lambda e, po=po, pbb=pbb: e.matmul(
                            po[:, 128:256], lhsT=ones[:, :], rhs=pbb[:, 128:256], start=False, stop=True),
                            reads=["ones", pbk], writes=[pok])
                        if g == GROUPS[0]:
                            p.add("dve", lambda e, po=po, qs=qs: e.tensor_copy(out=num[:, qs], in_=po[:, 0:128]),
                                  reads=[pok], writes=["num"])
                            p.add("dve", lambda e, po=po, qs=qs: e.tensor_copy(out=den[:, qs], in_=po[:, 128:256]),
                                  reads=[pok], writes=["den"])
                        else:
                            p.add("dve", lambda e, po=po, qs=qs: e.tensor_tensor(out=num[:, qs], in0=po[:, 0:128], in1=num[:, qs], op=ALU.add),
                                  reads=[pok, "num"], writes=["num"])
                            p.add("dve", lambda e, po=po, qs=qs: e.tensor_tensor(out=den[:, qs], in0=po[:, 128:256], in1=den[:, qs], op=ALU.add),
                                  reads=[pok, "den"], writes=["den"])
            else:
                for r in range(16):
                    qs = slice(r, T, 16)
                    kprev = slice(r, H, 16)
                    kcur = slice(H + r, H + T, 16)
                    ps, pk = psS.next()
                    p.add("pe", lambda e, ps=ps, k=k, q=q, kprev=kprev, qs=qs: e.matmul(
                        ps[:, 0:64], lhsT=k[:, kprev], rhs=q[:, qs], start=True, stop=True),
                        reads=[kk, qk], writes=[pk])
                    p.add("pe", lambda e, ps=ps, k=k, q=q, kcur=kcur, qs=qs: e.matmul(
                        ps[0:64, 64:128], lhsT=k[:, kcur], rhs=q[:, qs], start=True, stop=True),
                        reads=[kk, qk], writes=[pk])
                    f, fk = pf.next()
                    pbb, pbk = pb.next()
                    p.add("act", lambda e, f=f, ps=ps: e.activation(out=f[:, 0:64], in_=ps[:, 0:64], func=AF.Exp, scale=SCALE),
                          reads=[pk], writes=[fk])
                    p.add("act", lambda e, f=f, ps=ps: e.activation(out=f[0:64, 64:128], in_=ps[0:64, 64:128], func=AF.Exp, scale=SCALE),
                          reads=[pk, fk], writes=[fk])
                    p.add("dve", lambda e, f=f, pbb=pbb, t1=t1: e.tensor_tensor(out=pbb[:, 0:64], in0=f[:, 0:64], in1=t1[:, 0:64], op=ALU.mult),
                          reads=[fk, t1k], writes=[pbk])
                    p.add("dve", lambda e, f=f, pbb=pbb, t1=t1: e.tensor_tensor(out=pbb[0:64, 64:128], in0=f[0:64, 64:128], in1=t1[0:64, 128:192], op=ALU.mult),
                          reads=[fk, t1k, pbk], writes=[pbk])
                    po, pok = psO.next()
                    p.add("pe", lambda e, po=po, v=v, pbb=pbb, r=r: e.matmul(
                        po[:, 0:64], lhsT=v[:, r, :], rhs=pbb[:, 0:64], start=True, stop=False),
                        reads=vkeys + [pbk], writes=[pok])
                    p.add("pe", lambda e, po=po, v=v, pbb=pbb, r=r: e.matmul(
                        po[:, 0:64], lhsT=v[0:64, 16 + r, :], rhs=pbb[0:64, 64:128], start=False, stop=True),
                        reads=vkeys + [pbk], writes=[pok])
                    p.add("pe", lambda e, po=po, pbb=pbb: e.matmul(
                        po[:, 128:192], lhsT=ones[:, :], rhs=pbb[:, 0:64], start=True, stop=False),
                        reads=["ones", pbk], writes=[pok])
                    p.add("pe", lambda e, po=po, pbb=pbb: e.matmul(
                        po[:, 128:192], lhsT=ones[0:64, :], rhs=pbb[0:64, 64:128], start=False, stop=True),
                        reads=["ones", pbk], writes=[pok])
                    p.add("dve", lambda e, po=po, qs=qs: e.tensor_tensor(out=num[:, qs], in0=po[:, 0:64], in1=num[:, qs], op=ALU.add),
                          reads=[pok, "num"], writes=["num"])
                    p.add("dve", lambda e, po=po, qs=qs: e.tensor_tensor(out=den[:, qs], in0=po[:, 128:192], in1=den[:, qs], op=ALU.add),
                          reads=[pok, "den"], writes=["den"])
        y, yk = ya.next()
        p.add("dve", lambda e: e.reciprocal(out=den[:, :], in_=den[:, :]), reads=["den"], writes=["den"])
        p.add("dve", lambda e, y=y: e.tensor_tensor(out=y[:, :], in0=num[:, :], in1=den[:, :], op=ALU.mult),
              reads=["num", "den"], writes=[yk])
        p.dma(YAT[h * 128:(h + 1) * 128, :], y[:, :], reads=[yk], dsem=f"a_ya{yk[1]}", eng="act")


def emit_gla_fin(p, d):
    OL, QTL, UALL, DALL, CMASK, SGR, NG, YBT = (d[k] for k in ("OL", "QTL", "UALL", "DALL", "CMASK", "SGR", "NG", "YBT"))
    cm = p.sb("f_cm", [128, 8], F32)
    ng = p.sb("f_ng", [128, 4], F32)
    onesb = p.sb("f_ones", [128, 128], BF16)
    p.dma(cm[:, :], CMASK[:, :], writes=["cm"], dsem="f_c0")
    p.dma(ng[:, :], NG[:, :], writes=["ng"], dsem="f_c1")
    p.add("pool", lambda e: e.memset(onesb[:, :], 1.0), writes=["f_ones"])
    dall = p.sb("f_dall", [128, 8], F32)
    acoef = p.sb("f_a", [128, 8], F32)
    Sin = [p.sb(f"f_S{i}", [128, 512], F32) for i in range(2)]
    Sb = [p.sb(f"f_Sb{i}", [128, 512], BF16) for i in range(2)]
    ut = Rot([p.sb(f"f_u{i}", [128, 512], F32) for i in range(3)], "f_u")
    qtl = [p.sb(f"f_q{i}", [128, T], BF16) for i in range(2)]
    o = [p.sb(f"f_o{i}", [128, T], F32) for i in range(4)]
    osq = [p.sb(f"f_osq{i}", [128, T], BF16) for i in range(4)]
    rstd = p.sb("f_rstd", [128, T], F32)
    sgr = Rot([p.sb(f"f_sgr{i}", [128, T], BF16) for i in range(2)], "f_sgr")
    tmp = Rot([p.sb(f"f_tmp{i}", [128, T], F32) for i in range(2)], "f_tmp")
    yb = Rot([p.sb(f"f_yb{i}", [128, T], BF16) for i in range(2)], "f_yb")
    psC = Rot([p.ps(f"f_psC{i}", [128, 512], F32) for i in range(2)], "f_psC")

    for h in range(4):
        for dc in range(2):
            r0 = h * 256 + dc * 128
            p.dma(dall[:, :].rearrange("p (c o) -> p c o", o=1), DALL[r0:r0 + 128], writes=["dall"], dsem="f_dall",
                  allow_slow_non_contiguous=True)
            p.add("dve", lambda e: e.scalar_tensor_tensor(out=acoef[:, :], in0=dall[:, :], scalar=-1.0, in1=cm[:, :],
                                                          op0=ALU.add, op1=ALU.mult), reads=["dall", "cm"], writes=["acoef"])
            p.add("dve", lambda e: e.tensor_scalar(out=acoef[:, :], in0=acoef[:, :], scalar1=1.0, scalar2=None, op0=ALU.add),
                  reads=["acoef"], writes=["acoef"])
            p.add("pool", lambda e, dc=dc: e.memset(Sin[dc][:, :], 0.0), writes=[("Sin", dc)])
            for c in range(8):
                u, uk = ut.next()
                p.dma(u[:, :], UALL[c, r0:r0 + 128, :], writes=[uk], dsem=f"f_u{uk[1]}")
                p.add("dve", lambda e, u=u, c=c: e.tensor_scalar(out=u[:, :], in0=u[:, :], scalar1=cm[:, c:c + 1], scalar2=None, op0=ALU.mult),
                      reads=[uk, "cm"], writes=[uk])
                p.add("dve", lambda e, u=u, c=c, dc=dc: e.scalar_tensor_tensor(
                    out=Sin[dc][:, :], in0=Sin[dc][:, :], scalar=acoef[:, c:c + 1], in1=u[:, :], op0=ALU.mult, op1=ALU.add),
                    reads=[("Sin", dc), "acoef", uk], writes=[("Sin", dc)])
            p.add("act", lambda e, dc=dc: e.activation(out=Sb[dc][:, :], in_=Sin[dc][:, :], func=AF.Copy),
                  reads=[("Sin", dc)], writes=[("Sb", dc)])
            p.dma(qtl[dc][:, :], QTL[r0:r0 + 128, :], writes=[("qtl", dc)], dsem=f"f_q{dc}")
        for ec in range(4):
            r0 = h * 512 + ec * 128
            p.dma(o[ec][:, :], OL[r0:r0 + 128, :], writes=[("o", ec)], dsem=f"f_o{ec}")
            for th in range(2):
                ts = slice(th * 512, (th + 1) * 512)
                ps, pk = psC.next()
                for dc in range(2):
                    p.add("pe", lambda e, ps=ps, dc=dc, ec=ec, ts=ts: e.matmul(
                        ps[:, :], lhsT=Sb[dc][:, ec * 128:(ec + 1) * 128], rhs=qtl[dc][:, ts], start=(dc == 0), stop=(dc == 1)),
                        reads=[("Sb", dc), ("qtl", dc)], writes=[pk])
                p.add("dve", lambda e, ps=ps, ec=ec, ts=ts: e.tensor_tensor(out=o[ec][:, ts], in0=ps[:, :], in1=o[ec][:, ts], op=ALU.add),
                      reads=[pk, ("o", ec)], writes=[("o", ec)])
            p.add("act", lambda e, ec=ec: e.activation(out=osq[ec][:, :], in_=o[ec][:, :], func=AF.Square),
                  reads=[("o", ec)], writes=[("osq", ec)])
        for th in range(2):
            ts = slice(th * 512, (th + 1) * 512)
            ps, pk = psC.next()
            for ec in range(4):
                p.add("pe", lambda e, ps=ps, ec=ec, ts=ts: e.matmul(
                    ps[:, :], lhsT=onesb[:, :], rhs=osq[ec][:, ts], start=(ec == 0), stop=(ec == 3)),
                    reads=["f_ones", ("osq", ec)], writes=[pk])
            p.add("dve", lambda e, ps=ps, ts=ts: e.tensor_scalar(out=rstd[:, ts], in0=ps[:, :], scalar1=1.0 / 512.0, scalar2=1e-5,
                                                                 op0=ALU.mult, op1=ALU.add), reads=[pk], writes=["rstd"])
        p.add("act", lambda e: e.activation(out=rstd[:, :], in_=rstd[:, :], func=AF.Ln), reads=["rstd"], writes=["rstd"])
        p.add("act", lambda e: e.activation(out=rstd[:, :], in_=rstd[:, :], func=AF.Exp, scale=-0.5), reads=["rstd"], writes=["rstd"])
        for ec in range(4):
            r0 = h * 512 + ec * 128
            s, sk = sgr.next()
            p.dma(s[:, :], SGR[r0:r0 + 128, :], writes=[sk], dsem=f"f_sgr{sk[1]}")
            t, tk = tmp.next()
            y, yk = yb.next()
            p.add("dve", lambda e, t=t, ec=ec: e.scalar_tensor_tensor(out=t[:, :], in0=o[ec][:, :], scalar=ng[:, ec:ec + 1], in1=rstd[:, :],
                                                                     op0=ALU.mult, op1=ALU.mult), reads=[("o", ec), "ng", "rstd"], writes=[tk])
            p.add("dve", lambda e, t=t, y=y, s=s: e.tensor_tensor(out=y[:, :], in0=t[:, :], in1=s[:, :], op=ALU.mult),
                  reads=[tk, sk], writes=[yk])
            p.dma(YBT[r0:r0 + 128, :], y[:, :], reads=[yk], dsem=f"f_yb{yk[1]}", eng="act")


ALPHA = float((2 * 4) ** 0.25)
DFF = 5632
FC = DFF // 128


def emit_out_ln(p, pre, act, akeys, kcn, W, XT, LNG, LNB, OUT, SCR, psY):
    ws = WStream(p, kcn, wt=128, name=pre + "w")
    ones = p.sb(pre + "ones", [128, 128], F32)
    p.add("pool", lambda e: e.memset(ones[:, :], 1.0), writes=[pre + "ones"])
    lng = p.sb(pre + "lng", [128, 16], F32)
    lnb = p.sb(pre + "lnb", [128, 16], F32)
    p.dma(lng[:, :], LNG[:, :], writes=[pre + "lng"], dsem=pre + "lng")
    p.dma(lnb[:, :], LNB[:, :], writes=[pre + "lnb"], dsem=pre + "lnb")
    xt = Rot([p.sb(f"{pre}xt{i}", [128, 512], F32) for i in range(3)], pre + "xt")
    ut = Rot([p.sb(f"{pre}ut{i}", [128, 512], F32) for i in range(3)], pre + "ut")
    usq = Rot([p.sb(f"{pre}usq{i}", [128, 512], F32) for i in range(2)], pre + "usq")
    pst = [p.ps(f"{pre}pst{i}", [128, 512], F32) for i in range(4)]
    for cc in range(16):
        wb, wk = ws.load(W, cc * 128, 128)
        for th in range(2):
            ts = slice(th * 512, (th + 1) * 512)
            ps, pk = psY.next()
            for kc in range(kcn):
                p.add("pe", lambda e, ps=ps, wb=wb, kc=kc, ts=ts: e.matmul(
                    ps[:, :], lhsT=wb[:, kc, :], rhs=act[:, kc, ts], start=(kc == 0), stop=(kc == kcn - 1)),
                    reads=[wk, akeys[kc]], writes=[pk])
            x, xk = xt.next()
            p.dma(x[:, :], XT[cc * 128:(cc + 1) * 128, ts], writes=[xk], dsem=f"{pre}xt{xk[1]}")
            u, uk = ut.next()
            p.add("dve", lambda e, u=u, x=x, ps=ps: e.scalar_tensor_tensor(
                out=u[:, :], in0=x[:, :], scalar=ALPHA, in1=ps[:, :], op0=ALU.mult, op1=ALU.add),
                reads=[xk, pk], writes=[uk])
            sq, sqk = usq.next()
            p.add("act", lambda e, sq=sq, u=u: e.activation(out=sq[:, :], in_=u[:, :], func=AF.Square),
                  reads=[uk], writes=[sqk])
            p.add("pe", lambda e, u=u, th=th, cc=cc: e.matmul(pst[th][:, :], lhsT=ones[:, :], rhs=u[:, :],
                                                             start=(cc == 0), stop=(cc == 15)),
                  reads=[pre + "ones", uk], writes=[(pre + "pst", th)])
            p.add("pe", lambda e, sq=sq, th=th, cc=cc: e.matmul(pst[2 + th][:, :], lhsT=ones[:, :], rhs=sq[:, :],
                                                               start=(cc == 0), stop=(cc == 15)),
                  reads=[pre + "ones", sqk], writes=[(pre + "pst", 2 + th)])
            p.dma(SCR[cc * 128:(cc + 1) * 128, ts], u[:, :], reads=[uk], writes=[(pre + "scr", cc, th)],
                  dsem=f"{pre}ut{uk[1]}", eng="act")
    mean = p.sb(pre + "mean", [128, T], F32)
    rstd = p.sb(pre + "rstd", [128, T], F32)
    for th in range(2):
        ts = slice(th * 512, (th + 1) * 512)
        p.add("dve", lambda e, th=th, ts=ts: e.tensor_scalar(out=mean[:, ts], in0=pst[th][:, :], scalar1=1.0 / 2048.0,
                                                             scalar2=None, op0=ALU.mult),
              reads=[(pre + "pst", th)], writes=[pre + "mean"])
        p.add("dve", lambda e, ts=ts: e.tensor_tensor(out=rstd[:, ts], in0=mean[:, ts], in1=mean[:, ts], op=ALU.mult),
              reads=[pre + "mean"], writes=[pre + "rstd"])
        p.add("dve", lambda e, th=th, ts=ts: e.scalar_tensor_tensor(
            out=rstd[:, ts], in0=pst[2 + th][:, :], scalar=1.0 / 2048.0, in1=rstd[:, ts], op0=ALU.mult, op1=ALU.subtract),
            reads=[(pre + "pst", 2 + th), pre + "rstd"], writes=[pre + "rstd"])
    p.add("dve", lambda e: e.tensor_scalar(out=rstd[:, :], in0=rstd[:, :], scalar1=1e-5, scalar2=None, op0=ALU.add),
          reads=[pre + "rstd"], writes=[pre + "rstd"])
    p.add("act", lambda e: e.activation(out=rstd[:, :], in_=rstd[:, :], func=AF.Ln), reads=[pre + "rstd"], writes=[pre + "rstd"])
    p.add("act", lambda e: e.activation(out=rstd[:, :], in_=rstd[:, :], func=AF.Exp, scale=-0.5),
          reads=[pre + "rstd"], writes=[pre + "rstd"])
    for cc in range(16):
        for th in range(2):
            ts = slice(th * 512, (th + 1) * 512)
            u, uk = ut.next()
            p.dma(u[:, :], SCR[cc * 128:(cc + 1) * 128, ts], reads=[(pre + "scr", cc, th)], writes=[uk],
                  dsem=f"{pre}ut{uk[1]}")
            p.add("dve", lambda e, u=u, ts=ts: e.tensor_tensor(out=u[:, :], in0=u[:, :], in1=mean[:, ts], op=ALU.subtract),
                  reads=[uk, pre + "mean"], writes=[uk])
            p.add("dve", lambda e, u=u, ts=ts: e.tensor_tensor(out=u[:, :], in0=u[:, :], in1=rstd[:, ts], op=ALU.mult),
                  reads=[uk, pre + "rstd"], writes=[uk])
            p.add("dve", lambda e, u=u, cc=cc: e.tensor_scalar(out=u[:, :], in0=u[:, :], scalar1=lng[:, cc:cc + 1],
                                                               scalar2=lnb[:, cc:cc + 1], op0=ALU.mult, op1=ALU.add),
                  reads=[uk, pre + "lng", pre + "lnb"], writes=[uk])
            p.dma(OUT[cc * 128:(cc + 1) * 128, ts], u[:, :], reads=[uk], dsem=f"{pre}ut{uk[1]}", eng="act")


def build_stage3b():
    nc = bass.Bass("TRN2", target_bir_lowering=False)
    p = Prog(nc)
    YAT = p.dram("YAT", [1024, T], BF16, "ExternalInput")
    YBT = p.dram("YBT", [2048, T], BF16, "ExternalInput")
    SMA = p.dram("SMA", [2048, T], BF16, "ExternalInput")
    SMB = p.dram("SMB", [2048, T], BF16, "ExternalInput")
    XT = p.dram("xT", [2048, T], F32, "ExternalInput")
    WA = p.dram("w_proj_a", [1024, 2048], F32, "ExternalInput")
    WB = p.dram("w_proj_b", [2048, 2048], F32, "ExternalInput")
    WO = p.dram("w_out", [2048, 2048], F32, "ExternalInput")
    LNG = p.dram("LNG", [128, 16], F32, "ExternalInput")
    LNB = p.dram("LNB", [128, 16], F32, "ExternalInput")
    OUT = p.dram("X1T", [2048, T], F32, "ExternalOutput")
    SCR = p.dram("SCR", [2048, T], F32, "Internal")
    emit_stage3b(p, YAT, YBT, SMA, SMB, XT, WA, WB, WO, LNG, LNB, OUT, SCR)
    p.emit()
    return nc


def emit_stage3b(p, YAT, YBT, SMA, SMB, XT, WA, WB, WO, LNG, LNB, OUT, SCR):
    ya = p.sb("b_ya", [128, 8, T], BF16)
    yb = p.sb("b_yb", [128, 16, T], BF16)
    yT = p.sb("b_yT", [128, 16, T], BF16)
    for kc in range(8):
        p.dma(ya[:, kc, :], YAT[kc * 128:(kc + 1) * 128, :], writes=[("b_ya", kc)], dsem="b_ya")
    for kc in range(16):
        p.dma(yb[:, kc, :], YBT[kc * 128:(kc + 1) * 128, :], writes=[("b_yb", kc)], dsem="b_yb")
    wsa = WStream(p, 8, wt=128, name="b_wa")
    wsb = WStream(p, 16, wt=128, name="b_wb")
    psY = Rot([p.ps(f"b_ps{i}", [128, 512], F32) for i in range(4)], "b_ps")
    sm = Rot([p.sb(f"b_sm{i}", [128, T], BF16) for i in range(4)], "b_sm")
    t1 = Rot([p.sb(f"b_t1{i}", [128, 512], F32) for i in range(2)], "b_t1")
    t2 = Rot([p.sb(f"b_t2{i}", [128, 512], F32) for i in range(2)], "b_t2")
    for cc in range(16):
        wa, wak = wsa.load(WA, cc * 128, 128)
        wb, wbk = wsb.load(WB, cc * 128, 128)
        sa, sak = sm.next()
        sb_, sbk = sm.next()
        p.dma(sa[:, :], SMA[cc * 128:(cc + 1) * 128, :], writes=[sak], dsem=f"b_sm{sak[1]}")
        p.dma(sb_[:, :], SMB[cc * 128:(cc + 1) * 128, :], writes=[sbk], dsem=f"b_sm{sbk[1]}")
        for th in range(2):
            ts = slice(th * 512, (th + 1) * 512)
            pa, pak = psY.next()
            for kc in range(8):
                p.add("pe", lambda e, pa=pa, wa=wa, kc=kc, ts=ts: e.matmul(
                    pa[:, :], lhsT=wa[:, kc, :], rhs=ya[:, kc, ts], start=(kc == 0), stop=(kc == 7)),
                    reads=[wak, ("b_ya", kc)], writes=[pak])
            pb, pbk = psY.next()
            for kc in range(16):
                p.add("pe", lambda e, pb=pb, wb=wb, kc=kc, ts=ts: e.matmul(
                    pb[:, :], lhsT=wb[:, kc, :], rhs=yb[:, kc, ts], start=(kc == 0), stop=(kc == 15)),
                    reads=[wbk, ("b_yb", kc)], writes=[pbk])
            a, ak = t1.next()
            b, bk = t2.next()
            p.add("dve", lambda e, a=a, pa=pa, sa=sa, ts=ts: e.tensor_tensor(out=a[:, :], in0=pa[:, :], in1=sa[:, ts], op=ALU.mult),
                  reads=[pak, sak], writes=[ak])
            p.add("dve", lambda e, b=b, pb=pb, sb_=sb_, ts=ts: e.tensor_tensor(out=b[:, :], in0=pb[:, :], in1=sb_[:, ts], op=ALU.mult),
                  reads=[pbk, sbk], writes=[bk])
            p.add("dve", lambda e, a=a, b=b, cc=cc, ts=ts: e.tensor_tensor(out=yT[:, cc, ts], in0=a[:, :], in1=b[:, :], op=ALU.add),
                  reads=[ak, bk], writes=[("b_yT", cc)])
    emit_out_ln(p, "b_", yT, [("b_yT", kc) for kc in range(16)], 16, WO, XT, LNG, LNB, OUT, SCR, psY)


def build_stage4a():
    nc = bass.Bass("TRN2", target_bir_lowering=False)
    p = Prog(nc)
    X1T = p.dram("X1T", [2048, T], F32, "ExternalInput")
    X1H = p.dram("X1H", [2048, 2], F32, "ExternalInput")
    WG = p.dram("ffn_w_gate", [2048, DFF], F32, "ExternalInput")
    WU = p.dram("ffn_w_up", [2048, DFF], F32, "ExternalInput")
    CW = p.dram("CW", [128, FC, 3], F32, "ExternalInput")
    CB = p.dram("CB", [128, FC], F32, "ExternalInput")
    HT = p.dram("HT", [DFF, T], BF16, "ExternalOutput")
    emit_stage4a(p, X1T, X1H, WG, WU, CW, CB, HT)
    p.emit()
    return nc


def emit_stage4a(p, X1T, X1H, WG, WU, CW, CB, HT):
    xb = p.sb("c_xb", [128, 16, T + 2], BF16)
    xs = [p.sb(f"c_xs{i}", [128, T + 2], F32) for i in range(2)]
    for kc in range(16):
        s = xs[kc % 2]
        p.dma(s[:, 2:], X1T[kc * 128:(kc + 1) * 128, :], writes=[("c_xs", kc % 2, 0)], dsem=f"c_xs{kc % 2}")
        p.dma(s[:, 0:2], X1H[kc * 128:(kc + 1) * 128, :], writes=[("c_xs", kc % 2, 1)], dsem=f"c_xs{kc % 2}")
        p.add("dve" if kc % 2 == 0 else "pool", lambda e, s=s, kc=kc: e.tensor_copy(out=xb[:, kc, :], in_=s[:, :]),
              reads=[("c_xs", kc % 2, 0), ("c_xs", kc % 2, 1)], writes=[("c_xb", kc)])
    xkeys = [("c_xb", kc) for kc in range(16)]
    cw = p.sb("c_cw", [128, FC, 3], F32)
    cb = p.sb("c_cb", [128, FC], F32)
    p.dma(cw[:, :, :], CW[:, :, :], writes=["c_cw"], dsem="c_cw")
    p.dma(cb[:, :], CB[:, :], writes=["c_cb"], dsem="c_cb")
    ws = WStream(p, 16, wt=128, nbuf=3, name="c_w")
    psG = Rot([p.ps(f"c_psg{i}", [128, 512], F32) for i in range(3)], "c_psg")
    psH = Rot([p.ps(f"c_psh{i}", [128, 512], F32) for i in range(1)], "c_psh")
    psU = Rot([p.ps(f"c_psu{i}", [128, 512], F32) for i in range(4)], "c_psu")
    gx = Rot([p.sb(f"c_gx{i}", [128, T + 2], F32) for i in range(2)], "c_gx")
    acc = Rot([p.sb(f"c_acc{i}", [128, T], F32) for i in range(2)], "c_acc")
    hh = Rot([p.sb(f"c_h{i}", [128, T], BF16) for i in range(2)], "c_h")
    for fc in range(FC):
        wg, wgk = ws.load(WG, fc * 128, 128)
        wu, wuk = ws.load(WU, fc * 128, 128)
        g, gk = gx.next()
        ph, phk = psH.next()
        for kc in range(16):
            p.add("pe", lambda e, ph=ph, wg=wg, kc=kc: e.matmul(
                ph[:, 0:2], lhsT=wg[:, kc, :], rhs=xb[:, kc, 0:2], start=(kc == 0), stop=(kc == 15)),
                reads=[wgk, xkeys[kc]], writes=[phk])
        p.add("act", lambda e, g=g, ph=ph: e.activation(out=g[:, 0:2], in_=ph[:, 0:2], func=AF.Copy),
              reads=[phk], writes=[(gk, 2)])
        for th in range(2):
            pg, pgk = psG.next()
            for kc in range(16):
                p.add("pe", lambda e, pg=pg, wg=wg, kc=kc, th=th: e.matmul(
                    pg[:, :], lhsT=wg[:, kc, :], rhs=xb[:, kc, 2 + th * 512:2 + (th + 1) * 512], start=(kc == 0), stop=(kc == 15)),
                    reads=[wgk, xkeys[kc]], writes=[pgk])
            p.add("act", lambda e, g=g, pg=pg, th=th: e.activation(out=g[:, 2 + th * 512:2 + (th + 1) * 512], in_=pg[:, :], func=AF.Copy),
                  reads=[pgk], writes=[(gk, th)])
        gkeys = [(gk, 0), (gk, 1), (gk, 2)]
        a, ak = acc.next()
        p.add("dve", lambda e, a=a, g=g, fc=fc: e.tensor_scalar(out=a[:, :], in0=g[:, 0:T], scalar1=cw[:, fc, 0:1], scalar2=cb[:, fc:fc + 1],
                                                               op0=ALU.mult, op1=ALU.add), reads=gkeys + ["c_cw", "c_cb"], writes=[ak])
        p.add("dve", lambda e, a=a, g=g, fc=fc: e.scalar_tensor_tensor(out=a[:, :], in0=g[:, 1:T + 1], scalar=cw[:, fc, 1:2], in1=a[:, :],
                                                                      op0=ALU.mult, op1=ALU.add), reads=gkeys + ["c_cw", ak], writes=[ak])
        p.add("dve", lambda e, a=a, g=g, fc=fc: e.scalar_tensor_tensor(out=a[:, :], in0=g[:, 2:T + 2], scalar=cw[:, fc, 2:3], in1=a[:, :],
                                                                      op0=ALU.mult, op1=ALU.add), reads=gkeys + ["c_cw", ak], writes=[ak])
        p.add("act", lambda e, a=a: e.activation(out=a[:, :], in_=a[:, :], func=AF.Silu), reads=[ak], writes=[ak])
        h, hk = hh.next()
        for th in range(2):
            ts = slice(th * 512, (th + 1) * 512)
            pu, puk = psU.next()
            for kc in range(16):
                p.add("pe", lambda e, pu=pu, wu=wu, kc=kc, th=th: e.matmul(
                    pu[:, :], lhsT=wu[:, kc, :], rhs=xb[:, kc, 2 + th * 512:2 + (th + 1) * 512], start=(kc == 0), stop=(kc == 15)),
                    reads=[wuk, xkeys[kc]], writes=[puk])
            p.add("dve", lambda e, h=h, a=a, pu=pu, ts=ts: e.tensor_tensor(out=h[:, ts], in0=pu[:, :], in1=a[:, ts], op=ALU.mult),
                  reads=[puk, ak], writes=[(hk, th)])
        p.dma(HT[fc * 128:(fc + 1) * 128, :], h[:, :], reads=[(hk, 0), (hk, 1)], dsem=f"c_h{hk[1]}", eng="act")


def build_stage4b():
    nc = bass.Bass("TRN2", target_bir_lowering=False)
    p = Prog(nc)
    HT = p.dram("HT", [DFF, T], BF16, "ExternalInput")
    X1T = p.dram("X1T", [2048, T], F32, "ExternalInput")
    WD = p.dram("ffn_w_down", [DFF, 2048], F32, "ExternalInput")
    LNG = p.dram("LNG", [128, 16], F32, "ExternalInput")
    LNB = p.dram("LNB", [128, 16], F32, "ExternalInput")
    OUT = p.dram("X2T", [2048, T], F32, "ExternalOutput")
    SCR = p.dram("SCR", [2048, T], F32, "Internal")
    emit_stage4b(p, HT, X1T, WD, LNG, LNB, OUT, SCR)
    p.emit()
    return nc


def emit_stage4b(p, HT, X1T, WD, LNG, LNB, OUT, SCR):
    hT = p.sb("d_hT", [128, FC, T], BF16)
    for kc in range(FC):
        p.dma(hT[:, kc, :], HT[kc * 128:(kc + 1) * 128, :], writes=[("d_hT", kc)], dsem=f"d_hT{kc % 4}")
    psY = Rot([p.ps(f"d_ps{i}", [128, 512], F32) for i in range(4)], "d_ps")
    emit_out_ln(p, "d_", hT, [("d_hT", kc) for kc in range(FC)], FC, WD, X1T, LNG, LNB, OUT, SCR, psY)


HALO = (128, 512, 2048); DIL = (1, 4, 16)

def t5_bucket(dist):
    dist = np.asarray(dist)
    df = np.maximum(dist, 1).astype(np.float32)
    large = 16 + (np.log(df / np.float32(16)) / np.float32(np.log(2048 / 16)) * np.float32(16)).astype(np.int32)
    return np.where(dist < 16, dist, np.minimum(large, 31))

def bias_tables(rel_bias):
    ki = np.arange(128)[:, None]; qi = np.arange(128)[None, :]
    off_prev = 128 + qi - ki
    off_cur = qi - ki
    valid = np.concatenate([(off_prev <= 128), (off_cur >= 0)], axis=1).astype(np.float32)
    BT = np.zeros((24, 128, 256), np.float32)
    for g in range(3):
        bp = t5_bucket(DIL[g] * np.clip(off_prev, 0, 128))
        bc = t5_bucket(DIL[g] * np.clip(off_cur, 0, 128))
        for h in range(8):
            BT[g * 8 + h, :, 0:128] = rel_bias[bp, g * 8 + h]
            BT[g * 8 + h, :, 128:256] = rel_bias[bc, g * 8 + h]
    return BT, valid

def hmask(c):
    m = np.zeros((128, 3), np.float32)
    if c > 0:
        m[:, 0] = 1; m[:, 1] = 1
    if c == 1:
        m[64:, 2] = 1
    elif c >= 2:
        m[:, 2] = 1
    return m

def cmask(c):
    m = np.zeros((128, 8), np.float32)
    m[:, :c] = 1
    return m


_PROGS = {}


def _prog(name, builder):
    if name not in _PROGS:
        _PROGS[name] = builder()
    return _PROGS[name]


def _run(name, builder, in_maps):
    nc = _prog(name, builder)
    res = run_bass_kernel_spmd(nc, in_maps, core_ids=list(range(NCORES)))
    return res.results


NCORES = 8


def _lnp(v):
    return np.ascontiguousarray(v.reshape(16, 128).T)


def kernel(x, w_in, gla_gate_w, gla_gate_b, gla_norm_g, w_proj_a, w_proj_b, w_out, rel_bias,
           ln1_g, ln1_b, ffn_w_gate, ffn_w_up, ffn_conv_w, ffn_conv_b, ffn_w_down, ln2_g, ln2_b):
    f32 = np.float32
    x = np.asarray(x, f32)[0]
    C = NCORES
    xT = [np.ascontiguousarray(x[c * T:(c + 1) * T].T) for c in range(C)]
    BT, valid = bias_tables(np.asarray(rel_bias, f32))
    gcon = gla_consts()
    hm = [hmask(c) for c in range(C)]
    cm = [cmask(c) for c in range(C)]
    for l in range(4):
        gate = np.concatenate([np.asarray(gla_gate_w[l], f32), np.asarray(gla_gate_b[l], f32)[None]], 0)
        wl = np.asarray(w_in[l], f32)
        r1 = _run("s1", build_stage1, [{"xT": xT[c], "w_in": wl, "gate": gate} for c in range(C)])
        r2 = _run("s2", build_stage2, [{"GQT": r1[c]["GQT"], "GKT": r1[c]["GKT"], "GKM": r1[c]["GKM"], "GVM": r1[c]["GVM"],
                                        "LAM": r1[c]["LAM"], "GC": gcon} for c in range(C)])
        UALL = np.stack([r2[c]["U"] for c in range(C)], 0)
        DALL = np.ascontiguousarray(np.concatenate([r2[c]["DT"] for c in range(C)], 1))
        KH, VH = [], []
        for g in range(3):
            H = HALO[g]
            kt_all = np.concatenate([r1[c]["KT"][g * 1024:(g + 1) * 1024] for c in range(C)], 1)
            v_all = np.concatenate([r1[c]["V"][g] for c in range(C)], 0)
            kt_pad = np.concatenate([np.zeros((1024, H), kt_all.dtype), kt_all], 1)
            v_pad = np.concatenate([np.zeros((H, 1024), v_all.dtype), v_all], 0)
            KH.append([np.ascontiguousarray(kt_pad[:, c * T:c * T + H + T]) for c in range(C)])
            VH.append([np.ascontiguousarray(v_pad[c * T:c * T + H + T]) for c in range(C)])
        ng = np.ascontiguousarray(np.asarray(gla_norm_g[l], f32).reshape(4, 128).T)
        im = []
        for c in range(C):
            m = {"QT": r1[c]["QT"], "BT": BT, "VALID": valid, "HMASK": hm[c], "OL": r2[c]["OL"], "QTL": r2[c]["QTL"],
                 "UALL": UALL, "DALL": DALL, "CMASK": cm[c], "SGR": r1[c]["SGR"], "NG": ng}
            for g in range(3):
                m[f"KH{g}"] = KH[g][c]
                m[f"VH{g}"] = VH[g][c]
            im.append(m)
        r3 = _run("s3a", build_stage3a, im)
        r3b = _run("s3b", build_stage3b, [{"YAT": r3[c]["YAT"], "YBT": r3[c]["YBT"], "SMA": r1[c]["SMA"], "SMB": r1[c]["SMB"],
                                           "xT": xT[c], "w_proj_a": np.asarray(w_proj_a[l], f32),
                                           "w_proj_b": np.asarray(w_proj_b[l], f32), "w_out": np.asarray(w_out[l], f32),
                                           "LNG": _lnp(np.asarray(ln1_g[l], f32)), "LNB": _lnp(np.asarray(ln1_b[l], f32))}
                                          for c in range(C)])
        x1T = [r3b[c]["X1T"] for c in range(C)]
        x1h = [np.zeros((2048, 2), f32)] + [np.ascontiguousarray(x1T[c - 1][:, T - 2:T]) for c in range(1, C)]
        cw = np.ascontiguousarray(np.asarray(ffn_conv_w[l], f32).reshape(3, FC, 128).transpose(2, 1, 0))
        cb = np.ascontiguousarray(np.asarray(ffn_conv_b[l], f32).reshape(FC, 128).T)
        r4 = _run("s4a", build_stage4a, [{"X1T": x1T[c], "X1H": x1h[c], "ffn_w_gate": np.asarray(ffn_w_gate[l], f32),
                                          "ffn_w_up": np.asarray(ffn_w_up[l], f32), "CW": cw, "CB": cb} for c in range(C)])
        r5 = _run("s4b", build_stage4b, [{"HT": r4[c]["HT"], "X1T": x1T[c], "ffn_w_down": np.asarray(ffn_w_down[l], f32),
                                          "LNG": _lnp(np.asarray(ln2_g[l], f32)), "LNB": _lnp(np.asarray(ln2_b[l], f32))}
                                         for c in range(C)])
        xT = [r5[c]["X2T"] for c in range(C)]
    out = np.concatenate([np.ascontiguousarray(np.asarray(xT[c], f32).T) for c in range(C)], 0)
    return out[None].astype(f32)
```

```python
import contextlib
import numpy as np
import concourse.bass as bass
import concourse.mybir as mybir
from concourse.bass_utils import run_bass_kernel_spmd

F32 = mybir.dt.float32
BF16 = mybir.dt.bfloat16
AF = mybir.ActivationFunctionType
ALU = mybir.AluOpType

ENGS = ("pe", "act", "dve", "pool", "sp")
SAME_ENGINE_SYNC = True


class Prog:
    _uid = [0]

    def __init__(self, nc):
        self.nc = nc
        Prog._uid[0] += 1
        self.pfx = f"P{Prog._uid[0]}_"
        self.stack = contextlib.ExitStack()
        self.q = {e: [] for e in ENGS}
        self.cnt = {}
        self.seen = {e: {} for e in ENGS}
        self.res = {}
        self.semh = {}
        self.nsb = 0
        self.same_sync = SAME_ENGINE_SYNC
        self._defer = None

    GLOBAL_SEMS = {}

    def sem(self, key):
        if isinstance(key, tuple) and key[0] == "dma" and str(key[1]).startswith("GL_"):
            g = Prog.GLOBAL_SEMS
            if key[1] not in g:
                g[key[1]] = [self.nc.alloc_semaphore(name=key[1]), 0]
            if key not in self.semh:
                self.semh[key] = g[key[1]][0]
                self.cnt[key] = g[key[1]][1]
            return self.semh[key]
        if key not in self.semh:
            self.semh[key] = self.stack.enter_context(self.nc.semaphore(self.pfx + "s_" + str(key).replace(" ", "").replace("'", "").replace("(", "").replace(")", "").replace(",", "_")))
            self.cnt[key] = 0
        return self.semh[key]

    def sb(self, name, shape, dt):
        return self.stack.enter_context(self.nc.sbuf_tensor(self.pfx + name, list(shape), dt))

    def ps(self, name, shape, dt=F32):
        return self.stack.enter_context(self.nc.psum_tensor(self.pfx + name, list(shape), dt))

    def dram(self, name, shape, dt, kind="Internal"):
        return self.nc.dram_tensor(name, list(shape), dt, kind=kind).ap()

    def defer_start(self):
        self._defer = []

    def defer_stop(self):
        d, self._defer = self._defer, None
        return d

    def flush(self, lst):
        for a in lst or ():
            self.add(*a)

    def add(self, eng, fn, reads=(), writes=(), dsem=None, inc=16):
        if self._defer is not None:
            self._defer.append((eng, fn, list(reads), list(writes), dsem, inc))
            return None
        need = {}

        def want(tok):
            if tok is None:
                return
            k, v = tok
            if need.get(k, 0) < v:
                need[k] = v

        for k in reads:
            st = self.res.get(k)
            if st:
                want(st["w"])
        for k in writes:
            st = self.res.get(k)
            if st:
                want(st["w"])
                for t in st["r"].items():
                    want(t)
        waits = []
        for k, v in need.items():
            if k == eng and (eng == "pe" or not self.same_sync):
                continue
            if isinstance(k, tuple) and k[0] == "dma":
                v = self.cnt[k]
            if self.seen[eng].get(k, 0) >= v:
                continue
            self.seen[eng][k] = v
            waits.append((k, v))
        if dsem is None:
            sk = eng
            self.sem(sk)
            self.cnt[sk] += 1
            inc = 1
        else:
            sk = ("dma", dsem)
            self.sem(sk)
            self.cnt[sk] += inc
            if str(dsem).startswith("GL_"):
                Prog.GLOBAL_SEMS[dsem][1] = self.cnt[sk]
        tok = (sk, self.cnt[sk])
        self.q[eng].append((waits, fn, sk, inc))
        for k in reads:
            st = self.res.setdefault(k, {"w": None, "r": {}})
            if st["r"].get(sk, 0) < tok[1]:
                st["r"][sk] = tok[1]
        for k in writes:
            self.res[k] = {"w": tok, "r": {}}
        return tok

    def dma(self, out, in_, reads=(), writes=(), dsem=None, eng="sp", **kw):
        assert dsem is not None
        return self.add(eng, lambda e: e.dma_start(out=out, in_=in_, **kw), reads, writes, dsem=dsem)

    def final_wait(self, eng="sp"):
        self.finals = eng

    def emit(self):
        nc = self.nc
        semh = self.semh
        q = self.q
        cnt = self.cnt

        def replay(name, e, final=False):
            for waits, fn, sk, inc in q[name]:
                for k, v in waits:
                    e.wait_ge(semh[k], v)
                ins = fn(e)
                ins.then_inc(semh[sk], inc)
            if final:
                for k, h in semh.items():
                    if cnt[k] > 0:
                        e.wait_ge(h, cnt[k])

        with nc.Block() as block:
            @block.sync
            def _(e):
                replay("sp", e, final=True)

            @block.tensor
            def _(e):
                replay("pe", e)

            @block.scalar
            def _(e):
                replay("act", e)

            @block.vector
            def _(e):
                replay("dve", e)

            @block.gpsimd
            def _(e):
                replay("pool", e)
        self.stack.close()


T = 1024
D = 2048
KC = D // 128
INC = 19472
WT = 256


def build_stage1(nc=None):
    nc = nc or bass.Bass("TRN2", target_bir_lowering=False)
    p = Prog(nc)
    xT = p.dram("xT", [D, T], F32, "ExternalInput")
    w_in = p.dram("w_in", [D, INC], F32, "ExternalInput")
    gate = p.dram("gate", [17, 1024], F32, "ExternalInput")
    QT = p.dram("QT", [3 * 1024, T], BF16, "ExternalOutput")
    KT = p.dram("KT", [3 * 1024, T], BF16, "ExternalOutput")
    V = p.dram("V", [3, T, 1024], BF16, "ExternalOutput")
    GQT = p.dram("GQT", [1024, T], BF16, "ExternalOutput")
    GKT = p.dram("GKT", [1024, T], BF16, "ExternalOutput")
    GKM = p.dram("GKM", [T, 1024], BF16, "ExternalOutput")
    GVM = p.dram("GVM", [T, 2048], BF16, "ExternalOutput")
    SGR = p.dram("SGR", [2048, T], BF16, "ExternalOutput")
    SMA = p.dram("SMA", [2048, T], BF16, "ExternalOutput")
    SMB = p.dram("SMB", [2048, T], BF16, "ExternalOutput")
    LAM = p.dram("LAM", [T, 1024], F32, "ExternalOutput")
    emit_stage1(p, xT, w_in, gate, QT, KT, V, GQT, GKT, GKM, GVM, SGR, SMA, SMB, LAM)
    p.emit()
    return nc


class Rot:
    def __init__(self, tiles, key):
        self.tiles = tiles
        self.key = key
        self.i = 0

    def next(self):
        i = self.i % len(self.tiles)
        self.i += 1
        return self.tiles[i], (self.key, i)


def load_xT_bf16(p, xT, xb, nt=T):
    xs = [p.sb(f"xs{i}", [128, nt], F32) for i in range(2)]
    for kc in range(KC):
        s = xs[kc % 2]
        p.dma(s[:, :], xT[kc * 128:(kc + 1) * 128, :], writes=[("xs", kc % 2)], dsem=f"xs{kc % 2}")
        eng = "dve" if kc % 2 == 0 else "pool"
        p.add(eng, lambda e, s=s, kc=kc: e.tensor_copy(out=xb[:, kc, :], in_=s[:, :]),
              reads=[("xs", kc % 2)], writes=[("xb", kc)])


class WStream:
    def __init__(self, p, kc, wt=WT, nbuf=2, name="w"):
        self.p = p
        self.kc = kc
        self.wt = wt
        self.name = name
        self.ws = [p.sb(f"{name}s{i}", [128, kc, wt], F32) for i in range(nbuf)]
        self.wb = [p.sb(f"{name}b{i}", [128, kc, wt], BF16) for i in range(nbuf)]
        self.i = 0
        self.nbuf = nbuf

    def load(self, w, c0, nc_):
        p = self.p
        i = self.i % self.nbuf
        self.i += 1
        s, b = self.ws[i], self.wb[i]
        src = w[:, c0:c0 + nc_].rearrange("(k p) c -> p k c", p=128)
        half = self.kc // 2
        nm = self.name
        p.dma(s[:, 0:half, 0:nc_], src[:, 0:half, :], writes=[(nm + "s", i, 0)], dsem=f"{nm}s{i}")
        p.dma(s[:, half:, 0:nc_], src[:, half:, :], writes=[(nm + "s", i, 1)], dsem=f"{nm}s{i}")
        p.add("pool", lambda e: e.tensor_copy(out=b[:, :, 0:nc_], in_=s[:, :, 0:nc_]),
              reads=[(nm + "s", i, 0), (nm + "s", i, 1)], writes=[(nm + "b", i)])
        return b, (nm + "b", i)


def emit_stage1(p, xT, w_in, gate, QT, KT, V, GQT, GKT, GKM, GVM, SGR, SMA, SMB, LAM):
    xb = p.sb("xb", [128, KC, T], BF16)
    load_xT_bf16(p, xT, xb)
    xkeys = [("xb", kc) for kc in range(KC)]
    ws = WStream(p, KC)
    psum = Rot([p.ps(f"ps{i}", [128, 512], F32) for i in range(8)], "ps")
    ob = Rot([p.sb(f"ob{i}", [128, 512], BF16) for i in range(4)], "ob")
    evac_i = [0]

    def evac(dst_sb, src_ps, func, rk, wk):
        if func is None:
            evac_i[0] += 1
            if evac_i[0] % 2 == 0:
                p.add("dve", lambda e: e.tensor_copy(out=dst_sb, in_=src_ps), reads=rk, writes=wk)
                return
            func = AF.Copy
        p.add("act", lambda e: e.activation(out=dst_sb, in_=src_ps, func=func), reads=rk, writes=wk)

    def fm_job(c0, ncols, dst, func=None):
        for t0 in range(0, ncols, WT):
            n = min(WT, ncols - t0)
            wb, wk = ws.load(w_in, c0 + t0, n)
            for m0 in range(0, n, 128):
                mc = min(128, n - m0)
                for th in range(T // 512):
                    ps, pk = psum.next()
                    for kc in range(KC):
                        p.add("pe", lambda e, ps=ps, wb=wb, kc=kc, m0=m0, mc=mc, th=th: e.matmul(
                            ps[0:mc, :], lhsT=wb[:, kc, m0:m0 + mc], rhs=xb[:, kc, th * 512:(th + 1) * 512],
                            start=(kc == 0), stop=(kc == KC - 1)),
                            reads=[wk, xkeys[kc]], writes=[pk])
                    o, ok = ob.next()
                    evac(o[0:mc, :], ps[0:mc, :], func, [pk], [ok])
                    p.dma(dst[t0 + m0:t0 + m0 + mc, th * 512:(th + 1) * 512], o[0:mc, :],
                          reads=[ok], dsem=f"ob{ok[1]}", eng="act")

    def tm_job(c0, ncols, dst):
        for t0 in range(0, ncols, WT):
            n = min(WT, ncols - t0)
            wb, wk = ws.load(w_in, c0 + t0, n)
            for tp in range(T // 256):
                ps, pk = psum.next()
                for j in range(2):
                    tt = tp * 2 + j
                    for kc in range(KC):
                        p.add("pe", lambda e, ps=ps, wb=wb, kc=kc, tt=tt, j=j, n=n: e.matmul(
                            ps[:, j * 256:j * 256 + n], lhsT=xb[:, kc, tt * 128:(tt + 1) * 128], rhs=wb[:, kc, 0:n],
                            start=(kc == 0), stop=(kc == KC - 1)),
                            reads=[wk, xkeys[kc]], writes=[pk])
                o, ok = ob.next()
                evac(o[:, :], ps[:, :], None, [pk], [ok])
                for j in range(2):
                    tt = tp * 2 + j
                    p.dma(dst[tt * 128:(tt + 1) * 128, t0:t0 + n], o[:, j * 256:j * 256 + n],
                          reads=[ok], dsem=f"ob{ok[1]}", eng="act")

    for g in range(3):
        fm_job((3 * g) * 1024, 1024, QT[g * 1024:(g + 1) * 1024, :])
        fm_job((3 * g + 1) * 1024, 1024, KT[g * 1024:(g + 1) * 1024, :])
        tm_job((3 * g + 2) * 1024, 1024, V[g])
    fm_job(9216, 1024, GQT)
    fm_job(10240, 1024, GKT)
    tm_job(10240, 1024, GKM)
    tm_job(11264, 2048, GVM)
    fm_job(13312, 2048, SGR, AF.Silu)
    fm_job(15376, 2048, SMA, AF.Sigmoid)
    fm_job(17424, 2048, SMB, AF.Sigmoid)

    gl = p.sb("gl", [17, T], F32)
    gw = p.sb("gw", [17, 1024], F32)
    p.dma(gw[:, :], gate[:, :], writes=["gw"], dsem="gw")
    p.add("pool", lambda e: e.memset(gl[:, :], 1.0), writes=["gl"])
    wb, wk = ws.load(w_in, 15360, 16)
    for th in range(T // 512):
        ps, pk = psum.next()
        for kc in range(KC):
            p.add("pe", lambda e, ps=ps, wb=wb, kc=kc, th=th: e.matmul(
                ps[0:16, :], lhsT=wb[:, kc, 0:16], rhs=xb[:, kc, th * 512:(th + 1) * 512],
                start=(kc == 0), stop=(kc == KC - 1)), reads=[wk, xkeys[kc]], writes=[pk])
        p.add("dve", lambda e, ps=ps, th=th: e.tensor_copy(out=gl[0:16, th * 512:(th + 1) * 512], in_=ps[0:16, :]),
              reads=[pk], writes=["gl"])
    la = Rot([p.sb(f"la{i}", [128, 512], F32) for i in range(2)], "la")
    for tt in range(T // 128):
        for ch in range(2):
            ps, pk = psum.next()
            p.add("pe", lambda e, ps=ps, tt=tt, ch=ch: e.matmul(
                ps[:, :], lhsT=gl[:, tt * 128:(tt + 1) * 128], rhs=gw[:, ch * 512:(ch + 1) * 512],
                start=True, stop=True), reads=["gl", "gw"], writes=[pk])
            o, ok = la.next()
            p.add("act", lambda e, o=o, ps=ps: e.activation(out=o[:, :], in_=ps[:, :], func=AF.Exp, scale=-1.0),
                  reads=[pk], writes=[ok])
            p.add("act", lambda e, o=o: e.activation(out=o[:, :], in_=o[:, :], func=AF.Ln, bias=1.0),
                  reads=[ok], writes=[ok])
            p.dma(LAM[tt * 128:(tt + 1) * 128, ch * 512:(ch + 1) * 512], o[:, :], reads=[ok], dsem=f"la{ok[1]}", eng="act")


NT = T // 128


def gla_consts():
    j = np.arange(128)[:, None]
    t = np.arange(128)[None, :]
    same = (j // 64) == (t // 64)
    tri = np.where(same & (j <= t), -1.0 / 16.0, 0.0).astype(np.float32)
    trirev = np.where(same & (j > t), -1.0 / 16.0, 0.0).astype(np.float32)
    mask = np.where(same & (j <= t), 1.0, 0.0).astype(np.float32)
    return np.concatenate([tri, trirev, mask], axis=1)


def build_stage2():
    nc = bass.Bass("TRN2", target_bir_lowering=False)
    p = Prog(nc)
    GQT = p.dram("GQT", [1024, T], BF16, "ExternalInput")
    GKT = p.dram("GKT", [1024, T], BF16, "ExternalInput")
    GKM = p.dram("GKM", [T, 1024], BF16, "ExternalInput")
    GVM = p.dram("GVM", [T, 2048], BF16, "ExternalInput")
    LAM = p.dram("LAM", [T, 1024], F32, "ExternalInput")
    GC = p.dram("GC", [128, 384], F32, "ExternalInput")
    OL = p.dram("OL", [2048, T], F32, "ExternalOutput")
    QTL = p.dram("QTL", [1024, T], BF16, "ExternalOutput")
    U = p.dram("U", [1024, 512], F32, "ExternalOutput")
    DT = p.dram("DT", [1024, 1], F32, "ExternalOutput")
    emit_stage2(p, GQT, GKT, GKM, GVM, LAM, GC, OL, QTL, U, DT)
    p.emit()
    return nc


def emit_stage2(p, GQT, GKT, GKM, GVM, LAM, GC, OL, QTL, U, DT):
    gc = p.sb("gc", [128, 384], F32)
    p.dma(gc[:, :], GC[:, :], writes=["gc"], dsem="gc")
    tri, trirev, mask = gc[:, 0:128], gc[:, 128:256], gc[:, 256:384]
    qT = [p.sb(f"g_qT{i}", [128, T], BF16) for i in range(2)]
    kT = [p.sb(f"g_kT{i}", [128, T], BF16) for i in range(2)]
    ktm = p.sb("g_ktm", [128, NT, 256], BF16)
    vtm = p.sb("g_vtm", [128, NT, 512], BF16)
    lam = p.sb("g_lam", [128, NT, 256], F32)
    E = [p.sb(f"g_E{i}", [128, T], F32) for i in range(2)]
    qd = [p.sb(f"g_qd{i}", [128, T], BF16) for i in range(2)]
    ki = [p.sb(f"g_ki{i}", [128, T], BF16) for i in range(2)]
    ke = p.sb("g_ke", [128, NT, 256], BF16)
    qtl = [p.sb(f"g_qtl{i}", [128, T], BF16) for i in range(2)]
    S = [p.sb(f"g_S{i}", [128, 512], F32) for i in range(2)]
    Sb = [p.sb(f"g_Sb{i}", [128, 512], BF16) for i in range(2)]
    G = [p.sb(f"g_G{i}", [128, 1], F32) for i in range(2)]
    tmpf = Rot([p.sb(f"g_tmp{i}", [128, 256], F32) for i in range(3)], "g_tmp")
    attb = Rot([p.sb(f"g_att{i}", [128, 128], BF16) for i in range(2)], "g_att")
    osb = Rot([p.sb(f"g_o{i}", [128, 512], F32) for i in range(2)], "g_o")
    psA = Rot([p.ps(f"g_psA{i}", [128, 512], F32) for i in range(3)], "g_psA")
    psO = Rot([p.ps(f"g_psO{i}", [128, 512], F32) for i in range(2)], "g_psO")
    psS = Rot([p.ps(f"g_psS{i}", [128, 512], F32) for i in range(3)], "g_psS")

    for h in range(4):
        for dc in range(2):
            r0 = h * 256 + dc * 128
            p.dma(qT[dc][:, :], GQT[r0:r0 + 128, :], writes=[("qT", dc)], dsem=f"gq{dc}")
            p.dma(kT[dc][:, :], GKT[r0:r0 + 128, :], writes=[("kT", dc)], dsem=f"gk{dc}")
        p.dma(ktm[:, :, :], GKM[:, h * 256:(h + 1) * 256].rearrange("(n p) c -> p n c", p=128),
              writes=["ktm"], dsem="gktm")
        p.dma(vtm[:, :, :], GVM[:, h * 512:(h + 1) * 512].rearrange("(n p) c -> p n c", p=128),
              writes=["vtm"], dsem="gvtm")
        p.dma(lam[:, :, :], LAM[:, h * 256:(h + 1) * 256].rearrange("(n p) c -> p n c", p=128),
              writes=["lam"], dsem="glam")
        for dc in range(2):
            p.add("pool", lambda e, dc=dc: e.memset(S[dc][:, :], 0.0), writes=[("S", dc)])
            p.add("pool", lambda e, dc=dc: e.memset(Sb[dc][:, :], 0.0), writes=[("Sb", dc)])
            p.add("pool", lambda e, dc=dc: e.memset(G[dc][:, :], 1.0), writes=[("G", dc)])
        for tt in range(NT):
            cs = slice(tt * 128, (tt + 1) * 128)
            for dc in range(2):
                ps, pk = psA.next()
                p.add("pe", lambda e, ps=ps, tt=tt, dc=dc: e.matmul(
                    ps[:, 0:128], lhsT=lam[:, tt, dc * 128:(dc + 1) * 128], rhs=tri, start=True, stop=True),
                    reads=["lam", "gc"], writes=[pk])
                p.add("act", lambda e, ps=ps, dc=dc, cs=cs: e.activation(out=E[dc][:, cs], in_=ps[:, 0:128], func=AF.Exp),
                      reads=[pk], writes=[("E", dc, tt)])
                tm, tk = tmpf.next()
                p.add("act", lambda e, ps=ps, tm=tm: e.activation(out=tm[:, 0:128], in_=ps[:, 0:128], func=AF.Exp, scale=-1.0),
                      reads=[pk], writes=[tk])
                p.add("dve", lambda e, dc=dc, cs=cs: e.scalar_tensor_tensor(
                    out=qd[dc][:, cs], in0=qT[dc][:, cs], scalar=0.0625, in1=E[dc][:, cs], op0=ALU.mult, op1=ALU.mult),
                    reads=[("qT", dc), ("E", dc, tt)], writes=[("qd", dc, tt)])
                p.add("dve", lambda e, dc=dc, cs=cs, tm=tm: e.tensor_tensor(
                    out=ki[dc][:, cs], in0=kT[dc][:, cs], in1=tm[:, 0:128], op=ALU.mult),
                    reads=[("kT", dc), tk], writes=[("ki", dc, tt)])
            ps, pk = psA.next()
            p.add("pe", lambda e, ps=ps, tt=tt: e.matmul(
                ps[:, 0:256], lhsT=trirev, rhs=lam[:, tt, :], start=True, stop=True),
                reads=["lam", "gc"], writes=[pk])
            tm, tk = tmpf.next()
            p.add("act", lambda e, ps=ps, tm=tm: e.activation(out=tm[:, :], in_=ps[:, 0:256], func=AF.Exp),
                  reads=[pk], writes=[tk])
            p.add("dve", lambda e, tt=tt, tm=tm: e.tensor_tensor(
                out=ke[:, tt, :], in0=ktm[:, tt, :], in1=tm[:, :], op=ALU.mult),
                reads=["ktm", tk], writes=[("ke", tt)])
        for tt in range(NT):
            cs = slice(tt * 128, (tt + 1) * 128)
            ps, pk = psA.next()
            for dc in range(2):
                p.add("pe", lambda e, ps=ps, dc=dc, cs=cs: e.matmul(
                    ps[:, 0:128], lhsT=ki[dc][:, cs], rhs=qd[dc][:, cs], start=(dc == 0), stop=(dc == 1)),
                    reads=[("ki", dc, tt), ("qd", dc, tt)], writes=[pk])
            ab, ak = attb.next()
            p.add("dve", lambda e, ps=ps, ab=ab: e.tensor_tensor(out=ab[:, :], in0=ps[:, 0:128], in1=mask, op=ALU.mult),
                  reads=[pk, "gc"], writes=[ak])
            po, pok = psO.next()
            for par in range(2):
                c0 = tt * 128 + par * 64
                pr = slice(par * 64, par * 64 + 64)
                for dc in range(2):
                    p.add("dve", lambda e, dc=dc, c0=c0: e.tensor_scalar(
                        out=qtl[dc][:, c0:c0 + 64], in0=qd[dc][:, c0:c0 + 64], scalar1=G[dc][:, 0:1], scalar2=None,
                        op0=ALU.mult), reads=[("qd", dc, tt), ("G", dc)], writes=[("qtl", dc)])
                    p.add("dve", lambda e, dc=dc, c0=c0: e.tensor_tensor(
                        out=G[dc][:, :], in0=G[dc][:, :], in1=E[dc][:, c0 + 63:c0 + 64], op=ALU.mult),
                        reads=[("G", dc), ("E", dc, tt)], writes=[("G", dc)])
                for ec in range(4):
                    oc = slice(ec * 128 + par * 64, ec * 128 + par * 64 + 64)
                    es = slice(ec * 128, (ec + 1) * 128)
                    for dc in range(2):
                        p.add("pe", lambda e, po=po, oc=oc, es=es, dc=dc, c0=c0: e.matmul(
                            po[:, oc], lhsT=Sb[dc][:, es], rhs=qd[dc][:, c0:c0 + 64], start=(dc == 0), stop=False),
                            reads=[("Sb", dc), ("qd", dc, tt)], writes=[pok])
                    p.add("pe", lambda e, po=po, oc=oc, es=es, pr=pr, tt=tt, ab=ab: e.matmul(
                        po[:, oc], lhsT=vtm[pr, tt, es], rhs=ab[pr, pr], start=False, stop=True),
                        reads=["vtm", ak], writes=[pok])
                for dc in range(2):
                    pss, psk = psS.next()
                    p.add("pe", lambda e, pss=pss, pr=pr, tt=tt, dc=dc: e.matmul(
                        pss[:, :], lhsT=ke[pr, tt, dc * 128:(dc + 1) * 128], rhs=vtm[pr, tt, :], start=True, stop=True),
                        reads=[("ke", tt), "vtm"], writes=[psk])
                    p.add("dve", lambda e, pss=pss, dc=dc, c0=c0: e.scalar_tensor_tensor(
                        out=S[dc][:, :], in0=S[dc][:, :], scalar=E[dc][:, c0 + 63:c0 + 64], in1=pss[:, :],
                        op0=ALU.mult, op1=ALU.add), reads=[("S", dc), ("E", dc, tt), psk], writes=[("S", dc)])
                    p.add("act", lambda e, dc=dc: e.activation(out=Sb[dc][:, :], in_=S[dc][:, :], func=AF.Copy),
                          reads=[("S", dc)], writes=[("Sb", dc)])
            o, ok = osb.next()
            p.add("act", lambda e, o=o, po=po: e.activation(out=o[:, :], in_=po[:, :], func=AF.Copy),
                  reads=[pok], writes=[ok])
            p.dma(OL[h * 512:(h + 1) * 512, cs].rearrange("(c p) t -> p c t", p=128),
                  o[:, :].rearrange("p (c t) -> p c t", c=4), reads=[ok], dsem=f"go{ok[1]}", eng="act")
        for dc in range(2):
            r0 = h * 256 + dc * 128
            p.dma(QTL[r0:r0 + 128, :], qtl[dc][:, :], reads=[("qtl", dc)], dsem=f"gqtl{dc}")
            p.dma(U[r0:r0 + 128, :], S[dc][:, :], reads=[("S", dc)], dsem=f"gU{dc}")
            p.dma(DT[r0:r0 + 128, :], G[dc][:, :], reads=[("G", dc)], dsem=f"gD{dc}")


HALO = (128, 512, 2048)
DIL = (1, 4, 16)
SCALE = 128 ** -0.5
GROUPS = [0, 1, 2]


def build_stage3a(do_attn=True, do_gla=True):
    nc = bass.Bass("TRN2", target_bir_lowering=False)
    p = Prog(nc)
    d = {}
    d["QT"] = p.dram("QT", [3 * 1024, T], BF16, "ExternalInput")
    for g in range(3):
        d[f"KH{g}"] = p.dram(f"KH{g}", [1024, HALO[g] + T], BF16, "ExternalInput")
        d[f"VH{g}"] = p.dram(f"VH{g}", [HALO[g] + T, 1024], BF16, "ExternalInput")
    d["BT"] = p.dram("BT", [24, 128, 256], F32, "ExternalInput")
    d["VALID"] = p.dram("VALID", [128, 256], F32, "ExternalInput")
    d["HMASK"] = p.dram("HMASK", [128, 3], F32, "ExternalInput")
    d["OL"] = p.dram("OL", [2048, T], F32, "ExternalInput")
    d["QTL"] = p.dram("QTL", [1024, T], BF16, "ExternalInput")
    d["UALL"] = p.dram("UALL", [8, 1024, 512], F32, "ExternalInput")
    d["DALL"] = p.dram("DALL", [1024, 8], F32, "ExternalInput").rearrange("r (c o) -> r c o", o=1)
    d["CMASK"] = p.dram("CMASK", [128, 8], F32, "ExternalInput")
    d["SGR"] = p.dram("SGR", [2048, T], BF16, "ExternalInput")
    d["NG"] = p.dram("NG", [128, 4], F32, "ExternalInput")
    d["YAT"] = p.dram("YAT", [1024, T], BF16, "ExternalOutput")
    d["YBT"] = p.dram("YBT", [2048, T], BF16, "ExternalOutput")
    if do_attn: emit_attn(p, d)
    if do_gla: emit_gla_fin(p, d)
    p.emit()
    return nc


def emit_attn(p, d):
    QT, BT, VALID, HMASK, YAT = d["QT"], d["BT"], d["VALID"], d["HMASK"], d["YAT"]
    valid = p.sb("a_valid", [128, 256], F32)
    hmask = p.sb("a_hmask", [128, 3], F32)
    ones = p.sb("a_ones", [128, 128], BF16)
    p.dma(valid[:, :], VALID[:, :], writes=["valid"], dsem="a_c0")
    p.dma(hmask[:, :], HMASK[:, :], writes=["hmask"], dsem="a_c1")
    p.add("pool", lambda e: e.memset(ones[:, :], 1.0), writes=["ones"])
    qT = Rot([p.sb(f"a_q{i}", [128, T], BF16) for i in range(2)], "a_q")
    kT = Rot([p.sb(f"a_k{i}", [128, 2048 + T], BF16) for i in range(2)], "a_k")
    vt = Rot([p.sb(f"a_v{i}", [128, 32, 128], BF16) for i in range(2)], "a_v")
    bt = Rot([p.sb(f"a_bt{i}", [128, 256], F32) for i in range(2)], "a_bt")
    tfull = Rot([p.sb(f"a_tf{i}", [128, 256], F32) for i in range(2)], "a_tf")
    tfirst = Rot([p.sb(f"a_t1{i}", [128, 256], F32) for i in range(2)], "a_t1")
    pf = Rot([p.sb(f"a_pf{i}", [128, 256], F32) for i in range(3)], "a_pf")
    pb = Rot([p.sb(f"a_pb{i}", [128, 256], BF16) for i in range(3)], "a_pb")
    nd = p.sb("a_nd", [128, 2, T], F32)
    num = nd[:, 0, :]
    den = nd[:, 1, :]
    ya = Rot([p.sb(f"a_ya{i}", [128, T], BF16) for i in range(2)], "a_ya")
    psS = Rot([p.ps(f"a_psS{i}", [128, 512], F32) for i in range(3)], "a_psS")
    psO = Rot([p.ps(f"a_psO{i}", [128, 512], F32) for i in range(3)], "a_psO")

    pend = [None]
    for h in range(8):
        for g in GROUPS:
            H, dl = HALO[g], DIL[g]
            q, qk = qT.next()
            k, kk = kT.next()
            v, vk = vt.next()
            r0 = g * 1024 + h * 128
            p.dma(q[:, :], QT[r0:r0 + 128, :], writes=[qk], dsem=f"a_q{qk[1]}")
            p.dma(k[:, 0:H + T], d[f"KH{g}"][h * 128:(h + 1) * 128, :], writes=[kk], dsem=f"a_k{kk[1]}")
            VH = d[f"VH{g}"]
            if g < 2:
                ntile = (H + T) // (128 * dl)
                for r in range(dl):
                    src = VH[r:H + T:dl, h * 128:(h + 1) * 128] if dl > 1 else VH[:, h * 128:(h + 1) * 128]
                    p.dma(v[:, r * ntile:(r + 1) * ntile, :], src.rearrange("(j p) c -> p j c", p=128),
                          writes=[(vk, r)], dsem=f"a_v{vk[1]}")
            else:
                for r in range(16):
                    srcA = VH[r:H:16, h * 128:(h + 1) * 128]
                    p.dma(v[:, r, :], srcA, writes=[(vk, r)], dsem=f"a_v{vk[1]}")
                srcB = VH[H:H + T, h * 128:(h + 1) * 128].rearrange("(i r) c -> i r c", r=16)
                p.dma(v[0:64, 16:32, :], srcB, writes=[(vk, 16)], dsem=f"a_v{vk[1]}")
            b, bk = bt.next()
            tf, tfk = tfull.next()
            t1, t1k = tfirst.next()
            p.dma(b[:, :], BT[g * 8 + h], writes=[bk], dsem=f"a_bt{bk[1]}")
            p.add("act", lambda e, b=b: e.activation(out=b[:, :], in_=b[:, :], func=AF.Exp), reads=[bk], writes=[bk])
            p.add("dve", lambda e, b=b, tf=tf: e.tensor_tensor(out=tf[:, :], in0=b[:, :], in1=valid[:, :], op=ALU.mult),
                  reads=[bk, "valid"], writes=[tfk])
            p.add("dve", lambda e, tf=tf, t1=t1: e.tensor_copy(out=t1[:, 128:256], in_=tf[:, 128:256]),
                  reads=[tfk], writes=[t1k])
            p.add("dve", lambda e, tf=tf, t1=t1, g=g: e.tensor_scalar(
                out=t1[:, 0:128], in0=tf[:, 0:128], scalar1=hmask[:, g:g + 1], scalar2=None, op0=ALU.mult),
                reads=[tfk, "hmask", t1k], writes=[t1k])
            vkeys = [(vk, r) for r in range(18)]
            if g < 2:
                nq = T // (128 * dl)
                ntile = (H + T) // (128 * dl)
                for r in range(dl):
                    for m in range(nq):
                        qs = slice(r + dl * 128 * m, r + dl * 128 * m + dl * 127 + 1, dl)
                        kprev = slice(r + dl * 128 * m, r + dl * 128 * m + dl * 127 + 1, dl)
                        kcur = slice(r + dl * 128 * (m + 1), r + dl * 128 * (m + 1) + dl * 127 + 1, dl)
                        tab, tabk = (t1, t1k) if m == 0 else (tf, tfk)
                        ps, pk = psS.next()
                        p.add("pe", lambda e, ps=ps, k=k, q=q, kprev=kprev, qs=qs: e.matmul(
                            ps[:, 0:128], lhsT=k[:, kprev], rhs=q[:, qs], start=True, stop=True),
                            reads=[kk, qk], writes=[pk])
                        p.add("pe", lambda e, ps=ps, k=k, q=q, kcur=kcur, qs=qs: e.matmul(
                            ps[:, 128:256], lhsT=k[:, kcur], rhs=q[:, qs], start=True, stop=True),
                            reads=[kk, qk], writes=[pk])
                        f, fk = pf.next()
                        pbb, pbk = pb.next()
                        p.add("act", lambda e, f=f, ps=ps: e.activation(out=f[:, :], in_=ps[:, 0:256], func=AF.Exp, scale=SCALE),
                              reads=[pk], writes=[fk])
                        p.add("dve", lambda e, f=f, pbb=pbb, tab=tab: e.tensor_tensor(out=pbb[:, :], in0=f[:, :], in1=tab[:, :], op=ALU.mult),
                              reads=[fk, tabk], writes=[pbk])
                        p.defer_start()
                        po, pok = psO.next()
                        j0 = r * ntile + m
                        p.add("pe", lambda e, po=po, v=v, pbb=pbb, j0=j0: e.matmul(
                            po[:, 0:128], lhsT=v[:, j0, :], rhs=pbb[:, 0:128], start=True, stop=False),
                            reads=vkeys + [pbk], writes=[pok])
                        p.add("pe", lambda e, po=po, v=v, pbb=pbb, j0=j0: e.matmul(
                            po[:, 0:128], lhsT=v[:, j0 + 1, :], rhs=pbb[:, 128:256], start=False, stop=True),
                            reads=vkeys + [pbk], writes=[pok])
                        p.add("pe", lambda e, po=po, pbb=pbb: e.matmul(
                            po[:, 128:256], lhsT=ones[:, :], rhs=pbb[:, 0:128], start=True, stop=False),
                            reads=["ones", pbk], writes=[pok])
                        p.add("pe", lambda e, po=po, pbb=pbb: e.matmul(
                            po[:, 128:256], lhsT=ones[:, :], rhs=pbb[:, 128:256], start=False, stop=True),
                            reads=["ones", pbk], writes=[pok])
                        po2 = po[:, 0:256].rearrange("p (a b) -> p a b", a=2)
                        if g == GROUPS[0]:
                            p.add("dve", lambda e, po2=po2, qs=qs: e.tensor_copy(out=nd[:, :, qs], in_=po2),
                                  reads=[pok], writes=["num", "den"])
                        else:
                            p.add("dve", lambda e, po2=po2, qs=qs: e.tensor_tensor(out=nd[:, :, qs], in0=po2, in1=nd[:, :, qs], op=ALU.add),
                                  reads=[pok, "num", "den"], writes=["num", "den"])
                        blk = p.defer_stop()
                        p.flush(pend[0])
                        pend[0] = blk
            else:
                for r in range(16):
                    qs = slice(r, T, 16)
                    kprev = slice(r, H, 16)
                    kcur = slice(H + r, H + T, 16)
                    ps, pk = psS.next()
                    p.add("pe", lambda e, ps=ps, k=k, q=q, kprev=kprev, qs=qs: e.matmul(
                        ps[:, 0:64], lhsT=k[:, kprev], rhs=q[:, qs], start=True, stop=True),
                        reads=[kk, qk], writes=[pk])
                    p.add("pe", lambda e, ps=ps, k=k, q=q, kcur=kcur, qs=qs: e.matmul(
                        ps[0:64, 64:128], lhsT=k[:, kcur], rhs=q[:, qs], start=True, stop=True),
                        reads=[kk, qk], writes=[pk])
                    f, fk = pf.next()
                    pbb, pbk = pb.next()
                    p.add("act", lambda e, f=f, ps=ps: e.activation(out=f[:, 0:64], in_=ps[:, 0:64], func=AF.Exp, scale=SCALE),
                          reads=[pk], writes=[fk])
                    p.add("act", lambda e, f=f, ps=ps: e.activation(out=f[0:64, 64:128], in_=ps[0:64, 64:128], func=AF.Exp, scale=SCALE),
                          reads=[pk, fk], writes=[fk])
                    p.add("dve", lambda e, f=f, pbb=pbb, t1=t1: e.tensor_tensor(out=pbb[:, 0:64], in0=f[:, 0:64], in1=t1[:, 0:64], op=ALU.mult),
                          reads=[fk, t1k], writes=[pbk])
                    p.add("dve", lambda e, f=f, pbb=pbb, t1=t1: e.tensor_tensor(out=pbb[0:64, 64:128], in0=f[0:64, 64:128], in1=t1[0:64, 128:192], op=ALU.mult),
                          reads=[fk, t1k, pbk], writes=[pbk])
                    p.defer_start()
                    po, pok = psO.next()
                    p.add("pe", lambda e, po=po, v=v, pbb=pbb, r=r: e.matmul(
                        po[:, 0:64], lhsT=v[:, r, :], rhs=pbb[:, 0:64], start=True, stop=False),
                        reads=vkeys + [pbk], writes=[pok])
                    p.add("pe", lambda e, po=po, v=v, pbb=pbb, r=r: e.matmul(
                        po[:, 0:64], lhsT=v[0:64, 16 + r, :], rhs=pbb[0:64, 64:128], start=False, stop=True),
                        reads=vkeys + [pbk], writes=[pok])
                    p.add("pe", lambda e, po=po, pbb=pbb: e.matmul(
                        po[:, 128:192], lhsT=ones[:, :], rhs=pbb[:, 0:64], start=True, stop=False),
                        reads=["ones", pbk], writes=[pok])
                    p.add("pe", lambda e, po=po, pbb=pbb: e.matmul(
                        po[:, 128:192], lhsT=ones[0:64, :], rhs=pbb[0:64, 64:128], start=False, stop=True),
                        reads=["ones", pbk], writes=[pok])
                    po2 = po[:, 0:256].rearrange("p (a b) -> p a b", a=2)[:, :, 0:64]
                    p.add("dve", lambda e, po2=po2, qs=qs: e.tensor_tensor(out=nd[:, :, qs], in0=po2, in1=nd[:, :, qs], op=ALU.add),
                          reads=[pok, "num", "den"], writes=["num", "den"])
                    blk = p.defer_stop()
                    p.flush(pend[0])
                    pend[0] = blk
        p.flush(pend[0])
        pend[0] = None
        y, yk = ya.next()
        p.add("dve", lambda e: e.reciprocal(out=den[:, :], in_=den[:, :]), reads=["den"], writes=["den"])
        p.add("dve", lambda e, y=y: e.tensor_tensor(out=y[:, :], in0=num[:, :], in1=den[:, :], op=ALU.mult),
              reads=["num", "den"], writes=[yk])
        p.dma(YAT[h * 128:(h + 1) * 128, :], y[:, :], reads=[yk], dsem=f"a_ya{yk[1]}", eng="act")


def emit_gla_fin(p, d):
    OL, QTL, UALL, DALL, CMASK, SGR, NG, YBT = (d[k] for k in ("OL", "QTL", "UALL", "DALL", "CMASK", "SGR", "NG", "YBT"))
    cm = p.sb("f_cm", [128, 8], F32)
    ng = p.sb("f_ng", [128, 4], F32)
    onesb = p.sb("f_ones", [128, 128], BF16)
    p.dma(cm[:, :], CMASK[:, :], writes=["cm"], dsem="f_c0")
    p.dma(ng[:, :], NG[:, :], writes=["ng"], dsem="f_c1")
    p.add("pool", lambda e: e.memset(onesb[:, :], 1.0), writes=["f_ones"])
    dall = p.sb("f_dall", [128, 8], F32)
    acoef = p.sb("f_a", [128, 8], F32)
    Sin = [p.sb(f"f_S{i}", [128, 512], F32) for i in range(2)]
    Sb = [p.sb(f"f_Sb{i}", [128, 512], BF16) for i in range(2)]
    ut = Rot([p.sb(f"f_u{i}", [128, 512], F32) for i in range(3)], "f_u")
    qtl = [p.sb(f"f_q{i}", [128, T], BF16) for i in range(2)]
    o = [p.sb(f"f_o{i}", [128, T], F32) for i in range(4)]
    osq = [p.sb(f"f_osq{i}", [128, T], BF16) for i in range(4)]
    rstd = p.sb("f_rstd", [128, T], F32)
    sgr = Rot([p.sb(f"f_sgr{i}", [128, T], BF16) for i in range(2)], "f_sgr")
    tmp = Rot([p.sb(f"f_tmp{i}", [128, T], F32) for i in range(2)], "f_tmp")
    yb = Rot([p.sb(f"f_yb{i}", [128, T], BF16) for i in range(2)], "f_yb")
    psC = Rot([p.ps(f"f_psC{i}", [128, 512], F32) for i in range(2)], "f_psC")

    for h in range(4):
        for dc in range(2):
            r0 = h * 256 + dc * 128
            p.dma(dall[:, :].rearrange("p (c o) -> p c o", o=1), DALL[r0:r0 + 128], writes=["dall"], dsem="f_dall",
                  allow_slow_non_contiguous=True)
            p.add("dve", lambda e: e.scalar_tensor_tensor(out=acoef[:, :], in0=dall[:, :], scalar=-1.0, in1=cm[:, :],
                                                          op0=ALU.add, op1=ALU.mult), reads=["dall", "cm"], writes=["acoef"])
            p.add("dve", lambda e: e.tensor_scalar(out=acoef[:, :], in0=acoef[:, :], scalar1=1.0, scalar2=None, op0=ALU.add),
                  reads=["acoef"], writes=["acoef"])
            p.add("pool", lambda e, dc=dc: e.memset(Sin[dc][:, :], 0.0), writes=[("Sin", dc)])
            for c in range(8):
                u, uk = ut.next()
                p.dma(u[:, :], UALL[c, r0:r0 + 128, :], writes=[uk], dsem=f"f_u{uk[1]}")
                p.add("dve", lambda e, u=u, c=c: e.tensor_scalar(out=u[:, :], in0=u[:, :], scalar1=cm[:, c:c + 1], scalar2=None, op0=ALU.mult),
                      reads=[uk, "cm"], writes=[uk])
                p.add("dve", lambda e, u=u, c=c, dc=dc: e.scalar_tensor_tensor(
                    out=Sin[dc][:, :], in0=Sin[dc][:, :], scalar=acoef[:, c:c + 1], in1=u[:, :], op0=ALU.mult, op1=ALU.add),
                    reads=[("Sin", dc), "acoef", uk], writes=[("Sin", dc)])
            p.add("act", lambda e, dc=dc: e.activation(out=Sb[dc][:, :], in_=Sin[dc][:, :], func=AF.Copy),
                  reads=[("Sin", dc)], writes=[("Sb", dc)])
            p.dma(qtl[dc][:, :], QTL[r0:r0 + 128, :], writes=[("qtl", dc)], dsem=f"f_q{dc}")
        for ec in range(4):
            r0 = h * 512 + ec * 128
            p.dma(o[ec][:, :], OL[r0:r0 + 128, :], writes=[("o", ec)], dsem=f"f_o{ec}")
            for th in range(2):
                ts = slice(th * 512, (th + 1) * 512)
                ps, pk = psC.next()
                for dc in range(2):
                    p.add("pe", lambda e, ps=ps, dc=dc, ec=ec, ts=ts: e.matmul(
                        ps[:, :], lhsT=Sb[dc][:, ec * 128:(ec + 1) * 128], rhs=qtl[dc][:, ts], start=(dc == 0), stop=(dc == 1)),
                        reads=[("Sb", dc), ("qtl", dc)], writes=[pk])
                p.add("dve", lambda e, ps=ps, ec=ec, ts=ts: e.tensor_tensor(out=o[ec][:, ts], in0=ps[:, :], in1=o[ec][:, ts], op=ALU.add),
                      reads=[pk, ("o", ec)], writes=[("o", ec)])
            p.add("act", lambda e, ec=ec: e.activation(out=osq[ec][:, :], in_=o[ec][:, :], func=AF.Square),
                  reads=[("o", ec)], writes=[("osq", ec)])
        for th in range(2):
            ts = slice(th * 512, (th + 1) * 512)
            ps, pk = psC.next()
            for ec in range(4):
                p.add("pe", lambda e, ps=ps, ec=ec, ts=ts: e.matmul(
                    ps[:, :], lhsT=onesb[:, :], rhs=osq[ec][:, ts], start=(ec == 0), stop=(ec == 3)),
                    reads=["f_ones", ("osq", ec)], writes=[pk])
            p.add("dve", lambda e, ps=ps, ts=ts: e.tensor_scalar(out=rstd[:, ts], in0=ps[:, :], scalar1=1.0 / 512.0, scalar2=1e-5,
                                                                 op0=ALU.mult, op1=ALU.add), reads=[pk], writes=["rstd"])
        p.add("act", lambda e: e.activation(out=rstd[:, :], in_=rstd[:, :], func=AF.Ln), reads=["rstd"], writes=["rstd"])
        p.add("act", lambda e: e.activation(out=rstd[:, :], in_=rstd[:, :], func=AF.Exp, scale=-0.5), reads=["rstd"], writes=["rstd"])
        for ec in range(4):
            r0 = h * 512 + ec * 128
            s, sk = sgr.next()
            p.dma(s[:, :], SGR[r0:r0 + 128, :], writes=[sk], dsem=f"f_sgr{sk[1]}")
            t, tk = tmp.next()
            y, yk = yb.next()
            p.add("dve", lambda e, t=t, ec=ec: e.scalar_tensor_tensor(out=t[:, :], in0=o[ec][:, :], scalar=ng[:, ec:ec + 1], in1=rstd[:, :],
                                                                     op0=ALU.mult, op1=ALU.mult), reads=[("o", ec), "ng", "rstd"], writes=[tk])
            p.add("dve", lambda e, t=t, y=y, s=s: e.tensor_tensor(out=y[:, :], in0=t[:, :], in1=s[:, :], op=ALU.mult),
                  reads=[tk, sk], writes=[yk])
            p.dma(YBT[r0:r0 + 128, :], y[:, :], reads=[yk], dsem=f"f_yb{yk[1]}", eng="act")


ALPHA = float((2 * 4) ** 0.25)
DFF = 5632
FC = DFF // 128


def emit_out_ln(p, pre, act, akeys, kcn, W, XT, LNG, LNB, OUT, SCR, psY):
    ws = WStream(p, kcn, wt=128, name=pre + "w")
    ones = p.sb(pre + "ones", [128, 128], F32)
    p.add("pool", lambda e: e.memset(ones[:, :], 1.0), writes=[pre + "ones"])
    lng = p.sb(pre + "lng", [128, 16], F32)
    lnb = p.sb(pre + "lnb", [128, 16], F32)
    p.dma(lng[:, :], LNG[:, :], writes=[pre + "lng"], dsem=pre + "lng")
    p.dma(lnb[:, :], LNB[:, :], writes=[pre + "lnb"], dsem=pre + "lnb")
    xt = Rot([p.sb(f"{pre}xt{i}", [128, 512], F32) for i in range(3)], pre + "xt")
    ut = Rot([p.sb(f"{pre}ut{i}", [128, 512], F32) for i in range(3)], pre + "ut")
    usq = Rot([p.sb(f"{pre}usq{i}", [128, 512], F32) for i in range(2)], pre + "usq")
    pst = [p.ps(f"{pre}pst{i}", [128, 512], F32) for i in range(4)]
    for cc in range(16):
        wb, wk = ws.load(W, cc * 128, 128)
        for th in range(2):
            ts = slice(th * 512, (th + 1) * 512)
            ps, pk = psY.next()
            for kc in range(kcn):
                p.add("pe", lambda e, ps=ps, wb=wb, kc=kc, ts=ts: e.matmul(
                    ps[:, :], lhsT=wb[:, kc, :], rhs=act[:, kc, ts], start=(kc == 0), stop=(kc == kcn - 1)),
                    reads=[wk, akeys[kc]], writes=[pk])
            x, xk = xt.next()
            p.dma(x[:, :], XT[cc * 128:(cc + 1) * 128, ts], writes=[xk], dsem=f"{pre}xt{xk[1]}")
            u, uk = ut.next()
            p.add("dve", lambda e, u=u, x=x, ps=ps: e.scalar_tensor_tensor(
                out=u[:, :], in0=x[:, :], scalar=ALPHA, in1=ps[:, :], op0=ALU.mult, op1=ALU.add),
                reads=[xk, pk], writes=[uk])
            sq, sqk = usq.next()
            p.add("act", lambda e, sq=sq, u=u: e.activation(out=sq[:, :], in_=u[:, :], func=AF.Square),
                  reads=[uk], writes=[sqk])
            p.add("pe", lambda e, u=u, th=th, cc=cc: e.matmul(pst[th][:, :], lhsT=ones[:, :], rhs=u[:, :],
                                                             start=(cc == 0), stop=(cc == 15)),
                  reads=[pre + "ones", uk], writes=[(pre + "pst", th)])
            p.add("pe", lambda e, sq=sq, th=th, cc=cc: e.matmul(pst[2 + th][:, :], lhsT=ones[:, :], rhs=sq[:, :],
                                                               start=(cc == 0), stop=(cc == 15)),
                  reads=[pre + "ones", sqk], writes=[(pre + "pst", 2 + th)])
            p.dma(SCR[cc * 128:(cc + 1) * 128, ts], u[:, :], reads=[uk], writes=[(pre + "scr", cc, th)],
                  dsem=f"{pre}ut{uk[1]}", eng="act")
    mean = p.sb(pre + "mean", [128, T], F32)
    rstd = p.sb(pre + "rstd", [128, T], F32)
    for th in range(2):
        ts = slice(th * 512, (th + 1) * 512)
        p.add("dve", lambda e, th=th, ts=ts: e.tensor_scalar(out=mean[:, ts], in0=pst[th][:, :], scalar1=1.0 / 2048.0,
                                                             scalar2=None, op0=ALU.mult),
              reads=[(pre + "pst", th)], writes=[pre + "mean"])
        p.add("dve", lambda e, ts=ts: e.tensor_tensor(out=rstd[:, ts], in0=mean[:, ts], in1=mean[:, ts], op=ALU.mult),
              reads=[pre + "mean"], writes=[pre + "rstd"])
        p.add("dve", lambda e, th=th, ts=ts: e.scalar_tensor_tensor(
            out=rstd[:, ts], in0=pst[2 + th][:, :], scalar=1.0 / 2048.0, in1=rstd[:, ts], op0=ALU.mult, op1=ALU.subtract),
            reads=[(pre + "pst", 2 + th), pre + "rstd"], writes=[pre + "rstd"])
    p.add("dve", lambda e: e.tensor_scalar(out=rstd[:, :], in0=rstd[:, :], scalar1=1e-5, scalar2=None, op0=ALU.add),
          reads=[pre + "rstd"], writes=[pre + "rstd"])
    p.add("act", lambda e: e.activation(out=rstd[:, :], in_=rstd[:, :], func=AF.Ln), reads=[pre + "rstd"], writes=[pre + "rstd"])
    p.add("act", lambda e: e.activation(out=rstd[:, :], in_=rstd[:, :], func=AF.Exp, scale=-0.5),
          reads=[pre + "rstd"], writes=[pre + "rstd"])
    for cc in range(16):
        for th in range(2):
            ts = slice(th * 512, (th + 1) * 512)
            u, uk = ut.next()
            p.dma(u[:, :], SCR[cc * 128:(cc + 1) * 128, ts], reads=[(pre + "scr", cc, th)], writes=[uk],
                  dsem=f"{pre}ut{uk[1]}")
            p.add("dve", lambda e, u=u, ts=ts: e.tensor_tensor(out=u[:, :], in0=u[:, :], in1=mean[:, ts], op=ALU.subtract),
                  reads=[uk, pre + "mean"], writes=[uk])
            p.add("dve", lambda e, u=u, ts=ts: e.tensor_tensor(out=u[:, :], in0=u[:, :], in1=rstd[:, ts], op=ALU.mult),
                  reads=[uk, pre + "rstd"], writes=[uk])
            p.add("dve", lambda e, u=u, cc=cc: e.tensor_scalar(out=u[:, :], in0=u[:, :], scalar1=lng[:, cc:cc + 1],
                                                               scalar2=lnb[:, cc:cc + 1], op0=ALU.mult, op1=ALU.add),
                  reads=[uk, pre + "lng", pre + "lnb"], writes=[uk])
            p.dma(OUT[cc * 128:(cc + 1) * 128, ts], u[:, :], reads=[uk], dsem=f"{pre}ut{uk[1]}", eng="act")


def build_stage3b():
    nc = bass.Bass("TRN2", target_bir_lowering=False)
    p = Prog(nc)
    YAT = p.dram("YAT", [1024, T], BF16, "ExternalInput")
    YBT = p.dram("YBT", [2048, T], BF16, "ExternalInput")
    SMA = p.dram("SMA", [2048, T], BF16, "ExternalInput")
    SMB = p.dram("SMB", [2048, T], BF16, "ExternalInput")
    XT = p.dram("xT", [2048, T], F32, "ExternalInput")
    WA = p.dram("w_proj_a", [1024, 2048], F32, "ExternalInput")
    WB = p.dram("w_proj_b", [2048, 2048], F32, "ExternalInput")
    WO = p.dram("w_out", [2048, 2048], F32, "ExternalInput")
    LNG = p.dram("LNG", [128, 16], F32, "ExternalInput")
    LNB = p.dram("LNB", [128, 16], F32, "ExternalInput")
    OUT = p.dram("X1T", [2048, T], F32, "ExternalOutput")
    SCR = p.dram("SCR", [2048, T], F32, "Internal")
    emit_stage3b(p, YAT, YBT, SMA, SMB, XT, WA, WB, WO, LNG, LNB, OUT, SCR)
    p.emit()
    return nc


def emit_stage3b(p, YAT, YBT, SMA, SMB, XT, WA, WB, WO, LNG, LNB, OUT, SCR):
    ya = p.sb("b_ya", [128, 8, T], BF16)
    yb = p.sb("b_yb", [128, 16, T], BF16)
    yT = p.sb("b_yT", [128, 16, T], BF16)
    for kc in range(8):
        p.dma(ya[:, kc, :], YAT[kc * 128:(kc + 1) * 128, :], writes=[("b_ya", kc)], dsem="b_ya")
    for kc in range(16):
        p.dma(yb[:, kc, :], YBT[kc * 128:(kc + 1) * 128, :], writes=[("b_yb", kc)], dsem="b_yb")
    wsa = WStream(p, 8, wt=128, name="b_wa")
    wsb = WStream(p, 16, wt=128, name="b_wb")
    psY = Rot([p.ps(f"b_ps{i}", [128, 512], F32) for i in range(4)], "b_ps")
    sm = Rot([p.sb(f"b_sm{i}", [128, T], BF16) for i in range(4)], "b_sm")
    t1 = Rot([p.sb(f"b_t1{i}", [128, 512], F32) for i in range(2)], "b_t1")
    t2 = Rot([p.sb(f"b_t2{i}", [128, 512], F32) for i in range(2)], "b_t2")
    for cc in range(16):
        wa, wak = wsa.load(WA, cc * 128, 128)
        wb, wbk = wsb.load(WB, cc * 128, 128)
        sa, sak = sm.next()
        sb_, sbk = sm.next()
        p.dma(sa[:, :], SMA[cc * 128:(cc + 1) * 128, :], writes=[sak], dsem=f"b_sm{sak[1]}")
        p.dma(sb_[:, :], SMB[cc * 128:(cc + 1) * 128, :], writes=[sbk], dsem=f"b_sm{sbk[1]}")
        for th in range(2):
            ts = slice(th * 512, (th + 1) * 512)
            pa, pak = psY.next()
            for kc in range(8):
                p.add("pe", lambda e, pa=pa, wa=wa, kc=kc, ts=ts: e.matmul(
                    pa[:, :], lhsT=wa[:, kc, :], rhs=ya[:, kc, ts], start=(kc == 0), stop=(kc == 7)),
                    reads=[wak, ("b_ya", kc)], writes=[pak])
            pb, pbk = psY.next()
            for kc in range(16):
                p.add("pe", lambda e, pb=pb, wb=wb, kc=kc, ts=ts: e.matmul(
                    pb[:, :], lhsT=wb[:, kc, :], rhs=yb[:, kc, ts], start=(kc == 0), stop=(kc == 15)),
                    reads=[wbk, ("b_yb", kc)], writes=[pbk])
            a, ak = t1.next()
            b, bk = t2.next()
            p.add("dve", lambda e, a=a, pa=pa, sa=sa, ts=ts: e.tensor_tensor(out=a[:, :], in0=pa[:, :], in1=sa[:, ts], op=ALU.mult),
                  reads=[pak, sak], writes=[ak])
            p.add("dve", lambda e, b=b, pb=pb, sb_=sb_, ts=ts: e.tensor_tensor(out=b[:, :], in0=pb[:, :], in1=sb_[:, ts], op=ALU.mult),
                  reads=[pbk, sbk], writes=[bk])
            p.add("dve", lambda e, a=a, b=b, cc=cc, ts=ts: e.tensor_tensor(out=yT[:, cc, ts], in0=a[:, :], in1=b[:, :], op=ALU.add),
                  reads=[ak, bk], writes=[("b_yT", cc)])
    emit_out_ln(p, "b_", yT, [("b_yT", kc) for kc in range(16)], 16, WO, XT, LNG, LNB, OUT, SCR, psY)


def build_stage4a():
    nc = bass.Bass("TRN2", target_bir_lowering=False)
    p = Prog(nc)
    X1T = p.dram("X1T", [2048, T], F32, "ExternalInput")
    X1H = p.dram("X1H", [2048, 2], F32, "ExternalInput")
    WG = p.dram("ffn_w_gate", [2048, DFF], F32, "ExternalInput")
    WU = p.dram("ffn_w_up", [2048, DFF], F32, "ExternalInput")
    CW = p.dram("CW", [128, FC, 3], F32, "ExternalInput")
    CB = p.dram("CB", [128, FC], F32, "ExternalInput")
    HT = p.dram("HT", [DFF, T], BF16, "ExternalOutput")
    emit_stage4a(p, X1T, X1H, WG, WU, CW, CB, HT)
    p.emit()
    return nc


def emit_stage4a(p, X1T, X1H, WG, WU, CW, CB, HT):
    xb = p.sb("c_xb", [128, 16, T + 2], BF16)
    xs = [p.sb(f"c_xs{i}", [128, T + 2], F32) for i in range(2)]
    for kc in range(16):
        s = xs[kc % 2]
        p.dma(s[:, 2:], X1T[kc * 128:(kc + 1) * 128, :], writes=[("c_xs", kc % 2, 0)], dsem=f"c_xs{kc % 2}")
        p.dma(s[:, 0:2], X1H[kc * 128:(kc + 1) * 128, :], writes=[("c_xs", kc % 2, 1)], dsem=f"c_xs{kc % 2}")
        p.add("dve" if kc % 2 == 0 else "pool", lambda e, s=s, kc=kc: e.tensor_copy(out=xb[:, kc, :], in_=s[:, :]),
              reads=[("c_xs", kc % 2, 0), ("c_xs", kc % 2, 1)], writes=[("c_xb", kc)])
    xkeys = [("c_xb", kc) for kc in range(16)]
    cw = p.sb("c_cw", [128, FC, 3], F32)
    cb = p.sb("c_cb", [128, FC], F32)
    p.dma(cw[:, :, :], CW[:, :, :], writes=["c_cw"], dsem="c_cw")
    p.dma(cb[:, :], CB[:, :], writes=["c_cb"], dsem="c_cb")
    ws = WStream(p, 16, wt=128, nbuf=3, name="c_w")
    psG = Rot([p.ps(f"c_psg{i}", [128, 512], F32) for i in range(3)], "c_psg")
    psH = Rot([p.ps(f"c_psh{i}", [128, 512], F32) for i in range(1)], "c_psh")
    psU = Rot([p.ps(f"c_psu{i}", [128, 512], F32) for i in range(4)], "c_psu")
    gx = Rot([p.sb(f"c_gx{i}", [128, T + 2], F32) for i in range(2)], "c_gx")
    acc = Rot([p.sb(f"c_acc{i}", [128, T], F32) for i in range(2)], "c_acc")
    hh = Rot([p.sb(f"c_h{i}", [128, T], BF16) for i in range(2)], "c_h")
    for fc in range(FC):
        wg, wgk = ws.load(WG, fc * 128, 128)
        wu, wuk = ws.load(WU, fc * 128, 128)
        g, gk = gx.next()
        ph, phk = psH.next()
        for kc in range(16):
            p.add("pe", lambda e, ph=ph, wg=wg, kc=kc: e.matmul(
                ph[:, 0:2], lhsT=wg[:, kc, :], rhs=xb[:, kc, 0:2], start=(kc == 0), stop=(kc == 15)),
                reads=[wgk, xkeys[kc]], writes=[phk])
        p.add("act", lambda e, g=g, ph=ph: e.activation(out=g[:, 0:2], in_=ph[:, 0:2], func=AF.Copy),
              reads=[phk], writes=[(gk, 2)])
        for th in range(2):
            pg, pgk = psG.next()
            for kc in range(16):
                p.add("pe", lambda e, pg=pg, wg=wg, kc=kc, th=th: e.matmul(
                    pg[:, :], lhsT=wg[:, kc, :], rhs=xb[:, kc, 2 + th * 512:2 + (th + 1) * 512], start=(kc == 0), stop=(kc == 15)),
                    reads=[wgk, xkeys[kc]], writes=[pgk])
            p.add("act", lambda e, g=g, pg=pg, th=th: e.activation(out=g[:, 2 + th * 512:2 + (th + 1) * 512], in_=pg[:, :], func=AF.Copy),
                  reads=[pgk], writes=[(gk, th)])
        gkeys = [(gk, 0), (gk, 1), (gk, 2)]
        a, ak = acc.next()
        p.add("dve", lambda e, a=a, g=g, fc=fc: e.tensor_scalar(out=a[:, :], in0=g[:, 0:T], scalar1=cw[:, fc, 0:1], scalar2=cb[:, fc:fc + 1],
                                                               op0=ALU.mult, op1=ALU.add), reads=gkeys + ["c_cw", "c_cb"], writes=[ak])
        p.add("dve", lambda e, a=a, g=g, fc=fc: e.scalar_tensor_tensor(out=a[:, :], in0=g[:, 1:T + 1], scalar=cw[:, fc, 1:2], in1=a[:, :],
                                                                      op0=ALU.mult, op1=ALU.add), reads=gkeys + ["c_cw", ak], writes=[ak])
        p.add("dve", lambda e, a=a, g=g, fc=fc: e.scalar_tensor_tensor(out=a[:, :], in0=g[:, 2:T + 2], scalar=cw[:, fc, 2:3], in1=a[:, :],
                                                                      op0=ALU.mult, op1=ALU.add), reads=gkeys + ["c_cw", ak], writes=[ak])
        p.add("act", lambda e, a=a: e.activation(out=a[:, :], in_=a[:, :], func=AF.Silu), reads=[ak], writes=[ak])
        h, hk = hh.next()
        for th in range(2):
            ts = slice(th * 512, (th + 1) * 512)
            pu, puk = psU.next()
            for kc in range(16):
                p.add("pe", lambda e, pu=pu, wu=wu, kc=kc, th=th: e.matmul(
                    pu[:, :], lhsT=wu[:, kc, :], rhs=xb[:, kc, 2 + th * 512:2 + (th + 1) * 512], start=(kc == 0), stop=(kc == 15)),
                    reads=[wuk, xkeys[kc]], writes=[puk])
            p.add("dve", lambda e, h=h, a=a, pu=pu, ts=ts: e.tensor_tensor(out=h[:, ts], in0=pu[:, :], in1=a[:, ts], op=ALU.mult),
                  reads=[puk, ak], writes=[(hk, th)])
        p.dma(HT[fc * 128:(fc + 1) * 128, :], h[:, :], reads=[(hk, 0), (hk, 1)], dsem=f"c_h{hk[1]}", eng="act")


def build_stage4b():
    nc = bass.Bass("TRN2", target_bir_lowering=False)
    p = Prog(nc)
    HT = p.dram("HT", [DFF, T], BF16, "ExternalInput")
    X1T = p.dram("X1T", [2048, T], F32, "ExternalInput")
    WD = p.dram("ffn_w_down", [DFF, 2048], F32, "ExternalInput")
    LNG = p.dram("LNG", [128, 16], F32, "ExternalInput")
    LNB = p.dram("LNB", [128, 16], F32, "ExternalInput")
    OUT = p.dram("X2T", [2048, T], F32, "ExternalOutput")
    SCR = p.dram("SCR", [2048, T], F32, "Internal")
    emit_stage4b(p, HT, X1T, WD, LNG, LNB, OUT, SCR)
    p.emit()
    return nc


def emit_stage4b(p, HT, X1T, WD, LNG, LNB, OUT, SCR):
    hT = p.sb("d_hT", [128, FC, T], BF16)
    for kc in range(FC):
        p.dma(hT[:, kc, :], HT[kc * 128:(kc + 1) * 128, :], writes=[("d_hT", kc)], dsem=f"d_hT{kc % 4}")
    psY = Rot([p.ps(f"d_ps{i}", [128, 512], F32) for i in range(4)], "d_ps")
    emit_out_ln(p, "d_", hT, [("d_hT", kc) for kc in range(FC)], FC, WD, X1T, LNG, LNB, OUT, SCR, psY)


HALO = (128, 512, 2048); DIL = (1, 4, 16)

def t5_bucket(dist):
    dist = np.asarray(dist)
    df = np.maximum(dist, 1).astype(np.float32)
    large = 16 + (np.log(df / np.float32(16)) / np.float32(np.log(2048 / 16)) * np.float32(16)).astype(np.int32)
    return np.where(dist < 16, dist, np.minimum(large, 31))

def bias_tables(rel_bias):
    ki = np.arange(128)[:, None]; qi = np.arange(128)[None, :]
    off_prev = 128 + qi - ki
    off_cur = qi - ki
    valid = np.concatenate([(off_prev <= 128), (off_cur >= 0)], axis=1).astype(np.float32)
    BT = np.zeros((24, 128, 256), np.float32)
    for g in range(3):
        bp = t5_bucket(DIL[g] * np.clip(off_prev, 0, 128))
        bc = t5_bucket(DIL[g] * np.clip(off_cur, 0, 128))
        for h in range(8):
            BT[g * 8 + h, :, 0:128] = rel_bias[bp, g * 8 + h]
            BT[g * 8 + h, :, 128:256] = rel_bias[bc, g * 8 + h]
    return BT, valid

def hmask(c):
    m = np.zeros((128, 3), np.float32)
    if c > 0:
        m[:, 0] = 1; m[:, 1] = 1
    if c == 1:
        m[64:, 2] = 1
    elif c >= 2:
        m[:, 2] = 1
    return m

def cmask(c):
    m = np.zeros((128, 8), np.float32)
    m[:, :c] = 1
    return m


_PROGS = {}


def _prog(name, builder):
    if name not in _PROGS:
        _PROGS[name] = builder()
    return _PROGS[name]


def _run(name, builder, in_maps):
    nc = _prog(name, builder)
    res = run_bass_kernel_spmd(nc, in_maps, core_ids=list(range(NCORES)))
    return res.results


NCORES = 8


def _lnp(v):
    return np.ascontiguousarray(v.reshape(16, 128).T)


def kernel(x, w_in, gla_gate_w, gla_gate_b, gla_norm_g, w_proj_a, w_proj_b, w_out, rel_bias,
           ln1_g, ln1_b, ffn_w_gate, ffn_w_up, ffn_conv_w, ffn_conv_b, ffn_w_down, ln2_g, ln2_b):
    f32 = np.float32
    x = np.asarray(x, f32)[0]
    C = NCORES
    xT = [np.ascontiguousarray(x[c * T:(c + 1) * T].T) for c in range(C)]
    BT, valid = bias_tables(np.asarray(rel_bias, f32))
    gcon = gla_consts()
    hm = [hmask(c) for c in range(C)]
    cm = [cmask(c) for c in range(C)]
    for l in range(4):
        gate = np.concatenate([np.asarray(gla_gate_w[l], f32), np.asarray(gla_gate_b[l], f32)[None]], 0)
        wl = np.asarray(w_in[l], f32)
        r1 = _run("s1", build_stage1, [{"xT": xT[c], "w_in": wl, "gate": gate} for c in range(C)])
        r2 = _run("s2", build_stage2, [{"GQT": r1[c]["GQT"], "GKT": r1[c]["GKT"], "GKM": r1[c]["GKM"], "GVM": r1[c]["GVM"],
                                        "LAM": r1[c]["LAM"], "GC": gcon} for c in range(C)])
        UALL = np.stack([r2[c]["U"] for c in range(C)], 0)
        DALL = np.ascontiguousarray(np.concatenate([r2[c]["DT"] for c in range(C)], 1))
        KH, VH = [], []
        for g in range(3):
            H = HALO[g]
            kt_all = np.concatenate([r1[c]["KT"][g * 1024:(g + 1) * 1024] for c in range(C)], 1)
            v_all = np.concatenate([r1[c]["V"][g] for c in range(C)], 0)
            kt_pad = np.concatenate([np.zeros((1024, H), kt_all.dtype), kt_all], 1)
            v_pad = np.concatenate([np.zeros((H, 1024), v_all.dtype), v_all], 0)
            KH.append([np.ascontiguousarray(kt_pad[:, c * T:c * T + H + T]) for c in range(C)])
            VH.append([np.ascontiguousarray(v_pad[c * T:c * T + H + T]) for c in range(C)])
        ng = np.ascontiguousarray(np.asarray(gla_norm_g[l], f32).reshape(4, 128).T)
        im = []
        for c in range(C):
            m = {"QT": r1[c]["QT"], "BT": BT, "VALID": valid, "HMASK": hm[c], "OL": r2[c]["OL"], "QTL": r2[c]["QTL"],
                 "UALL": UALL, "DALL": DALL, "CMASK": cm[c], "SGR": r1[c]["SGR"], "NG": ng}
            for g in range(3):
                m[f"KH{g}"] = KH[g][c]
                m[f"VH{g}"] = VH[g][c]
            im.append(m)
        r3 = _run("s3a", build_stage3a, im)
        r3b = _run("s3b", build_stage3b, [{"YAT": r3[c]["YAT"], "YBT": r3[c]["YBT"], "SMA": r1[c]["SMA"], "SMB": r1[c]["SMB"],
                                           "xT": xT[c], "w_proj_a": np.asarray(w_proj_a[l], f32),
                                           "w_proj_b": np.asarray(w_proj_b[l], f32), "w_out": np.asarray(w_out[l], f32),
                                           "LNG": _lnp(np.asarray(ln1_g[l], f32)), "LNB": _lnp(np.asarray(ln1_b[l], f32))}
                                          for c in range(C)])
        x1T = [r3b[c]["X1T"] for c in range(C)]
        x1h = [np.zeros((2048, 2), f32)] + [np.ascontiguousarray(x1T[c - 1][:, T - 2:T]) for c in range(1, C)]
        cw = np.ascontiguousarray(np.asarray(ffn_conv_w[l], f32).reshape(3, FC, 128).transpose(2, 1, 0))
        cb = np.ascontiguousarray(np.asarray(ffn_conv_b[l], f32).reshape(FC, 128).T)
        r4 = _run("s4a", build_stage4a, [{"X1T": x1T[c], "X1H": x1h[c], "ffn_w_gate": np.asarray(ffn_w_gate[l], f32),
                                          "ffn_w_up": np.asarray(ffn_w_up[l], f32), "CW": cw, "CB": cb} for c in range(C)])
        r5 = _run("s4b", build_stage4b, [{"HT": r4[c]["HT"], "X1T": x1T[c], "ffn_w_down": np.asarray(ffn_w_down[l], f32),
                                          "LNG": _lnp(np.asarray(ln2_g[l], f32)), "LNB": _lnp(np.asarray(ln2_b[l], f32))}
                                         for c in range(C)])
        xT = [r5[c]["X2T"] for c in range(C)]
    out = np.concatenate([np.ascontiguousarray(np.asarray(xT[c], f32).T) for c in range(C)], 0)
    return out[None].astype(f32)
```

```python
import contextlib
import numpy as np
import concourse.bass as bass
import concourse.mybir as mybir
from concourse.bass_utils import run_bass_kernel_spmd

F32 = mybir.dt.float32
BF16 = mybir.dt.bfloat16
AF = mybir.ActivationFunctionType
ALU = mybir.AluOpType

ENGS = ("pe", "act", "dve", "pool", "sp")
SAME_ENGINE_SYNC = True


class Prog:
    _uid = [0]

    def __init__(self, nc):
        self.nc = nc
        Prog._uid[0] += 1
        self.pfx = f"P{Prog._uid[0]}_"
        self.stack = contextlib.ExitStack()
        self.q = {e: [] for e in ENGS}
        self.cnt = {}
        self.seen = {e: {} for e in ENGS}
        self.res = {}
        self.semh = {}
        self.nsb = 0
        self.same_sync = SAME_ENGINE_SYNC
        self._defer = None

    GLOBAL_SEMS = {}

    def sem(self, key):
        if isinstance(key, tuple) and key[0] == "dma" and str(key[1]).startswith("GL_"):
            g = Prog.GLOBAL_SEMS
            if key[1] not in g:
                g[key[1]] = [self.nc.alloc_semaphore(name=key[1]), 0]
            if key not in self.semh:
                self.semh[key] = g[key[1]][0]
                self.cnt[key] = g[key[1]][1]
            return self.semh[key]
        if key not in self.semh:
            self.semh[key] = self.stack.enter_context(self.nc.semaphore(self.pfx + "s_" + str(key).replace(" ", "").replace("'", "").replace("(", "").replace(")", "").replace(",", "_")))
            self.cnt[key] = 0
        return self.semh[key]

    def sb(self, name, shape, dt):
        return self.stack.enter_context(self.nc.sbuf_tensor(self.pfx + name, list(shape), dt))

    def ps(self, name, shape, dt=F32):
        return self.stack.enter_context(self.nc.psum_tensor(self.pfx + name, list(shape), dt))

    def dram(self, name, shape, dt, kind="Internal"):
        return self.nc.dram_tensor(name, list(shape), dt, kind=kind).ap()

    def defer_start(self):
        self._defer = []

    def defer_stop(self):
        d, self._defer = self._defer, None
        return d

    def flush(self, lst):
        for a in lst or ():
            self.add(*a)

    def add(self, eng, fn, reads=(), writes=(), dsem=None, inc=16):
        if self._defer is not None:
            self._defer.append((eng, fn, list(reads), list(writes), dsem, inc))
            return None
        need = {}

        def want(tok):
            if tok is None:
                return
            k, v = tok
            if need.get(k, 0) < v:
                need[k] = v

        for k in reads:
            st = self.res.get(k)
            if st:
                want(st["w"])
        for k in writes:
            st = self.res.get(k)
            if st:
                want(st["w"])
                for t in st["r"].items():
                    want(t)
        waits = []
        for k, v in need.items():
            if k == eng and (eng == "pe" or not self.same_sync):
                continue
            if isinstance(k, tuple) and k[0] == "dma":
                v = self.cnt[k]
            if self.seen[eng].get(k, 0) >= v:
                continue
            self.seen[eng][k] = v
            waits.append((k, v))
        if dsem is None:
            sk = eng
            self.sem(sk)
            self.cnt[sk] += 1
            inc = 1
        else:
            sk = ("dma", dsem)
            self.sem(sk)
            self.cnt[sk] += inc
            if str(dsem).startswith("GL_"):
                Prog.GLOBAL_SEMS[dsem][1] = self.cnt[sk]
        tok = (sk, self.cnt[sk])
        self.q[eng].append((waits, fn, sk, inc))
        for k in reads:
            st = self.res.setdefault(k, {"w": None, "r": {}})
            if st["r"].get(sk, 0) < tok[1]:
                st["r"][sk] = tok[1]
        for k in writes:
            self.res[k] = {"w": tok, "r": {}}
        return tok

    def dma(self, out, in_, reads=(), writes=(), dsem=None, eng="sp", **kw):
        assert dsem is not None
        return self.add(eng, lambda e: e.dma_start(out=out, in_=in_, **kw), reads, writes, dsem=dsem)

    def final_wait(self, eng="sp"):
        self.finals = eng

    def emit(self):
        nc = self.nc
        semh = self.semh
        q = self.q
        cnt = self.cnt

        def replay(name, e, final=False):
            for waits, fn, sk, inc in q[name]:
                for k, v in waits:
                    e.wait_ge(semh[k], v)
                ins = fn(e)
                ins.then_inc(semh[sk], inc)
            if final:
                for k, h in semh.items():
                    if cnt[k] > 0:
                        e.wait_ge(h, cnt[k])

        with nc.Block() as block:
            @block.sync
            def _(e):
                replay("sp", e, final=True)

            @block.tensor
            def _(e):
                replay("pe", e)

            @block.scalar
            def _(e):
                replay("act", e)

            @block.vector
            def _(e):
                replay("dve", e)

            @block.gpsimd
            def _(e):
                replay("pool", e)
        self.stack.close()


T = 1024
D = 2048
KC = D // 128
INC = 19472
WT = 256


def build_stage1(nc=None):
    nc = nc or bass.Bass("TRN2", target_bir_lowering=False)
    p = Prog(nc)
    xT = p.dram("xT", [D, T], F32, "ExternalInput")
    w_in = p.dram("w_in", [D, INC], F32, "ExternalInput")
    gate = p.dram("gate", [17, 1024], F32, "ExternalInput")
    QT = p.dram("QT", [3 * 1024, T], BF16, "ExternalOutput")
    KT = p.dram("KT", [3 * 1024, T], BF16, "ExternalOutput")
    V = p.dram("V", [3, T, 1024], BF16, "ExternalOutput")
    GQT = p.dram("GQT", [1024, T], BF16, "ExternalOutput")
    GKT = p.dram("GKT", [1024, T], BF16, "ExternalOutput")
    GKM = p.dram("GKM", [T, 1024], BF16, "ExternalOutput")
    GVM = p.dram("GVM", [T, 2048], BF16, "ExternalOutput")
    SGR = p.dram("SGR", [2048, T], BF16, "ExternalOutput")
    SMA = p.dram("SMA", [2048, T], BF16, "ExternalOutput")
    SMB = p.dram("SMB", [2048, T], BF16, "ExternalOutput")
    LAM = p.dram("LAM", [T, 1024], F32, "ExternalOutput")
    emit_stage1(p, xT, w_in, gate, QT, KT, V, GQT, GKT, GKM, GVM, SGR, SMA, SMB, LAM)
    p.emit()
    return nc


class Rot:
    def __init__(self, tiles, key):
        self.tiles = tiles
        self.key = key
        self.i = 0

    def next(self):
        i = self.i % len(self.tiles)
        self.i += 1
        return self.tiles[i], (self.key, i)


def load_xT_bf16(p, xT, xb, nt=T):
    xs = [p.sb(f"xs{i}", [128, nt], F32) for i in range(2)]
    for kc in range(KC):
        s = xs[kc % 2]
        p.dma(s[:, :], xT[kc * 128:(kc + 1) * 128, :], writes=[("xs", kc % 2)], dsem=f"xs{kc % 2}")
        eng = "dve" if kc % 2 == 0 else "pool"
        p.add(eng, lambda e, s=s, kc=kc: e.tensor_copy(out=xb[:, kc, :], in_=s[:, :]),
              reads=[("xs", kc % 2)], writes=[("xb", kc)])


class WStream:
    def __init__(self, p, kc, wt=WT, nbuf=2, name="w"):
        self.p = p
        self.kc = kc
        self.wt = wt
        self.name = name
        self.ws = [p.sb(f"{name}s{i}", [128, kc, wt], F32) for i in range(nbuf)]
        self.wb = [p.sb(f"{name}b{i}", [128, kc, wt], BF16) for i in range(nbuf)]
        self.i = 0
        self.nbuf = nbuf

    def load(self, w, c0, nc_):
        p = self.p
        i = self.i % self.nbuf
        self.i += 1
        s, b = self.ws[i], self.wb[i]
        src = w[:, c0:c0 + nc_].rearrange("(k p) c -> p k c", p=128)
        half = self.kc // 2
        nm = self.name
        p.dma(s[:, 0:half, 0:nc_], src[:, 0:half, :], writes=[(nm + "s", i, 0)], dsem=f"{nm}s{i}")
        p.dma(s[:, half:, 0:nc_], src[:, half:, :], writes=[(nm + "s", i, 1)], dsem=f"{nm}s{i}")
        p.add("pool", lambda e: e.tensor_copy(out=b[:, :, 0:nc_], in_=s[:, :, 0:nc_]),
              reads=[(nm + "s", i, 0), (nm + "s", i, 1)], writes=[(nm + "b", i)])
        return b, (nm + "b", i)


def emit_stage1(p, xT, w_in, gate, QT, KT, V, GQT, GKT, GKM, GVM, SGR, SMA, SMB, LAM):
    xb = p.sb("xb", [128, KC, T], BF16)
    load_xT_bf16(p, xT, xb)
    xkeys = [("xb", kc) for kc in range(KC)]
    ws = WStream(p, KC)
    psum = Rot([p.ps(f"ps{i}", [128, 512], F32) for i in range(8)], "ps")
    ob = Rot([p.sb(f"ob{i}", [128, 512], BF16) for i in range(4)], "ob")
    evac_i = [0]

    def evac(dst_sb, src_ps, func, rk, wk):
        if func is None:
            evac_i[0] += 1
            if evac_i[0] % 2 == 0:
                p.add("dve", lambda e: e.tensor_copy(out=dst_sb, in_=src_ps), reads=rk, writes=wk)
                return
            func = AF.Copy
        p.add("act", lambda e: e.activation(out=dst_sb, in_=src_ps, func=func), reads=rk, writes=wk)

    def fm_job(c0, ncols, dst, func=None):
        for t0 in range(0, ncols, WT):
            n = min(WT, ncols - t0)
            wb, wk = ws.load(w_in, c0 + t0, n)
            for m0 in range(0, n, 128):
                mc = min(128, n - m0)
                for th in range(T // 512):
                    ps, pk = psum.next()
                    for kc in range(KC):
                        p.add("pe", lambda e, ps=ps, wb=wb, kc=kc, m0=m0, mc=mc, th=th: e.matmul(
                            ps[0:mc, :], lhsT=wb[:, kc, m0:m0 + mc], rhs=xb[:, kc, th * 512:(th + 1) * 512],
                            start=(kc == 0), stop=(kc == KC - 1)),
                            reads=[wk, xkeys[kc]], writes=[pk])
                    o, ok = ob.next()
                    evac(o[0:mc, :], ps[0:mc, :], func, [pk], [ok])
                    p.dma(dst[t0 + m0:t0 + m0 + mc, th * 512:(th + 1) * 512], o[0:mc, :],
                          reads=[ok], dsem=f"ob{ok[1]}", eng="act")

    def tm_job(c0, ncols, dst):
        for t0 in range(0, ncols, WT):
            n = min(WT, ncols - t0)
            wb, wk = ws.load(w_in, c0 + t0, n)
            for tp in range(T // 256):
                ps, pk = psum.next()
                for j in range(2):
                    tt = tp * 2 + j
                    for kc in range(KC):
                        p.add("pe", lambda e, ps=ps, wb=wb, kc=kc, tt=tt, j=j, n=n: e.matmul(
                            ps[:, j * 256:j * 256 + n], lhsT=xb[:, kc, tt * 128:(tt + 1) * 128], rhs=wb[:, kc, 0:n],
                            start=(kc == 0), stop=(kc == KC - 1)),
                            reads=[wk, xkeys[kc]], writes=[pk])
                o, ok = ob.next()
                evac(o[:, :], ps[:, :], None, [pk], [ok])
                for j in range(2):
                    tt = tp * 2 + j
                    p.dma(dst[tt * 128:(tt + 1) * 128, t0:t0 + n], o[:, j * 256:j * 256 + n],
                          reads=[ok], dsem=f"ob{ok[1]}", eng="act")

    for g in range(3):
        fm_job((3 * g) * 1024, 1024, QT[g * 1024:(g + 1) * 1024, :])
        fm_job((3 * g + 1) * 1024, 1024, KT[g * 1024:(g + 1) * 1024, :])
        tm_job((3 * g + 2) * 1024, 1024, V[g])
    fm_job(9216, 1024, GQT)
    fm_job(10240, 1024, GKT)
    tm_job(10240, 1024, GKM)
    tm_job(11264, 2048, GVM)
    fm_job(13312, 2048, SGR, AF.Silu)
    fm_job(15376, 2048, SMA, AF.Sigmoid)
    fm_job(17424, 2048, SMB, AF.Sigmoid)

    gl = p.sb("gl", [17, T], F32)
    gw = p.sb("gw", [17, 1024], F32)
    p.dma(gw[:, :], gate[:, :], writes=["gw"], dsem="gw")
    p.add("pool", lambda e: e.memset(gl[:, :], 1.0), writes=["gl"])
    wb, wk = ws.load(w_in, 15360, 16)
    for th in range(T // 512):
        ps, pk = psum.next()
        for kc in range(KC):
            p.add("pe", lambda e, ps=ps, wb=wb, kc=kc, th=th: e.matmul(
                ps[0:16, :], lhsT=wb[:, kc, 0:16], rhs=xb[:, kc, th * 512:(th + 1) * 512],
                start=(kc == 0), stop=(kc == KC - 1)), reads=[wk, xkeys[kc]], writes=[pk])
        p.add("dve", lambda e, ps=ps, th=th: e.tensor_copy(out=gl[0:16, th * 512:(th + 1) * 512], in_=ps[0:16, :]),
              reads=[pk], writes=["gl"])
    la = Rot([p.sb(f"la{i}", [128, 512], F32) for i in range(2)], "la")
    for tt in range(T // 128):
        for ch in range(2):
            ps, pk = psum.next()
            p.add("pe", lambda e, ps=ps, tt=tt, ch=ch: e.matmul(
                ps[:, :], lhsT=gl[:, tt * 128:(tt + 1) * 128], rhs=gw[:, ch * 512:(ch + 1) * 512],
                start=True, stop=True), reads=["gl", "gw"], writes=[pk])
            o, ok = la.next()
            p.add("act", lambda e, o=o, ps=ps: e.activation(out=o[:, :], in_=ps[:, :], func=AF.Exp, scale=-1.0),
                  reads=[pk], writes=[ok])
            p.add("act", lambda e, o=o: e.activation(out=o[:, :], in_=o[:, :], func=AF.Ln, bias=1.0),
                  reads=[ok], writes=[ok])
            p.dma(LAM[tt * 128:(tt + 1) * 128, ch * 512:(ch + 1) * 512], o[:, :], reads=[ok], dsem=f"la{ok[1]}", eng="act")


NT = T // 128


def gla_consts():
    j = np.arange(128)[:, None]
    t = np.arange(128)[None, :]
    same = (j // 64) == (t // 64)
    tri = np.where(same & (j <= t), -1.0 / 16.0, 0.0).astype(np.float32)
    trirev = np.where(same & (j > t), -1.0 / 16.0, 0.0).astype(np.float32)
    mask = np.where(same & (j <= t), 1.0, 0.0).astype(np.float32)
    return np.concatenate([tri, trirev, mask], axis=1)


def build_stage2():
    nc = bass.Bass("TRN2", target_bir_lowering=False)
    p = Prog(nc)
    GQT = p.dram("GQT", [1024, T], BF16, "ExternalInput")
    GKT = p.dram("GKT", [1024, T], BF16, "ExternalInput")
    GKM = p.dram("GKM", [T, 1024], BF16, "ExternalInput")
    GVM = p.dram("GVM", [T, 2048], BF16, "ExternalInput")
    LAM = p.dram("LAM", [T, 1024], F32, "ExternalInput")
    GC = p.dram("GC", [128, 384], F32, "ExternalInput")
    OL = p.dram("OL", [2048, T], F32, "ExternalOutput")
    QTL = p.dram("QTL", [1024, T], BF16, "ExternalOutput")
    U = p.dram("U", [1024, 512], F32, "ExternalOutput")
    DT = p.dram("DT", [1024, 1], F32, "ExternalOutput")
    emit_stage2(p, GQT, GKT, GKM, GVM, LAM, GC, OL, QTL, U, DT)
    p.emit()
    return nc


def emit_stage2(p, GQT, GKT, GKM, GVM, LAM, GC, OL, QTL, U, DT):
    gc = p.sb("gc", [128, 384], F32)
    p.dma(gc[:, :], GC[:, :], writes=["gc"], dsem="gc")
    tri, trirev, mask = gc[:, 0:128], gc[:, 128:256], gc[:, 256:384]
    qT = [p.sb(f"g_qT{i}", [128, T], BF16) for i in range(2)]
    kT = [p.sb(f"g_kT{i}", [128, T], BF16) for i in range(2)]
    ktm = p.sb("g_ktm", [128, NT, 256], BF16)
    vtm = p.sb("g_vtm", [128, NT, 512], BF16)
    lam = p.sb("g_lam", [128, NT, 256], F32)
    E = [p.sb(f"g_E{i}", [128, T], F32) for i in range(2)]
    qd = [p.sb(f"g_qd{i}", [128, T], BF16) for i in range(2)]
    ki = [p.sb(f"g_ki{i}", [128, T], BF16) for i in range(2)]
    ke = p.sb("g_ke", [128, NT, 256], BF16)
    qtl = [p.sb(f"g_qtl{i}", [128, T], BF16) for i in range(2)]
    S = [p.sb(f"g_S{i}", [128, 512], F32) for i in range(2)]
    Sb = [p.sb(f"g_Sb{i}", [128, 512], BF16) for i in range(2)]
    G = [p.sb(f"g_G{i}", [128, 1], F32) for i in range(2)]
    tmpf = Rot([p.sb(f"g_tmp{i}", [128, 256], F32) for i in range(3)], "g_tmp")
    attb = Rot([p.sb(f"g_att{i}", [128, 128], BF16) for i in range(2)], "g_att")
    osb = Rot([p.sb(f"g_o{i}", [128, 512], F32) for i in range(2)], "g_o")
    psA = Rot([p.ps(f"g_psA{i}", [128, 512], F32) for i in range(3)], "g_psA")
    psO = Rot([p.ps(f"g_psO{i}", [128, 512], F32) for i in range(2)], "g_psO")
    psS = Rot([p.ps(f"g_psS{i}", [128, 512], F32) for i in range(3)], "g_psS")

    for h in range(4):
        for dc in range(2):
            r0 = h * 256 + dc * 128
            p.dma(qT[dc][:, :], GQT[r0:r0 + 128, :], writes=[("qT", dc)], dsem=f"gq{dc}")
            p.dma(kT[dc][:, :], GKT[r0:r0 + 128, :], writes=[("kT", dc)], dsem=f"gk{dc}")
        p.dma(ktm[:, :, :], GKM[:, h * 256:(h + 1) * 256].rearrange("(n p) c -> p n c", p=128),
              writes=["ktm"], dsem="gktm")
        p.dma(vtm[:, :, :], GVM[:, h * 512:(h + 1) * 512].rearrange("(n p) c -> p n c", p=128),
              writes=["vtm"], dsem="gvtm")
        p.dma(lam[:, :, :], LAM[:, h * 256:(h + 1) * 256].rearrange("(n p) c -> p n c", p=128),
              writes=["lam"], dsem="glam")
        for dc in range(2):
            p.add("pool", lambda e, dc=dc: e.memset(S[dc][:, :], 0.0), writes=[("S", dc)])
            p.add("pool", lambda e, dc=dc: e.memset(Sb[dc][:, :], 0.0), writes=[("Sb", dc)])
            p.add("pool", lambda e, dc=dc: e.memset(G[dc][:, :], 1.0), writes=[("G", dc)])
        for tt in range(NT):
            cs = slice(tt * 128, (tt + 1) * 128)
            for dc in range(2):
                ps, pk = psA.next()
                p.add("pe", lambda e, ps=ps, tt=tt, dc=dc: e.matmul(
                    ps[:, 0:128], lhsT=lam[:, tt, dc * 128:(dc + 1) * 128], rhs=tri, start=True, stop=True),
                    reads=["lam", "gc"], writes=[pk])
                p.add("act", lambda e, ps=ps, dc=dc, cs=cs: e.activation(out=E[dc][:, cs], in_=ps[:, 0:128], func=AF.Exp),
                      reads=[pk], writes=[("E", dc, tt)])
                tm, tk = tmpf.next()
                p.add("act", lambda e, ps=ps, tm=tm: e.activation(out=tm[:, 0:128], in_=ps[:, 0:128], func=AF.Exp, scale=-1.0),
                      reads=[pk], writes=[tk])
                p.add("dve", lambda e, dc=dc, cs=cs: e.scalar_tensor_tensor(
                    out=qd[dc][:, cs], in0=qT[dc][:, cs], scalar=0.0625, in1=E[dc][:, cs], op0=ALU.mult, op1=ALU.mult),
                    reads=[("qT", dc), ("E", dc, tt)], writes=[("qd", dc, tt)])
                p.add("dve", lambda e, dc=dc, cs=cs, tm=tm: e.tensor_tensor(
                    out=ki[dc][:, cs], in0=kT[dc][:, cs], in1=tm[:, 0:128], op=ALU.mult),
                    reads=[("kT", dc), tk], writes=[("ki", dc, tt)])
            ps, pk = psA.next()
            p.add("pe", lambda e, ps=ps, tt=tt: e.matmul(
                ps[:, 0:256], lhsT=trirev, rhs=lam[:, tt, :], start=True, stop=True),
                reads=["lam", "gc"], writes=[pk])
            tm, tk = tmpf.next()
            p.add("act", lambda e, ps=ps, tm=tm: e.activation(out=tm[:, :], in_=ps[:, 0:256], func=AF.Exp),
                  reads=[pk], writes=[tk])
            p.add("dve", lambda e, tt=tt, tm=tm: e.tensor_tensor(
                out=ke[:, tt, :], in0=ktm[:, tt, :], in1=tm[:, :], op=ALU.mult),
                reads=["ktm", tk], writes=[("ke", tt)])
        for tt in range(NT):
            cs = slice(tt * 128, (tt + 1) * 128)
            ps, pk = psA.next()
            for dc in range(2):
                p.add("pe", lambda e, ps=ps, dc=dc, cs=cs: e.matmul(
                    ps[:, 0:128], lhsT=ki[dc][:, cs], rhs=qd[dc][:, cs], start=(dc == 0), stop=(dc == 1)),
                    reads=[("ki", dc, tt), ("qd", dc, tt)], writes=[pk])
            ab, ak = attb.next()
            p.add("dve", lambda e, ps=ps, ab=ab: e.tensor_tensor(out=ab[:, :], in0=ps[:, 0:128], in1=mask, op=ALU.mult),
                  reads=[pk, "gc"], writes=[ak])
            po, pok = psO.next()
            for par in range(2):
                c0 = tt * 128 + par * 64
                pr = slice(par * 64, par * 64 + 64)
                for dc in range(2):
                    p.add("dve", lambda e, dc=dc, c0=c0: e.tensor_scalar(
                        out=qtl[dc][:, c0:c0 + 64], in0=qd[dc][:, c0:c0 + 64], scalar1=G[dc][:, 0:1], scalar2=None,
                        op0=ALU.mult), reads=[("qd", dc, tt), ("G", dc)], writes=[("qtl", dc)])
                    p.add("dve", lambda e, dc=dc, c0=c0: e.tensor_tensor(
                        out=G[dc][:, :], in0=G[dc][:, :], in1=E[dc][:, c0 + 63:c0 + 64], op=ALU.mult),
                        reads=[("G", dc), ("E", dc, tt)], writes=[("G", dc)])
                for ec in range(4):
                    oc = slice(ec * 128 + par * 64, ec * 128 + par * 64 + 64)
                    es = slice(ec * 128, (ec + 1) * 128)
                    for dc in range(2):
                        p.add("pe", lambda e, po=po, oc=oc, es=es, dc=dc, c0=c0: e.matmul(
                            po[:, oc], lhsT=Sb[dc][:, es], rhs=qd[dc][:, c0:c0 + 64], start=(dc == 0), stop=False),
                            reads=[("Sb", dc), ("qd", dc, tt)], writes=[pok])
                    p.add("pe", lambda e, po=po, oc=oc, es=es, pr=pr, tt=tt, ab=ab: e.matmul(
                        po[:, oc], lhsT=vtm[pr, tt, es], rhs=ab[pr, pr], start=False, stop=True),
                        reads=["vtm", ak], writes=[pok])
                for dc in range(2):
                    pss, psk = psS.next()
                    p.add("pe", lambda e, pss=pss, pr=pr, tt=tt, dc=dc: e.matmul(
                        pss[:, :], lhsT=ke[pr, tt, dc * 128:(dc + 1) * 128], rhs=vtm[pr, tt, :], start=True, stop=True),
                        reads=[("ke", tt), "vtm"], writes=[psk])
                    p.add("dve", lambda e, pss=pss, dc=dc, c0=c0: e.scalar_tensor_tensor(
                        out=S[dc][:, :], in0=S[dc][:, :], scalar=E[dc][:, c0 + 63:c0 + 64], in1=pss[:, :],
                        op0=ALU.mult, op1=ALU.add), reads=[("S", dc), ("E", dc, tt), psk], writes=[("S", dc)])
                    p.add("act", lambda e, dc=dc: e.activation(out=Sb[dc][:, :], in_=S[dc][:, :], func=AF.Copy),
                          reads=[("S", dc)], writes=[("Sb", dc)])
            o, ok = osb.next()
            p.add("act", lambda e, o=o, po=po: e.activation(out=o[:, :], in_=po[:, :], func=AF.Copy),
                  reads=[pok], writes=[ok])
            p.dma(OL[h * 512:(h + 1) * 512, cs].rearrange("(c p) t -> p c t", p=128),
                  o[:, :].rearrange("p (c t) -> p c t", c=4), reads=[ok], dsem=f"go{ok[1]}", eng="act")
        for dc in range(2):
            r0 = h * 256 + dc * 128
            p.dma(QTL[r0:r0 + 128, :], qtl[dc][:, :], reads=[("qtl", dc)], dsem=f"gqtl{dc}")
            p.dma(U[r0:r0 + 128, :], S[dc][:, :], reads=[("S", dc)], dsem=f"gU{dc}")
            p.dma(DT[r0:r0 + 128, :], G[dc][:, :], reads=[("G", dc)], dsem=f"gD{dc}")


HALO = (128, 512, 2048)
DIL = (1, 4, 16)
SCALE = 128 ** -0.5
GROUPS = [0, 1, 2]


def build_stage3a(do_attn=True, do_gla=True):
    nc = bass.Bass("TRN2", target_bir_lowering=False)
    p = Prog(nc)
    d = {}
    d["QT"] = p.dram("QT", [3 * 1024, T], BF16, "ExternalInput")
    for g in range(3):
        d[f"KH{g}"] = p.dram(f"KH{g}", [1024, HALO[g] + T], BF16, "ExternalInput")
        d[f"VH{g}"] = p.dram(f"VH{g}", [HALO[g] + T, 1024], BF16, "ExternalInput")
    d["BT"] = p.dram("BT", [24, 128, 256], F32, "ExternalInput")
    d["VALID"] = p.dram("VALID", [128, 256], F32, "ExternalInput")
    d["HMASK"] = p.dram("HMASK", [128, 3], F32, "ExternalInput")
    d["OL"] = p.dram("OL", [2048, T], F32, "ExternalInput")
    d["QTL"] = p.dram("QTL", [1024, T], BF16, "ExternalInput")
    d["UALL"] = p.dram("UALL", [8, 1024, 512], F32, "ExternalInput")
    d["DALL"] = p.dram("DALL", [1024, 8], F32, "ExternalInput").rearrange("r (c o) -> r c o", o=1)
    d["CMASK"] = p.dram("CMASK", [128, 8], F32, "ExternalInput")
    d["SGR"] = p.dram("SGR", [2048, T], BF16, "ExternalInput")
    d["NG"] = p.dram("NG", [128, 4], F32, "ExternalInput")
    d["YAT"] = p.dram("YAT", [1024, T], BF16, "ExternalOutput")
    d["YBT"] = p.dram("YBT", [2048, T], BF16, "ExternalOutput")
    if do_attn: emit_attn(p, d)
    if do_gla: emit_gla_fin(p, d)
    p.emit()
    return nc


def emit_attn(p, d):
    QT, BT, VALID, HMASK, YAT = d["QT"], d["BT"], d["VALID"], d["HMASK"], d["YAT"]
    valid = p.sb("a_valid", [128, 256], F32)
    hmask = p.sb("a_hmask", [128, 3], F32)
    ones = p.sb("a_ones", [128, 128], BF16)
    p.dma(valid[:, :], VALID[:, :], writes=["valid"], dsem="a_c0")
    p.dma(hmask[:, :], HMASK[:, :], writes=["hmask"], dsem="a_c1")
    p.add("pool", lambda e: e.memset(ones[:, :], 1.0), writes=["ones"])
    qT = Rot([p.sb(f"a_q{i}", [128, T], BF16) for i in range(2)], "a_q")
    kT = Rot([p.sb(f"a_k{i}", [128, 2048 + T], BF16) for i in range(2)], "a_k")
    vt = Rot([p.sb(f"a_v{i}", [128, 32, 128], BF16) for i in range(2)], "a_v")
    bt = Rot([p.sb(f"a_bt{i}", [128, 256], F32) for i in range(2)], "a_bt")
    tfull = Rot([p.sb(f"a_tf{i}", [128, 256], F32) for i in range(2)], "a_tf")
    tfirst = Rot([p.sb(f"a_t1{i}", [128, 256], F32) for i in range(2)], "a_t1")
    pf = Rot([p.sb(f"a_pf{i}", [128, 256], F32) for i in range(3)], "a_pf")
    pb = Rot([p.sb(f"a_pb{i}", [128, 256], BF16) for i in range(3)], "a_pb")
    nd = p.sb("a_nd", [128, 2, T], F32)
    num = nd[:, 0, :]
    den = nd[:, 1, :]
    ya = Rot([p.sb(f"a_ya{i}", [128, T], BF16) for i in range(2)], "a_ya")
    psS = Rot([p.ps(f"a_psS{i}", [128, 512], F32) for i in range(3)], "a_psS")
    psO = Rot([p.ps(f"a_psO{i}", [128, 512], F32) for i in range(3)], "a_psO")

    pend = [None]
    for h in range(8):
        for g in GROUPS:
            H, dl = HALO[g], DIL[g]
            q, qk = qT.next()
            k, kk = kT.next()
            v, vk = vt.next()
            r0 = g * 1024 + h * 128
            p.dma(q[:, :], QT[r0:r0 + 128, :], writes=[qk], dsem=f"a_q{qk[1]}")
            p.dma(k[:, 0:H + T], d[f"KH{g}"][h * 128:(h + 1) * 128, :], writes=[kk], dsem=f"a_k{kk[1]}")
            VH = d[f"VH{g}"]
            if g < 2:
                ntile = (H + T) // (128 * dl)
                for r in range(dl):
                    src = VH[r:H + T:dl, h * 128:(h + 1) * 128] if dl > 1 else VH[:, h * 128:(h + 1) * 128]
                    p.dma(v[:, r * ntile:(r + 1) * ntile, :], src.rearrange("(j p) c -> p j c", p=128),
                          writes=[(vk, r)], dsem=f"a_v{vk[1]}")
            else:
                for r in range(16):
                    srcA = VH[r:H:16, h * 128:(h + 1) * 128]
                    p.dma(v[:, r, :], srcA, writes=[(vk, r)], dsem=f"a_v{vk[1]}")
                srcB = VH[H:H + T, h * 128:(h + 1) * 128].rearrange("(i r) c -> i r c", r=16)
                p.dma(v[0:64, 16:32, :], srcB, writes=[(vk, 16)], dsem=f"a_v{vk[1]}")
            b, bk = bt.next()
            tf, tfk = tfull.next()
            t1, t1k = tfirst.next()
            p.dma(b[:, :], BT[g * 8 + h], writes=[bk], dsem=f"a_bt{bk[1]}")
            p.add("act", lambda e, b=b: e.activation(out=b[:, :], in_=b[:, :], func=AF.Exp), reads=[bk], writes=[bk])
            p.add("dve", lambda e, b=b, tf=tf: e.tensor_tensor(out=tf[:, :], in0=b[:, :], in1=valid[:, :], op=ALU.mult),
                  reads=[bk, "valid"], writes=[tfk])
            p.add("dve", lambda e, tf=tf, t1=t1: e.tensor_copy(out=t1[:, 128:256], in_=tf[:, 128:256]),
                  reads=[tfk], writes=[t1k])
            p.add("dve", lambda e, tf=tf, t1=t1, g=g: e.tensor_scalar(
                out=t1[:, 0:128], in0=tf[:, 0:128], scalar1=hmask[:, g:g + 1], scalar2=None, op0=ALU.mult),
                reads=[tfk, "hmask", t1k], writes=[t1k])
            vkeys = [(vk, r) for r in range(18)]
            if g < 2:
                nq = T // (128 * dl)
                ntile = (H + T) // (128 * dl)
                for r in range(dl):
                    for m in range(nq):
                        qs = slice(r + dl * 128 * m, r + dl * 128 * m + dl * 127 + 1, dl)
                        kprev = slice(r + dl * 128 * m, r + dl * 128 * m + dl * 127 + 1, dl)
                        kcur = slice(r + dl * 128 * (m + 1), r + dl * 128 * (m + 1) + dl * 127 + 1, dl)
                        tab, tabk = (t1, t1k) if m == 0 else (tf, tfk)
                        ps, pk = psS.next()
                        p.add("pe", lambda e, ps=ps, k=k, q=q, kprev=kprev, qs=qs: e.matmul(
                            ps[:, 0:128], lhsT=k[:, kprev], rhs=q[:, qs], start=True, stop=True),
                            reads=[kk, qk], writes=[pk])
                        p.add("pe", lambda e, ps=ps, k=k, q=q, kcur=kcur, qs=qs: e.matmul(
                            ps[:, 128:256], lhsT=k[:, kcur], rhs=q[:, qs], start=True, stop=True),
                            reads=[kk, qk], writes=[pk])
                        f, fk = pf.next()
                        pbb, pbk = pb.next()
                        p.add("act", lambda e, f=f, ps=ps: e.activation(out=f[:, :], in_=ps[:, 0:256], func=AF.Exp, scale=SCALE),
                              reads=[pk], writes=[fk])
                        p.add("dve", lambda e, f=f, pbb=pbb, tab=tab: e.tensor_tensor(out=pbb[:, :], in0=f[:, :], in1=tab[:, :], op=ALU.mult),
                              reads=[fk, tabk], writes=[pbk])
                        p.defer_start()
                        po, pok = psO.next()
                        j0 = r * ntile + m
                        p.add("pe", lambda e, po=po, v=v, pbb=pbb, j0=j0: e.matmul(
                            po[:, 0:128], lhsT=v[:, j0, :], rhs=pbb[:, 0:128], start=True, stop=False),
                            reads=vkeys + [pbk], writes=[pok])
                        p.add("pe", lambda e, po=po, v=v, pbb=pbb, j0=j0: e.matmul(
                            po[:, 0:128], lhsT=v[:, j0 + 1, :], rhs=pbb[:, 128:256], start=False, stop=True),
                            reads=vkeys + [pbk], writes=[pok])
                        p.add("pe", lambda e, po=po, pbb=pbb: e.matmul(
                            po[:, 128:256], lhsT=ones[:, :], rhs=pbb[:, 0:128], start=True, stop=False),
                            reads=["ones", pbk], writes=[pok])
                        p.add("pe", lambda e, po=po, pbb=pbb: e.matmul(
                            po[:, 128:256], lhsT=ones[:, :], rhs=pbb[:, 128:256], start=False, stop=True),
                            reads=["ones", pbk], writes=[pok])
                        po2 = po[:, 0:256].rearrange("p (a b) -> p a b", a=2)
                        if g == GROUPS[0]:
                            p.add("dve", lambda e, po2=po2, qs=qs: e.tensor_copy(out=nd[:, :, qs], in_=po2),
                                  reads=[pok], writes=["num", "den"])
                        else:
                            p.add("dve", lambda e, po2=po2, qs=qs: e.tensor_tensor(out=nd[:, :, qs], in0=po2, in1=nd[:, :, qs], op=ALU.add),
                                  reads=[pok, "num", "den"], writes=["num", "den"])
                        blk = p.defer_stop()
                        p.flush(pend[0])
                        pend[0] = blk
            else:
                for r in range(16):
                    qs = slice(r, T, 16)
                    kprev = slice(r, H, 16)
                    kcur = slice(H + r, H + T, 16)
                    ps, pk = psS.next()
                    p.add("pe", lambda e, ps=ps, k=k, q=q, kprev=kprev, qs=qs: e.matmul(
                        ps[:, 0:64], lhsT=k[:, kprev], rhs=q[:, qs], start=True, stop=True),
                        reads=[kk, qk], writes=[pk])
                    p.add("pe", lambda e, ps=ps, k=k, q=q, kcur=kcur, qs=qs: e.matmul(
                        ps[0:64, 64:128], lhsT=k[:, kcur], rhs=q[:, qs], start=True, stop=True),
                        reads=[kk, qk], writes=[pk])
                    f, fk = pf.next()
                    pbb, pbk = pb.next()
                    p.add("act", lambda e, f=f, ps=ps: e.activation(out=f[:, 0:64], in_=ps[:, 0:64], func=AF.Exp, scale=SCALE),
                          reads=[pk], writes=[fk])
                    p.add("act", lambda e, f=f, ps=ps: e.activation(out=f[0:64, 64:128], in_=ps[0:64, 64:128], func=AF.Exp, scale=SCALE),
                          reads=[pk, fk], writes=[fk])
                    p.add("dve", lambda e, f=f, pbb=pbb, t1=t1: e.tensor_tensor(out=pbb[:, 0:64], in0=f[:, 0:64], in1=t1[:, 0:64], op=ALU.mult),
                          reads=[fk, t1k], writes=[pbk])
                    p.add("dve", lambda e, f=f, pbb=pbb, t1=t1: e.tensor_tensor(out=pbb[0:64, 64:128], in0=f[0:64, 64:128], in1=t1[0:64, 128:192], op=ALU.mult),
                          reads=[fk, t1k, pbk], writes=[pbk])
                    p.defer_start()
                    po, pok = psO.next()
                    p.add("pe", lambda e, po=po, v=v, pbb=pbb, r=r: e.matmul(
                        po[:, 0:64], lhsT=v[:, r, :], rhs=pbb[:, 0:64], start=True, stop=False),
                        reads=vkeys + [pbk], writes=[pok])
                    p.add("pe", lambda e, po=po, v=v, pbb=pbb, r=r: e.matmul(
                        po[:, 0:64], lhsT=v[0:64, 16 + r, :], rhs=pbb[0:64, 64:128], start=False, stop=True),
                        reads=vkeys + [pbk], writes=[pok])
                    p.add("pe", lambda e, po=po, pbb=pbb: e.matmul(
                        po[:, 128:192], lhsT=ones[:, :], rhs=pbb[:, 0:64], start=True, stop=False),
                        reads=["ones", pbk], writes=[pok])
                    p.add("pe", lambda e, po=po, pbb=pbb: e.matmul(
                        po[:, 128:192], lhsT=ones[0:64, :], rhs=pbb[0:64, 64:128], start=False, stop=True),
                        reads=["ones", pbk], writes=[pok])
                    po2 = po[:, 0:256].rearrange("p (a b) -> p a b", a=2)[:, :, 0:64]
                    p.add("dve", lambda e, po2=po2, qs=qs: e.tensor_tensor(out=nd[:, :, qs], in0=po2, in1=nd[:, :, qs], op=ALU.add),
                          reads=[pok, "num", "den"], writes=["num", "den"])
                    blk = p.defer_stop()
                    p.flush(pend[0])
                    pend[0] = blk
        p.flush(pend[0])
        pend[0] = None
        y, yk = ya.next()
        p.add("dve", lambda e: e.reciprocal(out=den[:, :], in_=den[:, :]), reads=["den"], writes=["den"])
        p.add("dve", lambda e, y=y: e.tensor_tensor(out=y[:, :], in0=num[:, :], in1=den[:, :], op=ALU.mult),
              reads=["num", "den"], writes=[yk])
        p.dma(YAT[h * 128:(h + 1) * 128, :], y[:, :], reads=[yk], dsem=f"a_ya{yk[1]}", eng="act")


def emit_gla_fin(p, d):
    OL, QTL, UALL, DALL, CMASK, SGR, NG, YBT = (d[k] for k in ("OL", "QTL", "UALL", "DALL", "CMASK", "SGR", "NG", "YBT"))
    cm = p.sb("f_cm", [128, 8], F32)
    ng = p.sb("f_ng", [128, 4], F32)
    onesb = p.sb("f_ones", [128, 128], BF16)
    p.dma(cm[:, :], CMASK[:, :], writes=["cm"], dsem="f_c0")
    p.dma(ng[:, :], NG[:, :], writes=["ng"], dsem="f_c1")
    p.add("pool", lambda e: e.memset(onesb[:, :], 1.0), writes=["f_ones"])
    dall = p.sb("f_dall", [128, 8], F32)
    acoef = p.sb("f_a", [128, 8], F32)
    Sin = [p.sb(f"f_S{i}", [128, 512], F32) for i in range(2)]
    Sb = [p.sb(f"f_Sb{i}", [128, 512], BF16) for i in range(2)]
    ut = Rot([p.sb(f"f_u{i}", [128, 512], F32) for i in range(3)], "f_u")
    qtl = [p.sb(f"f_q{i}", [128, T], BF16) for i in range(2)]
    o = [p.sb(f"f_o{i}", [128, T], F32) for i in range(4)]
    osq = [p.sb(f"f_osq{i}", [128, T], BF16) for i in range(4)]
    rstd = p.sb("f_rstd", [128, T], F32)
    sgr = Rot([p.sb(f"f_sgr{i}", [128, T], BF16) for i in range(2)], "f_sgr")
    tmp = Rot([p.sb(f"f_tmp{i}", [128, T], F32) for i in range(2)], "f_tmp")
    yb = Rot([p.sb(f"f_yb{i}", [128, T], BF16) for i in range(2)], "f_yb")
    psC = Rot([p.ps(f"f_psC{i}", [128, 512], F32) for i in range(2)], "f_psC")

    for h in range(4):
        for dc in range(2):
            r0 = h * 256 + dc * 128
            p.dma(dall[:, :].rearrange("p (c o) -> p c o", o=1), DALL[r0:r0 + 128], writes=["dall"], dsem="f_dall",
                  allow_slow_non_contiguous=True)
            p.add("dve", lambda e: e.scalar_tensor_tensor(out=acoef[:, :], in0=dall[:, :], scalar=-1.0, in1=cm[:, :],
                                                          op0=ALU.add, op1=ALU.mult), reads=["dall", "cm"], writes=["acoef"])
            p.add("dve", lambda e: e.tensor_scalar(out=acoef[:, :], in0=acoef[:, :], scalar1=1.0, scalar2=None, op0=ALU.add),
                  reads=["acoef"], writes=["acoef"])
            p.add("pool", lambda e, dc=dc: e.memset(Sin[dc][:, :], 0.0), writes=[("Sin", dc)])
            for c in range(8):
                u, uk = ut.next()
                p.dma(u[:, :], UALL[c, r0:r0 + 128, :], writes=[uk], dsem=f"f_u{uk[1]}")
                p.add("dve", lambda e, u=u, c=c: e.tensor_scalar(out=u[:, :], in0=u[:, :], scalar1=cm[:, c:c + 1], scalar2=None, op0=ALU.mult),
                      reads=[uk, "cm"], writes=[uk])
                p.add("dve", lambda e, u=u, c=c, dc=dc: e.scalar_tensor_tensor(
                    out=Sin[dc][:, :], in0=Sin[dc][:, :], scalar=acoef[:, c:c + 1], in1=u[:, :], op0=ALU.mult, op1=ALU.add),
                    reads=[("Sin", dc), "acoef", uk], writes=[("Sin", dc)])
            p.add("act", lambda e, dc=dc: e.activation(out=Sb[dc][:, :], in_=Sin[dc][:, :], func=AF.Copy),
                  reads=[("Sin", dc)], writes=[("Sb", dc)])
            p.dma(qtl[dc][:, :], QTL[r0:r0 + 128, :], writes=[("qtl", dc)], dsem=f"f_q{dc}")
        for ec in range(4):
            r0 = h * 512 + ec * 128
            p.dma(o[ec][:, :], OL[r0:r0 + 128, :], writes=[("o", ec)], dsem=f"f_o{ec}")
            for th in range(2):
                ts = slice(th * 512, (th + 1) * 512)
                ps, pk = psC.next()
                for dc in range(2):
                    p.add("pe", lambda e, ps=ps, dc=dc, ec=ec, ts=ts: e.matmul(
                        ps[:, :], lhsT=Sb[dc][:, ec * 128:(ec + 1) * 128], rhs=qtl[dc][:, ts], start=(dc == 0), stop=(dc == 1)),
                        reads=[("Sb", dc), ("qtl", dc)], writes=[pk])
                p.add("dve", lambda e, ps=ps, ec=ec, ts=ts: e.tensor_tensor(out=o[ec][:, ts], in0=ps[:, :], in1=o[ec][:, ts], op=ALU.add),
                      reads=[pk, ("o", ec)], writes=[("o", ec)])
            p.add("act", lambda e, ec=ec: e.activation(out=osq[ec][:, :], in_=o[ec][:, :], func=AF.Square),
                  reads=[("o", ec)], writes=[("osq", ec)])
        for th in range(2):
            ts = slice(th * 512, (th + 1) * 512)
            ps, pk = psC.next()
            for ec in range(4):
                p.add("pe", lambda e, ps=ps, ec=ec, ts=ts: e.matmul(
                    ps[:, :], lhsT=onesb[:, :], rhs=osq[ec][:, ts], start=(ec == 0), stop=(ec == 3)),
                    reads=["f_ones", ("osq", ec)], writes=[pk])
            p.add("dve", lambda e, ps=ps, ts=ts: e.tensor_scalar(out=rstd[:, ts], in0=ps[:, :], scalar1=1.0 / 512.0, scalar2=1e-5,
                                                                 op0=ALU.mult, op1=ALU.add), reads=[pk], writes=["rstd"])
        p.add("act", lambda e: e.activation(out=rstd[:, :], in_=rstd[:, :], func=AF.Ln), reads=["rstd"], writes=["rstd"])
        p.add("act", lambda e: e.activation(out=rstd[:, :], in_=rstd[:, :], func=AF.Exp, scale=-0.5), reads=["rstd"], writes=["rstd"])
        for ec in range(4):
            r0 = h * 512 + ec * 128
            s, sk = sgr.next()
            p.dma(s[:, :], SGR[r0:r0 + 128, :], writes=[sk], dsem=f"f_sgr{sk[1]}")
            t, tk = tmp.next()
            y, yk = yb.next()
            p.add("dve", lambda e, t=t, ec=ec: e.scalar_tensor_tensor(out=t[:, :], in0=o[ec][:, :], scalar=ng[:, ec:ec + 1], in1=rstd[:, :],
                                                                     op0=ALU.mult, op1=ALU.mult), reads=[("o", ec), "ng", "rstd"], writes=[tk])
            p.add("dve", lambda e, t=t, y=y, s=s: e.tensor_tensor(out=y[:, :], in0=t[:, :], in1=s[:, :], op=ALU.mult),
                  reads=[tk, sk], writes=[yk])
            p.dma(YBT[r0:r0 + 128, :], y[:, :], reads=[yk], dsem=f"f_yb{yk[1]}", eng="act")


ALPHA = float((2 * 4) ** 0.25)
DFF = 5632
FC = DFF // 128


def emit_out_ln(p, pre, act, akeys, kcn, W, XT, LNG, LNB, OUT, SCR, psY):
    ws = WStream(p, kcn, wt=128, name=pre + "w")
    ones = p.sb(pre + "ones", [128, 128], F32)
    p.add("pool", lambda e: e.memset(ones[:, :], 1.0), writes=[pre + "ones"])
    lng = p.sb(pre + "lng", [128, 16], F32)
    lnb = p.sb(pre + "lnb", [128, 16], F32)
    p.dma(lng[:, :], LNG[:, :], writes=[pre + "lng"], dsem=pre + "lng")
    p.dma(lnb[:, :], LNB[:, :], writes=[pre + "lnb"], dsem=pre + "lnb")
    xt = Rot([p.sb(f"{pre}xt{i}", [128, 512], F32) for i in range(3)], pre + "xt")
    ut = Rot([p.sb(f"{pre}ut{i}", [128, 512], F32) for i in range(3)], pre + "ut")
    usq = Rot([p.sb(f"{pre}usq{i}", [128, 512], F32) for i in range(2)], pre + "usq")
    pst = [p.ps(f"{pre}pst{i}", [128, 512], F32) for i in range(4)]
    for cc in range(16):
        wb, wk = ws.load(W, cc * 128, 128)
        for th in range(2):
            ts = slice(th * 512, (th + 1) * 512)
            ps, pk = psY.next()
            for kc in range(kcn):
                p.add("pe", lambda e, ps=ps, wb=wb, kc=kc, ts=ts: e.matmul(
                    ps[:, :], lhsT=wb[:, kc, :], rhs=act[:, kc, ts], start=(kc == 0), stop=(kc == kcn - 1)),
                    reads=[wk, akeys[kc]], writes=[pk])
            x, xk = xt.next()
            p.dma(x[:, :], XT[cc * 128:(cc + 1) * 128, ts], writes=[xk], dsem=f"{pre}xt{xk[1]}")
            u, uk = ut.next()
            p.add("dve", lambda e, u=u, x=x, ps=ps: e.scalar_tensor_tensor(
                out=u[:, :], in0=x[:, :], scalar=ALPHA, in1=ps[:, :], op0=ALU.mult, op1=ALU.add),
                reads=[xk, pk], writes=[uk])
            sq, sqk = usq.next()
            p.add("act", lambda e, sq=sq, u=u: e.activation(out=sq[:, :], in_=u[:, :], func=AF.Square),
                  reads=[uk], writes=[sqk])
            p.add("pe", lambda e, u=u, th=th, cc=cc: e.matmul(pst[th][:, :], lhsT=ones[:, :], rhs=u[:, :],
                                                             start=(cc == 0), stop=(cc == 15)),
                  reads=[pre + "ones", uk], writes=[(pre + "pst", th)])
            p.add("pe", lambda e, sq=sq, th=th, cc=cc: e.matmul(pst[2 + th][:, :], lhsT=ones[:, :], rhs=sq[:, :],
                                                               start=(cc == 0), stop=(cc == 15)),
                  reads=[pre + "ones", sqk], writes=[(pre + "pst", 2 + th)])
            p.dma(SCR[cc * 128:(cc + 1) * 128, ts], u[:, :], reads=[uk], writes=[(pre + "scr", cc, th)],
                  dsem=f"{pre}ut{uk[1]}", eng="act")
    mean = p.sb(pre + "mean", [128, T], F32)
    rstd = p.sb(pre + "rstd", [128, T], F32)
    for th in range(2):
        ts = slice(th * 512, (th + 1) * 512)
        p.add("dve", lambda e, th=th, ts=ts: e.tensor_scalar(out=mean[:, ts], in0=pst[th][:, :], scalar1=1.0 / 2048.0,
                                                             scalar2=None, op0=ALU.mult),
              reads=[(pre + "pst", th)], writes=[pre + "mean"])
        p.add("dve", lambda e, ts=ts: e.tensor_tensor(out=rstd[:, ts], in0=mean[:, ts], in1=mean[:, ts], op=ALU.mult),
              reads=[pre + "mean"], writes=[pre + "rstd"])
        p.add("dve", lambda e, th=th, ts=ts: e.scalar_tensor_tensor(
            out=rstd[:, ts], in0=pst[2 + th][:, :], scalar=1.0 / 2048.0, in1=rstd[:, ts], op0=ALU.mult, op1=ALU.subtract),
            reads=[(pre + "pst", 2 + th), pre + "rstd"], writes=[pre + "rstd"])
    p.add("dve", lambda e: e.tensor_scalar(out=rstd[:, :], in0=rstd[:, :], scalar1=1e-5, scalar2=None, op0=ALU.add),
          reads=[pre + "rstd"], writes=[pre + "rstd"])
    p.add("act", lambda e: e.activation(out=rstd[:, :], in_=rstd[:, :], func=AF.Ln), reads=[pre + "rstd"], writes=[pre + "rstd"])
    p.add("act", lambda e: e.activation(out=rstd[:, :], in_=rstd[:, :], func=AF.Exp, scale=-0.5),
          reads=[pre + "rstd"], writes=[pre + "rstd"])
    for cc in range(16):
        for th in range(2):
            ts = slice(th * 512, (th + 1) * 512)
            u, uk = ut.next()
            p.dma(u[:, :], SCR[cc * 128:(cc + 1) * 128, ts], reads=[(pre + "scr", cc, th)], writes=[uk],
                  dsem=f"{pre}ut{uk[1]}")
            p.add("dve", lambda e, u=u, ts=ts: e.tensor_tensor(out=u[:, :], in0=u[:, :], in1=mean[:, ts], op=ALU.subtract),
                  reads=[uk, pre + "mean"], writes=[uk])
            p.add("dve", lambda e, u=u, ts=ts: e.tensor_tensor(out=u[:, :], in0=u[:, :], in1=rstd[:, ts], op=ALU.mult),
                  reads=[uk, pre + "rstd"], writes=[uk])
            p.add("dve", lambda e, u=u, cc=cc: e.tensor_scalar(out=u[:, :], in0=u[:, :], scalar1=lng[:, cc:cc + 1],
                                                               scalar2=lnb[:, cc:cc + 1], op0=ALU.mult, op1=ALU.add),
                  reads=[uk, pre + "lng", pre + "lnb"], writes=[uk])
            p.dma(OUT[cc * 128:(cc + 1) * 128, ts], u[:, :], reads=[uk], dsem=f"{pre}ut{uk[1]}", eng="act")


def build_stage3b():
    nc = bass.Bass("TRN2", target_bir_lowering=False)
    p = Prog(nc)
    YAT = p.dram("YAT", [1024, T], BF16, "ExternalInput")
    YBT = p.dram("YBT", [2048, T], BF16, "ExternalInput")
    SMA = p.dram("SMA", [2048, T], BF16, "ExternalInput")
    SMB = p.dram("SMB", [2048, T], BF16, "ExternalInput")
    XT = p.dram("xT", [2048, T], F32, "ExternalInput")
    WA = p.dram("w_proj_a", [1024, 2048], F32, "ExternalInput")
    WB = p.dram("w_proj_b", [2048, 2048], F32, "ExternalInput")
    WO = p.dram("w_out", [2048, 2048], F32, "ExternalInput")
    LNG = p.dram("LNG", [128, 16], F32, "ExternalInput")
    LNB = p.dram("LNB", [128, 16], F32, "ExternalInput")
    OUT = p.dram("X1T", [2048, T], F32, "ExternalOutput")
    SCR = p.dram("SCR", [2048, T], F32, "Internal")
    emit_stage3b(p, YAT, YBT, SMA, SMB, XT, WA, WB, WO, LNG, LNB, OUT, SCR)
    p.emit()
    return nc


def emit_stage3b(p, YAT, YBT, SMA, SMB, XT, WA, WB, WO, LNG, LNB, OUT, SCR):
    ya = p.sb("b_ya", [128, 8, T], BF16)
    yb = p.sb("b_yb", [128, 16, T], BF16)
    yT = p.sb("b_yT", [128, 16, T], BF16)
    for kc in range(8):
        p.dma(ya[:, kc, :], YAT[kc * 128:(kc + 1) * 128, :], writes=[("b_ya", kc)], dsem="b_ya")
    for kc in range(16):
        p.dma(yb[:, kc, :], YBT[kc * 128:(kc + 1) * 128, :], writes=[("b_yb", kc)], dsem="b_yb")
    wsa = WStream(p, 8, wt=128, name="b_wa")
    wsb = WStream(p, 16, wt=128, name="b_wb")
    psY = Rot([p.ps(f"b_ps{i}", [128, 512], F32) for i in range(4)], "b_ps")
    sm = Rot([p.sb(f"b_sm{i}", [128, T], BF16) for i in range(4)], "b_sm")
    t1 = Rot([p.sb(f"b_t1{i}", [128, 512], F32) for i in range(2)], "b_t1")
    t2 = Rot([p.sb(f"b_t2{i}", [128, 512], F32) for i in range(2)], "b_t2")
    for cc in range(16):
        wa, wak = wsa.load(WA, cc * 128, 128)
        wb, wbk = wsb.load(WB, cc * 128, 128)
        sa, sak = sm.next()
        sb_, sbk = sm.next()
        p.dma(sa[:, :], SMA[cc * 128:(cc + 1) * 128, :], writes=[sak], dsem=f"b_sm{sak[1]}")
        p.dma(sb_[:, :], SMB[cc * 128:(cc + 1) * 128, :], writes=[sbk], dsem=f"b_sm{sbk[1]}")
        for th in range(2):
            ts = slice(th * 512, (th + 1) * 512)
            pa, pak = psY.next()
            for kc in range(8):
                p.add("pe", lambda e, pa=pa, wa=wa, kc=kc, ts=ts: e.matmul(
                    pa[:, :], lhsT=wa[:, kc, :], rhs=ya[:, kc, ts], start=(kc == 0), stop=(kc == 7)),
                    reads=[wak, ("b_ya", kc)], writes=[pak])
            pb, pbk = psY.next()
            for kc in range(16):
                p.add("pe", lambda e, pb=pb, wb=wb, kc=kc, ts=ts: e.matmul(
                    pb[:, :], lhsT=wb[:, kc, :], rhs=yb[:, kc, ts], start=(kc == 0), stop=(kc == 15)),
                    reads=[wbk, ("b_yb", kc)], writes=[pbk])
            a, ak = t1.next()
            b, bk = t2.next()
            p.add("dve", lambda e, a=a, pa=pa, sa=sa, ts=ts: e.tensor_tensor(out=a[:, :], in0=pa[:, :], in1=sa[:, ts], op=ALU.mult),
                  reads=[pak, sak], writes=[ak])
            p.add("dve", lambda e, b=b, pb=pb, sb_=sb_, ts=ts: e.tensor_tensor(out=b[:, :], in0=pb[:, :], in1=sb_[:, ts], op=ALU.mult),
                  reads=[pbk, sbk], writes=[bk])
            p.add("dve", lambda e, a=a, b=b, cc=cc, ts=ts: e.tensor_tensor(out=yT[:, cc, ts], in0=a[:, :], in1=b[:, :], op=ALU.add),
                  reads=[ak, bk], writes=[("b_yT", cc)])
    emit_out_ln(p, "b_", yT, [("b_yT", kc) for kc in range(16)], 16, WO, XT, LNG, LNB, OUT, SCR, psY)


def build_stage4a():
    nc = bass.Bass("TRN2", target_bir_lowering=False)
    p = Prog(nc)
    X1T = p.dram("X1T", [2048, T], F32, "ExternalInput")
    X1H = p.dram("X1H", [2048, 2], F32, "ExternalInput")
    WG = p.dram("ffn_w_gate", [2048, DFF], F32, "ExternalInput")
    WU = p.dram("ffn_w_up", [2048, DFF], F32, "ExternalInput")
    CW = p.dram("CW", [128, FC, 3], F32, "ExternalInput")
    CB = p.dram("CB", [128, FC], F32, "ExternalInput")
    HT = p.dram("HT", [DFF, T], BF16, "ExternalOutput")
    emit_stage4a(p, X1T, X1H, WG, WU, CW, CB, HT)
    p.emit()
    return nc


def emit_stage4a(p, X1T, X1H, WG, WU, CW, CB, HT):
    xb = p.sb("c_xb", [128, 16, T + 2], BF16)
    xs = [p.sb(f"c_xs{i}", [128, T + 2], F32) for i in range(2)]
    for kc in range(16):
        s = xs[kc % 2]
        p.dma(s[:, 2:], X1T[kc * 128:(kc + 1) * 128, :], writes=[("c_xs", kc % 2, 0)], dsem=f"c_xs{kc % 2}")
        p.dma(s[:, 0:2], X1H[kc * 128:(kc + 1) * 128, :], writes=[("c_xs", kc % 2, 1)], dsem=f"c_xs{kc % 2}")
        p.add("dve" if kc % 2 == 0 else "pool", lambda e, s=s, kc=kc: e.tensor_copy(out=xb[:, kc, :], in_=s[:, :]),
              reads=[("c_xs", kc % 2, 0), ("c_xs", kc % 2, 1)], writes=[("c_xb", kc)])
    xkeys = [("c_xb", kc) for kc in range(16)]
    cw = p.sb("c_cw", [128, FC, 3], F32)
    cb = p.sb("c_cb", [128, FC], F32)
    p.dma(cw[:, :, :], CW[:, :, :], writes=["c_cw"], dsem="c_cw")
    p.dma(cb[:, :], CB[:, :], writes=["c_cb"], dsem="c_cb")
    ws = WStream(p, 16, wt=128, nbuf=3, name="c_w")
    psG = Rot([p.ps(f"c_psg{i}", [128, 512], F32) for i in range(3)], "c_psg")
    psH = Rot([p.ps(f"c_psh{i}", [128, 512], F32) for i in range(1)], "c_psh")
    psU = Rot([p.ps(f"c_psu{i}", [128, 512], F32) for i in range(4)], "c_psu")
    gx = Rot([p.sb(f"c_gx{i}", [128, T + 2], F32) for i in range(2)], "c_gx")
    acc = Rot([p.sb(f"c_acc{i}", [128, T], F32) for i in range(2)], "c_acc")
    hh = Rot([p.sb(f"c_h{i}", [128, T], BF16) for i in range(2)], "c_h")
    for fc in range(FC):
        wg, wgk = ws.load(WG, fc * 128, 128)
        wu, wuk = ws.load(WU, fc * 128, 128)
        g, gk = gx.next()
        ph, phk = psH.next()
        for kc in range(16):
            p.add("pe", lambda e, ph=ph, wg=wg, kc=kc: e.matmul(
                ph[:, 0:2], lhsT=wg[:, kc, :], rhs=xb[:, kc, 0:2], start=(kc == 0), stop=(kc == 15)),
                reads=[wgk, xkeys[kc]], writes=[phk])
        p.add("act", lambda e, g=g, ph=ph: e.activation(out=g[:, 0:2], in_=ph[:, 0:2], func=AF.Copy),
              reads=[phk], writes=[(gk, 2)])
        for th in range(2):
            pg, pgk = psG.next()
            for kc in range(16):
                p.add("pe", lambda e, pg=pg, wg=wg, kc=kc, th=th: e.matmul(
                    pg[:, :], lhsT=wg[:, kc, :], rhs=xb[:, kc, 2 + th * 512:2 + (th + 1) * 512], start=(kc == 0), stop=(kc == 15)),
                    reads=[wgk, xkeys[kc]], writes=[pgk])
            p.add("act", lambda e, g=g, pg=pg, th=th: e.activation(out=g[:, 2 + th * 512:2 + (th + 1) * 512], in_=pg[:, :], func=AF.Copy),
                  reads=[pgk], writes=[(gk, th)])
        gkeys = [(gk, 0), (gk, 1), (gk, 2)]
        a, ak = acc.next()
        p.add("dve", lambda e, a=a, g=g, fc=fc: e.tensor_scalar(out=a[:, :], in0=g[:, 0:T], scalar1=cw[:, fc, 0:1], scalar2=cb[:, fc:fc + 1],
                                                               op0=ALU.mult, op1=ALU.add), reads=gkeys + ["c_cw", "c_cb"], writes=[ak])
        p.add("dve", lambda e, a=a, g=g, fc=fc: e.scalar_tensor_tensor(out=a[:, :], in0=g[:, 1:T + 1], scalar=cw[:, fc, 1:2], in1=a[:, :],
                                                                      op0=ALU.mult, op1=ALU.add), reads=gkeys + ["c_cw", ak], writes=[ak])
        p.add("dve", lambda e, a=a, g=g, fc=fc: e.scalar_tensor_tensor(out=a[:, :], in0=g[:, 2:T + 2], scalar=cw[:, fc, 2:3], in1=a[:, :],
                                                                      op0=ALU.mult, op1=ALU.add), reads=gkeys + ["c_cw", ak], writes=[ak])
        p.add("act", lambda e, a=a: e.activation(out=a[:, :], in_=a[:, :], func=AF.Silu), reads=[ak], writes=[ak])
        h, hk = hh.next()
        for th in range(2):
            ts = slice(th * 512, (th + 1) * 512)
            pu, puk = psU.next()
            for kc in range(16):
                p.add("pe", lambda e, pu=pu, wu=wu, kc=kc, th=th: e.matmul(
                    pu[:, :], lhsT=wu[:, kc, :], rhs=xb[:, kc, 2 + th * 512:2 + (th + 1) * 512], start=(kc == 0), stop=(kc == 15)),
                    reads=[wuk, xkeys[kc]], writes=[puk])
            p.add("dve", lambda e, h=h, a=a, pu=pu, ts=ts: e.tensor_tensor(out=h[:, ts], in0=pu[:, :], in1=a[:, ts], op=ALU.mult),
                  reads=[puk, ak], writes=[(hk, th)])
        p.dma(HT[fc * 128:(fc + 1) * 128, :], h[:, :], reads=[(hk, 0), (hk, 1)], dsem=f"c_h{hk[1]}", eng="act")


def build_stage4b():
    nc = bass.Bass("TRN2", target_bir_lowering=False)
    p = Prog(nc)
    HT = p.dram("HT", [DFF, T], BF16, "ExternalInput")
    X1T = p.dram("X1T", [2048, T], F32, "ExternalInput")
    WD = p.dram("ffn_w_down", [DFF, 2048], F32, "ExternalInput")
    LNG = p.dram("LNG", [128, 16], F32, "ExternalInput")
    LNB = p.dram("LNB", [128, 16], F32, "ExternalInput")
    OUT = p.dram("X2T", [2048, T], F32, "ExternalOutput")
    SCR = p.dram("SCR", [2048, T], F32, "Internal")
    emit_stage4b(p, HT, X1T, WD, LNG, LNB, OUT, SCR)
    p.emit()
    return nc


def emit_stage4b(p, HT, X1T, WD, LNG, LNB, OUT, SCR):
    hT = p.sb("d_hT", [128, FC, T], BF16)
    for kc in range(FC):
        p.dma(hT[:, kc, :], HT[kc * 128:(kc + 1) * 128, :], writes=[("d_hT", kc)], dsem=f"d_hT{kc % 4}")
    psY = Rot([p.ps(f"d_ps{i}", [128, 512], F32) for i in range(4)], "d_ps")
    emit_out_ln(p, "d_", hT, [("d_hT", kc) for kc in range(FC)], FC, WD, X1T, LNG, LNB, OUT, SCR, psY)


def _D(nc):
    return lambda name, shape, dt, kind="Internal": nc.dram_tensor(name, list(shape), dt, kind=kind).ap()


def _phase(nc, emit_fn):
    with nc.cleanup_on_exit():
        p = Prog(nc)
        emit_fn(p)
        p.emit()
        nc.all_engine_barrier()


def build_g1():
    nc = bass.Bass("TRN2", target_bir_lowering=False)
    D = _D(nc)
    EI, EO = "ExternalInput", "ExternalOutput"
    xT = D("xT", [2048, T], F32, EI)
    w_in = D("w_in", [2048, INC], F32, EI)
    gate = D("gate", [17, 1024], F32, EI)
    GC = D("GC", [128, 384], F32, EI)
    QT = D("QT", [3 * 1024, T], BF16, EO)
    KT = D("KT", [3 * 1024, T], BF16, EO)
    V = D("V", [3, T, 1024], BF16, EO)
    SGR = D("SGR", [2048, T], BF16, EO)
    SMA = D("SMA", [2048, T], BF16, EO)
    SMB = D("SMB", [2048, T], BF16, EO)
    GQT = D("GQT", [1024, T], BF16)
    GKT = D("GKT", [1024, T], BF16)
    GKM = D("GKM", [T, 1024], BF16)
    GVM = D("GVM", [T, 2048], BF16)
    LAM = D("LAM", [T, 1024], F32)
    OL = D("OL", [2048, T], F32, EO)
    QTL = D("QTL", [1024, T], BF16, EO)
    U = D("U", [1024, 512], F32, EO)
    DT = D("DT", [1024, 1], F32, EO)
    _phase(nc, lambda p: emit_stage1(p, xT, w_in, gate, QT, KT, V, GQT, GKT, GKM, GVM, SGR, SMA, SMB, LAM))
    _phase(nc, lambda p: emit_stage2(p, GQT, GKT, GKM, GVM, LAM, GC, OL, QTL, U, DT))
    return nc


def build_g2():
    nc = bass.Bass("TRN2", target_bir_lowering=False)
    D = _D(nc)
    EI, EO = "ExternalInput", "ExternalOutput"
    d = {}
    d["QT"] = D("QT", [3 * 1024, T], BF16, EI)
    for g in range(3):
        d[f"KH{g}"] = D(f"KH{g}", [1024, HALO[g] + T], BF16, EI)
        d[f"VH{g}"] = D(f"VH{g}", [HALO[g] + T, 1024], BF16, EI)
    d["BT"] = D("BT", [24, 128, 256], F32, EI)
    d["VALID"] = D("VALID", [128, 256], F32, EI)
    d["HMASK"] = D("HMASK", [128, 3], F32, EI)
    d["OL"] = D("OL", [2048, T], F32, EI)
    d["QTL"] = D("QTL", [1024, T], BF16, EI)
    d["UALL"] = D("UALL", [8, 1024, 512], F32, EI)
    d["DALL"] = D("DALL", [1024, 8], F32, EI).rearrange("r (c o) -> r c o", o=1)
    d["CMASK"] = D("CMASK", [128, 8], F32, EI)
    d["SGR"] = D("SGR", [2048, T], BF16, EI)
    d["NG"] = D("NG", [128, 4], F32, EI)
    d["YAT"] = D("YAT", [1024, T], BF16)
    d["YBT"] = D("YBT", [2048, T], BF16)
    SMA = D("SMA", [2048, T], BF16, EI)
    SMB = D("SMB", [2048, T], BF16, EI)
    XT = D("xT", [2048, T], F32, EI)
    WA = D("w_proj_a", [1024, 2048], F32, EI)
    WB = D("w_proj_b", [2048, 2048], F32, EI)
    WO = D("w_out", [2048, 2048], F32, EI)
    LNG = D("LNG", [128, 16], F32, EI)
    LNB = D("LNB", [128, 16], F32, EI)
    OUT = D("X1T", [2048, T], F32, EO)
    SCR = D("SCR", [2048, T], F32)

    def ph1(p):
        emit_attn(p, d)
        emit_gla_fin(p, d)
    _phase(nc, ph1)
    _phase(nc, lambda p: emit_stage3b(p, d["YAT"], d["YBT"], SMA, SMB, XT, WA, WB, WO, LNG, LNB, OUT, SCR))
    return nc


def build_g3():
    nc = bass.Bass("TRN2", target_bir_lowering=False)
    D = _D(nc)
    EI, EO = "ExternalInput", "ExternalOutput"
    X1T = D("X1T", [2048, T], F32, EI)
    X1H = D("X1H", [2048, 2], F32, EI)
    WG = D("ffn_w_gate", [2048, DFF], F32, EI)
    WU = D("ffn_w_up", [2048, DFF], F32, EI)
    CW = D("CW", [128, FC, 3], F32, EI)
    CB = D("CB", [128, FC], F32, EI)
    WD = D("ffn_w_down", [DFF, 2048], F32, EI)
    LNG = D("LNG", [128, 16], F32, EI)
    LNB = D("LNB", [128, 16], F32, EI)
    HT = D("HT", [DFF, T], BF16)
    OUT = D("X2T", [2048, T], F32, EO)
    SCR = D("SCR", [2048, T], F32)
    _phase(nc, lambda p: emit_stage4a(p, X1T, X1H, WG, WU, CW, CB, HT))
    _phase(nc, lambda p: emit_stage4b(p, HT, X1T, WD, LNG, LNB, OUT, SCR))
    return nc


HALO = (128, 512, 2048); DIL = (1, 4, 16)

def t5_bucket(dist):
    dist = np.asarray(dist)
    df = np.maximum(dist, 1).astype(np.float32)
    large = 16 + (np.log(df / np.float32(16)) / np.float32(np.log(2048 / 16)) * np.float32(16)).astype(np.int32)
    return np.where(dist < 16, dist, np.minimum(large, 31))

def bias_tables(rel_bias):
    ki = np.arange(128)[:, None]; qi = np.arange(128)[None, :]
    off_prev = 128 + qi - ki
    off_cur = qi - ki
    valid = np.concatenate([(off_prev <= 128), (off_cur >= 0)], axis=1).astype(np.float32)
    BT = np.zeros((24, 128, 256), np.float32)
    for g in range(3):
        bp = t5_bucket(DIL[g] * np.clip(off_prev, 0, 128))
        bc = t5_bucket(DIL[g] * np.clip(off_cur, 0, 128))
        for h in range(8):
            BT[g * 8 + h, :, 0:128] = rel_bias[bp, g * 8 + h]
            BT[g * 8 + h, :, 128:256] = rel_bias[bc, g * 8 + h]
    return BT, valid

def hmask(c):
    m = np.zeros((128, 3), np.float32)
    if c > 0:
        m[:, 0] = 1; m[:, 1] = 1
    if c == 1:
        m[64:, 2] = 1
    elif c >= 2:
        m[:, 2] = 1
    return m

def cmask(c):
    m = np.zeros((128, 8), np.float32)
    m[:, :c] = 1
    return m


_PROGS = {}


def _prog(name, builder):
    if name not in _PROGS:
        _PROGS[name] = builder()
    return _PROGS[name]


def _run(name, builder, in_maps):
    nc = _prog(name, builder)
    res = run_bass_kernel_spmd(nc, in_maps, core_ids=list(range(NCORES)))
    return res.results


NCORES = 8


def _lnp(v):
    return np.ascontiguousarray(v.reshape(16, 128).T)


def kernel(x, w_in, gla_gate_w, gla_gate_b, gla_norm_g, w_proj_a, w_proj_b, w_out, rel_bias,
           ln1_g, ln1_b, ffn_w_gate, ffn_w_up, ffn_conv_w, ffn_conv_b, ffn_w_down, ln2_g, ln2_b):
    f32 = np.float32
    x = np.asarray(x, f32)[0]
    C = NCORES
    xT = [np.ascontiguousarray(x[c * T:(c + 1) * T].T) for c in range(C)]
    BT, valid = bias_tables(np.asarray(rel_bias, f32))
    gcon = gla_consts()
    hm = [hmask(c) for c in range(C)]
    cm = [cmask(c) for c in range(C)]
    for l in range(4):
        gate = np.concatenate([np.asarray(gla_gate_w[l], f32), np.asarray(gla_gate_b[l], f32)[None]], 0)
        wl = np.asarray(w_in[l], f32)
        r1 = _run("g1", build_g1, [{"xT": xT[c], "w_in": wl, "gate": gate, "GC": gcon} for c in range(C)])
        r2 = r1
        UALL = np.stack([r2[c]["U"] for c in range(C)], 0)
        DALL = np.ascontiguousarray(np.concatenate([r2[c]["DT"] for c in range(C)], 1))
        KH, VH = [], []
        for g in range(3):
            H = HALO[g]
            kt_all = np.concatenate([r1[c]["KT"][g * 1024:(g + 1) * 1024] for c in range(C)], 1)
            v_all = np.concatenate([r1[c]["V"][g] for c in range(C)], 0)
            kt_pad = np.concatenate([np.zeros((1024, H), kt_all.dtype), kt_all], 1)
            v_pad = np.concatenate([np.zeros((H, 1024), v_all.dtype), v_all], 0)
            KH.append([np.ascontiguousarray(kt_pad[:, c * T:c * T + H + T]) for c in range(C)])
            VH.append([np.ascontiguousarray(v_pad[c * T:c * T + H + T]) for c in range(C)])
        ng = np.ascontiguousarray(np.asarray(gla_norm_g[l], f32).reshape(4, 128).T)
        im = []
        for c in range(C):
            m = {"QT": r1[c]["QT"], "BT": BT, "VALID": valid, "HMASK": hm[c], "OL": r2[c]["OL"], "QTL": r2[c]["QTL"],
                 "UALL": UALL, "DALL": DALL, "CMASK": cm[c], "SGR": r1[c]["SGR"], "NG": ng}
            for g in range(3):
                m[f"KH{g}"] = KH[g][c]
                m[f"VH{g}"] = VH[g][c]
            im.append(m)
        for c in range(C):
            im[c].update({"SMA": r1[c]["SMA"], "SMB": r1[c]["SMB"], "xT": xT[c], "w_proj_a": np.asarray(w_proj_a[l], f32),
                          "w_proj_b": np.asarray(w_proj_b[l], f32), "w_out": np.asarray(w_out[l], f32),
                          "LNG": _lnp(np.asarray(ln1_g[l], f32)), "LNB": _lnp(np.asarray(ln1_b[l], f32))})
        r3b = _run("g2", build_g2, im)
        x1T = [r3b[c]["X1T"] for c in range(C)]
        x1h = [np.zeros((2048, 2), f32)] + [np.ascontiguousarray(x1T[c - 1][:, T - 2:T]) for c in range(1, C)]
        cw = np.ascontiguousarray(np.asarray(ffn_conv_w[l], f32).reshape(3, FC, 128).transpose(2, 1, 0))
        cb = np.ascontiguousarray(np.asarray(ffn_conv_b[l], f32).reshape(FC, 128).T)
        r5 = _run("g3", build_g3, [{"X1T": x1T[c], "X1H": x1h[c], "ffn_w_gate": np.asarray(ffn_w_gate[l], f32),
                                    "ffn_w_up": np.asarray(ffn_w_up[l], f32), "CW": cw, "CB": cb,
                                    "ffn_w_down": np.asarray(ffn_w_down[l], f32),
                                    "LNG": _lnp(np.asarray(ln2_g[l], f32)), "LNB": _lnp(np.asarray(ln2_b[l], f32))}
                                   for c in range(C)])
        xT = [r5[c]["X2T"] for c in range(C)]
    out = np.concatenate([np.ascontiguousarray(np.asarray(xT[c], f32).T) for c in range(C)], 0)
    return out[None].astype(f32)
```

```python
import contextlib
import numpy as np
import concourse.bass as bass
import concourse.mybir as mybir
from concourse.bass_utils import run_bass_kernel_spmd

F32 = mybir.dt.float32
BF16 = mybir.dt.bfloat16
AF = mybir.ActivationFunctionType
ALU = mybir.AluOpType

ENGS = ("pe", "act", "dve", "pool", "sp")
SAME_ENGINE_SYNC = True


class Prog:
    _uid = [0]

    def __init__(self, nc):
        self.nc = nc
        Prog._uid[0] += 1
        self.pfx = f"P{Prog._uid[0]}_"
        self.stack = contextlib.ExitStack()
        self.q = {e: [] for e in ENGS}
        self.cnt = {}
        self.seen = {e: {} for e in ENGS}
        self.res = {}
        self.semh = {}
        self.nsb = 0
        self.same_sync = SAME_ENGINE_SYNC
        self._defer = None

    GLOBAL_SEMS = {}

    def sem(self, key):
        if isinstance(key, tuple) and key[0] == "dma" and str(key[1]).startswith("GL_"):
            g = Prog.GLOBAL_SEMS
            if key[1] not in g:
                g[key[1]] = [self.nc.alloc_semaphore(name=key[1]), 0]
            if key not in self.semh:
                self.semh[key] = g[key[1]][0]
                self.cnt[key] = g[key[1]][1]
            return self.semh[key]
        if key not in self.semh:
            self.semh[key] = self.stack.enter_context(self.nc.semaphore(self.pfx + "s_" + str(key).replace(" ", "").replace("'", "").replace("(", "").replace(")", "").replace(",", "_")))
            self.cnt[key] = 0
        return self.semh[key]

    def sb(self, name, shape, dt):
        return self.stack.enter_context(self.nc.sbuf_tensor(self.pfx + name, list(shape), dt))

    def ps(self, name, shape, dt=F32):
        return self.stack.enter_context(self.nc.psum_tensor(self.pfx + name, list(shape), dt))

    def dram(self, name, shape, dt, kind="Internal"):
        return self.nc.dram_tensor(name, list(shape), dt, kind=kind).ap()

    def defer_start(self):
        self._defer = []

    def defer_stop(self):
        d, self._defer = self._defer, None
        return d

    def flush(self, lst):
        for a in lst or ():
            self.add(*a)

    def add(self, eng, fn, reads=(), writes=(), dsem=None, inc=16):
        if self._defer is not None:
            self._defer.append((eng, fn, list(reads), list(writes), dsem, inc))
            return None
        need = {}

        def want(tok):
            if tok is None:
                return
            k, v = tok
            if need.get(k, 0) < v:
                need[k] = v

        for k in reads:
            st = self.res.get(k)
            if st:
                want(st["w"])
        for k in writes:
            st = self.res.get(k)
            if st:
                want(st["w"])
                for t in st["r"].items():
                    want(t)
        waits = []
        for k, v in need.items():
            if k == eng and (eng == "pe" or not self.same_sync):
                continue
            if isinstance(k, tuple) and k[0] == "dma":
                v = self.cnt[k]
            if self.seen[eng].get(k, 0) >= v:
                continue
            self.seen[eng][k] = v
            waits.append((k, v))
        if dsem is None:
            sk = eng
            self.sem(sk)
            self.cnt[sk] += 1
            inc = 1
        else:
            sk = ("dma", dsem)
            self.sem(sk)
            self.cnt[sk] += inc
            if str(dsem).startswith("GL_"):
                Prog.GLOBAL_SEMS[dsem][1] = self.cnt[sk]
        tok = (sk, self.cnt[sk])
        self.q[eng].append((waits, fn, sk, inc))
        for k in reads:
            st = self.res.setdefault(k, {"w": None, "r": {}})
            if st["r"].get(sk, 0) < tok[1]:
                st["r"][sk] = tok[1]
        for k in writes:
            self.res[k] = {"w": tok, "r": {}}
        return tok

    def dma(self, out, in_, reads=(), writes=(), dsem=None, eng="sp", **kw):
        assert dsem is not None
        return self.add(eng, lambda e: e.dma_start(out=out, in_=in_, **kw), reads, writes, dsem=dsem)

    def final_wait(self, eng="sp"):
        self.finals = eng

    def emit(self):
        nc = self.nc
        semh = self.semh
        q = self.q
        cnt = self.cnt

        def replay(name, e, final=False):
            for waits, fn, sk, inc in q[name]:
                for k, v in waits:
                    e.wait_ge(semh[k], v)
                ins = fn(e)
                ins.then_inc(semh[sk], inc)
            if final:
                for k, h in semh.items():
                    if cnt[k] > 0:
                        e.wait_ge(h, cnt[k])

        with nc.Block() as block:
            @block.sync
            def _(e):
                replay("sp", e, final=True)

            @block.tensor
            def _(e):
                replay("pe", e)

            @block.scalar
            def _(e):
                replay("act", e)

            @block.vector
            def _(e):
                replay("dve", e)

            @block.gpsimd
            def _(e):
                replay("pool", e)
        self.stack.close()


T = 1024
D = 2048
KC = D // 128
INC = 19472
WT = 256


def build_stage1(nc=None):
    nc = nc or bass.Bass("TRN2", target_bir_lowering=False)
    p = Prog(nc)
    xT = p.dram("xT", [D, T], F32, "ExternalInput")
    w_in = p.dram("w_in", [D, INC], F32, "ExternalInput")
    gate = p.dram("gate", [17, 1024], F32, "ExternalInput")
    QT = p.dram("QT", [3 * 1024, T], BF16, "ExternalOutput")
    KT = p.dram("KT", [3 * 1024, T], BF16, "ExternalOutput")
    V = p.dram("V", [3, T, 1024], BF16, "ExternalOutput")
    GQT = p.dram("GQT", [1024, T], BF16, "ExternalOutput")
    GKT = p.dram("GKT", [1024, T], BF16, "ExternalOutput")
    GKM = p.dram("GKM", [T, 1024], BF16, "ExternalOutput")
    GVM = p.dram("GVM", [T, 2048], BF16, "ExternalOutput")
    SGR = p.dram("SGR", [2048, T], BF16, "ExternalOutput")
    SMA = p.dram("SMA", [2048, T], BF16, "ExternalOutput")
    SMB = p.dram("SMB", [2048, T], BF16, "ExternalOutput")
    LAM = p.dram("LAM", [T, 1024], F32, "ExternalOutput")
    emit_stage1(p, xT, w_in, gate, QT, KT, V, GQT, GKT, GKM, GVM, SGR, SMA, SMB, LAM)
    p.emit()
    return nc


class Rot:
    def __init__(self, tiles, key):
        self.tiles = tiles
        self.key = key
        self.i = 0

    def next(self):
        i = self.i % len(self.tiles)
        self.i += 1
        return self.tiles[i], (self.key, i)


def load_xT_bf16(p, xT, xb, nt=T):
    xs = [p.sb(f"xs{i}", [128, nt], F32) for i in range(2)]
    for kc in range(KC):
        s = xs[kc % 2]
        p.dma(s[:, :], xT[kc * 128:(kc + 1) * 128, :], writes=[("xs", kc % 2)], dsem=f"xs{kc % 2}")
        eng = "dve" if kc % 2 == 0 else "pool"
        p.add(eng, lambda e, s=s, kc=kc: e.tensor_copy(out=xb[:, kc, :], in_=s[:, :]),
              reads=[("xs", kc % 2)], writes=[("xb", kc)])


class WStream:
    def __init__(self, p, kc, wt=WT, nbuf=2, name="w"):
        self.p = p
        self.kc = kc
        self.wt = wt
        self.name = name
        self.ws = [p.sb(f"{name}s{i}", [128, kc, wt], F32) for i in range(nbuf)]
        self.wb = [p.sb(f"{name}b{i}", [128, kc, wt], BF16) for i in range(nbuf)]
        self.i = 0
        self.nbuf = nbuf

    def load(self, w, c0, nc_, cast="pool"):
        p = self.p
        i = self.i % self.nbuf
        self.i += 1
        s, b = self.ws[i], self.wb[i]
        src = w[:, c0:c0 + nc_].rearrange("(k p) c -> p k c", p=128)
        half = self.kc // 2
        nm = self.name
        p.dma(s[:, 0:half, 0:nc_], src[:, 0:half, :], writes=[(nm + "s", i, 0)], dsem=f"{nm}s{i}")
        p.dma(s[:, half:, 0:nc_], src[:, half:, :], writes=[(nm + "s", i, 1)], dsem=f"{nm}s{i}")
        if cast == "act":
            p.add("act", lambda e: e.activation(out=b[:, :, 0:nc_], in_=s[:, :, 0:nc_], func=AF.Copy),
                  reads=[(nm + "s", i, 0), (nm + "s", i, 1)], writes=[(nm + "b", i)])
        else:
            p.add(cast, lambda e: e.tensor_copy(out=b[:, :, 0:nc_], in_=s[:, :, 0:nc_]),
                  reads=[(nm + "s", i, 0), (nm + "s", i, 1)], writes=[(nm + "b", i)])
        return b, (nm + "b", i)


def emit_stage1(p, xT, w_in, gate, QT, KT, V, GQT, GKT, GKM, GVM, SGR, SMA, SMB, LAM):
    xb = p.sb("xb", [128, KC, T], BF16)
    load_xT_bf16(p, xT, xb)
    xkeys = [("xb", kc) for kc in range(KC)]
    ws = WStream(p, KC)
    psum = Rot([p.ps(f"ps{i}", [128, 512], F32) for i in range(8)], "ps")
    ob = Rot([p.sb(f"ob{i}", [128, 512], BF16) for i in range(4)], "ob")
    evac_i = [0]
    wtile = [0]

    def evac(dst_sb, src_ps, func, rk, wk):
        if func is None:
            evac_i[0] += 1
            if evac_i[0] % 2 == 0:
                p.add("dve", lambda e: e.tensor_copy(out=dst_sb, in_=src_ps), reads=rk, writes=wk)
                return
            func = AF.Copy
        p.add("act", lambda e: e.activation(out=dst_sb, in_=src_ps, func=func), reads=rk, writes=wk)

    def fm_job(c0, ncols, dst, func=None):
        for t0 in range(0, ncols, WT):
            n = min(WT, ncols - t0)
            wtile[0] += 1
            wb, wk = ws.load(w_in, c0 + t0, n)
            for m0 in range(0, n, 128):
                mc = min(128, n - m0)
                for th in range(T // 512):
                    ps, pk = psum.next()
                    for kc in range(KC):
                        p.add("pe", lambda e, ps=ps, wb=wb, kc=kc, m0=m0, mc=mc, th=th: e.matmul(
                            ps[0:mc, :], lhsT=wb[:, kc, m0:m0 + mc], rhs=xb[:, kc, th * 512:(th + 1) * 512],
                            start=(kc == 0), stop=(kc == KC - 1)),
                            reads=[wk, xkeys[kc]], writes=[pk])
                    o, ok = ob.next()
                    evac(o[0:mc, :], ps[0:mc, :], func, [pk], [ok])
                    p.dma(dst[t0 + m0:t0 + m0 + mc, th * 512:(th + 1) * 512], o[0:mc, :],
                          reads=[ok], dsem=f"ob{ok[1]}", eng="act")

    def tm_job(c0, ncols, dst):
        for t0 in range(0, ncols, WT):
            n = min(WT, ncols - t0)
            wtile[0] += 1
            wb, wk = ws.load(w_in, c0 + t0, n)
            for tp in range(T // 256):
                ps, pk = psum.next()
                for j in range(2):
                    tt = tp * 2 + j
                    for kc in range(KC):
                        p.add("pe", lambda e, ps=ps, wb=wb, kc=kc, tt=tt, j=j, n=n: e.matmul(
                            ps[:, j * 256:j * 256 + n], lhsT=xb[:, kc, tt * 128:(tt + 1) * 128], rhs=wb[:, kc, 0:n],
                            start=(kc == 0), stop=(kc == KC - 1)),
                            reads=[wk, xkeys[kc]], writes=[pk])
                o, ok = ob.next()
                evac(o[:, :], ps[:, :], None, [pk], [ok])
                for j in range(2):
                    tt = tp * 2 + j
                    p.dma(dst[tt * 128:(tt + 1) * 128, t0:t0 + n], o[:, j * 256:j * 256 + n],
                          reads=[ok], dsem=f"ob{ok[1]}", eng="act")

    for g in range(3):
        fm_job((3 * g) * 1024, 1024, QT[g * 1024:(g + 1) * 1024, :])
        fm_job((3 * g + 1) * 1024, 1024, KT[g * 1024:(g + 1) * 1024, :])
        tm_job((3 * g + 2) * 1024, 1024, V[g])
    fm_job(9216, 1024, GQT)
    fm_job(10240, 1024, GKT)
    tm_job(10240, 1024, GKM)
    tm_job(11264, 2048, GVM)
    fm_job(13312, 2048, SGR, AF.Silu)
    fm_job(15376, 2048, SMA, AF.Sigmoid)
    fm_job(17424, 2048, SMB, AF.Sigmoid)

    gl = p.sb("gl", [17, T], F32)
    gw = p.sb("gw", [17, 1024], F32)
    p.dma(gw[:, :], gate[:, :], writes=["gw"], dsem="gw")
    p.add("pool", lambda e: e.memset(gl[:, :], 1.0), writes=["gl"])
    wb, wk = ws.load(w_in, 15360, 16)
    for th in range(T // 512):
        ps, pk = psum.next()
        for kc in range(KC):
            p.add("pe", lambda e, ps=ps, wb=wb, kc=kc, th=th: e.matmul(
                ps[0:16, :], lhsT=wb[:, kc, 0:16], rhs=xb[:, kc, th * 512:(th + 1) * 512],
                start=(kc == 0), stop=(kc == KC - 1)), reads=[wk, xkeys[kc]], writes=[pk])
        p.add("dve", lambda e, ps=ps, th=th: e.tensor_copy(out=gl[0:16, th * 512:(th + 1) * 512], in_=ps[0:16, :]),
              reads=[pk], writes=["gl"])
    la = Rot([p.sb(f"la{i}", [128, 512], F32) for i in range(2)], "la")
    for tt in range(T // 128):
        for ch in range(2):
            ps, pk = psum.next()
            p.add("pe", lambda e, ps=ps, tt=tt, ch=ch: e.matmul(
                ps[:, :], lhsT=gl[:, tt * 128:(tt + 1) * 128], rhs=gw[:, ch * 512:(ch + 1) * 512],
                start=True, stop=True), reads=["gl", "gw"], writes=[pk])
            o, ok = la.next()
            p.add("act", lambda e, o=o, ps=ps: e.activation(out=o[:, :], in_=ps[:, :], func=AF.Exp, scale=-1.0),
                  reads=[pk], writes=[ok])
            p.add("act", lambda e, o=o: e.activation(out=o[:, :], in_=o[:, :], func=AF.Ln, bias=1.0),
                  reads=[ok], writes=[ok])
            p.dma(LAM[tt * 128:(tt + 1) * 128, ch * 512:(ch + 1) * 512], o[:, :], reads=[ok], dsem=f"la{ok[1]}", eng="act")


NT = T // 128


def gla_consts():
    j = np.arange(128)[:, None]
    t = np.arange(128)[None, :]
    same = (j // 64) == (t // 64)
    tri = np.where(same & (j <= t), -1.0 / 16.0, 0.0).astype(np.float32)
    trirev = np.where(same & (j > t), -1.0 / 16.0, 0.0).astype(np.float32)
    mask = np.where(same & (j <= t), 1.0, 0.0).astype(np.float32)
    return np.concatenate([tri, trirev, mask], axis=1)


def build_stage2():
    nc = bass.Bass("TRN2", target_bir_lowering=False)
    p = Prog(nc)
    GQT = p.dram("GQT", [1024, T], BF16, "ExternalInput")
    GKT = p.dram("GKT", [1024, T], BF16, "ExternalInput")
    GKM = p.dram("GKM", [T, 1024], BF16, "ExternalInput")
    GVM = p.dram("GVM", [T, 2048], BF16, "ExternalInput")
    LAM = p.dram("LAM", [T, 1024], F32, "ExternalInput")
    GC = p.dram("GC", [128, 384], F32, "ExternalInput")
    OL = p.dram("OL", [2048, T], F32, "ExternalOutput")
    QTL = p.dram("QTL", [1024, T], BF16, "ExternalOutput")
    U = p.dram("U", [1024, 512], F32, "ExternalOutput")
    DT = p.dram("DT", [1024, 1], F32, "ExternalOutput")
    emit_stage2(p, GQT, GKT, GKM, GVM, LAM, GC, OL, QTL, U, DT)
    p.emit()
    return nc


def emit_stage2(p, GQT, GKT, GKM, GVM, LAM, GC, OL, QTL, U, DT):
    gc = p.sb("gc", [128, 384], F32)
    p.dma(gc[:, :], GC[:, :], writes=["gc"], dsem="gc")
    tri, trirev, mask = gc[:, 0:128], gc[:, 128:256], gc[:, 256:384]
    qT = [p.sb(f"g_qT{i}", [128, T], BF16) for i in range(2)]
    kT = [p.sb(f"g_kT{i}", [128, T], BF16) for i in range(2)]
    ktm = p.sb("g_ktm", [128, NT, 256], BF16)
    vtm = p.sb("g_vtm", [128, NT, 512], BF16)
    lam = p.sb("g_lam", [128, NT, 256], F32)
    E = [p.sb(f"g_E{i}", [128, T], F32) for i in range(2)]
    qd = [p.sb(f"g_qd{i}", [128, T], BF16) for i in range(2)]
    ki = [p.sb(f"g_ki{i}", [128, T], BF16) for i in range(2)]
    ke = p.sb("g_ke", [128, NT, 256], BF16)
    qtl = [p.sb(f"g_qtl{i}", [128, T], BF16) for i in range(2)]
    S = [p.sb(f"g_S{i}", [128, 512], F32) for i in range(2)]
    Sb = [p.sb(f"g_Sb{i}", [128, 512], BF16) for i in range(2)]
    G = [p.sb(f"g_G{i}", [128, 1], F32) for i in range(2)]
    tmpf = Rot([p.sb(f"g_tmp{i}", [128, 256], F32) for i in range(3)], "g_tmp")
    attb = Rot([p.sb(f"g_att{i}", [128, 128], BF16) for i in range(2)], "g_att")
    osb = Rot([p.sb(f"g_o{i}", [128, 512], F32) for i in range(2)], "g_o")
    psA = Rot([p.ps(f"g_psA{i}", [128, 512], F32) for i in range(3)], "g_psA")
    psO = Rot([p.ps(f"g_psO{i}", [128, 512], F32) for i in range(2)], "g_psO")
    psS = Rot([p.ps(f"g_psS{i}", [128, 512], F32) for i in range(3)], "g_psS")

    for h in range(4):
        for dc in range(2):
            r0 = h * 256 + dc * 128
            p.dma(qT[dc][:, :], GQT[r0:r0 + 128, :], writes=[("qT", dc)], dsem=f"gq{dc}")
            p.dma(kT[dc][:, :], GKT[r0:r0 + 128, :], writes=[("kT", dc)], dsem=f"gk{dc}")
        p.dma(ktm[:, :, :], GKM[:, h * 256:(h + 1) * 256].rearrange("(n p) c -> p n c", p=128),
              writes=["ktm"], dsem="gktm")
        p.dma(vtm[:, :, :], GVM[:, h * 512:(h + 1) * 512].rearrange("(n p) c -> p n c", p=128),
              writes=["vtm"], dsem="gvtm")
        p.dma(lam[:, :, :], LAM[:, h * 256:(h + 1) * 256].rearrange("(n p) c -> p n c", p=128),
              writes=["lam"], dsem="glam")
        for dc in range(2):
            p.add("pool", lambda e, dc=dc: e.memset(S[dc][:, :], 0.0), writes=[("S", dc)])
            p.add("pool", lambda e, dc=dc: e.memset(Sb[dc][:, :], 0.0), writes=[("Sb", dc)])
            p.add("pool", lambda e, dc=dc: e.memset(G[dc][:, :], 1.0), writes=[("G", dc)])
        for tt in range(NT):
            cs = slice(tt * 128, (tt + 1) * 128)
            for dc in range(2):
                ps, pk = psA.next()
                p.add("pe", lambda e, ps=ps, tt=tt, dc=dc: e.matmul(
                    ps[:, 0:128], lhsT=lam[:, tt, dc * 128:(dc + 1) * 128], rhs=tri, start=True, stop=True),
                    reads=["lam", "gc"], writes=[pk])
                p.add("act", lambda e, ps=ps, dc=dc, cs=cs: e.activation(out=E[dc][:, cs], in_=ps[:, 0:128], func=AF.Exp),
                      reads=[pk], writes=[("E", dc, tt)])
                tm, tk = tmpf.next()
                p.add("act", lambda e, ps=ps, tm=tm: e.activation(out=tm[:, 0:128], in_=ps[:, 0:128], func=AF.Exp, scale=-1.0),
                      reads=[pk], writes=[tk])
                p.add("dve", lambda e, dc=dc, cs=cs: e.scalar_tensor_tensor(
                    out=qd[dc][:, cs], in0=qT[dc][:, cs], scalar=0.0625, in1=E[dc][:, cs], op0=ALU.mult, op1=ALU.mult),
                    reads=[("qT", dc), ("E", dc, tt)], writes=[("qd", dc, tt)])
                p.add("dve", lambda e, dc=dc, cs=cs, tm=tm: e.tensor_tensor(
                    out=ki[dc][:, cs], in0=kT[dc][:, cs], in1=tm[:, 0:128], op=ALU.mult),
                    reads=[("kT", dc), tk], writes=[("ki", dc, tt)])
            ps, pk = psA.next()
            p.add("pe", lambda e, ps=ps, tt=tt: e.matmul(
                ps[:, 0:256], lhsT=trirev, rhs=lam[:, tt, :], start=True, stop=True),
                reads=["lam", "gc"], writes=[pk])
            tm, tk = tmpf.next()
            p.add("act", lambda e, ps=ps, tm=tm: e.activation(out=tm[:, :], in_=ps[:, 0:256], func=AF.Exp),
                  reads=[pk], writes=[tk])
            p.add("dve", lambda e, tt=tt, tm=tm: e.tensor_tensor(
                out=ke[:, tt, :], in0=ktm[:, tt, :], in1=tm[:, :], op=ALU.mult),
                reads=["ktm", tk], writes=[("ke", tt)])
        for tt in range(NT):
            cs = slice(tt * 128, (tt + 1) * 128)
            ps, pk = psA.next()
            for dc in range(2):
                p.add("pe", lambda e, ps=ps, dc=dc, cs=cs: e.matmul(
                    ps[:, 0:128], lhsT=ki[dc][:, cs], rhs=qd[dc][:, cs], start=(dc == 0), stop=(dc == 1)),
                    reads=[("ki", dc, tt), ("qd", dc, tt)], writes=[pk])
            ab, ak = attb.next()
            p.add("dve", lambda e, ps=ps, ab=ab: e.tensor_tensor(out=ab[:, :], in0=ps[:, 0:128], in1=mask, op=ALU.mult),
                  reads=[pk, "gc"], writes=[ak])
            po, pok = psO.next()
            for par in range(2):
                c0 = tt * 128 + par * 64
                pr = slice(par * 64, par * 64 + 64)
                for dc in range(2):
                    p.add("dve", lambda e, dc=dc, c0=c0: e.tensor_scalar(
                        out=qtl[dc][:, c0:c0 + 64], in0=qd[dc][:, c0:c0 + 64], scalar1=G[dc][:, 0:1], scalar2=None,
                        op0=ALU.mult), reads=[("qd", dc, tt), ("G", dc)], writes=[("qtl", dc)])
                    p.add("dve", lambda e, dc=dc, c0=c0: e.tensor_tensor(
                        out=G[dc][:, :], in0=G[dc][:, :], in1=E[dc][:, c0 + 63:c0 + 64], op=ALU.mult),
                        reads=[("G", dc), ("E", dc, tt)], writes=[("G", dc)])
                for ec in range(4):
                    oc = slice(ec * 128 + par * 64, ec * 128 + par * 64 + 64)
                    es = slice(ec * 128, (ec + 1) * 128)
                    for dc in range(2):
                        p.add("pe", lambda e, po=po, oc=oc, es=es, dc=dc, c0=c0: e.matmul(
                            po[:, oc], lhsT=Sb[dc][:, es], rhs=qd[dc][:, c0:c0 + 64], start=(dc == 0), stop=False),
                            reads=[("Sb", dc), ("qd", dc, tt)], writes=[pok])
                    p.add("pe", lambda e, po=po, oc=oc, es=es, pr=pr, tt=tt, ab=ab: e.matmul(
                        po[:, oc], lhsT=vtm[pr, tt, es], rhs=ab[pr, pr], start=False, stop=True),
                        reads=["vtm", ak], writes=[pok])
                for dc in range(2):
                    pss, psk = psS.next()
                    p.add("pe", lambda e, pss=pss, pr=pr, tt=tt, dc=dc: e.matmul(
                        pss[:, :], lhsT=ke[pr, tt, dc * 128:(dc + 1) * 128], rhs=vtm[pr, tt, :], start=True, stop=True),
                        reads=[("ke", tt), "vtm"], writes=[psk])
                    p.add("dve", lambda e, pss=pss, dc=dc, c0=c0: e.scalar_tensor_tensor(
                        out=S[dc][:, :], in0=S[dc][:, :], scalar=E[dc][:, c0 + 63:c0 + 64], in1=pss[:, :],
                        op0=ALU.mult, op1=ALU.add), reads=[("S", dc), ("E", dc, tt), psk], writes=[("S", dc)])
                    p.add("act", lambda e, dc=dc: e.activation(out=Sb[dc][:, :], in_=S[dc][:, :], func=AF.Copy),
                          reads=[("S", dc)], writes=[("Sb", dc)])
            o, ok = osb.next()
            p.add("act", lambda e, o=o, po=po: e.activation(out=o[:, :], in_=po[:, :], func=AF.Copy),
                  reads=[pok], writes=[ok])
            p.dma(OL[h * 512:(h + 1) * 512, cs].rearrange("(c p) t -> p c t", p=128),
                  o[:, :].rearrange("p (c t) -> p c t", c=4), reads=[ok], dsem=f"go{ok[1]}", eng="act")
        for dc in range(2):
            r0 = h * 256 + dc * 128
            p.dma(QTL[r0:r0 + 128, :], qtl[dc][:, :], reads=[("qtl", dc)], dsem=f"gqtl{dc}")
            p.dma(U[r0:r0 + 128, :], S[dc][:, :], reads=[("S", dc)], dsem=f"gU{dc}")
            p.dma(DT[r0:r0 + 128, :], G[dc][:, :], reads=[("G", dc)], dsem=f"gD{dc}")


HALO = (128, 512, 2048)
DIL = (1, 4, 16)
SCALE = 128 ** -0.5
GROUPS = [0, 1, 2]


def build_stage3a(do_attn=True, do_gla=True):
    nc = bass.Bass("TRN2", target_bir_lowering=False)
    p = Prog(nc)
    d = {}
    d["QT"] = p.dram("QT", [3 * 1024, T], BF16, "ExternalInput")
    for g in range(3):
        d[f"KH{g}"] = p.dram(f"KH{g}", [1024, HALO[g] + T], BF16, "ExternalInput")
        d[f"VH{g}"] = p.dram(f"VH{g}", [HALO[g] + T, 1024], BF16, "ExternalInput")
    d["BT"] = p.dram("BT", [24, 128, 256], F32, "ExternalInput")
    d["VALID"] = p.dram("VALID", [128, 256], F32, "ExternalInput")
    d["HMASK"] = p.dram("HMASK", [128, 3], F32, "ExternalInput")
    d["OL"] = p.dram("OL", [2048, T], F32, "ExternalInput")
    d["QTL"] = p.dram("QTL", [1024, T], BF16, "ExternalInput")
    d["UALL"] = p.dram("UALL", [8, 1024, 512], F32, "ExternalInput")
    d["DALL"] = p.dram("DALL", [1024, 8], F32, "ExternalInput").rearrange("r (c o) -> r c o", o=1)
    d["CMASK"] = p.dram("CMASK", [128, 8], F32, "ExternalInput")
    d["SGR"] = p.dram("SGR", [2048, T], BF16, "ExternalInput")
    d["NG"] = p.dram("NG", [128, 4], F32, "ExternalInput")
    d["YAT"] = p.dram("YAT", [1024, T], BF16, "ExternalOutput")
    d["YBT"] = p.dram("YBT", [2048, T], BF16, "ExternalOutput")
    if do_attn: emit_attn(p, d)
    if do_gla: emit_gla_fin(p, d)
    p.emit()
    return nc


def emit_attn(p, d):
    QT, BT, VALID, HMASK, YAT = d["QT"], d["BT"], d["VALID"], d["HMASK"], d["YAT"]
    valid = p.sb("a_valid", [128, 256], F32)
    hmask = p.sb("a_hmask", [128, 3], F32)
    ones = p.sb("a_ones", [128, 128], BF16)
    p.dma(valid[:, :], VALID[:, :], writes=["valid"], dsem="a_c0")
    p.dma(hmask[:, :], HMASK[:, :], writes=["hmask"], dsem="a_c1")
    p.add("pool", lambda e: e.memset(ones[:, :], 1.0), writes=["ones"])
    qT = Rot([p.sb(f"a_q{i}", [128, T], BF16) for i in range(2)], "a_q")
    kT = Rot([p.sb(f"a_k{i}", [128, 2048 + T], BF16) for i in range(2)], "a_k")
    vt = Rot([p.sb(f"a_v{i}", [128, 32, 128], BF16) for i in range(2)], "a_v")
    bt = Rot([p.sb(f"a_bt{i}", [128, 256], F32) for i in range(2)], "a_bt")
    tfull = Rot([p.sb(f"a_tf{i}", [128, 256], F32) for i in range(2)], "a_tf")
    tfirst = Rot([p.sb(f"a_t1{i}", [128, 256], F32) for i in range(2)], "a_t1")
    pf = Rot([p.sb(f"a_pf{i}", [128, 256], F32) for i in range(3)], "a_pf")
    pb = Rot([p.sb(f"a_pb{i}", [128, 256], BF16) for i in range(3)], "a_pb")
    nd = p.sb("a_nd", [128, 2, T], F32)
    num = nd[:, 0, :]
    den = nd[:, 1, :]
    ya = Rot([p.sb(f"a_ya{i}", [128, T], BF16) for i in range(2)], "a_ya")
    psS = Rot([p.ps(f"a_psS{i}", [128, 512], F32) for i in range(3)], "a_psS")
    psO = Rot([p.ps(f"a_psO{i}", [128, 512], F32) for i in range(3)], "a_psO")

    pend = [None]
    for h in range(8):
        for g in GROUPS:
            H, dl = HALO[g], DIL[g]
            q, qk = qT.next()
            k, kk = kT.next()
            v, vk = vt.next()
            r0 = g * 1024 + h * 128
            p.dma(q[:, :], QT[r0:r0 + 128, :], writes=[qk], dsem=f"a_q{qk[1]}")
            p.dma(k[:, 0:H + T], d[f"KH{g}"][h * 128:(h + 1) * 128, :], writes=[kk], dsem=f"a_k{kk[1]}")
            VH = d[f"VH{g}"]
            if g < 2:
                ntile = (H + T) // (128 * dl)
                for r in range(dl):
                    src = VH[r:H + T:dl, h * 128:(h + 1) * 128] if dl > 1 else VH[:, h * 128:(h + 1) * 128]
                    p.dma(v[:, r * ntile:(r + 1) * ntile, :], src.rearrange("(j p) c -> p j c", p=128),
                          writes=[(vk, r)], dsem=f"a_v{vk[1]}")
            else:
                for r in range(16):
                    srcA = VH[r:H:16, h * 128:(h + 1) * 128]
                    p.dma(v[:, r, :], srcA, writes=[(vk, r)], dsem=f"a_v{vk[1]}")
                srcB = VH[H:H + T, h * 128:(h + 1) * 128].rearrange("(i r) c -> i r c", r=16)
                p.dma(v[0:64, 16:32, :], srcB, writes=[(vk, 16)], dsem=f"a_v{vk[1]}")
            b, bk = bt.next()
            tf, tfk = tfull.next()
            t1, t1k = tfirst.next()
            p.dma(b[:, :], BT[g * 8 + h], writes=[bk], dsem=f"a_bt{bk[1]}")
            p.add("act", lambda e, b=b: e.activation(out=b[:, :], in_=b[:, :], func=AF.Exp), reads=[bk], writes=[bk])
            p.add("dve", lambda e, b=b, tf=tf: e.tensor_tensor(out=tf[:, :], in0=b[:, :], in1=valid[:, :], op=ALU.mult),
                  reads=[bk, "valid"], writes=[tfk])
            p.add("dve", lambda e, tf=tf, t1=t1: e.tensor_copy(out=t1[:, 128:256], in_=tf[:, 128:256]),
                  reads=[tfk], writes=[t1k])
            p.add("dve", lambda e, tf=tf, t1=t1, g=g: e.tensor_scalar(
                out=t1[:, 0:128], in0=tf[:, 0:128], scalar1=hmask[:, g:g + 1], scalar2=None, op0=ALU.mult),
                reads=[tfk, "hmask", t1k], writes=[t1k])
            vkeys = [(vk, r) for r in range(18)]
            if g < 2:
                nq = T // (128 * dl)
                ntile = (H + T) // (128 * dl)
                for r in range(dl):
                    for m in range(nq):
                        qs = slice(r + dl * 128 * m, r + dl * 128 * m + dl * 127 + 1, dl)
                        kprev = slice(r + dl * 128 * m, r + dl * 128 * m + dl * 127 + 1, dl)
                        kcur = slice(r + dl * 128 * (m + 1), r + dl * 128 * (m + 1) + dl * 127 + 1, dl)
                        tab, tabk = (t1, t1k) if m == 0 else (tf, tfk)
                        ps, pk = psS.next()
                        p.add("pe", lambda e, ps=ps, k=k, q=q, kprev=kprev, qs=qs: e.matmul(
                            ps[:, 0:128], lhsT=k[:, kprev], rhs=q[:, qs], start=True, stop=True),
                            reads=[kk, qk], writes=[pk])
                        p.add("pe", lambda e, ps=ps, k=k, q=q, kcur=kcur, qs=qs: e.matmul(
                            ps[:, 128:256], lhsT=k[:, kcur], rhs=q[:, qs], start=True, stop=True),
                            reads=[kk, qk], writes=[pk])
                        f, fk = pf.next()
                        pbb, pbk = pb.next()
                        p.add("act", lambda e, f=f, ps=ps: e.activation(out=f[:, :], in_=ps[:, 0:256], func=AF.Exp, scale=SCALE),
                              reads=[pk], writes=[fk])
                        p.add("dve", lambda e, f=f, pbb=pbb, tab=tab: e.tensor_tensor(out=pbb[:, :], in0=f[:, :], in1=tab[:, :], op=ALU.mult),
                              reads=[fk, tabk], writes=[pbk])
                        p.defer_start()
                        po, pok = psO.next()
                        j0 = r * ntile + m
                        p.add("pe", lambda e, po=po, v=v, pbb=pbb, j0=j0: e.matmul(
                            po[:, 0:128], lhsT=v[:, j0, :], rhs=pbb[:, 0:128], start=True, stop=False),
                            reads=vkeys + [pbk], writes=[pok])
                        p.add("pe", lambda e, po=po, v=v, pbb=pbb, j0=j0: e.matmul(
                            po[:, 0:128], lhsT=v[:, j0 + 1, :], rhs=pbb[:, 128:256], start=False, stop=True),
                            reads=vkeys + [pbk], writes=[pok])
                        p.add("pe", lambda e, po=po, pbb=pbb: e.matmul(
                            po[:, 128:256], lhsT=ones[:, :], rhs=pbb[:, 0:128], start=True, stop=False),
                            reads=["ones", pbk], writes=[pok])
                        p.add("pe", lambda e, po=po, pbb=pbb: e.matmul(
                            po[:, 128:256], lhsT=ones[:, :], rhs=pbb[:, 128:256], start=False, stop=True),
                            reads=["ones", pbk], writes=[pok])
                        po2 = po[:, 0:256].rearrange("p (a b) -> p a b", a=2)
                        if g == GROUPS[0]:
                            p.add("dve", lambda e, po2=po2, qs=qs: e.tensor_copy(out=nd[:, :, qs], in_=po2),
                                  reads=[pok], writes=["num", "den"])
                        else:
                            p.add("dve", lambda e, po2=po2, qs=qs: e.tensor_tensor(out=nd[:, :, qs], in0=po2, in1=nd[:, :, qs], op=ALU.add),
                                  reads=[pok, "num", "den"], writes=["num", "den"])
                        blk = p.defer_stop()
                        p.flush(pend[0])
                        pend[0] = blk
            else:
                for r in range(16):
                    qs = slice(r, T, 16)
                    kprev = slice(r, H, 16)
                    kcur = slice(H + r, H + T, 16)
                    ps, pk = psS.next()
                    p.add("pe", lambda e, ps=ps, k=k, q=q, kprev=kprev, qs=qs: e.matmul(
                        ps[:, 0:64], lhsT=k[:, kprev], rhs=q[:, qs], start=True, stop=True),
                        reads=[kk, qk], writes=[pk])
                    p.add("pe", lambda e, ps=ps, k=k, q=q, kcur=kcur, qs=qs: e.matmul(
                        ps[0:64, 64:128], lhsT=k[:, kcur], rhs=q[:, qs], start=True, stop=True),
                        reads=[kk, qk], writes=[pk])
                    f, fk = pf.next()
                    pbb, pbk = pb.next()
                    p.add("act", lambda e, f=f, ps=ps: e.activation(out=f[:, 0:64], in_=ps[:, 0:64], func=AF.Exp, scale=SCALE),
                          reads=[pk], writes=[fk])
                    p.add("act", lambda e, f=f, ps=ps: e.activation(out=f[0:64, 64:128], in_=ps[0:64, 64:128], func=AF.Exp, scale=SCALE),
                          reads=[pk, fk], writes=[fk])
                    p.add("dve", lambda e, f=f, pbb=pbb, t1=t1: e.tensor_tensor(out=pbb[:, 0:64], in0=f[:, 0:64], in1=t1[:, 0:64], op=ALU.mult),
                          reads=[fk, t1k], writes=[pbk])
                    p.add("dve", lambda e, f=f, pbb=pbb, t1=t1: e.tensor_tensor(out=pbb[0:64, 64:128], in0=f[0:64, 64:128], in1=t1[0:64, 128:192], op=ALU.mult),
                          reads=[fk, t1k, pbk], writes=[pbk])
                    p.defer_start()
                    po, pok = psO.next()
                    p.add("pe", lambda e, po=po, v=v, pbb=pbb, r=r: e.matmul(
                        po[:, 0:64], lhsT=v[:, r, :], rhs=pbb[:, 0:64], start=True, stop=False),
                        reads=vkeys + [pbk], writes=[pok])
                    p.add("pe", lambda e, po=po, v=v, pbb=pbb, r=r: e.matmul(
                        po[:, 0:64], lhsT=v[0:64, 16 + r, :], rhs=pbb[0:64, 64:128], start=False, stop=True),
                        reads=vkeys + [pbk], writes=[pok])
                    p.add("pe", lambda e, po=po, pbb=pbb: e.matmul(
                        po[:, 128:192], lhsT=ones[:, :], rhs=pbb[:, 0:64], start=True, stop=False),
                        reads=["ones", pbk], writes=[pok])
                    p.add("pe", lambda e, po=po, pbb=pbb: e.matmul(
                        po[:, 128:192], lhsT=ones[0:64, :], rhs=pbb[0:64, 64:128], start=False, stop=True),
                        reads=["ones", pbk], writes=[pok])
                    po2 = po[:, 0:256].rearrange("p (a b) -> p a b", a=2)[:, :, 0:64]
                    p.add("dve", lambda e, po2=po2, qs=qs: e.tensor_tensor(out=nd[:, :, qs], in0=po2, in1=nd[:, :, qs], op=ALU.add),
                          reads=[pok, "num", "den"], writes=["num", "den"])
                    blk = p.defer_stop()
                    p.flush(pend[0])
                    pend[0] = blk
        p.flush(pend[0])
        pend[0] = None
        y, yk = ya.next()
        p.add("dve", lambda e: e.reciprocal(out=den[:, :], in_=den[:, :]), reads=["den"], writes=["den"])
        p.add("dve", lambda e, y=y: e.tensor_tensor(out=y[:, :], in0=num[:, :], in1=den[:, :], op=ALU.mult),
              reads=["num", "den"], writes=[yk])
        p.dma(YAT[h * 128:(h + 1) * 128, :], y[:, :], reads=[yk], dsem=f"a_ya{yk[1]}", eng="act")


def emit_gla_fin(p, d):
    OL, QTL, UALL, DALL, CMASK, SGR, NG, YBT = (d[k] for k in ("OL", "QTL", "UALL", "DALL", "CMASK", "SGR", "NG", "YBT"))
    cm = p.sb("f_cm", [128, 8], F32)
    ng = p.sb("f_ng", [128, 4], F32)
    onesb = p.sb("f_ones", [128, 128], BF16)
    p.dma(cm[:, :], CMASK[:, :], writes=["cm"], dsem="f_c0")
    p.dma(ng[:, :], NG[:, :], writes=["ng"], dsem="f_c1")
    p.add("pool", lambda e: e.memset(onesb[:, :], 1.0), writes=["f_ones"])
    dall = p.sb("f_dall", [128, 8], F32)
    acoef = p.sb("f_a", [128, 8], F32)
    Sin = [p.sb(f"f_S{i}", [128, 512], F32) for i in range(2)]
    Sb = [p.sb(f"f_Sb{i}", [128, 512], BF16) for i in range(2)]
    ut = Rot([p.sb(f"f_u{i}", [128, 512], F32) for i in range(3)], "f_u")
    qtl = [p.sb(f"f_q{i}", [128, T], BF16) for i in range(2)]
    o = [p.sb(f"f_o{i}", [128, T], F32) for i in range(4)]
    osq = [p.sb(f"f_osq{i}", [128, T], BF16) for i in range(4)]
    rstd = p.sb("f_rstd", [128, T], F32)
    sgr = Rot([p.sb(f"f_sgr{i}", [128, T], BF16) for i in range(2)], "f_sgr")
    tmp = Rot([p.sb(f"f_tmp{i}", [128, T], F32) for i in range(2)], "f_tmp")
    yb = Rot([p.sb(f"f_yb{i}", [128, T], BF16) for i in range(2)], "f_yb")
    psC = Rot([p.ps(f"f_psC{i}", [128, 512], F32) for i in range(2)], "f_psC")

    for h in range(4):
        for dc in range(2):
            r0 = h * 256 + dc * 128
            p.dma(dall[:, :].rearrange("p (c o) -> p c o", o=1), DALL[r0:r0 + 128], writes=["dall"], dsem="f_dall",
                  allow_slow_non_contiguous=True)
            p.add("dve", lambda e: e.scalar_tensor_tensor(out=acoef[:, :], in0=dall[:, :], scalar=-1.0, in1=cm[:, :],
                                                          op0=ALU.add, op1=ALU.mult), reads=["dall", "cm"], writes=["acoef"])
            p.add("dve", lambda e: e.tensor_scalar(out=acoef[:, :], in0=acoef[:, :], scalar1=1.0, scalar2=None, op0=ALU.add),
                  reads=["acoef"], writes=["acoef"])
            p.add("pool", lambda e, dc=dc: e.memset(Sin[dc][:, :], 0.0), writes=[("Sin", dc)])
            for c in range(8):
                u, uk = ut.next()
                p.dma(u[:, :], UALL[c, r0:r0 + 128, :], writes=[uk], dsem=f"f_u{uk[1]}")
                p.add("dve", lambda e, u=u, c=c: e.tensor_scalar(out=u[:, :], in0=u[:, :], scalar1=cm[:, c:c + 1], scalar2=None, op0=ALU.mult),
                      reads=[uk, "cm"], writes=[uk])
                p.add("dve", lambda e, u=u, c=c, dc=dc: e.scalar_tensor_tensor(
                    out=Sin[dc][:, :], in0=Sin[dc][:, :], scalar=acoef[:, c:c + 1], in1=u[:, :], op0=ALU.mult, op1=ALU.add),
                    reads=[("Sin", dc), "acoef", uk], writes=[("Sin", dc)])
            p.add("act", lambda e, dc=dc: e.activation(out=Sb[dc][:, :], in_=Sin[dc][:, :], func=AF.Copy),
                  reads=[("Sin", dc)], writes=[("Sb", dc)])
            p.dma(qtl[dc][:, :], QTL[r0:r0 + 128, :], writes=[("qtl", dc)], dsem=f"f_q{dc}")
        for ec in range(4):
            r0 = h * 512 + ec * 128
            p.dma(o[ec][:, :], OL[r0:r0 + 128, :], writes=[("o", ec)], dsem=f"f_o{ec}")
            for th in range(2):
                ts = slice(th * 512, (th + 1) * 512)
                ps, pk = psC.next()
                for dc in range(2):
                    p.add("pe", lambda e, ps=ps, dc=dc, ec=ec, ts=ts: e.matmul(
                        ps[:, :], lhsT=Sb[dc][:, ec * 128:(ec + 1) * 128], rhs=qtl[dc][:, ts], start=(dc == 0), stop=(dc == 1)),
                        reads=[("Sb", dc), ("qtl", dc)], writes=[pk])
                p.add("dve", lambda e, ps=ps, ec=ec, ts=ts: e.tensor_tensor(out=o[ec][:, ts], in0=ps[:, :], in1=o[ec][:, ts], op=ALU.add),
                      reads=[pk, ("o", ec)], writes=[("o", ec)])
            p.add("act", lambda e, ec=ec: e.activation(out=osq[ec][:, :], in_=o[ec][:, :], func=AF.Square),
                  reads=[("o", ec)], writes=[("osq", ec)])
        for th in range(2):
            ts = slice(th * 512, (th + 1) * 512)
            ps, pk = psC.next()
            for ec in range(4):
                p.add("pe", lambda e, ps=ps, ec=ec, ts=ts: e.matmul(
                    ps[:, :], lhsT=onesb[:, :], rhs=osq[ec][:, ts], start=(ec == 0), stop=(ec == 3)),
                    reads=["f_ones", ("osq", ec)], writes=[pk])
            p.add("dve", lambda e, ps=ps, ts=ts: e.tensor_scalar(out=rstd[:, ts], in0=ps[:, :], scalar1=1.0 / 512.0, scalar2=1e-5,
                                                                 op0=ALU.mult, op1=ALU.add), reads=[pk], writes=["rstd"])
        p.add("act", lambda e: e.activation(out=rstd[:, :], in_=rstd[:, :], func=AF.Ln), reads=["rstd"], writes=["rstd"])
        p.add("act", lambda e: e.activation(out=rstd[:, :], in_=rstd[:, :], func=AF.Exp, scale=-0.5), reads=["rstd"], writes=["rstd"])
        for ec in range(4):
            r0 = h * 512 + ec * 128
            s, sk = sgr.next()
            p.dma(s[:, :], SGR[r0:r0 + 128, :], writes=[sk], dsem=f"f_sgr{sk[1]}")
            t, tk = tmp.next()
            y, yk = yb.next()
            p.add("dve", lambda e, t=t, ec=ec: e.scalar_tensor_tensor(out=t[:, :], in0=o[ec][:, :], scalar=ng[:, ec:ec + 1], in1=rstd[:, :],
                                                                     op0=ALU.mult, op1=ALU.mult), reads=[("o", ec), "ng", "rstd"], writes=[tk])
            p.add("dve", lambda e, t=t, y=y, s=s: e.tensor_tensor(out=y[:, :], in0=t[:, :], in1=s[:, :], op=ALU.mult),
                  reads=[tk, sk], writes=[yk])
            p.dma(YBT[r0:r0 + 128, :], y[:, :], reads=[yk], dsem=f"f_yb{yk[1]}", eng="act")


ALPHA = float((2 * 4) ** 0.25)
DFF = 5632
FC = DFF // 128
CAST_UP = "act"
CAST_GATE = "pool"


def emit_out_ln(p, pre, act, akeys, kcn, W, XT, LNG, LNB, OUT, SCR, psY):
    ws = WStream(p, kcn, wt=128, name=pre + "w")
    ones = p.sb(pre + "ones", [128, 128], F32)
    p.add("pool", lambda e: e.memset(ones[:, :], 1.0), writes=[pre + "ones"])
    lng = p.sb(pre + "lng", [128, 16], F32)
    lnb = p.sb(pre + "lnb", [128, 16], F32)
    p.dma(lng[:, :], LNG[:, :], writes=[pre + "lng"], dsem=pre + "lng")
    p.dma(lnb[:, :], LNB[:, :], writes=[pre + "lnb"], dsem=pre + "lnb")
    xt = Rot([p.sb(f"{pre}xt{i}", [128, 512], F32) for i in range(3)], pre + "xt")
    ut = Rot([p.sb(f"{pre}ut{i}", [128, 512], F32) for i in range(3)], pre + "ut")
    usq = Rot([p.sb(f"{pre}usq{i}", [128, 512], F32) for i in range(2)], pre + "usq")
    pst = [p.ps(f"{pre}pst{i}", [128, 512], F32) for i in range(4)]
    for cc in range(16):
        wb, wk = ws.load(W, cc * 128, 128)
        for th in range(2):
            ts = slice(th * 512, (th + 1) * 512)
            ps, pk = psY.next()
            for kc in range(kcn):
                p.add("pe", lambda e, ps=ps, wb=wb, kc=kc, ts=ts: e.matmul(
                    ps[:, :], lhsT=wb[:, kc, :], rhs=act[:, kc, ts], start=(kc == 0), stop=(kc == kcn - 1)),
                    reads=[wk, akeys[kc]], writes=[pk])
            x, xk = xt.next()
            p.dma(x[:, :], XT[cc * 128:(cc + 1) * 128, ts], writes=[xk], dsem=f"{pre}xt{xk[1]}")
            u, uk = ut.next()
            p.add("dve", lambda e, u=u, x=x, ps=ps: e.scalar_tensor_tensor(
                out=u[:, :], in0=x[:, :], scalar=ALPHA, in1=ps[:, :], op0=ALU.mult, op1=ALU.add),
                reads=[xk, pk], writes=[uk])
            sq, sqk = usq.next()
            p.add("act", lambda e, sq=sq, u=u: e.activation(out=sq[:, :], in_=u[:, :], func=AF.Square),
                  reads=[uk], writes=[sqk])
            p.add("pe", lambda e, u=u, th=th, cc=cc: e.matmul(pst[th][:, :], lhsT=ones[:, :], rhs=u[:, :],
                                                             start=(cc == 0), stop=(cc == 15)),
                  reads=[pre + "ones", uk], writes=[(pre + "pst", th)])
            p.add("pe", lambda e, sq=sq, th=th, cc=cc: e.matmul(pst[2 + th][:, :], lhsT=ones[:, :], rhs=sq[:, :],
                                                               start=(cc == 0), stop=(cc == 15)),
                  reads=[pre + "ones", sqk], writes=[(pre + "pst", 2 + th)])
            p.dma(SCR[cc * 128:(cc + 1) * 128, ts], u[:, :], reads=[uk], writes=[(pre + "scr", cc, th)],
                  dsem=f"{pre}ut{uk[1]}", eng="act")
    mean = p.sb(pre + "mean", [128, T], F32)
    rstd = p.sb(pre + "rstd", [128, T], F32)
    for th in range(2):
        ts = slice(th * 512, (th + 1) * 512)
        p.add("dve", lambda e, th=th, ts=ts: e.tensor_scalar(out=mean[:, ts], in0=pst[th][:, :], scalar1=1.0 / 2048.0,
                                                             scalar2=None, op0=ALU.mult),
              reads=[(pre + "pst", th)], writes=[pre + "mean"])
        p.add("dve", lambda e, ts=ts: e.tensor_tensor(out=rstd[:, ts], in0=mean[:, ts], in1=mean[:, ts], op=ALU.mult),
              reads=[pre + "mean"], writes=[pre + "rstd"])
        p.add("dve", lambda e, th=th, ts=ts: e.scalar_tensor_tensor(
            out=rstd[:, ts], in0=pst[2 + th][:, :], scalar=1.0 / 2048.0, in1=rstd[:, ts], op0=ALU.mult, op1=ALU.subtract),
            reads=[(pre + "pst", 2 + th), pre + "rstd"], writes=[pre + "rstd"])
    p.add("dve", lambda e: e.tensor_scalar(out=rstd[:, :], in0=rstd[:, :], scalar1=1e-5, scalar2=None, op0=ALU.add),
          reads=[pre + "rstd"], writes=[pre + "rstd"])
    p.add("act", lambda e: e.activation(out=rstd[:, :], in_=rstd[:, :], func=AF.Ln), reads=[pre + "rstd"], writes=[pre + "rstd"])
    p.add("act", lambda e: e.activation(out=rstd[:, :], in_=rstd[:, :], func=AF.Exp, scale=-0.5),
          reads=[pre + "rstd"], writes=[pre + "rstd"])
    for cc in range(16):
        for th in range(2):
            ts = slice(th * 512, (th + 1) * 512)
            u, uk = ut.next()
            p.dma(u[:, :], SCR[cc * 128:(cc + 1) * 128, ts], reads=[(pre + "scr", cc, th)], writes=[uk],
                  dsem=f"{pre}ut{uk[1]}")
            p.add("dve", lambda e, u=u, ts=ts: e.tensor_tensor(out=u[:, :], in0=u[:, :], in1=mean[:, ts], op=ALU.subtract),
                  reads=[uk, pre + "mean"], writes=[uk])
            p.add("dve", lambda e, u=u, ts=ts: e.tensor_tensor(out=u[:, :], in0=u[:, :], in1=rstd[:, ts], op=ALU.mult),
                  reads=[uk, pre + "rstd"], writes=[uk])
            p.add("dve", lambda e, u=u, cc=cc: e.tensor_scalar(out=u[:, :], in0=u[:, :], scalar1=lng[:, cc:cc + 1],
                                                               scalar2=lnb[:, cc:cc + 1], op0=ALU.mult, op1=ALU.add),
                  reads=[uk, pre + "lng", pre + "lnb"], writes=[uk])
            p.dma(OUT[cc * 128:(cc + 1) * 128, ts], u[:, :], reads=[uk], dsem=f"{pre}ut{uk[1]}", eng="act")


def build_stage3b():
    nc = bass.Bass("TRN2", target_bir_lowering=False)
    p = Prog(nc)
    YAT = p.dram("YAT", [1024, T], BF16, "ExternalInput")
    YBT = p.dram("YBT", [2048, T], BF16, "ExternalInput")
    SMA = p.dram("SMA", [2048, T], BF16, "ExternalInput")
    SMB = p.dram("SMB", [2048, T], BF16, "ExternalInput")
    XT = p.dram("xT", [2048, T], F32, "ExternalInput")
    WA = p.dram("w_proj_a", [1024, 2048], F32, "ExternalInput")
    WB = p.dram("w_proj_b", [2048, 2048], F32, "ExternalInput")
    WO = p.dram("w_out", [2048, 2048], F32, "ExternalInput")
    LNG = p.dram("LNG", [128, 16], F32, "ExternalInput")
    LNB = p.dram("LNB", [128, 16], F32, "ExternalInput")
    OUT = p.dram("X1T", [2048, T], F32, "ExternalOutput")
    SCR = p.dram("SCR", [2048, T], F32, "Internal")
    emit_stage3b(p, YAT, YBT, SMA, SMB, XT, WA, WB, WO, LNG, LNB, OUT, SCR)
    p.emit()
    return nc


def emit_stage3b(p, YAT, YBT, SMA, SMB, XT, WA, WB, WO, LNG, LNB, OUT, SCR):
    ya = p.sb("b_ya", [128, 8, T], BF16)
    yb = p.sb("b_yb", [128, 16, T], BF16)
    yT = p.sb("b_yT", [128, 16, T], BF16)
    for kc in range(8):
        p.dma(ya[:, kc, :], YAT[kc * 128:(kc + 1) * 128, :], writes=[("b_ya", kc)], dsem="b_ya")
    for kc in range(16):
        p.dma(yb[:, kc, :], YBT[kc * 128:(kc + 1) * 128, :], writes=[("b_yb", kc)], dsem="b_yb")
    wsa = WStream(p, 8, wt=128, name="b_wa")
    wsb = WStream(p, 16, wt=128, name="b_wb")
    psY = Rot([p.ps(f"b_ps{i}", [128, 512], F32) for i in range(4)], "b_ps")
    sm = Rot([p.sb(f"b_sm{i}", [128, T], BF16) for i in range(4)], "b_sm")
    t1 = Rot([p.sb(f"b_t1{i}", [128, 512], F32) for i in range(2)], "b_t1")
    t2 = Rot([p.sb(f"b_t2{i}", [128, 512], F32) for i in range(2)], "b_t2")
    for cc in range(16):
        wa, wak = wsa.load(WA, cc * 128, 128)
        wb, wbk = wsb.load(WB, cc * 128, 128, cast="act")
        sa, sak = sm.next()
        sb_, sbk = sm.next()
        p.dma(sa[:, :], SMA[cc * 128:(cc + 1) * 128, :], writes=[sak], dsem=f"b_sm{sak[1]}")
        p.dma(sb_[:, :], SMB[cc * 128:(cc + 1) * 128, :], writes=[sbk], dsem=f"b_sm{sbk[1]}")
        for th in range(2):
            ts = slice(th * 512, (th + 1) * 512)
            pa, pak = psY.next()
            for kc in range(8):
                p.add("pe", lambda e, pa=pa, wa=wa, kc=kc, ts=ts: e.matmul(
                    pa[:, :], lhsT=wa[:, kc, :], rhs=ya[:, kc, ts], start=(kc == 0), stop=(kc == 7)),
                    reads=[wak, ("b_ya", kc)], writes=[pak])
            pb, pbk = psY.next()
            for kc in range(16):
                p.add("pe", lambda e, pb=pb, wb=wb, kc=kc, ts=ts: e.matmul(
                    pb[:, :], lhsT=wb[:, kc, :], rhs=yb[:, kc, ts], start=(kc == 0), stop=(kc == 15)),
                    reads=[wbk, ("b_yb", kc)], writes=[pbk])
            a, ak = t1.next()
            b, bk = t2.next()
            p.add("dve", lambda e, a=a, pa=pa, sa=sa, ts=ts: e.tensor_tensor(out=a[:, :], in0=pa[:, :], in1=sa[:, ts], op=ALU.mult),
                  reads=[pak, sak], writes=[ak])
            p.add("dve", lambda e, b=b, pb=pb, sb_=sb_, ts=ts: e.tensor_tensor(out=b[:, :], in0=pb[:, :], in1=sb_[:, ts], op=ALU.mult),
                  reads=[pbk, sbk], writes=[bk])
            p.add("dve", lambda e, a=a, b=b, cc=cc, ts=ts: e.tensor_tensor(out=yT[:, cc, ts], in0=a[:, :], in1=b[:, :], op=ALU.add),
                  reads=[ak, bk], writes=[("b_yT", cc)])
    emit_out_ln(p, "b_", yT, [("b_yT", kc) for kc in range(16)], 16, WO, XT, LNG, LNB, OUT, SCR, psY)


def build_stage4a():
    nc = bass.Bass("TRN2", target_bir_lowering=False)
    p = Prog(nc)
    X1T = p.dram("X1T", [2048, T], F32, "ExternalInput")
    X1H = p.dram("X1H", [2048, 2], F32, "ExternalInput")
    WG = p.dram("ffn_w_gate", [2048, DFF], F32, "ExternalInput")
    WU = p.dram("ffn_w_up", [2048, DFF], F32, "ExternalInput")
    CW = p.dram("CW", [128, FC, 3], F32, "ExternalInput")
    CB = p.dram("CB", [128, FC], F32, "ExternalInput")
    HT = p.dram("HT", [DFF, T], BF16, "ExternalOutput")
    emit_stage4a(p, X1T, X1H, WG, WU, CW, CB, HT)
    p.emit()
    return nc


def emit_stage4a(p, X1T, X1H, WG, WU, CW, CB, HT):
    xb = p.sb("c_xb", [128, 16, T + 2], BF16)
    xs = [p.sb(f"c_xs{i}", [128, T + 2], F32) for i in range(2)]
    for kc in range(16):
        s = xs[kc % 2]
        p.dma(s[:, 2:], X1T[kc * 128:(kc + 1) * 128, :], writes=[("c_xs", kc % 2, 0)], dsem=f"c_xs{kc % 2}")
        p.dma(s[:, 0:2], X1H[kc * 128:(kc + 1) * 128, :], writes=[("c_xs", kc % 2, 1)], dsem=f"c_xs{kc % 2}")
        p.add("dve" if kc % 2 == 0 else "pool", lambda e, s=s, kc=kc: e.tensor_copy(out=xb[:, kc, :], in_=s[:, :]),
              reads=[("c_xs", kc % 2, 0), ("c_xs", kc % 2, 1)], writes=[("c_xb", kc)])
    xkeys = [("c_xb", kc) for kc in range(16)]
    cw = p.sb("c_cw", [128, FC, 3], F32)
    cb = p.sb("c_cb", [128, FC], F32)
    p.dma(cw[:, :, :], CW[:, :, :], writes=["c_cw"], dsem="c_cw")
    p.dma(cb[:, :], CB[:, :], writes=["c_cb"], dsem="c_cb")
    ws = WStream(p, 16, wt=128, nbuf=3, name="c_w")
    psG = Rot([p.ps(f"c_psg{i}", [128, 512], F32) for i in range(3)], "c_psg")
    psH = Rot([p.ps(f"c_psh{i}", [128, 512], F32) for i in range(1)], "c_psh")
    psU = Rot([p.ps(f"c_psu{i}", [128, 512], F32) for i in range(4)], "c_psu")
    gx = Rot([p.sb(f"c_gx{i}", [128, T + 2], F32) for i in range(2)], "c_gx")
    acc = Rot([p.sb(f"c_acc{i}", [128, T], F32) for i in range(2)], "c_acc")
    hh = Rot([p.sb(f"c_h{i}", [128, T], BF16) for i in range(2)], "c_h")
    for fc in range(FC):
        wg, wgk = ws.load(WG, fc * 128, 128, cast=CAST_GATE)
        wu, wuk = ws.load(WU, fc * 128, 128, cast=CAST_UP)
        g, gk = gx.next()
        ph, phk = psH.next()
        for kc in range(16):
            p.add("pe", lambda e, ph=ph, wg=wg, kc=kc: e.matmul(
                ph[:, 0:2], lhsT=wg[:, kc, :], rhs=xb[:, kc, 0:2], start=(kc == 0), stop=(kc == 15)),
                reads=[wgk, xkeys[kc]], writes=[phk])
        p.add("act", lambda e, g=g, ph=ph: e.activation(out=g[:, 0:2], in_=ph[:, 0:2], func=AF.Copy),
              reads=[phk], writes=[(gk, 2)])
        for th in range(2):
            pg, pgk = psG.next()
            for kc in range(16):
                p.add("pe", lambda e, pg=pg, wg=wg, kc=kc, th=th: e.matmul(
                    pg[:, :], lhsT=wg[:, kc, :], rhs=xb[:, kc, 2 + th * 512:2 + (th + 1) * 512], start=(kc == 0), stop=(kc == 15)),
                    reads=[wgk, xkeys[kc]], writes=[pgk])
            p.add("act", lambda e, g=g, pg=pg, th=th: e.activation(out=g[:, 2 + th * 512:2 + (th + 1) * 512], in_=pg[:, :], func=AF.Copy),
                  reads=[pgk], writes=[(gk, th)])
        gkeys = [(gk, 0), (gk, 1), (gk, 2)]
        a, ak = acc.next()
        p.add("dve", lambda e, a=a, g=g, fc=fc: e.tensor_scalar(out=a[:, :], in0=g[:, 0:T], scalar1=cw[:, fc, 0:1], scalar2=cb[:, fc:fc + 1],
                                                               op0=ALU.mult, op1=ALU.add), reads=gkeys + ["c_cw", "c_cb"], writes=[ak])
        p.add("dve", lambda e, a=a, g=g, fc=fc: e.scalar_tensor_tensor(out=a[:, :], in0=g[:, 1:T + 1], scalar=cw[:, fc, 1:2], in1=a[:, :],
                                                                      op0=ALU.mult, op1=ALU.add), reads=gkeys + ["c_cw", ak], writes=[ak])
        p.add("dve", lambda e, a=a, g=g, fc=fc: e.scalar_tensor_tensor(out=a[:, :], in0=g[:, 2:T + 2], scalar=cw[:, fc, 2:3], in1=a[:, :],
                                                                      op0=ALU.mult, op1=ALU.add), reads=gkeys + ["c_cw", ak], writes=[ak])
        p.add("act", lambda e, a=a: e.activation(out=a[:, :], in_=a[:, :], func=AF.Silu), reads=[ak], writes=[ak])
        h, hk = hh.next()
        for th in range(2):
            ts = slice(th * 512, (th + 1) * 512)
            pu, puk = psU.next()
            for kc in range(16):
                p.add("pe", lambda e, pu=pu, wu=wu, kc=kc, th=th: e.matmul(
                    pu[:, :], lhsT=wu[:, kc, :], rhs=xb[:, kc, 2 + th * 512:2 + (th + 1) * 512], start=(kc == 0), stop=(kc == 15)),
                    reads=[wuk, xkeys[kc]], writes=[puk])
            p.add("dve", lambda e, h=h, a=a, pu=pu, ts=ts: e.tensor_tensor(out=h[:, ts], in0=pu[:, :], in1=a[:, ts], op=ALU.mult),
                  reads=[puk, ak], writes=[(hk, th)])
        p.dma(HT[fc * 128:(fc + 1) * 128, :], h[:, :], reads=[(hk, 0), (hk, 1)], dsem=f"c_h{hk[1]}", eng="act")


def build_stage4b():
    nc = bass.Bass("TRN2", target_bir_lowering=False)
    p = Prog(nc)
    HT = p.dram("HT", [DFF, T], BF16, "ExternalInput")
    X1T = p.dram("X1T", [2048, T], F32, "ExternalInput")
    WD = p.dram("ffn_w_down", [DFF, 2048], F32, "ExternalInput")
    LNG = p.dram("LNG", [128, 16], F32, "ExternalInput")
    LNB = p.dram("LNB", [128, 16], F32, "ExternalInput")
    OUT = p.dram("X2T", [2048, T], F32, "ExternalOutput")
    SCR = p.dram("SCR", [2048, T], F32, "Internal")
    emit_stage4b(p, HT, X1T, WD, LNG, LNB, OUT, SCR)
    p.emit()
    return nc


def emit_stage4b(p, HT, X1T, WD, LNG, LNB, OUT, SCR):
    hT = p.sb("d_hT", [128, FC, T], BF16)
    for kc in range(FC):
        p.dma(hT[:, kc, :], HT[kc * 128:(kc + 1) * 128, :], writes=[("d_hT", kc)], dsem=f"d_hT{kc % 4}")
    psY = Rot([p.ps(f"d_ps{i}", [128, 512], F32) for i in range(4)], "d_ps")
    emit_out_ln(p, "d_", hT, [("d_hT", kc) for kc in range(FC)], FC, WD, X1T, LNG, LNB, OUT, SCR, psY)


def _D(nc):
    return lambda name, shape, dt, kind="Internal": nc.dram_tensor(name, list(shape), dt, kind=kind).ap()


def _phase(nc, emit_fn):
    with nc.cleanup_on_exit():
        p = Prog(nc)
        emit_fn(p)
        p.emit()
        nc.all_engine_barrier()


def build_g1():
    nc = bass.Bass("TRN2", target_bir_lowering=False)
    D = _D(nc)
    EI, EO = "ExternalInput", "ExternalOutput"
    xT = D("xT", [2048, T], F32, EI)
    w_in = D("w_in", [2048, INC], F32, EI)
    gate = D("gate", [17, 1024], F32, EI)
    GC = D("GC", [128, 384], F32, EI)
    QT = D("QT", [3 * 1024, T], BF16, EO)
    KT = D("KT", [3 * 1024, T], BF16, EO)
    V = D("V", [3, T, 1024], BF16, EO)
    SGR = D("SGR", [2048, T], BF16, EO)
    SMA = D("SMA", [2048, T], BF16, EO)
    SMB = D("SMB", [2048, T], BF16, EO)
    GQT = D("GQT", [1024, T], BF16)
    GKT = D("GKT", [1024, T], BF16)
    GKM = D("GKM", [T, 1024], BF16)
    GVM = D("GVM", [T, 2048], BF16)
    LAM = D("LAM", [T, 1024], F32)
    OL = D("OL", [2048, T], F32, EO)
    QTL = D("QTL", [1024, T], BF16, EO)
    U = D("U", [1024, 512], F32, EO)
    DT = D("DT", [1024, 1], F32, EO)
    _phase(nc, lambda p: emit_stage1(p, xT, w_in, gate, QT, KT, V, GQT, GKT, GKM, GVM, SGR, SMA, SMB, LAM))
    _phase(nc, lambda p: emit_stage2(p, GQT, GKT, GKM, GVM, LAM, GC, OL, QTL, U, DT))
    return nc


def build_g2():
    nc = bass.Bass("TRN2", target_bir_lowering=False)
    D = _D(nc)
    EI, EO = "ExternalInput", "ExternalOutput"
    d = {}
    d["QT"] = D("QT", [3 * 1024, T], BF16, EI)
    for g in range(3):
        d[f"KH{g}"] = D(f"KH{g}", [1024, HALO[g] + T], BF16, EI)
        d[f"VH{g}"] = D(f"VH{g}", [HALO[g] + T, 1024], BF16, EI)
    d["BT"] = D("BT", [24, 128, 256], F32, EI)
    d["VALID"] = D("VALID", [128, 256], F32, EI)
    d["HMASK"] = D("HMASK", [128, 3], F32, EI)
    d["OL"] = D("OL", [2048, T], F32, EI)
    d["QTL"] = D("QTL", [1024, T], BF16, EI)
    d["UALL"] = D("UALL", [8, 1024, 512], F32, EI)
    d["DALL"] = D("DALL", [1024, 8], F32, EI).rearrange("r (c o) -> r c o", o=1)
    d["CMASK"] = D("CMASK", [128, 8], F32, EI)
    d["SGR"] = D("SGR", [2048, T], BF16, EI)
    d["NG"] = D("NG", [128, 4], F32, EI)
    d["YAT"] = D("YAT", [1024, T], BF16)
    d["YBT"] = D("YBT", [2048, T], BF16)
    SMA = D("SMA", [2048, T], BF16, EI)
    SMB = D("SMB", [2048, T], BF16, EI)
    XT = D("xT", [2048, T], F32, EI)
    WA = D("w_proj_a", [1024, 2048], F32, EI)
    WB = D("w_proj_b", [2048, 2048], F32, EI)
    WO = D("w_out", [2048, 2048], F32, EI)
    LNG = D("LNG", [128, 16], F32, EI)
    LNB = D("LNB", [128, 16], F32, EI)
    OUT = D("X1T", [2048, T], F32, EO)
    SCR = D("SCR", [2048, T], F32)

    def ph1(p):
        emit_attn(p, d)
        emit_gla_fin(p, d)
    _phase(nc, ph1)
    _phase(nc, lambda p: emit_stage3b(p, d["YAT"], d["YBT"], SMA, SMB, XT, WA, WB, WO, LNG, LNB, OUT, SCR))
    return nc


def build_g3():
    nc = bass.Bass("TRN2", target_bir_lowering=False)
    D = _D(nc)
    EI, EO = "ExternalInput", "ExternalOutput"
    X1T = D("X1T", [2048, T], F32, EI)
    X1H = D("X1H", [2048, 2], F32, EI)
    WG = D("ffn_w_gate", [2048, DFF], F32, EI)
    WU = D("ffn_w_up", [2048, DFF], F32, EI)
    CW = D("CW", [128, FC, 3], F32, EI)
    CB = D("CB", [128, FC], F32, EI)
    WD = D("ffn_w_down", [DFF, 2048], F32, EI)
    LNG = D("LNG", [128, 16], F32, EI)
    LNB = D("LNB", [128, 16], F32, EI)
    HT = D("HT", [DFF, T], BF16)
    OUT = D("X2T", [2048, T], F32, EO)
    SCR = D("SCR", [2048, T], F32)
    _phase(nc, lambda p: emit_stage4a(p, X1T, X1H, WG, WU, CW, CB, HT))
    _phase(nc, lambda p: emit_stage4b(p, HT, X1T, WD, LNG, LNB, OUT, SCR))
    return nc


HALO = (128, 512, 2048); DIL = (1, 4, 16)

def t5_bucket(dist):
    dist = np.asarray(dist)
    df = np.maximum(dist, 1).astype(np.float32)
    large = 16 + (np.log(df / np.float32(16)) / np.float32(np.log(2048 / 16)) * np.float32(16)).astype(np.int32)
    return np.where(dist < 16, dist, np.minimum(large, 31))

def bias_tables(rel_bias):
    ki = np.arange(128)[:, None]; qi = np.arange(128)[None, :]
    off_prev = 128 + qi - ki
    off_cur = qi - ki
    valid = np.concatenate([(off_prev <= 128), (off_cur >= 0)], axis=1).astype(np.float32)
    BT = np.zeros((24, 128, 256), np.float32)
    for g in range(3):
        bp = t5_bucket(DIL[g] * np.clip(off_prev, 0, 128))
        bc = t5_bucket(DIL[g] * np.clip(off_cur, 0, 128))
        for h in range(8):
            BT[g * 8 + h, :, 0:128] = rel_bias[bp, g * 8 + h]
            BT[g * 8 + h, :, 128:256] = rel_bias[bc, g * 8 + h]
    return BT, valid

def hmask(c):
    m = np.zeros((128, 3), np.float32)
    if c > 0:
        m[:, 0] = 1; m[:, 1] = 1
    if c == 1:
        m[64:, 2] = 1
    elif c >= 2:
        m[:, 2] = 1
    return m

def cmask(c):
    m = np.zeros((128, 8), np.float32)
    m[:, :c] = 1
    return m


_PROGS = {}


def _prog(name, builder):
    if name not in _PROGS:
        _PROGS[name] = builder()
    return _PROGS[name]


def _run(name, builder, in_maps):
    nc = _prog(name, builder)
    res = run_bass_kernel_spmd(nc, in_maps, core_ids=list(range(NCORES)))
    return res.results


NCORES = 8


def _lnp(v):
    return np.ascontiguousarray(v.reshape(16, 128).T)


def kernel(x, w_in, gla_gate_w, gla_gate_b, gla_norm_g, w_proj_a, w_proj_b, w_out, rel_bias,
           ln1_g, ln1_b, ffn_w_gate, ffn_w_up, ffn_conv_w, ffn_conv_b, ffn_w_down, ln2_g, ln2_b):
    f32 = np.float32
    x = np.asarray(x, f32)[0]
    C = NCORES
    xT = [np.ascontiguousarray(x[c * T:(c + 1) * T].T) for c in range(C)]
    BT, valid = bias_tables(np.asarray(rel_bias, f32))
    gcon = gla_consts()
    hm = [hmask(c) for c in range(C)]
    cm = [cmask(c) for c in range(C)]
    for l in range(4):
        gate = np.concatenate([np.asarray(gla_gate_w[l], f32), np.asarray(gla_gate_b[l], f32)[None]], 0)
        wl = np.asarray(w_in[l], f32)
        r1 = _run("g1", build_g1, [{"xT": xT[c], "w_in": wl, "gate": gate, "GC": gcon} for c in range(C)])
        r2 = r1
        UALL = np.stack([r2[c]["U"] for c in range(C)], 0)
        DALL = np.ascontiguousarray(np.concatenate([r2[c]["DT"] for c in range(C)], 1))
        KH, VH = [], []
        for g in range(3):
            H = HALO[g]
            kt_all = np.concatenate([r1[c]["KT"][g * 1024:(g + 1) * 1024] for c in range(C)], 1)
            v_all = np.concatenate([r1[c]["V"][g] for c in range(C)], 0)
            kt_pad = np.concatenate([np.zeros((1024, H), kt_all.dtype), kt_all], 1)
            v_pad = np.concatenate([np.zeros((H, 1024), v_all.dtype), v_all], 0)
            KH.append([np.ascontiguousarray(kt_pad[:, c * T:c * T + H + T]) for c in range(C)])
            VH.append([np.ascontiguousarray(v_pad[c * T:c * T + H + T]) for c in range(C)])
        ng = np.ascontiguousarray(np.asarray(gla_norm_g[l], f32).reshape(4, 128).T)
        im = []
        for c in range(C):
            m = {"QT": r1[c]["QT"], "BT": BT, "VALID": valid, "HMASK": hm[c], "OL": r2[c]["OL"], "QTL": r2[c]["QTL"],
                 "UALL": UALL, "DALL": DALL, "CMASK": cm[c], "SGR": r1[c]["SGR"], "NG": ng}
            for g in range(3):
                m[f"KH{g}"] = KH[g][c]
                m[f"VH{g}"] = VH[g][c]
            im.append(m)
        for c in range(C):
            im[c].update({"SMA": r1[c]["SMA"], "SMB": r1[c]["SMB"], "xT": xT[c], "w_proj_a": np.asarray(w_proj_a[l], f32),
                          "w_proj_b": np.asarray(w_proj_b[l], f32), "w_out": np.asarray(w_out[l], f32),
                          "LNG": _lnp(np.asarray(ln1_g[l], f32)), "LNB": _lnp(np.asarray(ln1_b[l], f32))})
        r3b = _run("g2", build_g2, im)
        x1T = [r3b[c]["X1T"] for c in range(C)]
        x1h = [np.zeros((2048, 2), f32)] + [np.ascontiguousarray(x1T[c - 1][:, T - 2:T]) for c in range(1, C)]
        cw = np.ascontiguousarray(np.asarray(ffn_conv_w[l], f32).reshape(3, FC, 128).transpose(2, 1, 0))
        cb = np.ascontiguousarray(np.asarray(ffn_conv_b[l], f32).reshape(FC, 128).T)
        r5 = _run("g3", build_g3, [{"X1T": x1T[c], "X1H": x1h[c], "ffn_w_gate": np.asarray(ffn_w_gate[l], f32),
                                    "ffn_w_up": np.asarray(ffn_w_up[l], f32), "CW": cw, "CB": cb,
                                    "ffn_w_down": np.asarray(ffn_w_down[l], f32),
                                    "LNG": _lnp(np.asarray(ln2_g[l], f32)), "LNB": _lnp(np.asarray(ln2_b[l], f32))}
                                   for c in range(C)])
        xT = [r5[c]["X2T"] for c in range(C)]
    out = np.concatenate([np.ascontiguousarray(np.asarray(xT[c], f32).T) for c in range(C)], 0)
    return out[None].astype(f32)
```

```python
import contextlib
import numpy as np
import concourse.bass as bass
import concourse.mybir as mybir
from concourse.bass_utils import run_bass_kernel_spmd

F32 = mybir.dt.float32
BF16 = mybir.dt.bfloat16
AF = mybir.ActivationFunctionType
ALU = mybir.AluOpType

ENGS = ("pe", "act", "dve", "pool", "sp")
SAME_ENGINE_SYNC = True


class Prog:
    _uid = [0]

    def __init__(self, nc):
        self.nc = nc
        Prog._uid[0] += 1
        self.pfx = f"P{Prog._uid[0]}_"
        self.stack = contextlib.ExitStack()
        self.q = {e: [] for e in ENGS}
        self.cnt = {}
        self.seen = {e: {} for e in ENGS}
        self.res = {}
        self.semh = {}
        self.nsb = 0
        self.same_sync = SAME_ENGINE_SYNC
        self._defer = None

    GLOBAL_SEMS = {}

    def sem(self, key):
        if isinstance(key, tuple) and key[0] == "dma" and str(key[1]).startswith("GL_"):
            g = Prog.GLOBAL_SEMS
            if key[1] not in g:
                g[key[1]] = [self.nc.alloc_semaphore(name=key[1]), 0]
            if key not in self.semh:
                self.semh[key] = g[key[1]][0]
                self.cnt[key] = g[key[1]][1]
            return self.semh[key]
        if key not in self.semh:
            self.semh[key] = self.stack.enter_context(self.nc.semaphore(self.pfx + "s_" + str(key).replace(" ", "").replace("'", "").replace("(", "").replace(")", "").replace(",", "_")))
            self.cnt[key] = 0
        return self.semh[key]

    def sb(self, name, shape, dt):
        return self.stack.enter_context(self.nc.sbuf_tensor(self.pfx + name, list(shape), dt))

    def ps(self, name, shape, dt=F32):
        return self.stack.enter_context(self.nc.psum_tensor(self.pfx + name, list(shape), dt))

    def dram(self, name, shape, dt, kind="Internal"):
        return self.nc.dram_tensor(name, list(shape), dt, kind=kind).ap()

    def defer_start(self):
        self._defer = []

    def defer_stop(self):
        d, self._defer = self._defer, None
        return d

    def flush(self, lst):
        for a in lst or ():
            self.add(*a)

    def add(self, eng, fn, reads=(), writes=(), dsem=None, inc=16):
        if self._defer is not None:
            self._defer.append((eng, fn, list(reads), list(writes), dsem, inc))
            return None
        need = {}

        def want(tok):
            if tok is None:
                return
            k, v = tok
            if need.get(k, 0) < v:
                need[k] = v

        for k in reads:
            st = self.res.get(k)
            if st:
                want(st["w"])
        for k in writes:
            st = self.res.get(k)
            if st:
                want(st["w"])
                for t in st["r"].items():
                    want(t)
        waits = []
        for k, v in need.items():
            if k == eng and (eng == "pe" or not self.same_sync):
                continue
            if isinstance(k, tuple) and k[0] == "dma":
                v = self.cnt[k]
            if self.seen[eng].get(k, 0) >= v:
                continue
            self.seen[eng][k] = v
            waits.append((k, v))
        if dsem is None:
            sk = eng
            self.sem(sk)
            self.cnt[sk] += 1
            inc = 1
        else:
            sk = ("dma", dsem)
            self.sem(sk)
            self.cnt[sk] += inc
            if str(dsem).startswith("GL_"):
                Prog.GLOBAL_SEMS[dsem][1] = self.cnt[sk]
        tok = (sk, self.cnt[sk])
        self.q[eng].append((waits, fn, sk, inc))
        for k in reads:
            st = self.res.setdefault(k, {"w": None, "r": {}})
            if st["r"].get(sk, 0) < tok[1]:
                st["r"][sk] = tok[1]
        for k in writes:
            self.res[k] = {"w": tok, "r": {}}
        return tok

    def dma(self, out, in_, reads=(), writes=(), dsem=None, eng="sp", **kw):
        assert dsem is not None
        return self.add(eng, lambda e: e.dma_start(out=out, in_=in_, **kw), reads, writes, dsem=dsem)

    def final_wait(self, eng="sp"):
        self.finals = eng

    def emit(self):
        nc = self.nc
        semh = self.semh
        q = self.q
        cnt = self.cnt

        def replay(name, e, final=False):
            for waits, fn, sk, inc in q[name]:
                for k, v in waits:
                    e.wait_ge(semh[k], v)
                ins = fn(e)
                ins.then_inc(semh[sk], inc)
            if final:
                for k, h in semh.items():
                    if cnt[k] > 0:
                        e.wait_ge(h, cnt[k])

        with nc.Block() as block:
            @block.sync
            def _(e):
                replay("sp", e, final=True)

            @block.tensor
            def _(e):
                replay("pe", e)

            @block.scalar
            def _(e):
                replay("act", e)

            @block.vector
            def _(e):
                replay("dve", e)

            @block.gpsimd
            def _(e):
                replay("pool", e)
        self.stack.close()


T = 1024
D = 2048
KC = D // 128
INC = 19472
WT = 256


def build_stage1(nc=None):
    nc = nc or bass.Bass("TRN2", target_bir_lowering=False)
    p = Prog(nc)
    xT = p.dram("xT", [D, T], F32, "ExternalInput")
    w_in = p.dram("w_in", [D, INC], F32, "ExternalInput")
    gate = p.dram("gate", [17, 1024], F32, "ExternalInput")
    QT = p.dram("QT", [3 * 1024, T], BF16, "ExternalOutput")
    KT = p.dram("KT", [3 * 1024, T], BF16, "ExternalOutput")
    V = p.dram("V", [3, T, 1024], BF16, "ExternalOutput")
    GQT = p.dram("GQT", [1024, T], BF16, "ExternalOutput")
    GKT = p.dram("GKT", [1024, T], BF16, "ExternalOutput")
    GKM = p.dram("GKM", [T, 1024], BF16, "ExternalOutput")
    GVM = p.dram("GVM", [T, 2048], BF16, "ExternalOutput")
    SGR = p.dram("SGR", [2048, T], BF16, "ExternalOutput")
    SMA = p.dram("SMA", [2048, T], BF16, "ExternalOutput")
    SMB = p.dram("SMB", [2048, T], BF16, "ExternalOutput")
    LAM = p.dram("LAM", [T, 1024], F32, "ExternalOutput")
    emit_stage1(p, xT, w_in, gate, QT, KT, V, GQT, GKT, GKM, GVM, SGR, SMA, SMB, LAM)
    p.emit()
    return nc


class Rot:
    def __init__(self, tiles, key):
        self.tiles = tiles
        self.key = key
        self.i = 0

    def next(self):
        i = self.i % len(self.tiles)
        self.i += 1
        return self.tiles[i], (self.key, i)


def load_xT_bf16(p, xT, xb, nt=T):
    xs = [p.sb(f"xs{i}", [128, nt], F32) for i in range(2)]
    for kc in range(KC):
        s = xs[kc % 2]
        p.dma(s[:, :], xT[kc * 128:(kc + 1) * 128, :], writes=[("xs", kc % 2)], dsem=f"xs{kc % 2}")
        eng = "dve" if kc % 2 == 0 else "pool"
        p.add(eng, lambda e, s=s, kc=kc: e.tensor_copy(out=xb[:, kc, :], in_=s[:, :]),
              reads=[("xs", kc % 2)], writes=[("xb", kc)])


class WStream:
    def __init__(self, p, kc, wt=WT, nbuf=2, name="w"):
        self.p = p
        self.kc = kc
        self.wt = wt
        self.name = name
        self.ws = [p.sb(f"{name}s{i}", [128, kc, wt], F32) for i in range(nbuf)]
        self.wb = [p.sb(f"{name}b{i}", [128, kc, wt], BF16) for i in range(nbuf)]
        self.i = 0
        self.nbuf = nbuf

    def load(self, w, c0, nc_, cast="pool"):
        p = self.p
        i = self.i % self.nbuf
        self.i += 1
        s, b = self.ws[i], self.wb[i]
        src = w[:, c0:c0 + nc_].rearrange("(k p) c -> p k c", p=128)
        half = self.kc // 2
        nm = self.name
        p.dma(s[:, 0:half, 0:nc_], src[:, 0:half, :], writes=[(nm + "s", i, 0)], dsem=f"{nm}s{i}")
        p.dma(s[:, half:, 0:nc_], src[:, half:, :], writes=[(nm + "s", i, 1)], dsem=f"{nm}s{i}")
        if cast == "act":
            p.add("act", lambda e: e.activation(out=b[:, :, 0:nc_], in_=s[:, :, 0:nc_], func=AF.Copy),
                  reads=[(nm + "s", i, 0), (nm + "s", i, 1)], writes=[(nm + "b", i)])
        else:
            p.add(cast, lambda e: e.tensor_copy(out=b[:, :, 0:nc_], in_=s[:, :, 0:nc_]),
                  reads=[(nm + "s", i, 0), (nm + "s", i, 1)], writes=[(nm + "b", i)])
        return b, (nm + "b", i)


def emit_stage1(p, xT, w_in, gate, QT, KT, V, GQT, GKT, GKM, GVM, SGR, SMA, SMB, LAM):
    xb = p.sb("xb", [128, KC, T], BF16)
    load_xT_bf16(p, xT, xb)
    xkeys = [("xb", kc) for kc in range(KC)]
    ws = WStream(p, KC)
    psum = Rot([p.ps(f"ps{i}", [128, 512], F32) for i in range(8)], "ps")
    ob = Rot([p.sb(f"ob{i}", [128, 512], BF16) for i in range(4)], "ob")
    evac_i = [0]
    wtile = [0]

    def evac(dst_sb, src_ps, func, rk, wk):
        if func is None:
            evac_i[0] += 1
            if evac_i[0] % 2 == 0:
                p.add("dve", lambda e: e.tensor_copy(out=dst_sb, in_=src_ps), reads=rk, writes=wk)
                return
            func = AF.Copy
        p.add("act", lambda e: e.activation(out=dst_sb, in_=src_ps, func=func), reads=rk, writes=wk)

    def fm_job(c0, ncols, dst, func=None):
        for t0 in range(0, ncols, WT):
            n = min(WT, ncols - t0)
            wtile[0] += 1
            wb, wk = ws.load(w_in, c0 + t0, n)
            for m0 in range(0, n, 128):
                mc = min(128, n - m0)
                for th in range(T // 512):
                    ps, pk = psum.next()
                    for kc in range(KC):
                        p.add("pe", lambda e, ps=ps, wb=wb, kc=kc, m0=m0, mc=mc, th=th: e.matmul(
                            ps[0:mc, :], lhsT=wb[:, kc, m0:m0 + mc], rhs=xb[:, kc, th * 512:(th + 1) * 512],
                            start=(kc == 0), stop=(kc == KC - 1)),
                            reads=[wk, xkeys[kc]], writes=[pk])
                    o, ok = ob.next()
                    evac(o[0:mc, :], ps[0:mc, :], func, [pk], [ok])
                    p.dma(dst[t0 + m0:t0 + m0 + mc, th * 512:(th + 1) * 512], o[0:mc, :],
                          reads=[ok], dsem=f"ob{ok[1]}", eng="act")

    def tm_job(c0, ncols, dst):
        for t0 in range(0, ncols, WT):
            n = min(WT, ncols - t0)
            wtile[0] += 1
            wb, wk = ws.load(w_in, c0 + t0, n)
            for tp in range(T // 256):
                ps, pk = psum.next()
                for j in range(2):
                    tt = tp * 2 + j
                    for kc in range(KC):
                        p.add("pe", lambda e, ps=ps, wb=wb, kc=kc, tt=tt, j=j, n=n: e.matmul(
                            ps[:, j * 256:j * 256 + n], lhsT=xb[:, kc, tt * 128:(tt + 1) * 128], rhs=wb[:, kc, 0:n],
                            start=(kc == 0), stop=(kc == KC - 1)),
                            reads=[wk, xkeys[kc]], writes=[pk])
                o, ok = ob.next()
                evac(o[:, :], ps[:, :], None, [pk], [ok])
                for j in range(2):
                    tt = tp * 2 + j
                    p.dma(dst[tt * 128:(tt + 1) * 128, t0:t0 + n], o[:, j * 256:j * 256 + n],
                          reads=[ok], dsem=f"ob{ok[1]}", eng="act")

    for g in range(3):
        fm_job((3 * g) * 1024, 1024, QT[g * 1024:(g + 1) * 1024, :])
        fm_job((3 * g + 1) * 1024, 1024, KT[g * 1024:(g + 1) * 1024, :])
        tm_job((3 * g + 2) * 1024, 1024, V[g])
    fm_job(9216, 1024, GQT)
    fm_job(10240, 1024, GKT)
    tm_job(10240, 1024, GKM)
    tm_job(11264, 2048, GVM)
    fm_job(13312, 2048, SGR, AF.Silu)
    fm_job(15376, 2048, SMA, AF.Sigmoid)
    fm_job(17424, 2048, SMB, AF.Sigmoid)

    gl = p.sb("gl", [17, T], F32)
    gw = p.sb("gw", [17, 1024], F32)
    p.dma(gw[:, :], gate[:, :], writes=["gw"], dsem="gw")
    p.add("pool", lambda e: e.memset(gl[:, :], 1.0), writes=["gl"])
    wb, wk = ws.load(w_in, 15360, 16)
    for th in range(T // 512):
        ps, pk = psum.next()
        for kc in range(KC):
            p.add("pe", lambda e, ps=ps, wb=wb, kc=kc, th=th: e.matmul(
                ps[0:16, :], lhsT=wb[:, kc, 0:16], rhs=xb[:, kc, th * 512:(th + 1) * 512],
                start=(kc == 0), stop=(kc == KC - 1)), reads=[wk, xkeys[kc]], writes=[pk])
        p.add("dve", lambda e, ps=ps, th=th: e.tensor_copy(out=gl[0:16, th * 512:(th + 1) * 512], in_=ps[0:16, :]),
              reads=[pk], writes=["gl"])
    la = Rot([p.sb(f"la{i}", [128, 512], F32) for i in range(2)], "la")
    for tt in range(T // 128):
        for ch in range(2):
            ps, pk = psum.next()
            p.add("pe", lambda e, ps=ps, tt=tt, ch=ch: e.matmul(
                ps[:, :], lhsT=gl[:, tt * 128:(tt + 1) * 128], rhs=gw[:, ch * 512:(ch + 1) * 512],
                start=True, stop=True), reads=["gl", "gw"], writes=[pk])
            o, ok = la.next()
            p.add("act", lambda e, o=o, ps=ps: e.activation(out=o[:, :], in_=ps[:, :], func=AF.Exp, scale=-1.0),
                  reads=[pk], writes=[ok])
            p.add("act", lambda e, o=o: e.activation(out=o[:, :], in_=o[:, :], func=AF.Ln, bias=1.0),
                  reads=[ok], writes=[ok])
            p.dma(LAM[tt * 128:(tt + 1) * 128, ch * 512:(ch + 1) * 512], o[:, :], reads=[ok], dsem=f"la{ok[1]}", eng="act")


NT = T // 128


def gla_consts():
    j = np.arange(128)[:, None]
    t = np.arange(128)[None, :]
    same = (j // 64) == (t // 64)
    tri = np.where(same & (j <= t), -1.0 / 16.0, 0.0).astype(np.float32)
    trirev = np.where(same & (j > t), -1.0 / 16.0, 0.0).astype(np.float32)
    mask = np.where(same & (j <= t), 1.0, 0.0).astype(np.float32)
    return np.concatenate([tri, trirev, mask], axis=1)


def build_stage2():
    nc = bass.Bass("TRN2", target_bir_lowering=False)
    p = Prog(nc)
    GQT = p.dram("GQT", [1024, T], BF16, "ExternalInput")
    GKT = p.dram("GKT", [1024, T], BF16, "ExternalInput")
    GKM = p.dram("GKM", [T, 1024], BF16, "ExternalInput")
    GVM = p.dram("GVM", [T, 2048], BF16, "ExternalInput")
    LAM = p.dram("LAM", [T, 1024], F32, "ExternalInput")
    GC = p.dram("GC", [128, 384], F32, "ExternalInput")
    OL = p.dram("OL", [2048, T], F32, "ExternalOutput")
    QTL = p.dram("QTL", [1024, T], BF16, "ExternalOutput")
    U = p.dram("U", [1024, 512], F32, "ExternalOutput")
    DT = p.dram("DT", [1024, 1], F32, "ExternalOutput")
    emit_stage2(p, GQT, GKT, GKM, GVM, LAM, GC, OL, QTL, U, DT)
    p.emit()
    return nc


def emit_stage2(p, GQT, GKT, GKM, GVM, LAM, GC, OL, QTL, U, DT):
    gc = p.sb("gc", [128, 384], F32)
    p.dma(gc[:, :], GC[:, :], writes=["gc"], dsem="gc")
    tri, trirev, mask = gc[:, 0:128], gc[:, 128:256], gc[:, 256:384]
    qT = [p.sb(f"g_qT{i}", [128, T], BF16) for i in range(2)]
    kT = [p.sb(f"g_kT{i}", [128, T], BF16) for i in range(2)]
    ktm = p.sb("g_ktm", [128, NT, 256], BF16)
    vtm = p.sb("g_vtm", [128, NT, 512], BF16)
    lam = p.sb("g_lam", [128, NT, 256], F32)
    E = [p.sb(f"g_E{i}", [128, T], F32) for i in range(2)]
    qd = [p.sb(f"g_qd{i}", [128, T], BF16) for i in range(2)]
    ki = [p.sb(f"g_ki{i}", [128, T], BF16) for i in range(2)]
    ke = p.sb("g_ke", [128, NT, 256], BF16)
    qtl = [p.sb(f"g_qtl{i}", [128, T], BF16) for i in range(2)]
    S = [p.sb(f"g_S{i}", [128, 512], F32) for i in range(2)]
    Sb = [p.sb(f"g_Sb{i}", [128, 512], BF16) for i in range(2)]
    G = [p.sb(f"g_G{i}", [128, 1], F32) for i in range(2)]
    tmpf = Rot([p.sb(f"g_tmp{i}", [128, 256], F32) for i in range(3)], "g_tmp")
    attb = Rot([p.sb(f"g_att{i}", [128, 128], BF16) for i in range(2)], "g_att")
    osb = Rot([p.sb(f"g_o{i}", [128, 512], F32) for i in range(2)], "g_o")
    psA = Rot([p.ps(f"g_psA{i}", [128, 512], F32) for i in range(3)], "g_psA")
    psO = Rot([p.ps(f"g_psO{i}", [128, 512], F32) for i in range(2)], "g_psO")
    psS = Rot([p.ps(f"g_psS{i}", [128, 512], F32) for i in range(3)], "g_psS")

    for h in range(4):
        for dc in range(2):
            r0 = h * 256 + dc * 128
            p.dma(qT[dc][:, :], GQT[r0:r0 + 128, :], writes=[("qT", dc)], dsem=f"gq{dc}")
            p.dma(kT[dc][:, :], GKT[r0:r0 + 128, :], writes=[("kT", dc)], dsem=f"gk{dc}")
        p.dma(ktm[:, :, :], GKM[:, h * 256:(h + 1) * 256].rearrange("(n p) c -> p n c", p=128),
              writes=["ktm"], dsem="gktm")
        p.dma(vtm[:, :, :], GVM[:, h * 512:(h + 1) * 512].rearrange("(n p) c -> p n c", p=128),
              writes=["vtm"], dsem="gvtm")
        p.dma(lam[:, :, :], LAM[:, h * 256:(h + 1) * 256].rearrange("(n p) c -> p n c", p=128),
              writes=["lam"], dsem="glam")
        for dc in range(2):
            p.add("pool", lambda e, dc=dc: e.memset(S[dc][:, :], 0.0), writes=[("S", dc)])
            p.add("pool", lambda e, dc=dc: e.memset(Sb[dc][:, :], 0.0), writes=[("Sb", dc)])
            p.add("pool", lambda e, dc=dc: e.memset(G[dc][:, :], 1.0), writes=[("G", dc)])
        for tt in range(NT):
            cs = slice(tt * 128, (tt + 1) * 128)
            for dc in range(2):
                ps, pk = psA.next()
                p.add("pe", lambda e, ps=ps, tt=tt, dc=dc: e.matmul(
                    ps[:, 0:128], lhsT=lam[:, tt, dc * 128:(dc + 1) * 128], rhs=tri, start=True, stop=True),
                    reads=["lam", "gc"], writes=[pk])
                p.add("act", lambda e, ps=ps, dc=dc, cs=cs: e.activation(out=E[dc][:, cs], in_=ps[:, 0:128], func=AF.Exp),
                      reads=[pk], writes=[("E", dc, tt)])
                tm, tk = tmpf.next()
                p.add("act", lambda e, ps=ps, tm=tm: e.activation(out=tm[:, 0:128], in_=ps[:, 0:128], func=AF.Exp, scale=-1.0),
                      reads=[pk], writes=[tk])
                p.add("dve", lambda e, dc=dc, cs=cs: e.scalar_tensor_tensor(
                    out=qd[dc][:, cs], in0=qT[dc][:, cs], scalar=0.0625, in1=E[dc][:, cs], op0=ALU.mult, op1=ALU.mult),
                    reads=[("qT", dc), ("E", dc, tt)], writes=[("qd", dc, tt)])
                p.add("dve", lambda e, dc=dc, cs=cs, tm=tm: e.tensor_tensor(
                    out=ki[dc][:, cs], in0=kT[dc][:, cs], in1=tm[:, 0:128], op=ALU.mult),
                    reads=[("kT", dc), tk], writes=[("ki", dc, tt)])
            ps, pk = psA.next()
            p.add("pe", lambda e, ps=ps, tt=tt: e.matmul(
                ps[:, 0:256], lhsT=trirev, rhs=lam[:, tt, :], start=True, stop=True),
                reads=["lam", "gc"], writes=[pk])
            tm, tk = tmpf.next()
            p.add("act", lambda e, ps=ps, tm=tm: e.activation(out=tm[:, :], in_=ps[:, 0:256], func=AF.Exp),
                  reads=[pk], writes=[tk])
            p.add("dve", lambda e, tt=tt, tm=tm: e.tensor_tensor(
                out=ke[:, tt, :], in0=ktm[:, tt, :], in1=tm[:, :], op=ALU.mult),
                reads=["ktm", tk], writes=[("ke", tt)])
        for tt in range(NT):
            cs = slice(tt * 128, (tt + 1) * 128)
            ps, pk = psA.next()
            for dc in range(2):
                p.add("pe", lambda e, ps=ps, dc=dc, cs=cs: e.matmul(
                    ps[:, 0:128], lhsT=ki[dc][:, cs], rhs=qd[dc][:, cs], start=(dc == 0), stop=(dc == 1)),
                    reads=[("ki", dc, tt), ("qd", dc, tt)], writes=[pk])
            ab, ak = attb.next()
            p.add("dve", lambda e, ps=ps, ab=ab: e.tensor_tensor(out=ab[:, :], in0=ps[:, 0:128], in1=mask, op=ALU.mult),
                  reads=[pk, "gc"], writes=[ak])
            po, pok = psO.next()
            for par in range(2):
                c0 = tt * 128 + par * 64
                pr = slice(par * 64, par * 64 + 64)
                for dc in range(2):
                    p.add("dve", lambda e, dc=dc, c0=c0: e.tensor_scalar(
                        out=qtl[dc][:, c0:c0 + 64], in0=qd[dc][:, c0:c0 + 64], scalar1=G[dc][:, 0:1], scalar2=None,
                        op0=ALU.mult), reads=[("qd", dc, tt), ("G", dc)], writes=[("qtl", dc)])
                    p.add("dve", lambda e, dc=dc, c0=c0: e.tensor_tensor(
                        out=G[dc][:, :], in0=G[dc][:, :], in1=E[dc][:, c0 + 63:c0 + 64], op=ALU.mult),
                        reads=[("G", dc), ("E", dc, tt)], writes=[("G", dc)])
                for ec in range(4):
                    oc = slice(ec * 128 + par * 64, ec * 128 + par * 64 + 64)
                    es = slice(ec * 128, (ec + 1) * 128)
                    for dc in range(2):
                        p.add("pe", lambda e, po=po, oc=oc, es=es, dc=dc, c0=c0: e.matmul(
                            po[:, oc], lhsT=Sb[dc][:, es], rhs=qd[dc][:, c0:c0 + 64], start=(dc == 0), stop=False),
                            reads=[("Sb", dc), ("qd", dc, tt)], writes=[pok])
                    p.add("pe", lambda e, po=po, oc=oc, es=es, pr=pr, tt=tt, ab=ab: e.matmul(
                        po[:, oc], lhsT=vtm[pr, tt, es], rhs=ab[pr, pr], start=False, stop=True),
                        reads=["vtm", ak], writes=[pok])
                for dc in range(2):
                    pss, psk = psS.next()
                    p.add("pe", lambda e, pss=pss, pr=pr, tt=tt, dc=dc: e.matmul(
                        pss[:, :], lhsT=ke[pr, tt, dc * 128:(dc + 1) * 128], rhs=vtm[pr, tt, :], start=True, stop=True),
                        reads=[("ke", tt), "vtm"], writes=[psk])
                    p.add("dve", lambda e, pss=pss, dc=dc, c0=c0: e.scalar_tensor_tensor(
                        out=S[dc][:, :], in0=S[dc][:, :], scalar=E[dc][:, c0 + 63:c0 + 64], in1=pss[:, :],
                        op0=ALU.mult, op1=ALU.add), reads=[("S", dc), ("E", dc, tt), psk], writes=[("S", dc)])
                    p.add("act", lambda e, dc=dc: e.activation(out=Sb[dc][:, :], in_=S[dc][:, :], func=AF.Copy),
                          reads=[("S", dc)], writes=[("Sb", dc)])
            o, ok = osb.next()
            p.add("act", lambda e, o=o, po=po: e.activation(out=o[:, :], in_=po[:, :], func=AF.Copy),
                  reads=[pok], writes=[ok])
            p.dma(OL[h * 512:(h + 1) * 512, cs].rearrange("(c p) t -> p c t", p=128),
                  o[:, :].rearrange("p (c t) -> p c t", c=4), reads=[ok], dsem=f"go{ok[1]}", eng="act")
        for dc in range(2):
            r0 = h * 256 + dc * 128
            p.dma(QTL[r0:r0 + 128, :], qtl[dc][:, :], reads=[("qtl", dc)], dsem=f"gqtl{dc}")
            p.dma(U[r0:r0 + 128, :], S[dc][:, :], reads=[("S", dc)], dsem=f"gU{dc}")
            p.dma(DT[r0:r0 + 128, :], G[dc][:, :], reads=[("G", dc)], dsem=f"gD{dc}")


HALO = (128, 512, 2048)
DIL = (1, 4, 16)
SCALE = 128 ** -0.5
NTV = (9, 12, 32)
GROUPS = [0, 1, 2]


def build_stage3a(do_attn=True, do_gla=True):
    nc = bass.Bass("TRN2", target_bir_lowering=False)
    p = Prog(nc)
    d = {}
    d["QT"] = p.dram("QT", [3 * 1024, T], BF16, "ExternalInput")
    for g in range(3):
        d[f"KH{g}"] = p.dram(f"KH{g}", [1024, HALO[g] + T], BF16, "ExternalInput")
        d[f"VH{g}"] = p.dram(f"VH{g}", [8, 128, NTV[g], 128], BF16, "ExternalInput")
    d["BT"] = p.dram("BT", [24, 128, 256], F32, "ExternalInput")
    d["VALID"] = p.dram("VALID", [128, 256], F32, "ExternalInput")
    d["HMASK"] = p.dram("HMASK", [128, 3], F32, "ExternalInput")
    d["OL"] = p.dram("OL", [2048, T], F32, "ExternalInput")
    d["QTL"] = p.dram("QTL", [1024, T], BF16, "ExternalInput")
    d["UALL"] = p.dram("UALL", [8, 1024, 512], F32, "ExternalInput")
    d["DALL"] = p.dram("DALL", [1024, 8], F32, "ExternalInput").rearrange("r (c o) -> r c o", o=1)
    d["CMASK"] = p.dram("CMASK", [128, 8], F32, "ExternalInput")
    d["SGR"] = p.dram("SGR", [2048, T], BF16, "ExternalInput")
    d["NG"] = p.dram("NG", [128, 4], F32, "ExternalInput")
    d["YAT"] = p.dram("YAT", [1024, T], BF16, "ExternalOutput")
    d["YBT"] = p.dram("YBT", [2048, T], BF16, "ExternalOutput")
    if do_attn: emit_attn(p, d)
    if do_gla: emit_gla_fin(p, d)
    p.emit()
    return nc


def emit_attn(p, d, after_head=None):
    QT, BT, VALID, HMASK, YAT = d["QT"], d["BT"], d["VALID"], d["HMASK"], d["YAT"]
    valid = p.sb("a_valid", [128, 256], F32)
    hmask = p.sb("a_hmask", [128, 3], F32)
    ones = p.sb("a_ones", [128, 128], BF16)
    p.dma(valid[:, :], VALID[:, :], writes=["valid"], dsem="a_c0")
    p.dma(hmask[:, :], HMASK[:, :], writes=["hmask"], dsem="a_c1")
    p.add("pool", lambda e: e.memset(ones[:, :], 1.0), writes=["ones"])
    qT = Rot([p.sb(f"a_q{i}", [128, T], BF16) for i in range(2)], "a_q")
    kT = Rot([p.sb(f"a_k{i}", [128, 2048 + T], BF16) for i in range(2)], "a_k")
    vt = Rot([p.sb(f"a_v{i}", [128, 32, 128], BF16) for i in range(2)], "a_v")
    bt = Rot([p.sb(f"a_bt{i}", [128, 256], F32) for i in range(2)], "a_bt")
    tfull = Rot([p.sb(f"a_tf{i}", [128, 256], F32) for i in range(2)], "a_tf")
    tfirst = Rot([p.sb(f"a_t1{i}", [128, 256], F32) for i in range(2)], "a_t1")
    pf = Rot([p.sb(f"a_pf{i}", [128, 256], F32) for i in range(3)], "a_pf")
    pb = Rot([p.sb(f"a_pb{i}", [128, 256], BF16) for i in range(3)], "a_pb")
    nd = p.sb("a_nd", [128, 2, T], F32)
    num = nd[:, 0, :]
    den = nd[:, 1, :]
    ya = Rot([p.sb(f"a_ya{i}", [128, T], BF16) for i in range(2)], "a_ya")
    psS = Rot([p.ps(f"a_psS{i}", [128, 512], F32) for i in range(3)], "a_psS")
    psO = Rot([p.ps(f"a_psO{i}", [128, 512], F32) for i in range(3)], "a_psO")

    ALLND = [("nd", b, rho) for b in range(8) for rho in range(16)]

    def ndk(g, r, m):
        if g == 0:
            return [("nd", m, rho) for rho in range(16)]
        if g == 1:
            return [("nd", b, rho) for b in range(4 * m, 4 * m + 4) for rho in range(r, 16, 4)]
        return [("nd", b, r) for b in range(8)]

    pend = [None]
    for h in range(8):
        for g in GROUPS:
            H, dl = HALO[g], DIL[g]
            q, qk = qT.next()
            k, kk = kT.next()
            v, vk = vt.next()
            r0 = g * 1024 + h * 128
            p.dma(q[:, :], QT[r0:r0 + 128, :], writes=[qk], dsem=f"a_q{qk[1]}")
            p.dma(k[:, 0:H + T], d[f"KH{g}"][h * 128:(h + 1) * 128, :], writes=[kk], dsem=f"a_k{kk[1]}")
            VH = d[f"VH{g}"]
            p.dma(v[:, 0:NTV[g], :], VH[h], writes=[(vk, 0)], dsem=f"a_v{vk[1]}")
            b, bk = bt.next()
            tf, tfk = tfull.next()
            t1, t1k = tfirst.next()
            p.dma(b[:, :], BT[g * 8 + h], writes=[bk], dsem=f"a_bt{bk[1]}")
            p.add("act", lambda e, b=b: e.activation(out=b[:, :], in_=b[:, :], func=AF.Exp), reads=[bk], writes=[bk])
            p.add("dve", lambda e, b=b, tf=tf: e.tensor_tensor(out=tf[:, :], in0=b[:, :], in1=valid[:, :], op=ALU.mult),
                  reads=[bk, "valid"], writes=[tfk])
            p.add("dve", lambda e, tf=tf, t1=t1: e.tensor_copy(out=t1[:, 128:256], in_=tf[:, 128:256]),
                  reads=[tfk], writes=[t1k])
            p.add("dve", lambda e, tf=tf, t1=t1, g=g: e.tensor_scalar(
                out=t1[:, 0:128], in0=tf[:, 0:128], scalar1=hmask[:, g:g + 1], scalar2=None, op0=ALU.mult),
                reads=[tfk, "hmask", t1k], writes=[t1k])
            vkeys = [(vk, 0)]
            if g < 2:
                nq = T // (128 * dl)
                ntile = (H + T) // (128 * dl)
                for r in range(dl):
                    for m in range(nq):
                        qs = slice(r + dl * 128 * m, r + dl * 128 * m + dl * 127 + 1, dl)
                        kprev = slice(r + dl * 128 * m, r + dl * 128 * m + dl * 127 + 1, dl)
                        kcur = slice(r + dl * 128 * (m + 1), r + dl * 128 * (m + 1) + dl * 127 + 1, dl)
                        tab, tabk = (t1, t1k) if m == 0 else (tf, tfk)
                        ps, pk = psS.next()
                        p.add("pe", lambda e, ps=ps, k=k, q=q, kprev=kprev, qs=qs: e.matmul(
                            ps[:, 0:128], lhsT=k[:, kprev], rhs=q[:, qs], start=True, stop=True),
                            reads=[kk, qk], writes=[pk])
                        p.add("pe", lambda e, ps=ps, k=k, q=q, kcur=kcur, qs=qs: e.matmul(
                            ps[:, 128:256], lhsT=k[:, kcur], rhs=q[:, qs], start=True, stop=True),
                            reads=[kk, qk], writes=[pk])
                        f, fk = pf.next()
                        pbb, pbk = pb.next()
                        p.add("act", lambda e, f=f, ps=ps: e.activation(out=f[:, :], in_=ps[:, 0:256], func=AF.Exp, scale=SCALE),
                              reads=[pk], writes=[fk])
                        p.add("dve", lambda e, f=f, pbb=pbb, tab=tab: e.tensor_tensor(out=pbb[:, :], in0=f[:, :], in1=tab[:, :], op=ALU.mult),
                              reads=[fk, tabk], writes=[pbk])
                        p.defer_start()
                        po, pok = psO.next()
                        j0 = r * ntile + m
                        p.add("pe", lambda e, po=po, v=v, pbb=pbb, j0=j0: e.matmul(
                            po[:, 0:128], lhsT=v[:, j0, :], rhs=pbb[:, 0:128], start=True, stop=False),
                            reads=vkeys + [pbk], writes=[pok])
                        p.add("pe", lambda e, po=po, v=v, pbb=pbb, j0=j0: e.matmul(
                            po[:, 0:128], lhsT=v[:, j0 + 1, :], rhs=pbb[:, 128:256], start=False, stop=True),
                            reads=vkeys + [pbk], writes=[pok])
                        p.add("pe", lambda e, po=po, pbb=pbb: e.matmul(
                            po[:, 128:256], lhsT=ones[:, :], rhs=pbb[:, 0:128], start=True, stop=False),
                            reads=["ones", pbk], writes=[pok])
                        p.add("pe", lambda e, po=po, pbb=pbb: e.matmul(
                            po[:, 128:256], lhsT=ones[:, :], rhs=pbb[:, 128:256], start=False, stop=True),
                            reads=["ones", pbk], writes=[pok])
                        po2 = po[:, 0:256].rearrange("p (a b) -> p a b", a=2)
                        if g == GROUPS[0]:
                            p.add("dve", lambda e, po2=po2, qs=qs: e.tensor_copy(out=nd[:, :, qs], in_=po2),
                                  reads=[pok], writes=ndk(g, r, m))
                        else:
                            p.add("dve", lambda e, po2=po2, qs=qs: e.tensor_tensor(out=nd[:, :, qs], in0=po2, in1=nd[:, :, qs], op=ALU.add),
                                  reads=[pok] + ndk(g, r, m), writes=ndk(g, r, m))
                        blk = p.defer_stop()
                        p.flush(pend[0])
                        pend[0] = blk
            else:
                for r in range(16):
                    qs = slice(r, T, 16)
                    kprev = slice(r, H, 16)
                    kcur = slice(H + r, H + T, 16)
                    ps, pk = psS.next()
                    p.add("pe", lambda e, ps=ps, k=k, q=q, kprev=kprev, qs=qs: e.matmul(
                        ps[:, 0:64], lhsT=k[:, kprev], rhs=q[:, qs], start=True, stop=True),
                        reads=[kk, qk], writes=[pk])
                    p.add("pe", lambda e, ps=ps, k=k, q=q, kcur=kcur, qs=qs: e.matmul(
                        ps[0:64, 64:128], lhsT=k[:, kcur], rhs=q[:, qs], start=True, stop=True),
                        reads=[kk, qk], writes=[pk])
                    f, fk = pf.next()
                    pbb, pbk = pb.next()
                    p.add("act", lambda e, f=f, ps=ps: e.activation(out=f[:, 0:64], in_=ps[:, 0:64], func=AF.Exp, scale=SCALE),
                          reads=[pk], writes=[fk])
                    p.add("act", lambda e, f=f, ps=ps: e.activation(out=f[0:64, 64:128], in_=ps[0:64, 64:128], func=AF.Exp, scale=SCALE),
                          reads=[pk, fk], writes=[fk])
                    p.add("dve", lambda e, f=f, pbb=pbb, t1=t1: e.tensor_tensor(out=pbb[:, 0:64], in0=f[:, 0:64], in1=t1[:, 0:64], op=ALU.mult),
                          reads=[fk, t1k], writes=[pbk])
                    p.add("dve", lambda e, f=f, pbb=pbb, t1=t1: e.tensor_tensor(out=pbb[0:64, 64:128], in0=f[0:64, 64:128], in1=t1[0:64, 128:192], op=ALU.mult),
                          reads=[fk, t1k, pbk], writes=[pbk])
                    p.defer_start()
                    po, pok = psO.next()
                    p.add("pe", lambda e, po=po, v=v, pbb=pbb, r=r: e.matmul(
                        po[:, 0:64], lhsT=v[:, r, :], rhs=pbb[:, 0:64], start=True, stop=False),
                        reads=vkeys + [pbk], writes=[pok])
                    p.add("pe", lambda e, po=po, v=v, pbb=pbb, r=r: e.matmul(
                        po[:, 0:64], lhsT=v[0:64, 16 + r, :], rhs=pbb[0:64, 64:128], start=False, stop=True),
                        reads=vkeys + [pbk], writes=[pok])
                    p.add("pe", lambda e, po=po, pbb=pbb: e.matmul(
                        po[:, 128:192], lhsT=ones[:, :], rhs=pbb[:, 0:64], start=True, stop=False),
                        reads=["ones", pbk], writes=[pok])
                    p.add("pe", lambda e, po=po, pbb=pbb: e.matmul(
                        po[:, 128:192], lhsT=ones[0:64, :], rhs=pbb[0:64, 64:128], start=False, stop=True),
                        reads=["ones", pbk], writes=[pok])
                    po2 = po[:, 0:256].rearrange("p (a b) -> p a b", a=2)[:, :, 0:64]
                    p.add("dve", lambda e, po2=po2, qs=qs: e.tensor_tensor(out=nd[:, :, qs], in0=po2, in1=nd[:, :, qs], op=ALU.add),
                          reads=[pok] + ndk(2, r, 0), writes=ndk(2, r, 0))
                    blk = p.defer_stop()
                    p.flush(pend[0])
                    pend[0] = blk
        p.flush(pend[0])
        pend[0] = None
        y, yk = ya.next()
        p.add("dve", lambda e: e.reciprocal(out=den[:, :], in_=den[:, :]), reads=ALLND, writes=ALLND)
        p.add("dve", lambda e, y=y: e.tensor_tensor(out=y[:, :], in0=num[:, :], in1=den[:, :], op=ALU.mult),
              reads=ALLND, writes=[yk])
        p.dma(YAT[h * 128:(h + 1) * 128, :], y[:, :], reads=[yk], dsem=f"a_ya{yk[1]}", eng="act")
        if after_head is not None:
            after_head(h)


def gla_fin_setup(p, d):
    OL, QTL, UALL, DALL, CMASK, SGR, NG, YBT = (d[k] for k in ("OL", "QTL", "UALL", "DALL", "CMASK", "SGR", "NG", "YBT"))
    cm = p.sb("f_cm", [128, 8], F32)
    ng = p.sb("f_ng", [128, 4], F32)
    onesb = p.sb("f_ones", [128, 128], BF16)
    p.dma(cm[:, :], CMASK[:, :], writes=["cm"], dsem="f_c0")
    p.dma(ng[:, :], NG[:, :], writes=["ng"], dsem="f_c1")
    p.add("pool", lambda e: e.memset(onesb[:, :], 1.0), writes=["f_ones"])
    dall = p.sb("f_dall", [128, 8], F32)
    acoef = p.sb("f_a", [128, 8], F32)
    Sin = [p.sb(f"f_S{i}", [128, 512], F32) for i in range(2)]
    Sb = [p.sb(f"f_Sb{i}", [128, 512], BF16) for i in range(2)]
    ut = Rot([p.sb(f"f_u{i}", [128, 512], F32) for i in range(3)], "f_u")
    qtl = [p.sb(f"f_q{i}", [128, T], BF16) for i in range(2)]
    o = [p.sb(f"f_o{i}", [128, T], F32) for i in range(4)]
    osq = [p.sb(f"f_osq{i}", [128, T], BF16) for i in range(4)]
    rstd = p.sb("f_rstd", [128, T], F32)
    sgr = Rot([p.sb(f"f_sgr{i}", [128, T], BF16) for i in range(2)], "f_sgr")
    tmp = Rot([p.sb(f"f_tmp{i}", [128, T], F32) for i in range(2)], "f_tmp")
    yb = Rot([p.sb(f"f_yb{i}", [128, T], BF16) for i in range(2)], "f_yb")
    psC = Rot([p.ps(f"f_psC{i}", [128, 512], F32) for i in range(2)], "f_psC")

    def do_head(h):
        _gla_fin_head(h)

    def _gla_fin_head(h):
        if True:
            for dc in range(2):
                r0 = h * 256 + dc * 128
                p.dma(dall[:, :].rearrange("p (c o) -> p c o", o=1), DALL[r0:r0 + 128], writes=["dall"], dsem="f_dall",
                      allow_slow_non_contiguous=True)
                p.add("dve", lambda e: e.scalar_tensor_tensor(out=acoef[:, :], in0=dall[:, :], scalar=-1.0, in1=cm[:, :],
                                                              op0=ALU.add, op1=ALU.mult), reads=["dall", "cm"], writes=["acoef"])
                p.add("dve", lambda e: e.tensor_scalar(out=acoef[:, :], in0=acoef[:, :], scalar1=1.0, scalar2=None, op0=ALU.add),
                      reads=["acoef"], writes=["acoef"])
                p.add("pool", lambda e, dc=dc: e.memset(Sin[dc][:, :], 0.0), writes=[("Sin", dc)])
                for c in range(8):
                    u, uk = ut.next()
                    p.dma(u[:, :], UALL[c, r0:r0 + 128, :], writes=[uk], dsem=f"f_u{uk[1]}")
                    p.add("dve", lambda e, u=u, c=c: e.tensor_scalar(out=u[:, :], in0=u[:, :], scalar1=cm[:, c:c + 1], scalar2=None, op0=ALU.mult),
                          reads=[uk, "cm"], writes=[uk])
                    p.add("dve", lambda e, u=u, c=c, dc=dc: e.scalar_tensor_tensor(
                        out=Sin[dc][:, :], in0=Sin[dc][:, :], scalar=acoef[:, c:c + 1], in1=u[:, :], op0=ALU.mult, op1=ALU.add),
                        reads=[("Sin", dc), "acoef", uk], writes=[("Sin", dc)])
                p.add("act", lambda e, dc=dc: e.activation(out=Sb[dc][:, :], in_=Sin[dc][:, :], func=AF.Copy),
                      reads=[("Sin", dc)], writes=[("Sb", dc)])
                p.dma(qtl[dc][:, :], QTL[r0:r0 + 128, :], writes=[("qtl", dc)], dsem=f"f_q{dc}")
            for ec in range(4):
                r0 = h * 512 + ec * 128
                p.dma(o[ec][:, :], OL[r0:r0 + 128, :], writes=[("o", ec)], dsem=f"f_o{ec}")
                for th in range(2):
                    ts = slice(th * 512, (th + 1) * 512)
                    ps, pk = psC.next()
                    for dc in range(2):
                        p.add("pe", lambda e, ps=ps, dc=dc, ec=ec, ts=ts: e.matmul(
                            ps[:, :], lhsT=Sb[dc][:, ec * 128:(ec + 1) * 128], rhs=qtl[dc][:, ts], start=(dc == 0), stop=(dc == 1)),
                            reads=[("Sb", dc), ("qtl", dc)], writes=[pk])
                    p.add("dve", lambda e, ps=ps, ec=ec, ts=ts: e.tensor_tensor(out=o[ec][:, ts], in0=ps[:, :], in1=o[ec][:, ts], op=ALU.add),
                          reads=[pk, ("o", ec)], writes=[("o", ec)])
                p.add("act", lambda e, ec=ec: e.activation(out=osq[ec][:, :], in_=o[ec][:, :], func=AF.Square),
                      reads=[("o", ec)], writes=[("osq", ec)])
            for th in range(2):
                ts = slice(th * 512, (th + 1) * 512)
                ps, pk = psC.next()
                for ec in range(4):
                    p.add("pe", lambda e, ps=ps, ec=ec, ts=ts: e.matmul(
                        ps[:, :], lhsT=onesb[:, :], rhs=osq[ec][:, ts], start=(ec == 0), stop=(ec == 3)),
                        reads=["f_ones", ("osq", ec)], writes=[pk])
                p.add("dve", lambda e, ps=ps, ts=ts: e.tensor_scalar(out=rstd[:, ts], in0=ps[:, :], scalar1=1.0 / 512.0, scalar2=1e-5,
                                                                     op0=ALU.mult, op1=ALU.add), reads=[pk], writes=["rstd"])
            p.add("act", lambda e: e.activation(out=rstd[:, :], in_=rstd[:, :], func=AF.Ln), reads=["rstd"], writes=["rstd"])
            p.add("act", lambda e: e.activation(out=rstd[:, :], in_=rstd[:, :], func=AF.Exp, scale=-0.5), reads=["rstd"], writes=["rstd"])
            for ec in range(4):
                r0 = h * 512 + ec * 128
                s, sk = sgr.next()
                p.dma(s[:, :], SGR[r0:r0 + 128, :], writes=[sk], dsem=f"f_sgr{sk[1]}")
                t, tk = tmp.next()
                y, yk = yb.next()
                p.add("dve", lambda e, t=t, ec=ec: e.scalar_tensor_tensor(out=t[:, :], in0=o[ec][:, :], scalar=ng[:, ec:ec + 1], in1=rstd[:, :],
                                                                         op0=ALU.mult, op1=ALU.mult), reads=[("o", ec), "ng", "rstd"], writes=[tk])
                p.add("dve", lambda e, t=t, y=y, s=s: e.tensor_tensor(out=y[:, :], in0=t[:, :], in1=s[:, :], op=ALU.mult),
                      reads=[tk, sk], writes=[yk])
                p.dma(YBT[r0:r0 + 128, :], y[:, :], reads=[yk], dsem=f"f_yb{yk[1]}", eng="act")

    return do_head


def emit_gla_fin(p, d):
    f = gla_fin_setup(p, d)
    for h in range(4):
        f(h)


ALPHA = float((2 * 4) ** 0.25)
DFF = 5632
FC = DFF // 128
CAST_UP = "act"
CAST_GATE = "pool"


def emit_out_ln(p, pre, act, akeys, kcn, W, XT, LNG, LNB, OUT, SCR, psY):
    ws = WStream(p, kcn, wt=128, name=pre + "w")
    ones = p.sb(pre + "ones", [128, 128], F32)
    p.add("pool", lambda e: e.memset(ones[:, :], 1.0), writes=[pre + "ones"])
    lng = p.sb(pre + "lng", [128, 16], F32)
    lnb = p.sb(pre + "lnb", [128, 16], F32)
    p.dma(lng[:, :], LNG[:, :], writes=[pre + "lng"], dsem=pre + "lng")
    p.dma(lnb[:, :], LNB[:, :], writes=[pre + "lnb"], dsem=pre + "lnb")
    xt = Rot([p.sb(f"{pre}xt{i}", [128, 512], F32) for i in range(3)], pre + "xt")
    ut = Rot([p.sb(f"{pre}ut{i}", [128, 512], F32) for i in range(3)], pre + "ut")
    usq = Rot([p.sb(f"{pre}usq{i}", [128, 512], F32) for i in range(2)], pre + "usq")
    pst = [p.ps(f"{pre}pst{i}", [128, 512], F32) for i in range(4)]
    for cc in range(16):
        wb, wk = ws.load(W, cc * 128, 128)
        for th in range(2):
            ts = slice(th * 512, (th + 1) * 512)
            ps, pk = psY.next()
            for kc in range(kcn):
                p.add("pe", lambda e, ps=ps, wb=wb, kc=kc, ts=ts: e.matmul(
                    ps[:, :], lhsT=wb[:, kc, :], rhs=act[:, kc, ts], start=(kc == 0), stop=(kc == kcn - 1)),
                    reads=[wk, akeys[kc]], writes=[pk])
            x, xk = xt.next()
            p.dma(x[:, :], XT[cc * 128:(cc + 1) * 128, ts], writes=[xk], dsem=f"{pre}xt{xk[1]}")
            u, uk = ut.next()
            p.add("dve", lambda e, u=u, x=x, ps=ps: e.scalar_tensor_tensor(
                out=u[:, :], in0=x[:, :], scalar=ALPHA, in1=ps[:, :], op0=ALU.mult, op1=ALU.add),
                reads=[xk, pk], writes=[uk])
            sq, sqk = usq.next()
            p.add("act", lambda e, sq=sq, u=u: e.activation(out=sq[:, :], in_=u[:, :], func=AF.Square),
                  reads=[uk], writes=[sqk])
            p.add("pe", lambda e, u=u, th=th, cc=cc: e.matmul(pst[th][:, :], lhsT=ones[:, :], rhs=u[:, :],
                                                             start=(cc == 0), stop=(cc == 15)),
                  reads=[pre + "ones", uk], writes=[(pre + "pst", th)])
            p.add("pe", lambda e, sq=sq, th=th, cc=cc: e.matmul(pst[2 + th][:, :], lhsT=ones[:, :], rhs=sq[:, :],
                                                               start=(cc == 0), stop=(cc == 15)),
                  reads=[pre + "ones", sqk], writes=[(pre + "pst", 2 + th)])
            p.dma(SCR[cc * 128:(cc + 1) * 128, ts], u[:, :], reads=[uk], writes=[(pre + "scr", cc, th)],
                  dsem=f"{pre}ut{uk[1]}", eng="act")
    mean = p.sb(pre + "mean", [128, T], F32)
    rstd = p.sb(pre + "rstd", [128, T], F32)
    for th in range(2):
        ts = slice(th * 512, (th + 1) * 512)
        p.add("dve", lambda e, th=th, ts=ts: e.tensor_scalar(out=mean[:, ts], in0=pst[th][:, :], scalar1=1.0 / 2048.0,
                                                             scalar2=None, op0=ALU.mult),
              reads=[(pre + "pst", th)], writes=[pre + "mean"])
        p.add("dve", lambda e, ts=ts: e.tensor_tensor(out=rstd[:, ts], in0=mean[:, ts], in1=mean[:, ts], op=ALU.mult),
              reads=[pre + "mean"], writes=[pre + "rstd"])
        p.add("dve", lambda e, th=th, ts=ts: e.scalar_tensor_tensor(
            out=rstd[:, ts], in0=pst[2 + th][:, :], scalar=1.0 / 2048.0, in1=rstd[:, ts], op0=ALU.mult, op1=ALU.subtract),
            reads=[(pre + "pst", 2 + th), pre + "rstd"], writes=[pre + "rstd"])
    p.add("dve", lambda e: e.tensor_scalar(out=rstd[:, :], in0=rstd[:, :], scalar1=1e-5, scalar2=None, op0=ALU.add),
          reads=[pre + "rstd"], writes=[pre + "rstd"])
    p.add("act", lambda e: e.activation(out=rstd[:, :], in_=rstd[:, :], func=AF.Ln), reads=[pre + "rstd"], writes=[pre + "rstd"])
    p.add("act", lambda e: e.activation(out=rstd[:, :], in_=rstd[:, :], func=AF.Exp, scale=-0.5),
          reads=[pre + "rstd"], writes=[pre + "rstd"])
    for cc in range(16):
        for th in range(2):
            ts = slice(th * 512, (th + 1) * 512)
            u, uk = ut.next()
            p.dma(u[:, :], SCR[cc * 128:(cc + 1) * 128, ts], reads=[(pre + "scr", cc, th)], writes=[uk],
                  dsem=f"{pre}ut{uk[1]}")
            p.add("dve", lambda e, u=u, ts=ts: e.tensor_tensor(out=u[:, :], in0=u[:, :], in1=mean[:, ts], op=ALU.subtract),
                  reads=[uk, pre + "mean"], writes=[uk])
            p.add("dve", lambda e, u=u, ts=ts: e.tensor_tensor(out=u[:, :], in0=u[:, :], in1=rstd[:, ts], op=ALU.mult),
                  reads=[uk, pre + "rstd"], writes=[uk])
            p.add("dve", lambda e, u=u, cc=cc: e.tensor_scalar(out=u[:, :], in0=u[:, :], scalar1=lng[:, cc:cc + 1],
                                                               scalar2=lnb[:, cc:cc + 1], op0=ALU.mult, op1=ALU.add),
                  reads=[uk, pre + "lng", pre + "lnb"], writes=[uk])
            p.dma(OUT[cc * 128:(cc + 1) * 128, ts], u[:, :], reads=[uk], dsem=f"{pre}ut{uk[1]}", eng="act")


def build_stage3b():
    nc = bass.Bass("TRN2", target_bir_lowering=False)
    p = Prog(nc)
    YAT = p.dram("YAT", [1024, T], BF16, "ExternalInput")
    YBT = p.dram("YBT", [2048, T], BF16, "ExternalInput")
    SMA = p.dram("SMA", [2048, T], BF16, "ExternalInput")
    SMB = p.dram("SMB", [2048, T], BF16, "ExternalInput")
    XT = p.dram("xT", [2048, T], F32, "ExternalInput")
    WA = p.dram("w_proj_a", [1024, 2048], F32, "ExternalInput")
    WB = p.dram("w_proj_b", [2048, 2048], F32, "ExternalInput")
    WO = p.dram("w_out", [2048, 2048], F32, "ExternalInput")
    LNG = p.dram("LNG", [128, 16], F32, "ExternalInput")
    LNB = p.dram("LNB", [128, 16], F32, "ExternalInput")
    OUT = p.dram("X1T", [2048, T], F32, "ExternalOutput")
    SCR = p.dram("SCR", [2048, T], F32, "Internal")
    emit_stage3b(p, YAT, YBT, SMA, SMB, XT, WA, WB, WO, LNG, LNB, OUT, SCR)
    p.emit()
    return nc


def emit_stage3b(p, YAT, YBT, SMA, SMB, XT, WA, WB, WO, LNG, LNB, OUT, SCR):
    ya = p.sb("b_ya", [128, 8, T], BF16)
    yb = p.sb("b_yb", [128, 16, T], BF16)
    yT = p.sb("b_yT", [128, 16, T], BF16)
    for kc in range(8):
        p.dma(ya[:, kc, :], YAT[kc * 128:(kc + 1) * 128, :], writes=[("b_ya", kc)], dsem="b_ya")
    for kc in range(16):
        p.dma(yb[:, kc, :], YBT[kc * 128:(kc + 1) * 128, :], writes=[("b_yb", kc)], dsem="b_yb")
    wsa = WStream(p, 8, wt=128, name="b_wa")
    wsb = WStream(p, 16, wt=128, name="b_wb")
    psY = Rot([p.ps(f"b_ps{i}", [128, 512], F32) for i in range(4)], "b_ps")
    sm = Rot([p.sb(f"b_sm{i}", [128, T], BF16) for i in range(4)], "b_sm")
    t1 = Rot([p.sb(f"b_t1{i}", [128, 512], F32) for i in range(2)], "b_t1")
    t2 = Rot([p.sb(f"b_t2{i}", [128, 512], F32) for i in range(2)], "b_t2")
    for cc in range(16):
        wa, wak = wsa.load(WA, cc * 128, 128)
        wb, wbk = wsb.load(WB, cc * 128, 128, cast="act")
        sa, sak = sm.next()
        sb_, sbk = sm.next()
        p.dma(sa[:, :], SMA[cc * 128:(cc + 1) * 128, :], writes=[sak], dsem=f"b_sm{sak[1]}")
        p.dma(sb_[:, :], SMB[cc * 128:(cc + 1) * 128, :], writes=[sbk], dsem=f"b_sm{sbk[1]}")
        for th in range(2):
            ts = slice(th * 512, (th + 1) * 512)
            pa, pak = psY.next()
            for kc in range(8):
                p.add("pe", lambda e, pa=pa, wa=wa, kc=kc, ts=ts: e.matmul(
                    pa[:, :], lhsT=wa[:, kc, :], rhs=ya[:, kc, ts], start=(kc == 0), stop=(kc == 7)),
                    reads=[wak, ("b_ya", kc)], writes=[pak])
            pb, pbk = psY.next()
            for kc in range(16):
                p.add("pe", lambda e, pb=pb, wb=wb, kc=kc, ts=ts: e.matmul(
                    pb[:, :], lhsT=wb[:, kc, :], rhs=yb[:, kc, ts], start=(kc == 0), stop=(kc == 15)),
                    reads=[wbk, ("b_yb", kc)], writes=[pbk])
            a, ak = t1.next()
            b, bk = t2.next()
            p.add("dve", lambda e, a=a, pa=pa, sa=sa, ts=ts: e.tensor_tensor(out=a[:, :], in0=pa[:, :], in1=sa[:, ts], op=ALU.mult),
                  reads=[pak, sak], writes=[ak])
            p.add("dve", lambda e, b=b, pb=pb, sb_=sb_, ts=ts: e.tensor_tensor(out=b[:, :], in0=pb[:, :], in1=sb_[:, ts], op=ALU.mult),
                  reads=[pbk, sbk], writes=[bk])
            p.add("dve", lambda e, a=a, b=b, cc=cc, ts=ts: e.tensor_tensor(out=yT[:, cc, ts], in0=a[:, :], in1=b[:, :], op=ALU.add),
                  reads=[ak, bk], writes=[("b_yT", cc)])
    emit_out_ln(p, "b_", yT, [("b_yT", kc) for kc in range(16)], 16, WO, XT, LNG, LNB, OUT, SCR, psY)


def build_stage4a():
    nc = bass.Bass("TRN2", target_bir_lowering=False)
    p = Prog(nc)
    X1T = p.dram("X1T", [2048, T], F32, "ExternalInput")
    X1H = p.dram("X1H", [2048, 2], F32, "ExternalInput")
    WG = p.dram("ffn_w_gate", [2048, DFF], F32, "ExternalInput")
    WU = p.dram("ffn_w_up", [2048, DFF], F32, "ExternalInput")
    CW = p.dram("CW", [128, FC, 3], F32, "ExternalInput")
    CB = p.dram("CB", [128, FC], F32, "ExternalInput")
    HT = p.dram("HT", [DFF, T], BF16, "ExternalOutput")
    emit_stage4a(p, X1T, X1H, WG, WU, CW, CB, HT)
    p.emit()
    return nc


def emit_stage4a(p, X1T, X1H, WG, WU, CW, CB, HT):
    xb = p.sb("c_xb", [128, 16, T + 2], BF16)
    xs = [p.sb(f"c_xs{i}", [128, T + 2], F32) for i in range(2)]
    for kc in range(16):
        s = xs[kc % 2]
        p.dma(s[:, 2:], X1T[kc * 128:(kc + 1) * 128, :], writes=[("c_xs", kc % 2, 0)], dsem=f"c_xs{kc % 2}")
        p.dma(s[:, 0:2], X1H[kc * 128:(kc + 1) * 128, :], writes=[("c_xs", kc % 2, 1)], dsem=f"c_xs{kc % 2}")
        p.add("dve" if kc % 2 == 0 else "pool", lambda e, s=s, kc=kc: e.tensor_copy(out=xb[:, kc, :], in_=s[:, :]),
              reads=[("c_xs", kc % 2, 0), ("c_xs", kc % 2, 1)], writes=[("c_xb", kc)])
    xkeys = [("c_xb", kc) for kc in range(16)]
    cw = p.sb("c_cw", [128, FC, 3], F32)
    cb = p.sb("c_cb", [128, FC], F32)
    p.dma(cw[:, :, :], CW[:, :, :], writes=["c_cw"], dsem="c_cw")
    p.dma(cb[:, :], CB[:, :], writes=["c_cb"], dsem="c_cb")
    ws = WStream(p, 16, wt=128, nbuf=3, name="c_w")
    psG = Rot([p.ps(f"c_psg{i}", [128, 512], F32) for i in range(3)], "c_psg")
    psH = Rot([p.ps(f"c_psh{i}", [128, 512], F32) for i in range(1)], "c_psh")
    psU = Rot([p.ps(f"c_psu{i}", [128, 512], F32) for i in range(4)], "c_psu")
    gx = Rot([p.sb(f"c_gx{i}", [128, T + 2], F32) for i in range(2)], "c_gx")
    acc = Rot([p.sb(f"c_acc{i}", [128, T], F32) for i in range(2)], "c_acc")
    hh = Rot([p.sb(f"c_h{i}", [128, T], BF16) for i in range(2)], "c_h")
    for fc in range(FC):
        wg, wgk = ws.load(WG, fc * 128, 128, cast=CAST_GATE)
        wu, wuk = ws.load(WU, fc * 128, 128, cast=CAST_UP)
        g, gk = gx.next()
        ph, phk = psH.next()
        for kc in range(16):
            p.add("pe", lambda e, ph=ph, wg=wg, kc=kc: e.matmul(
                ph[:, 0:2], lhsT=wg[:, kc, :], rhs=xb[:, kc, 0:2], start=(kc == 0), stop=(kc == 15)),
                reads=[wgk, xkeys[kc]], writes=[phk])
        p.add("act", lambda e, g=g, ph=ph: e.activation(out=g[:, 0:2], in_=ph[:, 0:2], func=AF.Copy),
              reads=[phk], writes=[(gk, 2)])
        for th in range(2):
            pg, pgk = psG.next()
            for kc in range(16):
                p.add("pe", lambda e, pg=pg, wg=wg, kc=kc, th=th: e.matmul(
                    pg[:, :], lhsT=wg[:, kc, :], rhs=xb[:, kc, 2 + th * 512:2 + (th + 1) * 512], start=(kc == 0), stop=(kc == 15)),
                    reads=[wgk, xkeys[kc]], writes=[pgk])
            p.add("act", lambda e, g=g, pg=pg, th=th: e.activation(out=g[:, 2 + th * 512:2 + (th + 1) * 512], in_=pg[:, :], func=AF.Copy),
                  reads=[pgk], writes=[(gk, th)])
        gkeys = [(gk, 0), (gk, 1), (gk, 2)]
        a, ak = acc.next()
        p.add("dve", lambda e, a=a, g=g, fc=fc: e.tensor_scalar(out=a[:, :], in0=g[:, 0:T], scalar1=cw[:, fc, 0:1], scalar2=cb[:, fc:fc + 1],
                                                               op0=ALU.mult, op1=ALU.add), reads=gkeys + ["c_cw", "c_cb"], writes=[ak])
        p.add("dve", lambda e, a=a, g=g, fc=fc: e.scalar_tensor_tensor(out=a[:, :], in0=g[:, 1:T + 1], scalar=cw[:, fc, 1:2], in1=a[:, :],
                                                                      op0=ALU.mult, op1=ALU.add), reads=gkeys + ["c_cw", ak], writes=[ak])
        p.add("dve", lambda e, a=a, g=g, fc=fc: e.scalar_tensor_tensor(out=a[:, :], in0=g[:, 2:T + 2], scalar=cw[:, fc, 2:3], in1=a[:, :],
                                                                      op0=ALU.mult, op1=ALU.add), reads=gkeys + ["c_cw", ak], writes=[ak])
        p.add("act", lambda e, a=a: e.activation(out=a[:, :], in_=a[:, :], func=AF.Silu), reads=[ak], writes=[ak])
        h, hk = hh.next()
        for th in range(2):
            ts = slice(th * 512, (th + 1) * 512)
            pu, puk = psU.next()
            for kc in range(16):
                p.add("pe", lambda e, pu=pu, wu=wu, kc=kc, th=th: e.matmul(
                    pu[:, :], lhsT=wu[:, kc, :], rhs=xb[:, kc, 2 + th * 512:2 + (th + 1) * 512], start=(kc == 0), stop=(kc == 15)),
                    reads=[wuk, xkeys[kc]], writes=[puk])
            p.add("dve", lambda e, h=h, a=a, pu=pu, ts=ts: e.tensor_tensor(out=h[:, ts], in0=pu[:, :], in1=a[:, ts], op=ALU.mult),
                  reads=[puk, ak], writes=[(hk, th)])
        p.dma(HT[fc * 128:(fc + 1) * 128, :], h[:, :], reads=[(hk, 0), (hk, 1)], dsem=f"c_h{hk[1]}", eng="act")


def build_stage4b():
    nc = bass.Bass("TRN2", target_bir_lowering=False)
    p = Prog(nc)
    HT = p.dram("HT", [DFF, T], BF16, "ExternalInput")
    X1T = p.dram("X1T", [2048, T], F32, "ExternalInput")
    WD = p.dram("ffn_w_down", [DFF, 2048], F32, "ExternalInput")
    LNG = p.dram("LNG", [128, 16], F32, "ExternalInput")
    LNB = p.dram("LNB", [128, 16], F32, "ExternalInput")
    OUT = p.dram("X2T", [2048, T], F32, "ExternalOutput")
    SCR = p.dram("SCR", [2048, T], F32, "Internal")
    emit_stage4b(p, HT, X1T, WD, LNG, LNB, OUT, SCR)
    p.emit()
    return nc


def emit_stage4b(p, HT, X1T, WD, LNG, LNB, OUT, SCR):
    hT = p.sb("d_hT", [128, FC, T], BF16)
    for kc in range(FC):
        p.dma(hT[:, kc, :], HT[kc * 128:(kc + 1) * 128, :], writes=[("d_hT", kc)], dsem=f"d_hT{kc % 4}")
    psY = Rot([p.ps(f"d_ps{i}", [128, 512], F32) for i in range(4)], "d_ps")
    emit_out_ln(p, "d_", hT, [("d_hT", kc) for kc in range(FC)], FC, WD, X1T, LNG, LNB, OUT, SCR, psY)


def _D(nc):
    return lambda name, shape, dt, kind="Internal": nc.dram_tensor(name, list(shape), dt, kind=kind).ap()


def _phase(nc, emit_fn):
    with nc.cleanup_on_exit():
        p = Prog(nc)
        emit_fn(p)
        p.emit()
        nc.all_engine_barrier()


def build_g1():
    nc = bass.Bass("TRN2", target_bir_lowering=False)
    D = _D(nc)
    EI, EO = "ExternalInput", "ExternalOutput"
    xT = D("xT", [2048, T], F32, EI)
    w_in = D("w_in", [2048, INC], F32, EI)
    gate = D("gate", [17, 1024], F32, EI)
    GC = D("GC", [128, 384], F32, EI)
    QT = D("QT", [3 * 1024, T], BF16, EO)
    KT = D("KT", [3 * 1024, T], BF16, EO)
    V = D("V", [3, T, 1024], BF16, EO)
    SGR = D("SGR", [2048, T], BF16, EO)
    SMA = D("SMA", [2048, T], BF16, EO)
    SMB = D("SMB", [2048, T], BF16, EO)
    GQT = D("GQT", [1024, T], BF16)
    GKT = D("GKT", [1024, T], BF16)
    GKM = D("GKM", [T, 1024], BF16)
    GVM = D("GVM", [T, 2048], BF16)
    LAM = D("LAM", [T, 1024], F32)
    OL = D("OL", [2048, T], F32, EO)
    QTL = D("QTL", [1024, T], BF16, EO)
    U = D("U", [1024, 512], F32, EO)
    DT = D("DT", [1024, 1], F32, EO)
    _phase(nc, lambda p: emit_stage1(p, xT, w_in, gate, QT, KT, V, GQT, GKT, GKM, GVM, SGR, SMA, SMB, LAM))
    _phase(nc, lambda p: emit_stage2(p, GQT, GKT, GKM, GVM, LAM, GC, OL, QTL, U, DT))
    return nc


def build_g2():
    nc = bass.Bass("TRN2", target_bir_lowering=False)
    D = _D(nc)
    EI, EO = "ExternalInput", "ExternalOutput"
    d = {}
    d["QT"] = D("QT", [3 * 1024, T], BF16, EI)
    for g in range(3):
        d[f"KH{g}"] = D(f"KH{g}", [1024, HALO[g] + T], BF16, EI)
        d[f"VH{g}"] = D(f"VH{g}", [8, 128, NTV[g], 128], BF16, EI)
    d["BT"] = D("BT", [24, 128, 256], F32, EI)
    d["VALID"] = D("VALID", [128, 256], F32, EI)
    d["HMASK"] = D("HMASK", [128, 3], F32, EI)
    d["OL"] = D("OL", [2048, T], F32, EI)
    d["QTL"] = D("QTL", [1024, T], BF16, EI)
    d["UALL"] = D("UALL", [8, 1024, 512], F32, EI)
    d["DALL"] = D("DALL", [1024, 8], F32, EI).rearrange("r (c o) -> r c o", o=1)
    d["CMASK"] = D("CMASK", [128, 8], F32, EI)
    d["SGR"] = D("SGR", [2048, T], BF16, EI)
    d["NG"] = D("NG", [128, 4], F32, EI)
    d["YAT"] = D("YAT", [1024, T], BF16)
    d["YBT"] = D("YBT", [2048, T], BF16)
    SMA = D("SMA", [2048, T], BF16, EI)
    SMB = D("SMB", [2048, T], BF16, EI)
    XT = D("xT", [2048, T], F32, EI)
    WA = D("w_proj_a", [1024, 2048], F32, EI)
    WB = D("w_proj_b", [2048, 2048], F32, EI)
    WO = D("w_out", [2048, 2048], F32, EI)
    LNG = D("LNG", [128, 16], F32, EI)
    LNB = D("LNB", [128, 16], F32, EI)
    OUT = D("X1T", [2048, T], F32, EO)
    SCR = D("SCR", [2048, T], F32)

    def ph1(p):
        emit_attn(p, d)
        emit_gla_fin(p, d)
    _phase(nc, ph1)
    _phase(nc, lambda p: emit_stage3b(p, d["YAT"], d["YBT"], SMA, SMB, XT, WA, WB, WO, LNG, LNB, OUT, SCR))
    return nc


def build_g3():
    nc = bass.Bass("TRN2", target_bir_lowering=False)
    D = _D(nc)
    EI, EO = "ExternalInput", "ExternalOutput"
    X1T = D("X1T", [2048, T], F32, EI)
    X1H = D("X1H", [2048, 2], F32, EI)
    WG = D("ffn_w_gate", [2048, DFF], F32, EI)
    WU = D("ffn_w_up", [2048, DFF], F32, EI)
    CW = D("CW", [128, FC, 3], F32, EI)
    CB = D("CB", [128, FC], F32, EI)
    WD = D("ffn_w_down", [DFF, 2048], F32, EI)
    LNG = D("LNG", [128, 16], F32, EI)
    LNB = D("LNB", [128, 16], F32, EI)
    HT = D("HT", [DFF, T], BF16)
    OUT = D("X2T", [2048, T], F32, EO)
    SCR = D("SCR", [2048, T], F32)
    _phase(nc, lambda p: emit_stage4a(p, X1T, X1H, WG, WU, CW, CB, HT))
    _phase(nc, lambda p: emit_stage4b(p, HT, X1T, WD, LNG, LNB, OUT, SCR))
    return nc


HALO = (128, 512, 2048); DIL = (1, 4, 16)

def t5_bucket(dist):
    dist = np.asarray(dist)
    df = np.maximum(dist, 1).astype(np.float32)
    large = 16 + (np.log(df / np.float32(16)) / np.float32(np.log(2048 / 16)) * np.float32(16)).astype(np.int32)
    return np.where(dist < 16, dist, np.minimum(large, 31))

def bias_tables(rel_bias):
    ki = np.arange(128)[:, None]; qi = np.arange(128)[None, :]
    off_prev = 128 + qi - ki
    off_cur = qi - ki
    valid = np.concatenate([(off_prev <= 128), (off_cur >= 0)], axis=1).astype(np.float32)
    BT = np.zeros((24, 128, 256), np.float32)
    for g in range(3):
        bp = t5_bucket(DIL[g] * np.clip(off_prev, 0, 128))
        bc = t5_bucket(DIL[g] * np.clip(off_cur, 0, 128))
        for h in range(8):
            BT[g * 8 + h, :, 0:128] = rel_bias[bp, g * 8 + h]
            BT[g * 8 + h, :, 128:256] = rel_bias[bc, g * 8 + h]
    return BT, valid

def hmask(c):
    m = np.zeros((128, 3), np.float32)
    if c > 0:
        m[:, 0] = 1; m[:, 1] = 1
    if c == 1:
        m[64:, 2] = 1
    elif c >= 2:
        m[:, 2] = 1
    return m

def cmask(c):
    m = np.zeros((128, 8), np.float32)
    m[:, :c] = 1
    return m


def v_layout(vh, g):
    nt = (9, 12, 32)[g]
    out = np.zeros((8, 128, nt, 128), vh.dtype)
    v4 = vh.reshape(vh.shape[0], 8, 128)
    if g == 0:
        out[:] = v4.reshape(9, 128, 8, 128).transpose(2, 1, 0, 3)
    elif g == 1:
        for r in range(4):
            sub = v4[r::4]
            out[:, :, r * 3:(r + 1) * 3, :] = sub.reshape(3, 128, 8, 128).transpose(2, 1, 0, 3)
    else:
        for r in range(16):
            out[:, :, r, :] = v4[r:2048:16].transpose(1, 0, 2)
            out[:, 0:64, 16 + r, :] = v4[2048 + r::16].transpose(1, 0, 2)
    return out


_PROGS = {}


def _prog(name, builder):
    if name not in _PROGS:
        _PROGS[name] = builder()
    return _PROGS[name]


def _run(name, builder, in_maps):
    nc = _prog(name, builder)
    res = run_bass_kernel_spmd(nc, in_maps, core_ids=list(range(NCORES)))
    return res.results


NCORES = 8


def _lnp(v):
    return np.ascontiguousarray(v.reshape(16, 128).T)


def kernel(x, w_in, gla_gate_w, gla_gate_b, gla_norm_g, w_proj_a, w_proj_b, w_out, rel_bias,
           ln1_g, ln1_b, ffn_w_gate, ffn_w_up, ffn_conv_w, ffn_conv_b, ffn_w_down, ln2_g, ln2_b):
    f32 = np.float32
    x = np.asarray(x, f32)[0]
    C = NCORES
    xT = [np.ascontiguousarray(x[c * T:(c + 1) * T].T) for c in range(C)]
    BT, valid = bias_tables(np.asarray(rel_bias, f32))
    gcon = gla_consts()
    hm = [hmask(c) for c in range(C)]
    cm = [cmask(c) for c in range(C)]
    for l in range(4):
        gate = np.concatenate([np.asarray(gla_gate_w[l], f32), np.asarray(gla_gate_b[l], f32)[None]], 0)
        wl = np.asarray(w_in[l], f32)
        r1 = _run("g1", build_g1, [{"xT": xT[c], "w_in": wl, "gate": gate, "GC": gcon} for c in range(C)])
        r2 = r1
        UALL = np.stack([r2[c]["U"] for c in range(C)], 0)
        DALL = np.ascontiguousarray(np.concatenate([r2[c]["DT"] for c in range(C)], 1))
        KH, VH = [], []
        for g in range(3):
            H = HALO[g]
            kt_all = np.concatenate([r1[c]["KT"][g * 1024:(g + 1) * 1024] for c in range(C)], 1)
            v_all = np.concatenate([r1[c]["V"][g] for c in range(C)], 0)
            kt_pad = np.concatenate([np.zeros((1024, H), kt_all.dtype), kt_all], 1)
            v_pad = np.concatenate([np.zeros((H, 1024), v_all.dtype), v_all], 0)
            KH.append([np.ascontiguousarray(kt_pad[:, c * T:c * T + H + T]) for c in range(C)])
            VH.append([v_layout(v_pad[c * T:c * T + H + T], g) for c in range(C)])
        ng = np.ascontiguousarray(np.asarray(gla_norm_g[l], f32).reshape(4, 128).T)
        im = []
        for c in range(C):
            m = {"QT": r1[c]["QT"], "BT": BT, "VALID": valid, "HMASK": hm[c], "OL": r2[c]["OL"], "QTL": r2[c]["QTL"],
                 "UALL": UALL, "DALL": DALL, "CMASK": cm[c], "SGR": r1[c]["SGR"], "NG": ng}
            for g in range(3):
                m[f"KH{g}"] = KH[g][c]
                m[f"VH{g}"] = VH[g][c]
            im.append(m)
        for c in range(C):
            im[c].update({"SMA": r1[c]["SMA"], "SMB": r1[c]["SMB"], "xT": xT[c], "w_proj_a": np.asarray(w_proj_a[l], f32),
                          "w_proj_b": np.asarray(w_proj_b[l], f32), "w_out": np.asarray(w_out[l], f32),
                          "LNG": _lnp(np.asarray(ln1_g[l], f32)), "LNB": _lnp(np.asarray(ln1_b[l], f32))})
        r3b = _run("g2", build_g2, im)
        x1T = [r3b[c]["X1T"] for c in range(C)]
        x1h = [np.zeros((2048, 2), f32)] + [np.ascontiguousarray(x1T[c - 1][:, T - 2:T]) for c in range(1, C)]
        cw = np.ascontiguousarray(np.asarray(ffn_conv_w[l], f32).reshape(3, FC, 128).transpose(2, 1, 0))
        cb = np.ascontiguousarray(np.asarray(ffn_conv_b[l], f32).reshape(FC, 128).T)
        r5 = _run("g3", build_g3, [{"X1T": x1T[c], "X1H": x1h[c], "ffn_w_gate": np.asarray(ffn_w_gate[l], f32),
                                    "ffn_w_up": np.asarray(ffn_w_up[l], f32), "CW": cw, "CB": cb,
                                    "ffn_w_down": np.asarray(ffn_w_down[l], f32),
                                    "LNG": _lnp(np.asarray(ln2_g[l], f32)), "LNB": _lnp(np.asarray(ln2_b[l], f32))}
                                   for c in range(C)])
        xT = [r5[c]["X2T"] for c in range(C)]
    out = np.concatenate([np.ascontiguousarray(np.asarray(xT[c], f32).T) for c in range(C)], 0)
    return out[None].astype(f32)
```
